# Optimizing a Trainium2 kernel written in Bass

```python
import jax, jax.numpy as jnp
from jax import lax
import numpy as np

D_MODEL = 1024
BATCH = 2
SEQ = 8192
DEPTH = 1
DEC_BATCH = 128
DEC_SEQ = 4
PAST_LEN = 8192
PAGE_SIZE = 128

HEAD_DIM = 64
NSA_WIDTH = D_MODEL // 2
N_HEADS = NSA_WIDTH // HEAD_DIM
N_KV = 2
HPG = N_HEADS // N_KV
KV_W = N_KV * HEAD_DIM
N_BRANCH = 3
L_CMP = 32
L_SEL = 64
N_SEL = 16
WINDOW = 512
Q_BLOCK = 128
POOL_WIDTH = D_MODEL - NSA_WIDTH
POOL_WINDOWS = (2, 4, 8, 16)
POOL_GROUP = POOL_WIDTH // len(POOL_WINDOWS)
POOL_STATE = max(POOL_WINDOWS) - 1
D_FF = 2816
N_MOD = 9
IN_WIDTHS = (NSA_WIDTH, KV_W, KV_W, KV_W, KV_W, KV_W, KV_W, N_HEADS * N_BRANCH, POOL_WIDTH)
IN_DIM = sum(IN_WIDTHS)
SCALE = HEAD_DIM ** -0.5
RMS_EPS = 1e-6
NEG = -1e30
FORCE = 1e4

kernel_name = 'nsa_pool_macaron_hybrid_step'


def rmsnorm(x, g):
    xf = x.astype(jnp.float32)
    y = xf * lax.rsqrt(jnp.mean(xf * xf, axis=-1, keepdims=True) + RMS_EPS)
    return (y * g.astype(jnp.float32)).astype(x.dtype)


def swiglu(h, wi, wo):
    a, b = jnp.split(h @ wi, 2, axis=-1)
    return (jax.nn.silu(a) * b) @ wo


def in_projection(h, w_in):
    B, T, _ = h.shape
    z = h @ w_in
    parts, o = [], 0
    for w in IN_WIDTHS:
        parts.append(z[..., o:o + w])
        o += w
    q, kc, vc, ks, vs, kw, vw, g, u = parts
    kv = lambda a: a.reshape(B, T, N_KV, HEAD_DIM)
    gates = jax.nn.sigmoid(g.astype(jnp.float32)).astype(h.dtype).reshape(B, T, N_HEADS, N_BRANCH)
    return (q.reshape(B, T, N_HEADS, HEAD_DIM), kv(kc), kv(vc), kv(ks), kv(vs), kv(kw), kv(vw), gates, u)


def compress(rows, w):
    B, n, G, hd = rows.shape
    blocks = rows.reshape(B, n // L_CMP, L_CMP, G, hd)
    return jnp.einsum('bnlgh,l->bngh', blocks, w)


def cmp_attend(q, qpos, kc, vc):
    B, T = q.shape[:2]
    nc = kc.shape[1]
    qg = q.reshape(B, T, N_KV, HPG, HEAD_DIM)
    s = jnp.einsum('btgph,bngh->btgpn', qg, kc, preferred_element_type=jnp.float32) * SCALE
    valid = ((jnp.arange(nc) + 1) * L_CMP - 1)[None, :] <= qpos[:, None]
    m = valid[None, :, None, None, :]
    p = jnp.where(m, jax.nn.softmax(jnp.where(m, s, NEG), axis=-1), 0.0)
    o = jnp.einsum('btgpn,bngh->btgph', p.astype(vc.dtype), vc)
    return o.reshape(B, T, N_HEADS, HEAD_DIM), p


def select_blocks(p, qpos, n_blk):
    B, T, G, _, nc = p.shape
    r = L_SEL // L_CMP
    imp = jnp.pad(p.sum(3), ((0, 0), (0, 0), (0, 0), (0, n_blk * r - nc)))
    imp = imp.reshape(B, T, G, n_blk, r).sum(-1)
    blk = jnp.arange(n_blk)[None, :]
    cur = (qpos // L_SEL)[:, None]
    start_ok = blk * L_SEL <= qpos[:, None]
    forced = (blk == 0) | (blk == cur) | (blk == cur - 1)
    bonus = jnp.where(forced, FORCE, jnp.where(start_ok, 0.0, NEG))
    _, idx = lax.top_k(imp + bonus[None, :, None, :], min(N_SEL, n_blk))
    return idx


def sel_attend(q, qpos, idx, kg, vg):
    B, T = q.shape[:2]
    n = idx.shape[-1]
    qg = q.reshape(B, T, N_KV, HPG, HEAD_DIM)
    s = jnp.einsum('btgph,btgnlh->btgpnl', qg, kg, preferred_element_type=jnp.float32) * SCALE
    kpos = idx[..., None] * L_SEL + jnp.arange(L_SEL)
    m = (kpos <= qpos[None, :, None, None, None])[:, :, :, None]
    s = jnp.where(m, s, NEG).reshape(B, T, N_KV, HPG, n * L_SEL)
    p = jax.nn.softmax(s, axis=-1).reshape(B, T, N_KV, HPG, n, L_SEL)
    o = jnp.einsum('btgpnl,btgnlh->btgph', p.astype(vg.dtype), vg)
    return o.reshape(B, T, N_HEADS, HEAD_DIM)


def band_attend(q, qpos, k, v, kpos):
    s = jnp.einsum('bnqgph,bnkgh->bnqgpk', q, k, preferred_element_type=jnp.float32) * SCALE
    dq = qpos[:, :, None]
    dk = kpos[:, None, :]
    m = (dk <= dq) & (dk > dq - WINDOW) & (dk >= 0)
    p = jax.nn.softmax(jnp.where(m[None, :, :, None, None, :], s, NEG), axis=-1)
    return jnp.einsum('bnqgpk,bnkgh->bnqgph', p.astype(v.dtype), v)


def gather_contig(src, idx):
    B = src.shape[0]
    rows = idx[..., None] * L_SEL + jnp.arange(L_SEL)
    bi = jnp.arange(B)[:, None, None, None, None]
    gi = jnp.arange(N_KV)[None, None, :, None, None]
    return src[bi, rows, gi]


def gather_sel_paged(pool, page_table, new, idx, n_past_blk):
    DB, T = new.shape[:2]
    n_new_blk = -(-T // L_SEL)
    l = jnp.arange(L_SEL)
    bi = jnp.arange(DB)[:, None, None, None, None]
    gi = jnp.arange(N_KV)[None, None, :, None, None]
    past = idx < n_past_blk
    pos = jnp.where(past, idx, 0)[..., None] * L_SEL + l
    phys = page_table[bi, pos // PAGE_SIZE]
    from_pool = pool[phys, pos % PAGE_SIZE, gi]
    new_pad = jnp.pad(new, ((0, 0), (0, n_new_blk * L_SEL - T), (0, 0), (0, 0)))
    nrow = jnp.where(past, 0, idx - n_past_blk)[..., None] * L_SEL + l
    from_new = new_pad[bi, nrow, gi]
    return jnp.where(past[..., None, None], from_pool, from_new)


def pool_mix(u_ext, pos0, pool_w, pool_scale):
    B, L, _ = u_ext.shape
    T = L - POOL_STATE
    cs = jnp.pad(jnp.cumsum(u_ext.astype(jnp.float32), axis=1), ((0, 0), (1, 0), (0, 0)))
    t = pos0 + jnp.arange(T)
    hi = cs[:, POOL_STATE + 1:]
    u_new = u_ext[:, POOL_STATE:].astype(jnp.float32)
    diffs = []
    for gi, w in enumerate(POOL_WINDOWS):
        c0, c1 = gi * POOL_GROUP, (gi + 1) * POOL_GROUP
        lo = cs[:, POOL_STATE + 1 - w:POOL_STATE + 1 - w + T, c0:c1]
        cnt = jnp.minimum(w, t + 1).astype(jnp.float32)[None, :, None]
        diffs.append((hi[..., c0:c1] - lo) / cnt - u_new[..., c0:c1])
    d = jnp.stack(diffs, axis=2).astype(u_ext.dtype)
    y = jnp.einsum('btgc,gcd->btgd', d, pool_w).reshape(B, T, POOL_WIDTH)
    return y * pool_scale


def mixer_out(gates, o_cmp, o_sel, o_win, y_pool, w_out):
    B, T = gates.shape[:2]
    o_nsa = (gates[..., 0:1] * o_cmp + gates[..., 1:2] * o_sel + gates[..., 2:3] * o_win).reshape(B, T, NSA_WIDTH)
    return jnp.concatenate([o_nsa, y_pool], axis=-1) @ w_out


def mixer_prompt(h, w_in, w_out, cmp_w, pool_w, pool_scale):
    B, S, _ = h.shape
    q, kc_r, vc_r, ks, vs, kw, vw, gates, u = in_projection(h, w_in)
    qpos = jnp.arange(S)
    kc = compress(kc_r, cmp_w[0])
    vc = compress(vc_r, cmp_w[1])
    o_cmp, p_cmp = cmp_attend(q, qpos, kc, vc)
    idx = select_blocks(p_cmp, qpos, S // L_SEL)
    nqb = S // Q_BLOCK

    def to_blocks(a):
        return jnp.moveaxis(a.reshape(B, nqb, Q_BLOCK, *a.shape[2:]), 1, 0)

    def sel_block(args):
        qb, pb, ib = args
        return sel_attend(qb, pb, ib, gather_contig(ks, ib), gather_contig(vs, ib))

    o_sel = lax.map(sel_block, (to_blocks(q), qpos.reshape(nqb, Q_BLOCK), to_blocks(idx)))
    o_sel = jnp.moveaxis(o_sel, 0, 1).reshape(B, S, N_HEADS, HEAD_DIM)
    nwb = WINDOW // Q_BLOCK

    def band(a):
        ap = jnp.pad(a, ((0, 0), (WINDOW, 0), (0, 0), (0, 0))).reshape(B, nqb + nwb, Q_BLOCK, N_KV, HEAD_DIM)
        return jnp.concatenate([ap[:, j:j + nqb] for j in range(nwb + 1)], axis=2)

    kpos = (jnp.arange(nqb) * Q_BLOCK)[:, None] - WINDOW + jnp.arange((nwb + 1) * Q_BLOCK)[None, :]
    qb = q.reshape(B, nqb, Q_BLOCK, N_KV, HPG, HEAD_DIM)
    o_win = band_attend(qb, qpos.reshape(nqb, Q_BLOCK), band(kw), band(vw), kpos).reshape(B, S, N_HEADS, HEAD_DIM)
    u_ext = jnp.pad(u, ((0, 0), (POOL_STATE, 0), (0, 0)))
    y_pool = pool_mix(u_ext, 0, pool_w, pool_scale)
    y = mixer_out(gates, o_cmp, o_sel, o_win, y_pool, w_out)
    keep = min(WINDOW, S)
    state = (kc_r, vc_r, ks, vs, kw[:, S - keep:], vw[:, S - keep:], u_ext[:, u_ext.shape[1] - POOL_STATE:])
    return y, state


def mixer_sample(h, cache_cmp_k, cache_cmp_v, cache_sel_k, cache_sel_v, state_win_k, state_win_v,
                 state_pool, page_table, w_in, w_out, cmp_w, pool_w, pool_scale):
    DB, T, _ = h.shape
    P = page_table.shape[1] * PAGE_SIZE
    q, kc_r, vc_r, ks, vs, kw, vw, gates, u = in_projection(h, w_in)
    qpos = P + jnp.arange(T)
    past = lambda pool: pool[page_table].reshape(DB, P, N_KV, HEAD_DIM)
    n_full = (T // L_CMP) * L_CMP
    kc = jnp.concatenate([compress(past(cache_cmp_k), cmp_w[0]), compress(kc_r[:, :n_full], cmp_w[0])], axis=1)
    vc = jnp.concatenate([compress(past(cache_cmp_v), cmp_w[1]), compress(vc_r[:, :n_full], cmp_w[1])], axis=1)
    o_cmp, p_cmp = cmp_attend(q, qpos, kc, vc)
    n_past_blk = P // L_SEL
    n_blk = n_past_blk + -(-T // L_SEL)
    idx = select_blocks(p_cmp, qpos, n_blk)
    kg = gather_sel_paged(cache_sel_k, page_table, ks, idx, n_past_blk)
    vg = gather_sel_paged(cache_sel_v, page_table, vs, idx, n_past_blk)
    o_sel = sel_attend(q, qpos, idx, kg, vg)
    wb = state_win_k.shape[1]
    kwin = jnp.concatenate([state_win_k, kw], axis=1)
    vwin = jnp.concatenate([state_win_v, vw], axis=1)
    kpos = jnp.concatenate([P - wb + jnp.arange(wb), qpos])
    qb = q.reshape(DB, 1, T, N_KV, HPG, HEAD_DIM)
    o_win = band_attend(qb, qpos[None], kwin[:, None], vwin[:, None], kpos[None]).reshape(DB, T, N_HEADS, HEAD_DIM)
    u_ext = jnp.concatenate([state_pool, u], axis=1)
    y_pool = pool_mix(u_ext, P, pool_w, pool_scale)
    y = mixer_out(gates, o_cmp, o_sel, o_win, y_pool, w_out)
    keep = min(WINDOW, wb + T)
    state = (kc_r, vc_r, ks, vs, kwin[:, wb + T - keep:], vwin[:, wb + T - keep:], u_ext[:, u_ext.shape[1] - POOL_STATE:])
    return y, state


def layer(x, c, mix, w_ada, b_ada, gains, ffn1_wi, ffn1_wo, ffn2_wi, ffn2_wo):
    mod = (jax.nn.silu(c) @ w_ada + b_ada).reshape(c.shape[0], N_MOD, 1, D_MODEL)
    sh1, sc1, g1, sh2, sc2, g2, sh3, sc3, g3 = [mod[:, i] for i in range(N_MOD)]
    h = rmsnorm(x, gains[0]) * (1 + sc1) + sh1
    x = x + 0.5 * g1 * rmsnorm(swiglu(h, ffn1_wi, ffn1_wo), gains[1])
    h = rmsnorm(x, gains[2]) * (1 + sc2) + sh2
    y, state = mix(h)
    x = x + g2 * rmsnorm(y, gains[3])
    h = rmsnorm(x, gains[4]) * (1 + sc3) + sh3
    x = x + 0.5 * g3 * rmsnorm(swiglu(h, ffn2_wi, ffn2_wo), gains[5])
    return x, state


def setup_inputs(seed: int = 0) -> dict:
    key = jax.random.key(seed)
    k = jax.random.split(key, 24)
    f32 = jnp.float32
    n_pages = PAST_LEN // PAGE_SIZE
    n_used = DEC_BATCH * n_pages
    n_phys = n_used + max(1, n_used // 4)
    win_buf = min(WINDOW, PAST_LEN)

    def nrm(kk, shape, s=1.0):
        return jax.random.normal(kk, shape, f32) * s

    page_table = jax.random.permutation(k[0], n_phys)[:n_used].reshape(DEC_BATCH, n_pages).astype(jnp.int32)
    cache_shape = (DEPTH, n_phys, PAGE_SIZE, N_KV, HEAD_DIM)
    win_shape = (DEPTH, DEC_BATCH, win_buf, N_KV, HEAD_DIM)
    return {
        'x_prompt': nrm(k[1], (BATCH, SEQ, D_MODEL)),
        'x_sample': nrm(k[2], (DEC_BATCH, DEC_SEQ, D_MODEL)),
        'cache_cmp_k': nrm(k[3], cache_shape),
        'cache_cmp_v': nrm(k[4], cache_shape),
        'cache_sel_k': nrm(k[5], cache_shape),
        'cache_sel_v': nrm(k[6], cache_shape),
        'state_win_k': nrm(k[7], win_shape),
        'state_win_v': nrm(k[8], win_shape),
        'state_pool': nrm(k[9], (DEPTH, DEC_BATCH, POOL_STATE, POOL_WIDTH)),
        'page_table': page_table,
        'c_prompt': nrm(k[10], (BATCH, D_MODEL)),
        'c_sample': nrm(k[11], (DEC_BATCH, D_MODEL)),
        'w_ada': nrm(k[12], (DEPTH, D_MODEL, N_MOD * D_MODEL), 0.5 * D_MODEL ** -0.5),
        'b_ada': nrm(k[13], (DEPTH, N_MOD * D_MODEL), 0.01),
        'norm_gains': 1.0 + nrm(k[14], (DEPTH, 6, D_MODEL), 0.05),
        'ffn1_wi': nrm(k[15], (DEPTH, D_MODEL, 2 * D_FF), D_MODEL ** -0.5),
        'ffn1_wo': nrm(k[16], (DEPTH, D_FF, D_MODEL), D_FF ** -0.5),
        'ffn2_wi': nrm(k[17], (DEPTH, D_MODEL, 2 * D_FF), D_MODEL ** -0.5),
        'ffn2_wo': nrm(k[18], (DEPTH, D_FF, D_MODEL), D_FF ** -0.5),
        'w_in': nrm(k[19], (DEPTH, D_MODEL, IN_DIM), D_MODEL ** -0.5),
        'w_out': nrm(k[20], (DEPTH, NSA_WIDTH + POOL_WIDTH, D_MODEL), D_MODEL ** -0.5),
        'cmp_w': (1.0 + nrm(k[21], (DEPTH, 2, L_CMP), 0.1)) * L_CMP ** -0.5,
        'pool_w': nrm(k[22], (DEPTH, len(POOL_WINDOWS), POOL_GROUP, POOL_GROUP), POOL_GROUP ** -0.5),
        'pool_scale': 1.0 + nrm(k[23], (DEPTH, POOL_WIDTH), 0.1),
    }


def reference(x_prompt, x_sample, cache_cmp_k, cache_cmp_v, cache_sel_k, cache_sel_v, state_win_k,
              state_win_v, state_pool, page_table, c_prompt, c_sample, w_ada, b_ada, norm_gains,
              ffn1_wi, ffn1_wo, ffn2_wi, ffn2_wo, w_in, w_out, cmp_w, pool_w, pool_scale):
    xp, xs = x_prompt, x_sample
    acc_p = [[] for _ in range(7)]
    acc_s = [[] for _ in range(7)]
    for l in range(DEPTH):
        def mix_p(h):
            return mixer_prompt(h, w_in[l], w_out[l], cmp_w[l], pool_w[l], pool_scale[l])

        def mix_s(h):
            return mixer_sample(h, cache_cmp_k[l], cache_cmp_v[l], cache_sel_k[l], cache_sel_v[l],
                                state_win_k[l], state_win_v[l], state_pool[l], page_table,
                                w_in[l], w_out[l], cmp_w[l], pool_w[l], pool_scale[l])

        xp, st_p = layer(xp, c_prompt, mix_p, w_ada[l], b_ada[l], norm_gains[l],
                         ffn1_wi[l], ffn1_wo[l], ffn2_wi[l], ffn2_wo[l])
        xs, st_s = layer(xs, c_sample, mix_s, w_ada[l], b_ada[l], norm_gains[l],
                         ffn1_wi[l], ffn1_wo[l], ffn2_wi[l], ffn2_wo[l])
        for a, s in zip(acc_p, st_p):
            a.append(s)
        for a, s in zip(acc_s, st_s):
            a.append(s)
    p_cmp_k, p_cmp_v, p_sel_k, p_sel_v, p_win_k, p_win_v, p_pool = [jnp.stack(a) for a in acc_p]
    s_cmp_k, s_cmp_v, s_sel_k, s_sel_v, s_win_k, s_win_v, s_pool = [jnp.stack(a) for a in acc_s]
    return (xp, xs, p_cmp_k, p_cmp_v, p_sel_k, p_sel_v, p_win_k, p_win_v, p_pool,
            s_cmp_k, s_cmp_v, s_sel_k, s_sel_v, s_win_k, s_win_v, s_pool)
```

```python
from contextlib import ExitStack
import numpy as np
import ml_dtypes
import concourse.bass as bass
import concourse.mybir as mybir
from concourse.bass import ds
from concourse.bass_utils import run_bass_kernel_spmd

F32 = mybir.dt.float32
BF16 = mybir.dt.bfloat16
I32 = mybir.dt.int32
AF = mybir.ActivationFunctionType
ALU = mybir.AluOpType

NGRP = 1
NCORES = 2 * NGRP
D = 1024
S = 8192
NBLK = 64
NOWN = 64 // NGRP
NSEQ = 16
NTS = 64
DFF = 2816
NF = 22
INW = 1816
EPS = 1e-6
ENG = ("sync", "scalar", "vector", "gpsimd", "tensor")


class Res:
    def __init__(self, name):
        self.name = name
        self.lw = None
        self.rd = []
        self.sem = None
        self.cnt = 0


class _Rec:
    def __init__(self):
        self.call = None

    def __getattr__(self, name):
        def m(*a, **k):
            self.call = (name, a, k)
            return self
        return m


class Sched:
    def __init__(self, nc, stack):
        self.nc = nc
        self.stack = stack
        self.q = {e: [] for e in ENG}
        self.cnt = {e: 0 for e in ENG}
        self.waited = {e: {} for e in ENG}
        self.ep = 0
        self.esem = {(e, 0): stack.enter_context(nc.semaphore("es_" + e)) for e in ENG}
        self.miles = {(e, 0): set() for e in ENG}
        self.dres = []

    def new_sem_epoch(self):
        self.barrier()
        self.ep += 1
        for e in ENG:
            self.esem[(e, self.ep)] = self.stack.enter_context(self.nc.semaphore("es%d_%s" % (self.ep, e)))
            self.miles[(e, self.ep)] = set()
            self.cnt[e] = 0
            self.waited[e] = {k: v for k, v in self.waited[e].items() if k[0] == "D"}

    def _wait(self, eng, tok):
        if tok[0] == "E":
            if tok[3] != self.ep:
                return
            if tok[1] == eng and eng == "tensor":
                return
            key = ("E", tok[1]); val = tok[2]
        else:
            key = ("D", id(tok[1])); val = tok[2]
        if self.waited[eng].get(key, 0) >= val:
            return
        self.waited[eng][key] = val
        if tok[0] == "E":
            self.miles[(tok[1], self.ep)].add(val)
            self.q[eng].append(("we", (tok[1], self.ep), val))
        else:
            self.q[eng].append(("wd", tok[1].sem, val))

    def op(self, eng, fn, rd=(), wr=(), dma=None, deferred=False, indep=False):
        if not deferred:
            rec = _Rec()
            fn(rec)
            name, a, k = rec.call
            fn = (lambda e, name=name, a=a, k=k: getattr(e, name)(*a, **k))
        deps = []
        for b in rd:
            if b.lw is not None:
                deps.append(b.lw)
        for b in wr:
            if b.lw is not None and not indep:
                deps.append(b.lw)
            deps.extend(b.rd)
        for d in deps:
            self._wait(eng, d)
        if dma is None:
            self.cnt[eng] += 1
            tok = ("E", eng, self.cnt[eng], self.ep)
            self.q[eng].append(("oe", fn, self.cnt[eng], self.ep))
        else:
            if dma.sem is None:
                dma.sem = self.stack.enter_context(self.nc.semaphore("ds_" + dma.name))
                self.dres.append(dma)
            dma.cnt += 16
            assert dma.cnt < 30000, ("dma semaphore would overflow", dma.name)
            tok = ("D", dma, dma.cnt)
            self.q[eng].append(("od", fn, dma.sem))
        for b in rd:
            b.rd.append(tok)
        for b in wr:
            b.lw = tok
            b.rd = []
        return tok

    def barrier(self):
        for e in ENG:
            for e2 in ENG:
                if self.cnt[e2] > 0:
                    self._wait(e, ("E", e2, self.cnt[e2], self.ep))
            for d in self.dres:
                if d.cnt > 0:
                    self._wait(e, ("D", d, d.cnt))

    def emit(self):
        nc = self.nc
        self.barrier()
        rank = {}
        for key_ in self.miles:
            ks = sorted(self.miles[key_])
            rank[key_] = {k: i + 1 for i, k in enumerate(ks)}
            assert len(ks) < 30000, ("engine semaphore would overflow", key_, len(ks))
        self.n_miles = {k: len(v) for k, v in rank.items()}
        with nc.Block() as block:
            def mk(ename):
                def run(eng):
                    for it in self.q[ename]:
                        if it[0] == "we":
                            eng.wait_ge(self.esem[it[1]], rank[it[1]][it[2]])
                        elif it[0] == "wd":
                            eng.wait_ge(it[1], it[2])
                        elif it[0] == "oe":
                            ins = it[1](eng)
                            if it[2] in rank[(ename, it[3])]:
                                ins.then_inc(self.esem[(ename, it[3])], 1)
                        else:
                            ins = it[1](eng)
                            ins.then_inc(it[2], 16)
                return run
            block.sync(mk("sync"))
            block.scalar(mk("scalar"))
            block.vector(mk("vector"))
            block.gpsimd(mk("gpsimd"))
            block.tensor(mk("tensor"))


class Arena:
    def __init__(self, nc, stack, nbytes):
        self.t = stack.enter_context(nc.sbuf_tensor("arena", [128, nbytes // 4], F32))
        self.nbytes = nbytes
        self.top = 0
        self.n = 0

    def mark(self):
        return self.top

    def reset(self, m):
        self.top = m

    def alloc(self, name, free_shape, dtype, res=None):
        esz = 2 if dtype == BF16 else 4
        n = int(np.prod(free_shape))
        nb = (n * esz + 31) // 32 * 32
        assert self.top + nb <= self.nbytes, (name, self.top, nb, self.nbytes)
        off = self.top
        self.top += nb
        v = self.t[:, off // 4:(off + nb) // 4]
        if dtype != F32:
            v = v.bitcast(dtype)
        v = v[:, 0:n]
        if len(free_shape) == 2:
            v = v.rearrange("p (a b) -> p a b", a=free_shape[0])
        elif len(free_shape) == 3:
            v = v.rearrange("p (a b c) -> p a b c", a=free_shape[0], b=free_shape[1])
        self.n += 1
        return v, (res if res is not None else Res(name + str(self.n)))


SCALE = 0.125
NEGM = -30000.0


def bc_mid(a, n):
    return bass.AP(tensor=a.tensor, offset=a.offset, ap=[list(a.ap[0]), [0, n], list(a.ap[1])])


def bc_last(a, n):
    return bass.AP(tensor=a.tensor, offset=a.offset, ap=[list(a.ap[0]), list(a.ap[1]), [0, n]])


def build_nc(NSEQ=64, NPHYS=10240):
    NSG = NSEQ // 16
    NTS = 4 * NSEQ
    XROWS = S + NTS
    OROWS = NOWN * 128 + NTS
    M17 = 1 + NSEQ
    nc = bass.Bass("TRN2", target_bir_lowering=False)
    stack = ExitStack()
    dt_in = lambda n, s, d=F32: nc.dram_tensor(n, s, d, kind="ExternalInput").ap()
    dt_out = lambda n, s, d=F32: nc.dram_tensor(n, s, d, kind="ExternalOutput").ap()
    dt_int = lambda n, s, d=F32: nc.dram_tensor(n, s, d, kind="Internal").ap()
    xp = dt_in("xp", [S, D])
    xs = dt_in("xs", [NTS, D])
    c17 = dt_in("c17", [M17, D])
    w_ada = dt_in("w_ada", [D, 9 * D])
    b_ada = dt_in("b_ada", [1, 9 * D])
    gains = dt_in("gains", [6, D])
    f1wi = dt_in("f1wi", [D, 2 * DFF])
    f1wo = dt_in("f1wo", [DFF, D])
    f2wi = dt_in("f2wi", [D, 2 * DFF])
    f2wo = dt_in("f2wo", [DFF, D])
    w_in = dt_in("w_in", [D, INW])
    w_out = dt_in("w_out", [D, D])
    cmpw = dt_in("cmpw", [2, 32])
    pool_w = dt_in("pool_w", [4, 128, 128])
    pool_scale = dt_in("pool_scale", [1, 512])
    ident_d = dt_in("ident", [128, 128])
    pair_d = dt_in("pair", [128, 64])
    bmask_d = dt_in("bmask", [128, 4])
    rinfo_d = dt_in("rinfo", [1, 8], I32)
    cm_d = dt_in("cm", [NGRP, 128, 128])
    bonus_d = dt_in("bonus", [NOWN, 128, 128])
    cmpmask_d = dt_in("cmpmask", [NOWN, 2, 128, 128])
    wm_d = dt_in("wm", [5, 5, 128, 128])
    wp_d = dt_in("wp", [2, 4, 2, 128, 128])
    st_wk = dt_in("st_wk", [NSEQ, 512, 128])
    st_wv = dt_in("st_wv", [NSEQ, 512, 128])
    st_pool = dt_in("st_pool", [NSEQ, 15, 512])
    pt_d = dt_in("pt", [NSEQ, 64], I32)
    caches = [dt_in(nm, [NPHYS * 128, 128]) for nm in ("c_cmp_k", "c_cmp_v", "c_sel_k", "c_sel_v")]
    bonus_s_d = dt_in("bonus_s", [4, 128])
    causal4_d = dt_in("causal4", [4, 4])
    wms0_d = dt_in("wms0", [128, 4])
    wsa_d = dt_in("wsa", [120, 4, 64])
    wsb_d = dt_in("wsb", [120, 4, 64])
    wn_d = dt_in("wn", [64, 4, 64])

    o_kv = dt_out("o_kv", [6, XROWS, 128])
    o_u = dt_out("o_u", [XROWS, 512])
    o_y = dt_out("o_y", [OROWS, D])
    o_swin = dt_out("o_swin", [2, NSEQ, 512, 128])
    o_spool = dt_out("o_spool", [NSEQ, 15, 512])

    modd = dt_int("modd", [M17, 9 * D])
    x1s = dt_int("x1s", [XROWS, D])
    x2s = dt_int("x2s", [OROWS, D])
    ksT_s = dt_int("ksT_s", [128, S], BF16)
    vs_s = dt_int("vs_s", [S, 130], BF16)
    kwT_s = dt_int("kwT_s", [128, (4 + NBLK) * 128], BF16)
    vw_s = dt_int("vw_s", [(4 + NBLK) * 128, 130], BF16)
    u_s = dt_int("u_s", [(1 + NBLK) * 128, 512], BF16)

    with stack:
        sc = Sched(nc, stack)
        ar = Arena(nc, stack, 204 * 1024)
        ps = []
        for i in range(8):
            t = stack.enter_context(nc.psum_tensor("ps%d" % i, [128, 512], F32))
            ps.append((t, Res("ps%d" % i)))
        R_modd = Res("modd"); R_okv = Res("okv"); R_ou = Res("ou"); R_oy = Res("oy")
        R_oswin = Res("oswin"); R_ospool = Res("ospool")
        R_x1s = Res("x1s"); R_x2s = Res("x2s")
        R_ksTs = Res("ksTs"); R_vss = Res("vss"); R_kwTs = Res("kwTs"); R_vws = Res("vws"); R_us = Res("us")

        def op(eng, fn, rd=(), wr=(), dma=None, deferred=False):
            return sc.op(eng, fn, rd=rd, wr=wr, dma=dma, deferred=deferred)

        def new_epoch():
            sc.barrier()

        dyn_ctr = [0]

        def dyn_dma(out_ap, base_ap, axis, idx_ap, mult, const, size, maxv, rd, wr, dma, rearr=None):
            dyn_ctr[0] += 1
            nm = "dr%d" % dyn_ctr[0]

            def fn(e):
                r = e.alloc_register(nm)
                e.reg_load(r, idx_ap)
                v = e.snap(r, donate=True, min_val=0, max_val=maxv)
                e1 = v * mult
                e2 = e1 + const if const else e1
                if axis == 0:
                    src = base_ap[ds(e2, size), :]
                else:
                    src = base_ap[:, ds(e2, size)]
                exprs = [e1, e2, src.offset, src.offset * 2, src.offset * 4]
                if rearr is not None:
                    src = src.rearrange(rearr, p=128)
                ins = e.dma_start(out=out_ap, in_=src)
                vc = e.get_value_cache()
                seen = set()
                for ex in exprs:
                    try:
                        al = vc.lookup(ex)
                    except Exception:
                        al = None
                    if al is not None and al.val.name not in seen and al.val.name != r.name:
                        seen.add(al.val.name)
                        e.free_register(al.val)
                e.free_register(r)
                return ins
            return op("sync", fn, rd=rd, wr=wr, dma=dma, deferred=True)

        ident_f, R_identf = ar.alloc("identf", [128], F32)
        ident_b, R_identb = ar.alloc("identb", [128], BF16)
        identrep, R_identrep = ar.alloc("identrep", [4, 128], BF16)
        op("sync", lambda e: e.dma_start(out=ident_f, in_=ident_d[:, :]), wr=[R_identf], dma=R_identf)
        op("vector", lambda e: e.tensor_copy(out=ident_b, in_=ident_f), rd=[R_identf], wr=[R_identb])
        for h in range(4):
            op("vector", lambda e, h=h: e.tensor_copy(out=identrep[:, h, :], in_=ident_f), rd=[R_identf], wr=[R_identrep])
        ones_f, R_ones = ar.alloc("ones", [128], F32)
        op("vector", lambda e: e.memset(ones_f, 1.0), wr=[R_ones])
        rinfo_t = stack.enter_context(nc.sbuf_tensor("rinfo_sb", [1, 8], I32))
        R_rinfo = Res("rinfo")
        op("sync", lambda e: e.dma_start(out=rinfo_t[:], in_=rinfo_d[:, :]), wr=[R_rinfo], dma=R_rinfo)
        ridx = rinfo_t[0:1, 0:1]
        kcT_sb, R_kcT = ar.alloc("kcT", [256], BF16)
        vcT_sb, R_vcT = ar.alloc("vcT", [256], BF16)
        vc_sb, R_vc = ar.alloc("vc", [2, 2, 65], BF16)
        W4k, R_W4k = ar.alloc("W4k", [4], BF16)
        W4v, R_W4v = ar.alloc("W4v", [4], BF16)
        wcol, R_wcol = ar.alloc("wcol", [2], F32)
        bmask, R_bmask = ar.alloc("bmask", [4], F32)
        pair_b, R_pair = ar.alloc("pair", [64], BF16)
        op("gpsimd", lambda e: e.dma_start(out=pair_b, in_=pair_d[:, :]), wr=[R_pair], dma=R_pair)
        op("sync", lambda e: e.dma_start(out=bmask, in_=bmask_d[:, :]), wr=[R_bmask], dma=R_bmask)
        for n in range(2):
            for qd in range(4):
                op("sync", lambda e, n=n, qd=qd: e.dma_start(out=wcol[32 * qd:32 * qd + 32, n:n + 1],
                                                            in_=cmpw[n:n + 1, :].rearrange("a l -> l a")),
                   wr=[R_wcol], dma=R_wcol)
        op("vector", lambda e: e.tensor_scalar(out=W4k, in0=bmask, scalar1=wcol[:, 0:1], scalar2=None, op0=ALU.mult),
           rd=[R_bmask, R_wcol], wr=[R_W4k])
        op("vector", lambda e: e.tensor_scalar(out=W4v, in0=bmask, scalar1=wcol[:, 1:2], scalar2=None, op0=ALU.mult),
           rd=[R_bmask, R_wcol], wr=[R_W4v])
        op("vector", lambda e: e.memset(vc_sb, 1.0), wr=[R_vc])
        base_mark = ar.mark()

        cs, R_cs = ar.alloc("cs", [D], F32)
        csb, R_csb = ar.alloc("csb", [D], BF16)
        siluT, R_siluT = ar.alloc("siluT", [8, M17], BF16)
        modsb, R_mod = ar.alloc("modsb", [9 * D], F32)
        bada, R_bada = ar.alloc("bada", [9 * D], F32)
        op("sync", lambda e: e.dma_start(out=cs[0:M17, :], in_=c17[:, :]), wr=[R_cs], dma=R_cs)
        op("sync", lambda e: e.dma_start(out=bada[0:1, :], in_=b_ada[:, :]), wr=[R_bada], dma=R_bada)
        op("scalar", lambda e: e.activation(out=csb[0:M17, :], in_=cs[0:M17, :], func=AF.Silu), rd=[R_cs], wr=[R_csb])
        pT, R_pT = ps[0]
        for k in range(8):
            pTk, R_pTk = ps[k % 2]
            op("tensor", lambda e, k=k, pTk=pTk: e.matmul(out=pTk[:, (k // 2) * M17:(k // 2 + 1) * M17], lhsT=csb[0:M17, k * 128:(k + 1) * 128],
                                                           rhs=ident_b[0:M17, 0:M17], start=True, stop=True),
               rd=[R_csb, R_identb], wr=[R_pTk])
        for k in range(8):
            pTk, R_pTk = ps[k % 2]
            op("vector", lambda e, k=k, pTk=pTk: e.tensor_copy(out=siluT[:, k, :], in_=pTk[:, (k // 2) * M17:(k // 2 + 1) * M17]),
               rd=[R_pTk], wr=[R_siluT])
        wad = [ar.alloc("wad%d" % i, [8, 512], BF16) for i in range(2)]
        w_ada_v = w_ada.rearrange("(k p) n -> p k n", p=128)
        for cg in range(18):
            wt, R_wt = wad[cg % 2]
            op("gpsimd", lambda e, cg=cg, wt=wt: e.dma_start(out=wt, in_=w_ada_v[:, :, cg * 512:(cg + 1) * 512]),
               wr=[R_wt], dma=R_wt)
            pm, R_pm = ps[2 + cg % 2]
            for k in range(8):
                op("tensor", lambda e, k=k, wt=wt, pm=pm: e.matmul(out=pm[0:M17, :], lhsT=siluT[:, k, :], rhs=wt[:, k, :],
                                                                   start=(k == 0), stop=False),
                   rd=[R_siluT, R_wt], wr=[R_pm])
            op("tensor", lambda e, cg=cg, pm=pm: e.matmul(out=pm[0:M17, :], lhsT=ones_f[0:1, 0:M17],
                                                          rhs=bada[0:1, cg * 512:(cg + 1) * 512], start=False, stop=True),
               rd=[R_ones, R_bada], wr=[R_pm])
            op("vector", lambda e, cg=cg, pm=pm: e.tensor_copy(out=modsb[0:M17, cg * 512:(cg + 1) * 512], in_=pm[0:M17, :]),
               rd=[R_pm], wr=[R_mod])
        op("sync", lambda e: e.dma_start(out=modd[:, :], in_=modsb[0:M17, :]), rd=[R_mod], wr=[R_modd], dma=R_modd)
        zt, R_zt = ar.alloc("zt", [5, 130], BF16)
        op("vector", lambda e: e.memset(zt, 0.0), wr=[R_zt])
        op("sync", lambda e: e.dma_start(out=kwT_s[:, 0:512], in_=zt.rearrange("p a b -> p (a b)")[:, 0:512]), rd=[R_zt], wr=[R_kwTs], dma=R_zt)
        op("sync", lambda e: e.dma_start(out=vw_s[0:512, :].rearrange("(m p) c -> p m c", p=128), in_=zt[:, 0:4, :]),
           rd=[R_zt], wr=[R_vws], dma=R_zt)
        op("sync", lambda e: e.dma_start(out=u_s[0:128, :], in_=zt.rearrange("p a b -> p (a b)")[:, 0:512]), rd=[R_zt], wr=[R_us], dma=R_zt)
        new_epoch()
        ar.reset(base_mark)

        def load_row(dst, R_dst, kind, mod_i, gain_i, mode, tmpg, R_tmpg):
            msrc = lambda a, b: modd[a:b, mod_i * D:(mod_i + 1) * D]
            if kind == 0:
                op("sync", lambda e: e.dma_start(out=dst, in_=msrc(0, 1).to_broadcast([128, D])), rd=[R_modd], wr=[R_dst], dma=R_dst)
            else:
                r0 = 1 + 16 * (kind - 1)
                for t in range(4):
                    op("sync", lambda e, t=t: e.dma_start(out=dst[16 * t:16 * t + 16, :], in_=msrc(r0, r0 + 16)), rd=[R_modd], wr=[R_dst], dma=R_dst)
            if mode == "B":
                return
            op("sync", lambda e: e.dma_start(out=tmpg, in_=gains[gain_i:gain_i + 1, :].to_broadcast([128, D])), wr=[R_tmpg], dma=R_tmpg)
            if mode == "A":
                op("vector", lambda e: e.scalar_tensor_tensor(out=dst, in0=dst, scalar=1.0, in1=tmpg, op0=ALU.add, op1=ALU.mult),
                   rd=[R_dst, R_tmpg], wr=[R_dst])
            else:
                sclr = 0.5 if mode == "G5" else 1.0
                op("vector", lambda e: e.scalar_tensor_tensor(out=dst, in0=dst, scalar=sclr, in1=tmpg, op0=ALU.mult, op1=ALU.mult),
                   rd=[R_dst, R_tmpg], wr=[R_dst])

        class Work:
            pass

        def alloc_work():
            w = Work()
            w.h32, w.R_h32 = ar.alloc("h32", [D], F32)
            w.hb, w.R_hb = ar.alloc("hb", [D], BF16)
            w.junk, w.R_junk = ar.alloc("junk", [D], BF16)
            w.ssq, w.R_ssq = ar.alloc("ssq", [1], F32)
            w.rstd, w.R_rstd = ar.alloc("rstd", [1], F32)
            w.hT, w.R_hT = ar.alloc("hT", [8, 512], BF16)
            w.tmpg, w.R_tmpg = ar.alloc("tmpg", [D], F32)
            return w

        def rms_rstd(w, nt):
            op("vector", lambda e: e.tensor_scalar(out=w.rstd[0:nt, :], in0=w.ssq[0:nt, :], scalar1=1.0 / D, scalar2=EPS,
                                                    op0=ALU.mult, op1=ALU.add), rd=[w.R_ssq], wr=[w.R_rstd])
            op("scalar", lambda e: e.activation(out=w.rstd[0:nt, :], in_=w.rstd[0:nt, :], func=AF.Sqrt), rd=[w.R_rstd], wr=[w.R_rstd])
            op("vector", lambda e: e.reciprocal(out=w.rstd[0:nt, :], in_=w.rstd[0:nt, :]), rd=[w.R_rstd], wr=[w.R_rstd])

        def norm_mod(w, x_ap, R_x, nt, A, R_A, Bt, R_B, col0):
            op("scalar", lambda e: e.activation(out=w.junk[0:nt, :], in_=x_ap, func=AF.Square, accum_out=w.ssq[0:nt, :]),
               rd=[R_x], wr=[w.R_junk, w.R_ssq])
            rms_rstd(w, nt)
            op("vector", lambda e: e.scalar_tensor_tensor(out=w.h32[0:nt, :], in0=x_ap, scalar=w.rstd[0:nt, :], in1=A[0:nt, :],
                                                           op0=ALU.mult, op1=ALU.mult), rd=[R_x, w.R_rstd, R_A], wr=[w.R_h32])
            op("vector", lambda e: e.tensor_tensor(out=w.hb[0:nt, :], in0=w.h32[0:nt, :], in1=Bt[0:nt, :], op=ALU.add),
               rd=[w.R_h32, R_B], wr=[w.R_hb])
            for half in range(2):
                pt_, R_pt = ps[half]
                for kk in range(4):
                    k = half * 4 + kk
                    op("tensor", lambda e, k=k, kk=kk, pt_=pt_: e.matmul(out=pt_[:, kk * 128:kk * 128 + nt],
                                                                          lhsT=w.hb[0:nt, k * 128:(k + 1) * 128],
                                                                          rhs=ident_b[0:nt, 0:nt], start=True, stop=True),
                       rd=[w.R_hb, R_identb], wr=[R_pt])
                src = pt_[:, :].rearrange("p (a b) -> p a b", a=4)[:, :, 0:nt]
                if half == 0:
                    op("scalar", lambda e, half=half, src=src: e.activation(out=w.hT[:, half * 4:half * 4 + 4, col0:col0 + nt],
                                                                              in_=src, func=AF.Copy), rd=[R_pt], wr=[w.R_hT])
                else:
                    op("vector", lambda e, half=half, src=src: e.tensor_copy(out=w.hT[:, half * 4:half * 4 + 4, col0:col0 + nt],
                                                                               in_=src), rd=[R_pt], wr=[w.R_hT])

        def post_residual(w, py, nt, x_ap, R_x, G, R_G):
            op("scalar", lambda e: e.activation(out=w.junk[0:nt, 0:512], in_=py[0][0][0:nt, :], func=AF.Square,
                                                 accum_out=w.ssq[0:nt, :]), rd=[py[0][1]], wr=[w.R_junk, w.R_ssq])
            op("scalar", lambda e: e.activation(out=w.junk[0:nt, 512:1024], in_=py[1][0][0:nt, :], func=AF.Square,
                                                 accum_out=w.rstd[0:nt, :]), rd=[py[1][1]], wr=[w.R_junk, w.R_rstd])
            op("vector", lambda e: e.tensor_tensor(out=w.ssq[0:nt, :], in0=w.ssq[0:nt, :], in1=w.rstd[0:nt, :], op=ALU.add),
               rd=[w.R_ssq, w.R_rstd], wr=[w.R_ssq])
            rms_rstd(w, nt)
            for half in range(2):
                pyh, R_pyh = py[half]
                hs = slice(half * 512, (half + 1) * 512)
                op("vector", lambda e, pyh=pyh, hs=hs: e.scalar_tensor_tensor(
                    out=w.h32[0:nt, hs], in0=pyh[0:nt, :], scalar=w.rstd[0:nt, :], in1=G[0:nt, hs], op0=ALU.mult, op1=ALU.mult),
                    rd=[R_pyh, w.R_rstd, R_G], wr=[w.R_h32])
            op("vector", lambda e: e.tensor_tensor(out=x_ap[0:nt, :], in0=x_ap[0:nt, :], in1=w.h32[0:nt, :], op=ALU.add),
               rd=[w.R_h32, R_x], wr=[R_x])

        wi_ctr = [0]

        def ffn_pass(w, f, tiles, wi_v, wo, R_wo, rowsA, rowsB, rowsG, post):
            ntok = sum(t[1] for t in tiles)
            A, R_A = rowsA; Bt, R_B = rowsB; G, R_G = rowsG
            col = 0
            cols = []
            for ti, (load_fn, nt, tag) in enumerate(tiles):
                x_ap, R_x = f.xt[ti]
                load_fn(x_ap, R_x)
                norm_mod(w, x_ap[0:nt, :], R_x, nt, A, R_A, Bt, R_B, col)
                cols.append(col)
                col += nt
            for fc in range(NF):
                wb, R_wb = f.wib[wi_ctr[0] % 3]
                wi_ctr[0] += 1
                op("gpsimd", lambda e, fc=fc, wb=wb: e.dma_start(out=wb[:, :, 0:128], in_=wi_v[:, :, fc * 128:(fc + 1) * 128]),
                   wr=[R_wb], dma=R_wb)
                op("gpsimd", lambda e, fc=fc, wb=wb: e.dma_start(out=wb[:, :, 128:256],
                                                                 in_=wi_v[:, :, DFF + fc * 128:DFF + (fc + 1) * 128]),
                   wr=[R_wb], dma=R_wb)
                pa, R_pa = ps[2 + 2 * (fc % 2)]
                pb, R_pb = ps[3 + 2 * (fc % 2)]
                for k in range(8):
                    op("tensor", lambda e, k=k, wb=wb, pa=pa: e.matmul(out=pa[:, 0:ntok], lhsT=wb[:, k, 0:128], rhs=w.hT[:, k, 0:ntok],
                                                                       start=(k == 0), stop=(k == 7)), rd=[R_wb, w.R_hT], wr=[R_pa])
                for k in range(8):
                    op("tensor", lambda e, k=k, wb=wb, pb=pb: e.matmul(out=pb[:, 0:ntok], lhsT=wb[:, k, 128:256], rhs=w.hT[:, k, 0:ntok],
                                                                       start=(k == 0), stop=(k == 7)), rd=[R_wb, w.R_hT], wr=[R_pb])
                op("scalar", lambda e, pa=pa: e.activation(out=f.sa[:, 0:ntok], in_=pa[:, 0:ntok], func=AF.Silu), rd=[R_pa], wr=[f.R_sa])
                op("vector", lambda e, fc=fc, pb=pb: e.tensor_tensor(out=f.gT[:, fc, 0:ntok], in0=f.sa[:, 0:ntok], in1=pb[:, 0:ntok], op=ALU.mult),
                   rd=[f.R_sa, R_pb], wr=[f.R_gT])
            for ti, (load_fn, nt, tag) in enumerate(tiles):
                x_ap, R_x = f.xt[ti]
                c0 = cols[ti]
                py = [ps[0], ps[1]]
                for half in range(2):
                    pyh, R_pyh = py[half]
                    for fc in range(NF):
                        op("tensor", lambda e, fc=fc, half=half, pyh=pyh, c0=c0, nt=nt: e.matmul(
                            out=pyh[0:nt, :], lhsT=f.gT[:, fc, c0:c0 + nt], rhs=wo[:, fc, half * 512:(half + 1) * 512],
                            start=(fc == 0), stop=(fc == NF - 1)), rd=[f.R_gT, R_wo], wr=[R_pyh])
                post_residual(w, py, nt, x_ap, R_x, G, R_G)
                post(ti, x_ap, R_x, nt, c0, tag)

        class FBuf:
            pass

        def alloc_ffn():
            f = FBuf()
            f.xt = [ar.alloc("xt%d" % i, [D], F32) for i in range(4)]
            f.gT, f.R_gT = ar.alloc("gT", [NF, 512], BF16)
            f.sa, f.R_sa = ar.alloc("sa", [512], F32)
            f.wib = [ar.alloc("wib%d" % i, [8, 256], BF16) for i in range(3)]
            return f

        wo1, R_wo1 = ar.alloc("wo1", [NF, D], BF16)
        op("gpsimd", lambda e: e.dma_start(out=wo1, in_=f1wo.rearrange("(f p) n -> p f n", p=128)), wr=[R_wo1], dma=R_wo1)
        winT, R_win = ar.alloc("winT", [8, INW], BF16)
        w_in_v = w_in.rearrange("(k p) n -> p k n", p=128)
        op("gpsimd", lambda e: e.dma_start(out=winT, in_=w_in_v), wr=[R_win], dma=R_win)
        w = alloc_work()
        f = alloc_ffn()
        R_rows1 = Res("rows1")
        rows1 = {nm: ar.alloc(nm, [D], F32, res=R_rows1) for nm in ("A1", "B1", "G1", "A2", "B2")}
        kvst = [ar.alloc("kvst%d" % i, [768], F32) for i in range(2)]
        ust = [ar.alloc("ust%d" % i, [512], F32) for i in range(2)]
        ubs = [ar.alloc("ub%d" % i, [512], BF16) for i in range(2)]
        vst = [ar.alloc("vst%d" % i, [2, 2, 65], BF16) for i in range(2)]
        kTst = [ar.alloc("kTst%d" % i, [256], BF16) for i in range(2)]
        kcrb = [ar.alloc("kcrb%d" % i, [256], BF16) for i in range(2)]
        for i in range(2):
            op("vector", lambda e, i=i: e.memset(vst[i][0], 1.0), wr=[vst[i][1]])

        def load_rows1(kind):
            load_row(*rows1["A1"], kind, 1, 0, "A", w.tmpg, w.R_tmpg)
            load_row(*rows1["B1"], kind, 0, 0, "B", w.tmpg, w.R_tmpg)
            load_row(*rows1["G1"], kind, 2, 1, "G5", w.tmpg, w.R_tmpg)
            load_row(*rows1["A2"], kind, 4, 2, "A", w.tmpg, w.R_tmpg)
            load_row(*rows1["B2"], kind, 3, 2, "B", w.tmpg, w.R_tmpg)

        f1wi_v = f1wi.rearrange("(k p) n -> p k n", p=128)
        f2wi_v = f2wi.rearrange("(k p) n -> p k n", p=128)
        tctr = [0]

        def post1(ti, x_ap, R_x, nt, c0, tag):
            is_s = isinstance(tag, tuple)
            sg = tag[1] if is_s else None
            row0 = (S + 64 * sg) if is_s else tag * 128
            tc_ = tctr[0]
            tctr[0] += 1
            op("sync", lambda e: e.dma_start(out=x1s[row0:row0 + nt, :], in_=x_ap[0:nt, :]), rd=[R_x], wr=[R_x1s], dma=R_x)
            norm_mod(w, x_ap[0:nt, :], R_x, nt, *rows1["A2"], *rows1["B2"], c0)
            pk0, R_pk0 = ps[6]
            pk1, R_pk1 = ps[7]
            pu, R_pu = ps[5]
            for k in range(8):
                op("tensor", lambda e, k=k: e.matmul(out=pk0[0:nt, :], lhsT=w.hT[:, k, c0:c0 + nt], rhs=winT[:, k, 512:1024],
                                                      start=(k == 0), stop=(k == 7)), rd=[w.R_hT, R_win], wr=[R_pk0])
            for k in range(8):
                op("tensor", lambda e, k=k: e.matmul(out=pk1[0:nt, 0:256], lhsT=w.hT[:, k, c0:c0 + nt], rhs=winT[:, k, 1024:1280],
                                                      start=(k == 0), stop=(k == 7)), rd=[w.R_hT, R_win], wr=[R_pk1])
            for k in range(8):
                op("tensor", lambda e, k=k: e.matmul(out=pu[0:nt, :], lhsT=w.hT[:, k, c0:c0 + nt], rhs=winT[:, k, 1304:1816],
                                                      start=(k == 0), stop=(k == 7)), rd=[w.R_hT, R_win], wr=[R_pu])
            ks_, R_ks = kvst[tc_ % 2]
            us_, R_us_ = ust[tc_ % 2]
            op("scalar", lambda e: e.activation(out=ks_[0:nt, 0:512], in_=pk0[0:nt, :], func=AF.Copy), rd=[R_pk0], wr=[R_ks])
            op("scalar", lambda e: e.activation(out=ks_[0:nt, 512:768], in_=pk1[0:nt, 0:256], func=AF.Copy), rd=[R_pk1], wr=[R_ks])
            op("vector", lambda e: e.tensor_copy(out=us_[0:nt, :], in_=pu[0:nt, :]), rd=[R_pu], wr=[R_us_])
            op("sync", lambda e: e.dma_start(out=o_kv[:, row0:row0 + nt, :].rearrange("s t c -> t s c"),
                                             in_=ks_[0:nt, :].rearrange("p (s c) -> p s c", s=6)), rd=[R_ks], wr=[R_okv], dma=R_ks)
            op("sync", lambda e: e.dma_start(out=o_u[row0:row0 + nt, :], in_=us_[0:nt, :]), rd=[R_us_], wr=[R_ou], dma=R_us_)
            if is_s:
                for t in range(4):
                    for n in range(2):
                        dst = bass.AP(tensor=o_swin.tensor, offset=(n * NSEQ + 16 * sg) * 65536 + (508 + t) * 128, ap=[[65536, 16], [1, 128]])
                        op("sync", lambda e, t=t, n=n, dst=dst: e.dma_start(out=dst, in_=ks_[16 * t:16 * t + 16, 512 + 128 * n:640 + 128 * n]),
                           rd=[R_ks], wr=[R_oswin], dma=R_ks)
                    dstp = bass.AP(tensor=o_spool.tensor, offset=(16 * sg * 15 + 11 + t) * 512, ap=[[15 * 512, 16], [1, 512]])
                    op("sync", lambda e, t=t, dstp=dstp: e.dma_start(out=dstp, in_=us_[16 * t:16 * t + 16, :]),
                       rd=[R_us_], wr=[R_ospool], dma=R_us_)
                return
            j = tag
            ub_, R_ub = ubs[tc_ % 2]
            op("vector", lambda e: e.tensor_copy(out=ub_, in_=us_), rd=[R_us_], wr=[R_ub])
            op("sync", lambda e: e.dma_start(out=u_s[(1 + j) * 128:(2 + j) * 128, :], in_=ub_), rd=[R_ub], wr=[R_us], dma=R_ub)
            vs_, R_vs = vst[tc_ % 2]
            op("vector", lambda e: e.tensor_copy(out=vs_[:, 0, :, 0:64], in_=ks_[:, 384:512].rearrange("p (g d) -> p g d", g=2)),
               rd=[R_ks], wr=[R_vs])
            op("vector", lambda e: e.tensor_copy(out=vs_[:, 1, :, 0:64], in_=ks_[:, 640:768].rearrange("p (g d) -> p g d", g=2)),
               rd=[R_ks], wr=[R_vs])
            op("sync", lambda e: e.dma_start(out=vs_s[j * 128:(j + 1) * 128, :], in_=vs_[:, 0, :, :].rearrange("p g c -> p (g c)")),
               rd=[R_vs], wr=[R_vss], dma=R_vs)
            op("sync", lambda e: e.dma_start(out=vw_s[(4 + j) * 128:(5 + j) * 128, :], in_=vs_[:, 1, :, :].rearrange("p g c -> p (g c)")),
               rd=[R_vs], wr=[R_vws], dma=R_vs)
            pf, R_pf = ps[4]
            for n, c_lo in enumerate((768, 1024)):
                for k in range(8):
                    op("tensor", lambda e, k=k, n=n, c_lo=c_lo: e.matmul(out=pf[:, n * 128:(n + 1) * 128], lhsT=winT[:, k, c_lo:c_lo + 128],
                                                                          rhs=w.hT[:, k, c0:c0 + 128], start=(k == 0), stop=(k == 7)),
                       rd=[w.R_hT, R_win], wr=[R_pf])
            kt_, R_kt = kTst[tc_ % 2]
            op("scalar", lambda e: e.activation(out=kt_, in_=pf[:, 0:256], func=AF.Copy), rd=[R_pf], wr=[R_kt])
            op("sync", lambda e: e.dma_start(out=ksT_s[:, j * 128:(j + 1) * 128], in_=kt_[:, 0:128]), rd=[R_kt], wr=[R_ksTs], dma=R_kt)
            op("sync", lambda e: e.dma_start(out=kwT_s[:, (4 + j) * 128:(5 + j) * 128], in_=kt_[:, 128:256]), rd=[R_kt], wr=[R_kwTs], dma=R_kt)
            kb_, R_kb = kcrb[tc_ % 2]
            op("vector", lambda e: e.tensor_copy(out=kb_, in_=ks_[:, 0:256]), rd=[R_ks], wr=[R_kb])
            pc, R_pc = ps[4]
            op("tensor", lambda e: e.matmul(out=pc[:, 256:260], lhsT=kb_[:, 0:128], rhs=W4k, start=True, stop=True),
               rd=[R_kb, R_W4k], wr=[R_pc])
            op("tensor", lambda e: e.matmul(out=pc[:, 260:264], lhsT=kb_[:, 128:256], rhs=W4v, start=True, stop=True),
               rd=[R_kb, R_W4v], wr=[R_pc])
            op("vector", lambda e: e.tensor_copy(out=kcT_sb[:, 4 * j:4 * j + 4], in_=pc[:, 256:260]), rd=[R_pc], wr=[R_kcT])
            op("vector", lambda e: e.tensor_copy(out=vcT_sb[:, 4 * j:4 * j + 4], in_=pc[:, 260:264]), rd=[R_pc], wr=[R_vcT])

        def xload(src):
            return lambda x_ap, R_x: op("sync", lambda e: e.dma_start(out=x_ap[0:src.shape[0], :], in_=src), wr=[R_x], dma=R_x)

        load_rows1(0)
        for p in range(NBLK // 4):
            tiles = [(xload(xp[j * 128:(j + 1) * 128, :]), 128, j) for j in range(4 * p, 4 * p + 4)]
            ffn_pass(w, f, tiles, f1wi_v, wo1, R_wo1, rows1["A1"], rows1["B1"], rows1["G1"], post1)
            if p % 4 == 3:
                new_epoch()
        for sg in range(NSG):
            new_epoch()
            load_rows1(1 + sg)
            ffn_pass(w, f, [(xload(xs[64 * sg:64 * sg + 64, :]), 64, ("S", sg))], f1wi_v, wo1, R_wo1, rows1["A1"], rows1["B1"], rows1["G1"], post1)
        for jt in range(2):
            pv_, R_pv = ps[jt]
            op("tensor", lambda e, jt=jt, pv_=pv_: e.matmul(out=pv_[:, 0:128], lhsT=vcT_sb[:, jt * 128:(jt + 1) * 128], rhs=ident_b,
                                                            start=True, stop=True), rd=[R_vcT, R_identb], wr=[R_pv])
            op("vector", lambda e, jt=jt, pv_=pv_: e.tensor_copy(out=vc_sb[:, jt, :, 0:64], in_=pv_[:, 0:128].rearrange("p (g d) -> p g d", g=2)),
               rd=[R_pv], wr=[R_vc])
        new_epoch()
        ar.reset(base_mark)
        cw, R_cw = ar.alloc("cw", [16, 512], F32)
        cpl, R_cpl = ar.alloc("cpl", [11 * 512], F32)
        for sg in range(NSG):
            for n, st in enumerate((st_wk, st_wv)):
                srcw = bass.AP(tensor=st.tensor, offset=16 * sg * 65536 + 512, ap=[[512, 127], [65536, 16], [1, 512]])
                dstw = bass.AP(tensor=o_swin.tensor, offset=(n * NSEQ + 16 * sg) * 65536, ap=[[512, 127], [65536, 16], [1, 512]])
                op("sync", lambda e, srcw=srcw: e.dma_start(out=cw[0:127, :, :], in_=srcw), wr=[R_cw], dma=R_cw)
                op("sync", lambda e, dstw=dstw: e.dma_start(out=dstw, in_=cw[0:127, :, :]), rd=[R_cw], wr=[R_oswin], dma=R_cw)
            srcp = bass.AP(tensor=st_pool.tensor, offset=(16 * sg * 15 + 4) * 512, ap=[[15 * 512, 16], [1, 11 * 512]])
            dstp2 = bass.AP(tensor=o_spool.tensor, offset=16 * sg * 15 * 512, ap=[[15 * 512, 16], [1, 11 * 512]])
            op("sync", lambda e, srcp=srcp: e.dma_start(out=cpl[0:16, :], in_=srcp), wr=[R_cpl], dma=R_cpl)
            op("sync", lambda e, dstp2=dstp2: e.dma_start(out=dstp2, in_=cpl[0:16, :]), rd=[R_cpl], wr=[R_ospool], dma=R_cpl)
        new_epoch()
        ar.reset(base_mark)

        w = alloc_work()
        R_p2a = Res("p2a"); R_p2b = Res("p2b")
        wq, R_wq = ar.alloc("wq", [8, 4, 128], BF16, res=R_p2a)
        for h in range(8):
            op("gpsimd", lambda e, h=h: e.dma_start(out=wq[:, :, h % 4, (h // 4) * 64:(h // 4) * 64 + 64], in_=w_in_v[:, :, h * 64:(h + 1) * 64]),
               wr=[R_wq], dma=R_wq)
        wg, R_wg = ar.alloc("wg", [8, 24], BF16, res=R_p2a)
        op("gpsimd", lambda e: e.dma_start(out=wg, in_=w_in_v[:, :, 1280:1304]), wr=[R_wg], dma=R_wg)
        wout, R_wout = ar.alloc("wout", [8, D], BF16, res=R_p2a)
        op("gpsimd", lambda e: e.dma_start(out=wout, in_=w_out.rearrange("(k p) n -> p k n", p=128)), wr=[R_wout], dma=R_wout)
        pw, R_pw = ar.alloc("pw", [4, 128], BF16, res=R_p2a)
        op("gpsimd", lambda e: e.dma_start(out=pw, in_=pool_w.rearrange("g c d -> c g d")), wr=[R_pw], dma=R_pw)
        pscl, R_pscl = ar.alloc("pscl", [4], F32)
        for gi in range(4):
            op("sync", lambda e, gi=gi: e.dma_start(out=pscl[:, gi:gi + 1], in_=pool_scale[0:1, gi * 128:(gi + 1) * 128].rearrange("a d -> d a")),
               wr=[R_pscl], dma=R_pscl)
        cm_sb, R_cm = ar.alloc("cm", [NGRP, 128], BF16, res=R_p2b)
        op("gpsimd", lambda e: e.dma_start(out=cm_sb, in_=cm_d.rearrange("k p q -> p k q")), wr=[R_cm], dma=R_cm)
        wm_sb, R_wm = ar.alloc("wm", [25, 128], BF16, res=R_p2b)
        op("gpsimd", lambda e: e.dma_start(out=wm_sb, in_=wm_d.rearrange("a m p q -> p (a m) q")), wr=[R_wm], dma=R_wm)
        wp_sb, R_wp = ar.alloc("wp", [16, 128], BF16, res=R_p2b)
        op("gpsimd", lambda e: e.dma_start(out=wp_sb, in_=wp_d.rearrange("a g b p q -> p (a g b) q")), wr=[R_wp], dma=R_wp)
        R_rows2 = Res("rows2")
        rows2 = {nm: ar.alloc(nm, [D], F32, res=R_rows2) for nm in ("A2", "B2", "G2")}

        def load_rows2(kind):
            load_row(*rows2["A2"], kind, 4, 2, "A", w.tmpg, w.R_tmpg)
            load_row(*rows2["B2"], kind, 3, 2, "B", w.tmpg, w.R_tmpg)
            load_row(*rows2["G2"], kind, 5, 3, "G1", w.tmpg, w.R_tmpg)

        ksT_buf, _ = ar.alloc("ksTb", [S], BF16)
        vs_buf, _ = ar.alloc("vsb", [NBLK, 130], BF16)
        R_ksb = [Res("ksb")] * NOWN
        R_vsb = [Res("vsb")] * NOWN
        kwb = [ar.alloc("kwb%d" % i, [640], BF16) for i in range(2)]
        vwb = [ar.alloc("vwb%d" % i, [5, 130], BF16) for i in range(2)]
        u2b = [ar.alloc("u2b%d" % i, [2, 512], BF16) for i in range(2)]
        x1t = [ar.alloc("x1t%d" % i, [D], F32) for i in range(2)]
        qT_sb, R_qT = ar.alloc("qT", [4, 128], BF16)
        gates, R_gates = ar.alloc("gates", [24], F32)
        cmk = [ar.alloc("cmk%d" % i, [2, 128], BF16) for i in range(2)]
        bon = [ar.alloc("bon%d" % i, [128], F32) for i in range(2)]
        PTb = [ar.alloc("PT%d" % i, [512], BF16) for i in range(4)]
        nexp = [ar.alloc("nexp%d" % g, [S], BF16) for g in range(2)]
        rden, R_rden = ar.alloc("rden", [4], F32)
        coef, R_coef = ar.alloc("coef", [4], F32)
        tmpo, R_tmpo = ar.alloc("tmpo", [4, 64], F32)
        onsa, R_onsa = ar.alloc("onsa", [512], F32)
        onsab, R_onsab = ar.alloc("onsab", [512], BF16)
        onsaT, R_onsaT = ar.alloc("onsaT", [4, 128], BF16)
        tmpi, R_tmpi = ar.alloc("tmpi", [4, 128], F32)
        vals, R_vals = ar.alloc("vals", [128], F32)
        vals2, R_vals2 = ar.alloc("vals2", [128], F32)
        m8, R_m8 = ar.alloc("m8", [16], F32)
        selm, R_selm = ar.alloc("selm", [128], F32)
        negs, R_negs = ar.alloc("negs", [128], BF16)
        dT_sb, R_dT = ar.alloc("dT", [4, 128], BF16)
        ypT, R_ypT = ar.alloc("ypT", [4, 128], BF16)
        pt_ctr = [0]

        def attend(g, nq, q_rhs, R_q, tiles, pO, R_pO, pI=None, R_pI=None, irep=None):
            n = len(tiles)
            oview = pO[:, 0:260].rearrange("p (h c) -> p h c", h=4)
            irep_ap, R_irep = (identrep.rearrange("p a b -> p (a b)"), R_identrep) if irep is None else irep
            for idx, t in enumerate(tiles):
                nk = t.get("nk", 128)
                pS, R_pS = ps[2 + (pt_ctr[0] % 2)]
                PT, R_PT = PTb[pt_ctr[0] % 4]
                pt_ctr[0] += 1
                has_add = t.get("add") is not None
                op("tensor", lambda e: e.matmul(out=pS[0:nk, 0:4 * nq], lhsT=t["kT"], rhs=q_rhs, start=True, stop=not has_add),
                   rd=[t["Rk"], R_q], wr=[R_pS])
                if has_add:
                    a_ap, R_a = t["add"]
                    op("tensor", lambda e: e.matmul(out=pS[0:nk, 0:4 * nq], lhsT=a_ap, rhs=irep_ap, start=False, stop=True),
                       rd=[R_a, R_irep], wr=[R_pS])
                op("scalar", lambda e: e.activation(out=PT[0:nk, 0:4 * nq], in_=pS[0:nk, 0:4 * nq], func=AF.Exp, scale=SCALE),
                   rd=[R_pS], wr=[R_PT])
                if t.get("mul") is not None:
                    m_ap, R_m = t["mul"]
                    pv3 = PT[0:nk, 0:4 * nq].rearrange("p (h q) -> p h q", h=4)
                    op("vector", lambda e: e.tensor_tensor(out=pv3, in0=pv3, in1=bc_mid(m_ap, 4), op=ALU.mult),
                       rd=[R_PT, R_m], wr=[R_PT])
                for hh in range(4):
                    op("tensor", lambda e: e.matmul(out=oview[0:nq, hh, :], lhsT=PT[0:nk, hh * nq:(hh + 1) * nq], rhs=t["v"],
                                                     start=(idx == 0 and hh == 0), stop=(idx == n - 1)),
                       rd=[R_PT, t["Rv"]], wr=[R_pO])
                if pI is not None:
                    jt = t["jt"]
                    iview = pI[:, :].rearrange("p (h b) -> p h b", h=4)
                    for hh in range(4):
                        op("tensor", lambda e: e.matmul(out=iview[0:nq, hh, jt * 64:(jt + 1) * 64], lhsT=PT[0:nk, hh * nq:(hh + 1) * nq],
                                                         rhs=pair_b[0:nk, :], start=True, stop=True),
                           rd=[R_PT, R_pair], wr=[R_pI])

        def finish_branch(g, nq, br, pO, R_pO, gates_ap, first):
            oview = pO[:, 0:260].rearrange("p (h c) -> p h c", h=4)
            op("vector", lambda e: e.tensor_scalar(out=rden[0:nq, :], in0=oview[0:nq, :, 64], scalar1=1e-30, scalar2=None, op0=ALU.max),
               rd=[R_pO], wr=[R_rden])
            op("vector", lambda e: e.reciprocal(out=rden[0:nq, :], in_=rden[0:nq, :]), rd=[R_rden], wr=[R_rden])
            gv = gates_ap.rearrange("p (h b) -> p h b", b=3)[:, 4 * g:4 * g + 4, br]
            op("vector", lambda e: e.tensor_tensor(out=coef[0:nq, :], in0=rden[0:nq, :], in1=gv, op=ALU.mult),
               rd=[R_rden, R_gates], wr=[R_coef])
            ov = onsa[0:nq, g * 256:(g + 1) * 256].rearrange("p (h d) -> p h d", h=4)
            if first:
                op("vector", lambda e: e.tensor_tensor(out=ov, in0=oview[0:nq, :, 0:64], in1=bc_last(coef[0:nq, :], 64), op=ALU.mult),
                   rd=[R_pO, R_coef], wr=[R_onsa])
            else:
                op("vector", lambda e: e.tensor_tensor(out=tmpo[0:nq, :, :], in0=oview[0:nq, :, 0:64], in1=bc_last(coef[0:nq, :], 64), op=ALU.mult),
                   rd=[R_pO, R_coef], wr=[R_tmpo])
                op("vector", lambda e: e.tensor_tensor(out=ov, in0=ov, in1=tmpo[0:nq, :, :], op=ALU.add), rd=[R_tmpo, R_onsa], wr=[R_onsa])

        def select_blocks(g, nq, pI, R_pI, bonus_ap, R_bonus, kth, nblk_exp):
            iview = pI[:, :].rearrange("p (h b) -> p h b", h=4)
            op("vector", lambda e: e.tensor_tensor(out=tmpi[0:nq, :, :], in0=iview[0:nq, :, :], in1=bc_last(rden[0:nq, :], 128), op=ALU.mult),
               rd=[R_pI, R_rden], wr=[R_tmpi])
            op("vector", lambda e: e.tensor_reduce(out=vals[0:nq, :], in_=tmpi[0:nq, :, :].rearrange("p h b -> p b h"),
                                                    axis=mybir.AxisListType.X, op=ALU.add), rd=[R_tmpi], wr=[R_vals])
            op("vector", lambda e: e.tensor_tensor(out=vals[0:nq, :], in0=vals[0:nq, :], in1=bonus_ap, op=ALU.add),
               rd=[R_vals, R_bonus], wr=[R_vals])
            op("vector", lambda e: e.max(out=m8[0:nq, 0:8], in_=vals[0:nq, :]), rd=[R_vals], wr=[R_m8])
            op("vector", lambda e: e.match_replace(out=vals2[0:nq, :], in_to_replace=m8[0:nq, 0:8], in_values=vals[0:nq, :], imm_value=-3.0e38),
               rd=[R_vals, R_m8], wr=[R_vals2])
            op("vector", lambda e: e.max(out=m8[0:nq, 8:16], in_=vals2[0:nq, :]), rd=[R_vals2], wr=[R_m8])
            op("vector", lambda e: e.tensor_scalar(out=selm[0:nq, :], in0=vals[0:nq, :], scalar1=m8[0:nq, 8 + kth - 9:8 + kth - 8], scalar2=None,
                                                    op0=ALU.is_ge), rd=[R_vals, R_m8], wr=[R_selm])
            op("vector", lambda e: e.tensor_scalar(out=negs[0:nq, :], in0=selm[0:nq, :], scalar1=-1.0, scalar2=-NEGM, op0=ALU.add, op1=ALU.mult),
               rd=[R_selm], wr=[R_negs])
            nx, R_nx = nexp[g]
            op("vector", lambda e: e.tensor_copy(out=nx[0:nq, 0:nblk_exp * 64].rearrange("p (b l) -> p b l", l=64),
                                                  in_=bc_last(negs[0:nq, 0:nblk_exp], 64)), rd=[R_negs], wr=[R_nx])

        load_rows2(0)
        for i in range(NOWN):
            x_ap, R_x = x1t[i % 2]
            nkt = NGRP * i + NGRP
            kind = 0 if i == 0 else 1
            kw5 = min(NGRP * i, 4)
            c128 = 128 * NGRP * i
            dyn_dma(x_ap, x1s, 0, ridx, 128, c128, 128, NGRP - 1, rd=[R_x1s, R_rinfo], wr=[R_x], dma=R_x)
            op("sync", lambda e, i=i: e.dma_start(out=ksT_buf[:, c128:c128 + 128 * NGRP], in_=ksT_s[:, c128:c128 + 128 * NGRP]),
               rd=[R_ksTs], wr=[R_ksb[i]], dma=R_ksb[i])
            op("sync", lambda e, i=i: e.dma_start(out=vs_buf[:, NGRP * i:NGRP * i + NGRP, :], in_=vs_s[c128:c128 + 128 * NGRP, :].rearrange("(m p) c -> p m c", p=128)),
               rd=[R_vss], wr=[R_vsb[i]], dma=R_vsb[i])
            kw_, R_kw = kwb[i % 2]
            vw_, R_vw = vwb[i % 2]
            u2_, R_u2 = u2b[i % 2]
            dyn_dma(kw_, kwT_s, 1, ridx, 128, c128, 640, NGRP - 1, rd=[R_kwTs, R_rinfo], wr=[R_kw], dma=R_kw)
            dyn_dma(vw_, vw_s, 0, ridx, 128, c128, 640, NGRP - 1, rd=[R_vws, R_rinfo], wr=[R_vw], dma=R_vw, rearr="(m p) c -> p m c")
            dyn_dma(u2_, u_s, 0, ridx, 128, c128, 256, NGRP - 1, rd=[R_us, R_rinfo], wr=[R_u2], dma=R_u2, rearr="(m p) c -> p m c")
            cmk_, R_cmk = cmk[i % 2]
            bon_, R_bon = bon[i % 2]
            op("gpsimd", lambda e, i=i, cmk_=cmk_: e.dma_start(out=cmk_, in_=cmpmask_d[i].rearrange("t p q -> p t q")), wr=[R_cmk], dma=R_cmk)
            op("sync", lambda e, i=i, bon_=bon_: e.dma_start(out=bon_, in_=bonus_d[i]), wr=[R_bon], dma=R_bon)
            norm_mod(w, x_ap, R_x, 128, *rows2["A2"], *rows2["B2"], 0)
            pq, R_pq = ps[7]
            for c in range(4):
                for k in range(8):
                    op("tensor", lambda e, c=c, k=k: e.matmul(out=pq[:, c * 128:(c + 1) * 128], lhsT=wq[:, k, c, :], rhs=w.hT[:, k, 0:128],
                                                               start=(k == 0), stop=(k == 7)), rd=[R_wq, w.R_hT], wr=[R_pq])
            op("scalar", lambda e: e.activation(out=qT_sb.rearrange("p a b -> p (a b)"), in_=pq[:, :], func=AF.Copy), rd=[R_pq], wr=[R_qT])
            pg_, R_pg = ps[6]
            for k in range(8):
                op("tensor", lambda e, k=k: e.matmul(out=pg_[:, 0:24], lhsT=w.hT[:, k, 0:128], rhs=wg[:, k, :], start=(k == 0), stop=(k == 7)),
                   rd=[R_wg, w.R_hT], wr=[R_pg])
            op("scalar", lambda e: e.activation(out=gates, in_=pg_[:, 0:24], func=AF.Sigmoid), rd=[R_pg], wr=[R_gates])
            for g in range(2):
                gs = slice(64 * g, 64 * g + 64)
                q_rhs = qT_sb[gs, :, :].rearrange("p a b -> p (a b)")
                pO, R_pO = ps[4 + g]
                pI, R_pI = ps[6]
                tl = [dict(kT=kcT_sb[gs, jt * 128:(jt + 1) * 128], Rk=R_kcT, v=vc_sb[:, jt, g, :], Rv=R_vc, mul=(cmk_[:, jt, :], R_cmk), jt=jt)
                      for jt in range(2)]
                attend(g, 128, q_rhs, R_qT, tl, pO, R_pO, pI, R_pI)
                finish_branch(g, 128, 0, pO, R_pO, gates, True)
                select_blocks(g, 128, pI, R_pI, bon_, R_bon, 16, 2 * nkt)
                nx, R_nx = nexp[g]
                tl = []
                for kt in range(nkt):
                    d = dict(kT=ksT_buf[gs, kt * 128:(kt + 1) * 128], Rk=R_ksb[0], v=vs_buf[:, kt, 65 * g:65 * g + 65], Rv=R_vsb[0],
                             add=(nx[:, kt * 128:(kt + 1) * 128], R_nx))
                    if kt >= NGRP * i:
                        d["mul"] = (cm_sb[:, kt - NGRP * i, :], R_cm)
                    tl.append(d)
                attend(g, 128, q_rhs, R_qT, tl, pO, R_pO)
                finish_branch(g, 128, 1, pO, R_pO, gates, False)
                tl = []
                for m in range(5):
                    d = dict(kT=kw_[gs, m * 128:(m + 1) * 128], Rk=R_kw, v=vw_[:, m, 65 * g:65 * g + 65], Rv=R_vw)
                    if kw5 < 4 or m in (0, 4):
                        d["mul"] = (wm_sb[:, kw5 * 5 + m, :], R_wm)
                    tl.append(d)
                attend(g, 128, q_rhs, R_qT, tl, pO, R_pO)
                finish_branch(g, 128, 2, pO, R_pO, gates, False)
            op("scalar", lambda e: e.activation(out=onsab, in_=onsa, func=AF.Copy), rd=[R_onsa], wr=[R_onsab])
            pt_, R_pt = ps[0]
            for c in range(4):
                op("tensor", lambda e, c=c: e.matmul(out=pt_[:, c * 128:(c + 1) * 128], lhsT=onsab[:, c * 128:(c + 1) * 128], rhs=ident_b,
                                                      start=True, stop=True), rd=[R_onsab, R_identb], wr=[R_pt])
            op("vector", lambda e: e.tensor_copy(out=onsaT.rearrange("p a b -> p (a b)"), in_=pt_[:, :]), rd=[R_pt], wr=[R_onsaT])
            pd, R_pd = ps[1]
            for gi in range(4):
                for wh in range(2):
                    op("tensor", lambda e, gi=gi, wh=wh: e.matmul(out=pd[:, gi * 128:(gi + 1) * 128], lhsT=u2_[:, wh, gi * 128:(gi + 1) * 128],
                                                                   rhs=wp_sb[:, (kind * 4 + gi) * 2 + wh, :], start=(wh == 0), stop=(wh == 1)),
                       rd=[R_u2, R_wp], wr=[R_pd])
            op("vector", lambda e: e.tensor_copy(out=dT_sb.rearrange("p a b -> p (a b)"), in_=pd[:, :]), rd=[R_pd], wr=[R_dT])
            pyp, R_pyp = ps[7]
            for gi in range(4):
                op("tensor", lambda e, gi=gi: e.matmul(out=pyp[:, gi * 128:(gi + 1) * 128], lhsT=pw[:, gi, :], rhs=dT_sb[:, gi, :], start=True, stop=True),
                   rd=[R_pw, R_dT], wr=[R_pyp])
            for gi in range(4):
                op("scalar", lambda e, gi=gi: e.activation(out=ypT[:, gi, :], in_=pyp[:, gi * 128:(gi + 1) * 128], func=AF.Identity, scale=pscl[:, gi:gi + 1]),
                   rd=[R_pyp, R_pscl], wr=[R_ypT])
            py = [ps[2], ps[3]]
            for half in range(2):
                pyh, R_pyh = py[half]
                for c in range(8):
                    lh = onsaT[:, c, :] if c < 4 else ypT[:, c - 4, :]
                    Rl = R_onsaT if c < 4 else R_ypT
                    op("tensor", lambda e, c=c, half=half, pyh=pyh, lh=lh: e.matmul(out=pyh[:, :], lhsT=lh, rhs=wout[:, c, half * 512:(half + 1) * 512],
                                                                                     start=(c == 0), stop=(c == 7)), rd=[Rl, R_wout], wr=[R_pyh])
            post_residual(w, py, 128, x_ap, R_x, *rows2["G2"])
            op("sync", lambda e, i=i, x_ap=x_ap: e.dma_start(out=x2s[i * 128:(i + 1) * 128, :], in_=x_ap), rd=[R_x], wr=[R_x2s], dma=R_x)
            if i % 8 == 7:
                new_epoch()
        new_epoch()
        sc.new_sem_epoch()
        ar.reset(base_mark)
        w = alloc_work()
        R_rowsS = Res("rowsS")
        rowsS = {nm: ar.alloc(nm, [D], F32, res=R_rowsS) for nm in ("A2", "B2", "G2")}
        R_p3a = Res("p3a")
        wq, R_wq = ar.alloc("wq", [8, 4, 128], BF16, res=R_p3a)
        for h in range(8):
            op("gpsimd", lambda e: e.dma_start(out=wq[:, :, h % 4, (h // 4) * 64:(h // 4) * 64 + 64], in_=w_in_v[:, :, h * 64:(h + 1) * 64]),
               wr=[R_wq], dma=R_wq)
        wkv, R_wkv = ar.alloc("wkv", [8, 792], BF16, res=R_p3a)
        op("gpsimd", lambda e: e.dma_start(out=wkv, in_=w_in_v[:, :, 512:1304]), wr=[R_wkv], dma=R_wkv)
        R_sc = Res("sconst")
        pt_sb = stack.enter_context(nc.sbuf_tensor("pt_sb", [NSEQ, 64], I32))
        op("sync", lambda e: e.dma_start(out=pt_sb[:], in_=pt_d[:, :]), wr=[R_sc], dma=R_sc)
        bonus_s, _ = ar.alloc("bonus_s", [128], F32)
        op("sync", lambda e: e.dma_start(out=bonus_s[0:4, :], in_=bonus_s_d[:, :]), wr=[R_sc], dma=R_sc)
        causal4, _ = ar.alloc("causal4", [4], BF16)
        op("gpsimd", lambda e: e.dma_start(out=causal4[0:4, :], in_=causal4_d[:, :]), wr=[R_sc], dma=R_sc)
        wms0, _ = ar.alloc("wms0", [4], BF16)
        op("gpsimd", lambda e: e.dma_start(out=wms0, in_=wms0_d[:, :]), wr=[R_sc], dma=R_sc)
        irep4, R_irep4 = ar.alloc("irep4", [4, 4], BF16)
        for h in range(4):
            op("vector", lambda e: e.tensor_copy(out=irep4[0:4, h, :], in_=ident_f[0:4, 0:4]), rd=[R_identf], wr=[R_irep4])
        PTb = [ar.alloc("PT%d" % i, [512], BF16) for i in range(4)]
        nx1, R_nx1 = ar.alloc("nexp", [S], BF16)
        nexp = [(nx1, R_nx1), (nx1, R_nx1)]
        rden, R_rden = ar.alloc("rden", [4], F32)
        coef, R_coef = ar.alloc("coef", [4], F32)
        tmpo, R_tmpo = ar.alloc("tmpo", [4, 64], F32)
        onsa, R_onsa = ar.alloc("onsa", [512], F32)
        onsab, R_onsab = ar.alloc("onsab", [512], BF16)
        tmpi, R_tmpi = ar.alloc("tmpi", [4, 128], F32)
        vals, R_vals = ar.alloc("vals", [128], F32)
        vals2, R_vals2 = ar.alloc("vals2", [128], F32)
        m8, R_m8 = ar.alloc("m8", [16], F32)
        selm, R_selm = ar.alloc("selm", [128], F32)
        negs, R_negs = ar.alloc("negs", [128], BF16)
        gates, R_gates = ar.alloc("gates4", [24], F32)
        x1g, R_x1g = ar.alloc("x1g", [D], F32)
        qs_sb, R_qs = ar.alloc("qs", [16, 4, 4], BF16)
        kTn, R_kTn = ar.alloc("kTn", [2, 64], BF16)
        zn, R_zn = ar.alloc("zn", [792], F32)
        vnew, R_vnew = ar.alloc("vnew", [2, 2, 65], BF16)
        op("vector", lambda e: e.memset(vnew, 1.0), wr=[R_vnew])
        kcT_q, R_kcTq = ar.alloc("kcTq", [256], BF16)
        vcT_q, R_vcTq = ar.alloc("vcTq", [256], BF16)
        vc_q, R_vcq = ar.alloc("vcq", [2, 2, 65], BF16)
        op("vector", lambda e: e.memset(vc_q, 1.0), wr=[R_vcq])
        wst = [ar.alloc("wst%d" % i, [4, 128], F32) for i in range(2)]
        wkb, R_wkb = ar.alloc("wkb", [4, 128], BF16)
        kwT_q, R_kwTq = ar.alloc("kwTq", [512], BF16)
        vwq, R_vwq = ar.alloc("vwq", [4, 2, 65], BF16)
        op("vector", lambda e: e.memset(vwq, 1.0), wr=[R_vwq])
        onsaT_s, R_onsaTs = ar.alloc("onsaTs", [4, 64], BF16)
        mark_loop = ar.mark()
        stg = [ar.alloc("stg%d" % i, [64, 128], F32) for i in range(1)]
        pgb, R_pgb = ar.alloc("pgb", [64, 128], BF16)
        ksT_q, R_ksTq = ar.alloc("ksTq", [S], BF16)
        vpb, R_vpb = ar.alloc("vpb", [64, 2, 65], BF16)
        op("vector", lambda e: e.memset(vpb, 1.0), wr=[R_vpb])
        stg_ctr = [0]

        R_stg8 = [Res("stg%d" % i) for i in range(16)]

        def load_pages(ci, s_glob):
            st_, R_st0 = stg[0]
            R_dm = R_stg8[ci * 4 + (s_glob % 4)]
            stg_ctr[0] += 1
            for pg in range(64):
                dyn_ctr[0] += 1
                nm = "pr%d" % dyn_ctr[0]
                out_ap = st_[:, pg, :]
                idx_ap = pt_sb[s_glob:s_glob + 1, pg:pg + 1]
                base = caches[ci]

                def fn(e, nm=nm, out_ap=out_ap, idx_ap=idx_ap, base=base):
                    r = e.alloc_register(nm)
                    e.reg_load(r, idx_ap)
                    v = e.snap(r, donate=True, min_val=0, max_val=NPHYS - 1)
                    e1 = v * 128
                    src = base[ds(e1, 128), :]
                    ins = e.dma_start(out=out_ap, in_=src)
                    vc = e.get_value_cache()
                    seen = set()
                    for ex in (e1, src.offset, src.offset * 4):
                        try:
                            al = vc.lookup(ex)
                        except Exception:
                            al = None
                        if al is not None and al.val.name not in seen and al.val.name != r.name:
                            seen.add(al.val.name)
                            e.free_register(al.val)
                    e.free_register(r)
                    return ins
                sc.op("sync", fn, rd=[R_sc], wr=[R_st0], dma=R_dm, deferred=True, indep=(pg > 0))
            return st_, R_st0

        def cols4(ap2d_col):
            return bass.AP(tensor=ap2d_col.tensor, offset=ap2d_col.offset, ap=[list(ap2d_col.ap[0]), [16, 4]])

        cast_ctr = [0]

        def cast_eng():
            cast_ctr[0] += 1
            return ("vector", "gpsimd")[cast_ctr[0] % 2]

        R_p3b = Res("p3b")
        R_psclS = Res("psclS")
        for sg in range(NSG):
            load_row(*rowsS["A2"], 1 + sg, 4, 2, "A", w.tmpg, w.R_tmpg)
            load_row(*rowsS["B2"], 1 + sg, 3, 2, "B", w.tmpg, w.R_tmpg)
            load_row(*rowsS["G2"], 1 + sg, 5, 3, "G1", w.tmpg, w.R_tmpg)
            op("sync", lambda e: e.dma_start(out=x1g[0:64, :], in_=x1s[S + 64 * sg:S + 64 * sg + 64, :]), rd=[R_x1s], wr=[R_x1g], dma=R_x1g)
            norm_mod(w, x1g[0:64, :], R_x1g, 64, *rowsS["A2"], *rowsS["B2"], 0)
            pq, R_pq = ps[7]
            for c in range(4):
                for k in range(8):
                    op("tensor", lambda e: e.matmul(out=pq[:, c * 64:(c + 1) * 64], lhsT=wq[:, k, c, :], rhs=w.hT[:, k, 0:64],
                                                     start=(k == 0), stop=(k == 7)), rd=[R_wq, w.R_hT], wr=[R_pq])
            op("scalar", lambda e: e.activation(out=qs_sb, in_=pq[:, 0:256].rearrange("p (c t s) -> p s c t", c=4, t=4), func=AF.Copy),
               rd=[R_pq], wr=[R_qs])
            pkn, R_pkn = ps[6]
            for n, c_lo in enumerate((256, 512)):
                for k in range(8):
                    op("tensor", lambda e: e.matmul(out=pkn[:, n * 64:(n + 1) * 64], lhsT=wkv[:, k, c_lo:c_lo + 128], rhs=w.hT[:, k, 0:64],
                                                     start=(k == 0), stop=(k == 7)), rd=[R_wkv, w.R_hT], wr=[R_pkn])
            op("vector", lambda e: e.tensor_copy(out=kTn.rearrange("p a b -> p (a b)"), in_=pkn[:, 0:128]), rd=[R_pkn], wr=[R_kTn])
            po, R_po = ps[0]
            po_v = po[:, 0:256].rearrange("p (c t) -> p c t", c=4)
            for sl in range(16):
                s_glob = 16 * sg + sl
                pz, R_pz = ps[5]
                pz2, R_pz2 = ps[4]
                for k in range(8):
                    op("tensor", lambda e: e.matmul(out=pz[0:4, :], lhsT=cols4(w.hT[:, k, sl:sl + 1]), rhs=wkv[:, k, 0:512],
                                                     start=(k == 0), stop=(k == 7)), rd=[w.R_hT, R_wkv], wr=[R_pz])
                for k in range(8):
                    op("tensor", lambda e: e.matmul(out=pz2[0:4, 0:280], lhsT=cols4(w.hT[:, k, sl:sl + 1]), rhs=wkv[:, k, 512:792],
                                                     start=(k == 0), stop=(k == 7)), rd=[w.R_hT, R_wkv], wr=[R_pz2])
                op("scalar", lambda e: e.activation(out=zn[0:4, 0:512], in_=pz[0:4, :], func=AF.Copy), rd=[R_pz], wr=[R_zn])
                op("scalar", lambda e: e.activation(out=zn[0:4, 512:768], in_=pz2[0:4, 0:256], func=AF.Copy), rd=[R_pz2], wr=[R_zn])
                op("scalar", lambda e: e.activation(out=gates[0:4, :], in_=pz2[0:4, 256:280], func=AF.Sigmoid), rd=[R_pz2], wr=[R_gates])
                op("vector", lambda e: e.tensor_copy(out=vnew[0:4, 0, :, 0:64], in_=zn[0:4, 384:512].rearrange("p (g d) -> p g d", g=2)),
                   rd=[R_zn], wr=[R_vnew])
                op("vector", lambda e: e.tensor_copy(out=vnew[0:4, 1, :, 0:64], in_=zn[0:4, 640:768].rearrange("p (g d) -> p g d", g=2)),
                   rd=[R_zn], wr=[R_vnew])
                for ci, (dstT, R_dT_, W4) in enumerate(((kcT_q, R_kcTq, W4k), (vcT_q, R_vcTq, W4v))):
                    st_, R_st = load_pages(ci, s_glob)
                    op(cast_eng(), lambda e: e.tensor_copy(out=pgb, in_=st_), rd=[R_st], wr=[R_pgb])
                    pc_, R_pc_ = ps[6]
                    for pg in range(64):
                        op("tensor", lambda e: e.matmul(out=pc_[:, 4 * pg:4 * pg + 4], lhsT=pgb[:, pg, :], rhs=W4, start=True, stop=True),
                           rd=[R_pgb, R_W4k, R_W4v], wr=[R_pc_])
                    op("vector", lambda e: e.tensor_copy(out=dstT, in_=pc_[:, 0:256]), rd=[R_pc_], wr=[R_dT_])
                for jt in range(2):
                    pv_, R_pv = ps[7]
                    op("tensor", lambda e: e.matmul(out=pv_[:, jt * 128:(jt + 1) * 128], lhsT=vcT_q[:, jt * 128:(jt + 1) * 128], rhs=ident_b,
                                                     start=True, stop=True), rd=[R_vcTq, R_identb], wr=[R_pv])
                op("vector", lambda e: e.tensor_copy(out=vc_q[:, :, :, 0:64], in_=ps[7][0][:, 0:256].rearrange("p (j g d) -> p j g d", j=2, g=2)),
                   rd=[ps[7][1]], wr=[R_vcq])
                st_, R_st = load_pages(2, s_glob)
                op(cast_eng(), lambda e: e.tensor_copy(out=pgb, in_=st_), rd=[R_st], wr=[R_pgb])
                for q4 in range(16):
                    ptq, R_ptq = ps[1] if q4 % 2 == 0 else ps[7]
                    for pp in range(4):
                        pg = 4 * q4 + pp
                        op("tensor", lambda e: e.matmul(out=ptq[:, pp * 128:(pp + 1) * 128], lhsT=pgb[:, pg, :], rhs=ident_b, start=True, stop=True),
                           rd=[R_pgb, R_identb], wr=[R_ptq])
                    if q4 % 2 == 0:
                        op("scalar", lambda e: e.activation(out=ksT_q[:, q4 * 512:(q4 + 1) * 512], in_=ptq[:, :], func=AF.Copy), rd=[R_ptq], wr=[R_ksTq])
                    else:
                        op("vector", lambda e: e.tensor_copy(out=ksT_q[:, q4 * 512:(q4 + 1) * 512], in_=ptq[:, :]), rd=[R_ptq], wr=[R_ksTq])
                st_, R_st = load_pages(3, s_glob)
                op(cast_eng(), lambda e: e.tensor_copy(out=vpb[:, :, :, 0:64], in_=st_.rearrange("p a (g d) -> p a g d", g=2)), rd=[R_st], wr=[R_vpb])
                for n, st in enumerate((st_wk, st_wv)):
                    ws_, R_ws = wst[n]
                    op("sync", lambda e: e.dma_start(out=ws_, in_=st[s_glob].rearrange("(m p) c -> p m c", p=128)), wr=[R_ws], dma=R_ws)
                op("vector", lambda e: e.tensor_copy(out=wkb, in_=wst[0][0]), rd=[wst[0][1]], wr=[R_wkb])
                pw_, R_pw_ = ps[1]
                for m in range(4):
                    op("tensor", lambda e: e.matmul(out=pw_[:, m * 128:(m + 1) * 128], lhsT=wkb[:, m, :], rhs=ident_b, start=True, stop=True),
                       rd=[R_wkb, R_identb], wr=[R_pw_])
                op("scalar", lambda e: e.activation(out=kwT_q, in_=pw_[:, :], func=AF.Copy), rd=[R_pw_], wr=[R_kwTq])
                op("vector", lambda e: e.tensor_copy(out=vwq[:, :, :, 0:64], in_=wst[1][0].rearrange("p a (g d) -> p a g d", g=2)), rd=[wst[1][1]], wr=[R_vwq])
                for g in range(2):
                    gs = slice(64 * g, 64 * g + 64)
                    q_rhs = qs_sb[gs, sl, :, :].rearrange("p a b -> p (a b)")
                    pO, R_pO = ps[4 + g]
                    pI, R_pI = ps[6]
                    ir4 = (irep4[0:4, :, :].rearrange("p a b -> p (a b)"), R_irep4)
                    tl = [dict(kT=kcT_q[gs, jt * 128:(jt + 1) * 128], Rk=R_kcTq, v=vc_q[:, jt, g, :], Rv=R_vcq, jt=jt) for jt in range(2)]
                    attend(g, 4, q_rhs, R_qs, tl, pO, R_pO, pI, R_pI, irep=ir4)
                    finish_branch(g, 4, 0, pO, R_pO, gates[0:4, :], True)
                    select_blocks(g, 4, pI, R_pI, bonus_s[0:4, :], R_sc, 15, 128)
                    nx, R_nx = nexp[g]
                    tl = [dict(kT=ksT_q[gs, pg * 128:(pg + 1) * 128], Rk=R_ksTq, v=vpb[:, pg, g, :], Rv=R_vpb,
                               add=(nx[0:4, pg * 128:(pg + 1) * 128], R_nx)) for pg in range(64)]
                    tl.append(dict(kT=cols4(kTn[gs, 0, sl:sl + 1]), Rk=R_kTn, v=vnew[0:4, 0, g, :], Rv=R_vnew, mul=(causal4[0:4, :], R_sc), nk=4))
                    attend(g, 4, q_rhs, R_qs, tl, pO, R_pO, irep=ir4)
                    finish_branch(g, 4, 1, pO, R_pO, gates[0:4, :], False)
                    tl = []
                    for m in range(4):
                        d = dict(kT=kwT_q[gs, m * 128:(m + 1) * 128], Rk=R_kwTq, v=vwq[:, m, g, :], Rv=R_vwq)
                        if m == 0:
                            d["mul"] = (wms0, R_sc)
                        tl.append(d)
                    tl.append(dict(kT=cols4(kTn[gs, 1, sl:sl + 1]), Rk=R_kTn, v=vnew[0:4, 1, g, :], Rv=R_vnew, mul=(causal4[0:4, :], R_sc), nk=4))
                    attend(g, 4, q_rhs, R_qs, tl, pO, R_pO, irep=ir4)
                    finish_branch(g, 4, 2, pO, R_pO, gates[0:4, :], False)
                op("scalar", lambda e: e.activation(out=onsab[0:4, :], in_=onsa[0:4, :], func=AF.Copy), rd=[R_onsa], wr=[R_onsab])
                for c in range(4):
                    op("tensor", lambda e: e.matmul(out=cols4(po_v[:, c, sl:sl + 1]), lhsT=onsab[0:4, c * 128:(c + 1) * 128], rhs=ident_b[0:4, 0:4],
                                                     start=True, stop=True), rd=[R_onsab, R_identb], wr=[R_po])
                if sl % 4 == 3:
                    new_epoch()
            op("vector", lambda e: e.tensor_copy(out=onsaT_s.rearrange("p a b -> p (a b)"), in_=po[:, 0:256]), rd=[R_po], wr=[R_onsaTs])
            new_epoch()
            ar.reset(mark_loop)
            wu, R_wu = ar.alloc("wu", [8, 512], BF16, res=R_p3b)
            op("gpsimd", lambda e: e.dma_start(out=wu, in_=w_in_v[:, :, 1304:1816]), wr=[R_wu], dma=R_wu)
            wout, R_wout = ar.alloc("wout", [8, D], BF16, res=R_p3b)
            op("gpsimd", lambda e: e.dma_start(out=wout, in_=w_out.rearrange("(k p) n -> p k n", p=128)), wr=[R_wout], dma=R_wout)
            pw, R_pw = ar.alloc("pw", [4, 128], BF16, res=R_p3b)
            op("gpsimd", lambda e: e.dma_start(out=pw, in_=pool_w.rearrange("g c d -> c g d")), wr=[R_pw], dma=R_pw)
            wsab, R_wsab = ar.alloc("wsab", [2, 4, 64], BF16, res=R_p3b)
            op("gpsimd", lambda e: e.dma_start(out=wsab[0:120, 0, :, :], in_=wsa_d[:, :, :]), wr=[R_wsab], dma=R_wsab)
            op("gpsimd", lambda e: e.dma_start(out=wsab[0:120, 1, :, :], in_=wsb_d[:, :, :]), wr=[R_wsab], dma=R_wsab)
            wnb, R_wnb = ar.alloc("wnb", [4, 64], BF16, res=R_p3b)
            op("gpsimd", lambda e: e.dma_start(out=wnb[0:64, :, :], in_=wn_d[:, :, :]), wr=[R_wnb], dma=R_wnb)
            pscl, R_pscl = ar.alloc("pscl", [4], F32, res=R_psclS)
            for gi in range(4):
                op("sync", lambda e: e.dma_start(out=pscl[:, gi:gi + 1], in_=pool_scale[0:1, gi * 128:(gi + 1) * 128].rearrange("a d -> d a")),
                   wr=[R_pscl], dma=R_pscl)
            stpb, R_stpb = ar.alloc("stpb", [2, 512], BF16, res=R_p3b)
            for hf in range(2):
                op("gpsimd", lambda e: e.dma_start(out=stpb[0:120, hf, :], in_=st_pool[16 * sg + 8 * hf:16 * sg + 8 * hf + 8].rearrange("s r c -> (s r) c")),
                   wr=[R_stpb], dma=R_stpb)
            un, R_un = ar.alloc("un", [512], BF16)
            dTs, R_dTs = ar.alloc("dTs", [4, 64], BF16)
            ypTs, R_ypTs = ar.alloc("ypTs", [4, 64], BF16)
            pu_, R_pu_ = ps[5]
            for k in range(8):
                op("tensor", lambda e: e.matmul(out=pu_[0:64, :], lhsT=w.hT[:, k, 0:64], rhs=wu[:, k, :], start=(k == 0), stop=(k == 7)),
                   rd=[w.R_hT, R_wu], wr=[R_pu_])
            op("vector", lambda e: e.tensor_copy(out=un[0:64, :], in_=pu_[0:64, :]), rd=[R_pu_], wr=[R_un])
            pd, R_pd = ps[1]
            for gi in range(4):
                cs_ = slice(gi * 128, (gi + 1) * 128)
                op("tensor", lambda e: e.matmul(out=pd[:, gi * 64:(gi + 1) * 64], lhsT=stpb[0:120, 0, cs_], rhs=wsab[0:120, 0, gi, :], start=True, stop=False),
                   rd=[R_stpb, R_wsab], wr=[R_pd])
                op("tensor", lambda e: e.matmul(out=pd[:, gi * 64:(gi + 1) * 64], lhsT=stpb[0:120, 1, cs_], rhs=wsab[0:120, 1, gi, :], start=False, stop=False),
                   rd=[R_stpb, R_wsab], wr=[R_pd])
                op("tensor", lambda e: e.matmul(out=pd[:, gi * 64:(gi + 1) * 64], lhsT=un[0:64, cs_], rhs=wnb[0:64, gi, :], start=False, stop=True),
                   rd=[R_un, R_wnb], wr=[R_pd])
            op("vector", lambda e: e.tensor_copy(out=dTs.rearrange("p a b -> p (a b)"), in_=pd[:, 0:256]), rd=[R_pd], wr=[R_dTs])
            pyp, R_pyp = ps[7]
            for gi in range(4):
                op("tensor", lambda e: e.matmul(out=pyp[:, gi * 64:(gi + 1) * 64], lhsT=pw[:, gi, :], rhs=dTs[:, gi, :], start=True, stop=True),
                   rd=[R_pw, R_dTs], wr=[R_pyp])
            for gi in range(4):
                op("scalar", lambda e: e.activation(out=ypTs[:, gi, :], in_=pyp[:, gi * 64:(gi + 1) * 64], func=AF.Identity, scale=pscl[:, gi:gi + 1]),
                   rd=[R_pyp, R_pscl], wr=[R_ypTs])
            py = [ps[2], ps[3]]
            for half in range(2):
                pyh, R_pyh = py[half]
                for c in range(8):
                    lh = onsaT_s[:, c, :] if c < 4 else ypTs[:, c - 4, :]
                    Rl = R_onsaTs if c < 4 else R_ypTs
                    op("tensor", lambda e: e.matmul(out=pyh[0:64, :], lhsT=lh, rhs=wout[:, c, half * 512:(half + 1) * 512],
                                                     start=(c == 0), stop=(c == 7)), rd=[Rl, R_wout], wr=[R_pyh])
            post_residual(w, py, 64, x1g, R_x1g, *rowsS["G2"])
            op("sync", lambda e: e.dma_start(out=x2s[NOWN * 128 + 64 * sg:NOWN * 128 + 64 * sg + 64, :], in_=x1g[0:64, :]), rd=[R_x1g], wr=[R_x2s], dma=R_x1g)
            new_epoch()
            ar.reset(mark_loop)
            if sg + 1 < NSG:
                stg = [ar.alloc("stg%d" % i, [64, 128], F32, res=stg[i][1]) for i in range(1)]
                pgb, _ = ar.alloc("pgb", [64, 128], BF16, res=R_pgb)
                ksT_q, _ = ar.alloc("ksTq", [S], BF16, res=R_ksTq)
                vpb, _ = ar.alloc("vpb", [64, 2, 65], BF16, res=R_vpb)
                op("vector", lambda e: e.memset(vpb, 1.0), wr=[R_vpb])
        new_epoch()
        ar.reset(base_mark)

        wo2, R_wo2 = ar.alloc("wo2", [NF, D], BF16)
        op("gpsimd", lambda e: e.dma_start(out=wo2, in_=f2wo.rearrange("(f p) n -> p f n", p=128)), wr=[R_wo2], dma=R_wo2)
        w = alloc_work()
        f = alloc_ffn()
        R_rows3 = Res("rows3")
        rows3 = {nm: ar.alloc(nm, [D], F32, res=R_rows3) for nm in ("A3", "B3", "G3")}

        def load_rows3(kind):
            load_row(*rows3["A3"], kind, 7, 4, "A", w.tmpg, w.R_tmpg)
            load_row(*rows3["B3"], kind, 6, 4, "B", w.tmpg, w.R_tmpg)
            load_row(*rows3["G3"], kind, 8, 5, "G5", w.tmpg, w.R_tmpg)

        def x2load(row0, nt):
            return lambda x_ap, R_x: op("sync", lambda e: e.dma_start(out=x_ap[0:nt, :], in_=x2s[row0:row0 + nt, :]), rd=[R_x2s], wr=[R_x], dma=R_x)

        def post3(ti, x_ap, R_x, nt, c0, tag):
            op("sync", lambda e: e.dma_start(out=o_y[tag:tag + nt, :], in_=x_ap[0:nt, :]), rd=[R_x], wr=[R_oy], dma=R_x)

        load_rows3(0)
        for p in range(NOWN // 4):
            tiles = [(x2load(i * 128, 128), 128, i * 128) for i in range(4 * p, 4 * p + 4)]
            ffn_pass(w, f, tiles, f2wi_v, wo2, R_wo2, rows3["A3"], rows3["B3"], rows3["G3"], post3)
        for sg in range(NSG):
            new_epoch()
            load_rows3(1 + sg)
            ffn_pass(w, f, [(x2load(NOWN * 128 + 64 * sg, 64), 64, NOWN * 128 + 64 * sg)], f2wi_v, wo2, R_wo2, rows3["A3"], rows3["B3"], rows3["G3"], post3)
        sc.emit()
    return nc


_NC = {}


def _tables(r, ngrp, nown):
    key = np.arange(128)[:, None]
    q = np.arange(128)[None, :]
    cm = np.zeros((ngrp, 128, 128), np.float32)
    for dk in range(ngrp):
        if dk < r:
            cm[dk] = 1.0
        elif dk == r:
            cm[dk] = (key <= q)
    bonus = np.zeros((nown, 128, 128), np.float32)
    cmpmask = np.zeros((nown, 2, 128, 128), np.float32)
    blk = np.arange(128)[None, :]
    for i in range(nown):
        j = ngrp * i + r
        t = j * 128 + np.arange(128)[:, None]
        cur = t // 64
        forced = (blk == 0) | (blk == cur) | (blk == cur - 1)
        start_ok = blk * 64 <= t
        bonus[i] = np.where(forced, 1e4, np.where(start_ok, 0.0, -1e30))
        for jt in range(2):
            jc = jt * 128 + np.arange(128)[:, None]
            tq = j * 128 + np.arange(128)[None, :]
            cmpmask[i, jt] = ((jc + 1) * 32 - 1 <= tq)
    wm = np.zeros((5, 5, 128, 128), np.float32)
    for kw in range(5):
        j = kw + r if kw < 4 else 4 + r
        for m in range(5):
            if j - 4 + m < 0:
                continue
            if m == 0:
                wm[kw, m] = (key > q)
            elif m == 4:
                wm[kw, m] = (key <= q)
            else:
                wm[kw, m] = 1.0
    wp = np.zeros((2, 4, 2, 128, 128), np.float32)
    for kind in range(2):
        j = r if kind == 0 else ngrp + r
        for gi, wdw in enumerate((2, 4, 8, 16)):
            tpos = j * 128 + np.arange(128)[None, :]
            cnt = np.minimum(wdw, tpos + 1).astype(np.float32)
            for wh in range(2):
                spos = (j - 1 + wh) * 128 + np.arange(128)[:, None]
                inwin = (spos > tpos - wdw) & (spos <= tpos)
                wp[kind, gi, wh] = inwin / cnt - (spos == tpos)
    return cm, bonus, cmpmask, wm, wp


def _sample_tables():
    bonus_s = np.zeros((4, 128), np.float32)
    bonus_s[:, 0] = 1e4
    bonus_s[:, 127] = 1e4
    n = np.arange(4)[:, None]
    t = np.arange(4)[None, :]
    causal4 = (n <= t).astype(np.float32)
    wms0 = (np.arange(128)[:, None] > t).astype(np.float32)
    wsa = np.zeros((120, 4, 64), np.float32)
    wsb = np.zeros((120, 4, 64), np.float32)
    wn = np.zeros((64, 4, 64), np.float32)
    for gi, wdw in enumerate((2, 4, 8, 16)):
        for tt in range(4):
            for s in range(16):
                col = tt * 16 + s
                for rr in range(15):
                    if rr > 15 + tt - wdw:
                        if s < 8:
                            wsa[s * 15 + rr, gi, col] = 1.0 / wdw
                        else:
                            wsb[(s - 8) * 15 + rr, gi, col] = 1.0 / wdw
                for t2 in range(4):
                    val = (1.0 / wdw if (t2 <= tt and t2 > tt - wdw) else 0.0) - (1.0 if t2 == tt else 0.0)
                    wn[t2 * 16 + s, gi, col] = val
    return bonus_s, causal4, wms0, wsa, wsb, wn


def kernel(**inputs):
    f = lambda k: np.asarray(inputs[k])
    x_prompt, x_sample = f("x_prompt"), f("x_sample")
    DB = x_sample.shape[0]
    nseq = DB // NCORES
    nsg = nseq // 16
    nphys = f("cache_cmp_k").shape[1]
    key = (nseq, nphys)
    if key not in _NC:
        _NC[key] = build_nc(NSEQ=nseq, NPHYS=nphys)
    nc = _NC[key]
    ident = np.eye(128, dtype=np.float32)
    pair = (np.arange(128)[:, None] // 2 == np.arange(64)[None, :]).astype(np.float32)
    bmask = (np.arange(128)[:, None] // 32 == np.arange(4)[None, :]).astype(np.float32)
    bonus_s, causal4, wms0, wsa, wsb, wn = _sample_tables()
    caches = [np.ascontiguousarray(f(k)[0]).reshape(nphys * 128, 128) for k in ("cache_cmp_k", "cache_cmp_v", "cache_sel_k", "cache_sel_v")]
    in_maps = []
    for c in range(NCORES):
        b, r = c // NGRP, c % NGRP
        cm, bonus, cmpmask, wm, wp = _tables(r, NGRP, NOWN)
        sl = slice(nseq * c, nseq * (c + 1))
        xs_c = np.ascontiguousarray(x_sample[sl].reshape(nsg, 16, 4, D).transpose(0, 2, 1, 3).reshape(4 * nseq, D))
        c17 = np.concatenate([f("c_prompt")[b:b + 1], f("c_sample")[sl]], axis=0)
        rinfo = np.zeros((1, 8), np.int32)
        rinfo[0, 0] = r
        m = {
            "xp": np.ascontiguousarray(x_prompt[b]), "xs": xs_c, "c17": np.ascontiguousarray(c17),
            "w_ada": f("w_ada")[0], "b_ada": f("b_ada")[0][None, :], "gains": f("norm_gains")[0],
            "f1wi": f("ffn1_wi")[0], "f1wo": f("ffn1_wo")[0], "f2wi": f("ffn2_wi")[0], "f2wo": f("ffn2_wo")[0],
            "w_in": f("w_in")[0], "w_out": f("w_out")[0], "cmpw": f("cmp_w")[0],
            "pool_w": f("pool_w")[0], "pool_scale": f("pool_scale")[0][None, :],
            "ident": ident, "pair": pair, "bmask": bmask, "rinfo": rinfo,
            "cm": cm, "bonus": bonus, "cmpmask": cmpmask, "wm": wm, "wp": wp,
            "st_wk": np.ascontiguousarray(f("state_win_k")[0, sl].reshape(nseq, 512, 128)),
            "st_wv": np.ascontiguousarray(f("state_win_v")[0, sl].reshape(nseq, 512, 128)),
            "st_pool": np.ascontiguousarray(f("state_pool")[0, sl]),
            "pt": np.ascontiguousarray(f("page_table")[sl]).astype(np.int32),
            "c_cmp_k": caches[0], "c_cmp_v": caches[1], "c_sel_k": caches[2], "c_sel_v": caches[3],
            "bonus_s": bonus_s, "causal4": causal4, "wms0": wms0, "wsa": wsa, "wsb": wsb, "wn": wn,
        }
        in_maps.append(m)
    res = run_bass_kernel_spmd(nc, in_maps, core_ids=list(range(NCORES)))
    kernel.last = res
    y_p = np.zeros((2, S, D), np.float32)
    y_s = np.zeros((DB, 4, D), np.float32)
    kv_p = [np.zeros((1, 2, S, 2, 64), np.float32) for _ in range(4)]
    kv_s = [np.zeros((1, DB, 4, 2, 64), np.float32) for _ in range(4)]
    p_win = [np.zeros((1, 2, 512, 2, 64), np.float32) for _ in range(2)]
    p_pool = np.zeros((1, 2, 15, 512), np.float32)
    s_win = [np.zeros((1, DB, 512, 2, 64), np.float32) for _ in range(2)]
    s_pool = np.zeros((1, DB, 15, 512), np.float32)
    unperm = lambda a, last: a.reshape((nsg, 4, 16) + last).transpose(0, 2, 1, *range(3, 3 + len(last))).reshape((nseq, 4) + last)
    for c in range(NCORES):
        b, r = c // NGRP, c % NGRP
        sl = slice(nseq * c, nseq * (c + 1))
        rr = res.results[c]
        okv = np.asarray(rr["o_kv"])
        oy = np.asarray(rr["o_y"])
        y_p[b].reshape(NBLK, 128, D)[r::NGRP] = oy[:NOWN * 128].reshape(NOWN, 128, D)
        y_s[sl] = unperm(oy[NOWN * 128:], (D,))
        for n in range(4):
            kv_p[n][0, b].reshape(NBLK, 128, 2, 64)[r::NGRP] = okv[n, :S].reshape(NBLK, 128, 2, 64)[r::NGRP]
            kv_s[n][0, sl] = unperm(okv[n, S:], (2, 64))
        if r == NGRP - 1:
            for n in range(2):
                p_win[n][0, b] = okv[4 + n, S - 512:S].reshape(512, 2, 64)
            p_pool[0, b] = np.asarray(rr["o_u"])[S - 15:S]
        sw = np.asarray(rr["o_swin"])
        for n in range(2):
            s_win[n][0, sl] = sw[n].reshape(nseq, 512, 2, 64)
        s_pool[0, sl] = np.asarray(rr["o_spool"])
    return (y_p, y_s, *kv_p, *p_win, p_pool, *kv_s, *s_win, s_pool)
```

```python
from contextlib import ExitStack
import numpy as np
import ml_dtypes
import concourse.bass as bass
import concourse.mybir as mybir
from concourse.bass import ds
from concourse.bass_utils import run_bass_kernel_spmd

F32 = mybir.dt.float32
BF16 = mybir.dt.bfloat16
I32 = mybir.dt.int32
AF = mybir.ActivationFunctionType
ALU = mybir.AluOpType

NGRP = 1
NCORES = 2 * NGRP
D = 1024
S = 8192
NBLK = 64
NOWN = 64 // NGRP
NSEQ = 16
NTS = 64
DFF = 2816
NF = 22
INW = 1816
EPS = 1e-6
ENG = ("sync", "scalar", "vector", "gpsimd", "tensor")


class Res:
    def __init__(self, name):
        self.name = name
        self.lw = None
        self.rd = []
        self.sem = None
        self.cnt = 0


class _Rec:
    def __init__(self):
        self.call = None

    def __getattr__(self, name):
        def m(*a, **k):
            self.call = (name, a, k)
            return self
        return m


class Sched:
    def __init__(self, nc, stack):
        self.nc = nc
        self.stack = stack
        self.q = {e: [] for e in ENG}
        self.cnt = {e: 0 for e in ENG}
        self.waited = {e: {} for e in ENG}
        self.ep = 0
        self.esem = {(e, 0): stack.enter_context(nc.semaphore("es_" + e)) for e in ENG}
        self.miles = {(e, 0): set() for e in ENG}
        self.dres = []

    def new_sem_epoch(self):
        self.barrier()
        self.ep += 1
        for e in ENG:
            self.esem[(e, self.ep)] = self.stack.enter_context(self.nc.semaphore("es%d_%s" % (self.ep, e)))
            self.miles[(e, self.ep)] = set()
            self.cnt[e] = 0
            self.waited[e] = {k: v for k, v in self.waited[e].items() if k[0] == "D"}

    def _wait(self, eng, tok):
        if tok[0] == "E":
            if tok[3] != self.ep:
                return
            if tok[1] == eng and eng == "tensor":
                return
            key = ("E", tok[1]); val = tok[2]
        else:
            key = ("D", id(tok[1])); val = tok[2]
        if self.waited[eng].get(key, 0) >= val:
            return
        self.waited[eng][key] = val
        if tok[0] == "E":
            self.miles[(tok[1], self.ep)].add(val)
            self.q[eng].append(("we", (tok[1], self.ep), val))
        else:
            self.q[eng].append(("wd", tok[1].sem, val))

    def op(self, eng, fn, rd=(), wr=(), dma=None, deferred=False, indep=False):
        if not deferred:
            rec = _Rec()
            fn(rec)
            name, a, k = rec.call
            fn = (lambda e, name=name, a=a, k=k: getattr(e, name)(*a, **k))
        deps = []
        for b in rd:
            if b.lw is not None:
                deps.append(b.lw)
        for b in wr:
            if b.lw is not None and not indep:
                deps.append(b.lw)
            deps.extend(b.rd)
        for d in deps:
            self._wait(eng, d)
        if dma is None:
            self.cnt[eng] += 1
            tok = ("E", eng, self.cnt[eng], self.ep)
            self.q[eng].append(("oe", fn, self.cnt[eng], self.ep))
        else:
            if dma.sem is None:
                dma.sem = self.stack.enter_context(self.nc.semaphore("ds_" + dma.name))
                self.dres.append(dma)
            dma.cnt += 16
            assert dma.cnt < 30000, ("dma semaphore would overflow", dma.name)
            tok = ("D", dma, dma.cnt)
            self.q[eng].append(("od", fn, dma.sem))
        for b in rd:
            b.rd.append(tok)
        for b in wr:
            b.lw = tok
            b.rd = []
        return tok

    def barrier(self):
        for e in ENG:
            for e2 in ENG:
                if self.cnt[e2] > 0:
                    self._wait(e, ("E", e2, self.cnt[e2], self.ep))
            for d in self.dres:
                if d.cnt > 0:
                    self._wait(e, ("D", d, d.cnt))

    def emit(self):
        nc = self.nc
        self.barrier()
        rank = {}
        for key_ in self.miles:
            ks = sorted(self.miles[key_])
            rank[key_] = {k: i + 1 for i, k in enumerate(ks)}
            assert len(ks) < 30000, ("engine semaphore would overflow", key_, len(ks))
        self.n_miles = {k: len(v) for k, v in rank.items()}
        with nc.Block() as block:
            def mk(ename):
                def run(eng):
                    for it in self.q[ename]:
                        if it[0] == "we":
                            eng.wait_ge(self.esem[it[1]], rank[it[1]][it[2]])
                        elif it[0] == "wd":
                            eng.wait_ge(it[1], it[2])
                        elif it[0] == "oe":
                            ins = it[1](eng)
                            if it[2] in rank[(ename, it[3])]:
                                ins.then_inc(self.esem[(ename, it[3])], 1)
                        else:
                            ins = it[1](eng)
                            ins.then_inc(it[2], 16)
                return run
            block.sync(mk("sync"))
            block.scalar(mk("scalar"))
            block.vector(mk("vector"))
            block.gpsimd(mk("gpsimd"))
            block.tensor(mk("tensor"))


class Arena:
    def __init__(self, nc, stack, nbytes):
        self.t = stack.enter_context(nc.sbuf_tensor("arena", [128, nbytes // 4], F32))
        self.nbytes = nbytes
        self.top = 0
        self.n = 0

    def mark(self):
        return self.top

    def reset(self, m):
        self.top = m

    def alloc(self, name, free_shape, dtype, res=None):
        esz = 2 if dtype == BF16 else 4
        n = int(np.prod(free_shape))
        nb = (n * esz + 31) // 32 * 32
        assert self.top + nb <= self.nbytes, (name, self.top, nb, self.nbytes)
        off = self.top
        self.top += nb
        v = self.t[:, off // 4:(off + nb) // 4]
        if dtype != F32:
            v = v.bitcast(dtype)
        v = v[:, 0:n]
        if len(free_shape) == 2:
            v = v.rearrange("p (a b) -> p a b", a=free_shape[0])
        elif len(free_shape) == 3:
            v = v.rearrange("p (a b c) -> p a b c", a=free_shape[0], b=free_shape[1])
        self.n += 1
        return v, (res if res is not None else Res(name + str(self.n)))


SCALE = 0.125
NEGM = -30000.0


def bc_mid(a, n):
    return bass.AP(tensor=a.tensor, offset=a.offset, ap=[list(a.ap[0]), [0, n], list(a.ap[1])])


def bc_last(a, n):
    return bass.AP(tensor=a.tensor, offset=a.offset, ap=[list(a.ap[0]), list(a.ap[1]), [0, n]])


def build_nc(NSEQ=64, NPHYS=10240):
    NSG = NSEQ // 16
    NTS = 4 * NSEQ
    XROWS = S + NTS
    OROWS = NOWN * 128 + NTS
    M17 = 1 + NSEQ
    nc = bass.Bass("TRN2", target_bir_lowering=False)
    stack = ExitStack()
    dt_in = lambda n, s, d=F32: nc.dram_tensor(n, s, d, kind="ExternalInput").ap()
    dt_out = lambda n, s, d=F32: nc.dram_tensor(n, s, d, kind="ExternalOutput").ap()
    dt_int = lambda n, s, d=F32: nc.dram_tensor(n, s, d, kind="Internal").ap()
    xp = dt_in("xp", [S, D])
    xs = dt_in("xs", [NTS, D])
    c17 = dt_in("c17", [M17, D])
    w_ada = dt_in("w_ada", [D, 9 * D])
    b_ada = dt_in("b_ada", [1, 9 * D])
    gains = dt_in("gains", [6, D])
    f1wi = dt_in("f1wi", [D, 2 * DFF])
    f1wo = dt_in("f1wo", [DFF, D])
    f2wi = dt_in("f2wi", [D, 2 * DFF])
    f2wo = dt_in("f2wo", [DFF, D])
    w_in = dt_in("w_in", [D, INW])
    w_out = dt_in("w_out", [D, D])
    cmpw = dt_in("cmpw", [2, 32])
    pool_w = dt_in("pool_w", [4, 128, 128])
    pool_scale = dt_in("pool_scale", [1, 512])
    ident_d = dt_in("ident", [128, 128])
    pair_d = dt_in("pair", [128, 64])
    bmask_d = dt_in("bmask", [128, 4])
    rinfo_d = dt_in("rinfo", [1, 8], I32)
    cm_d = dt_in("cm", [NGRP, 128, 128])
    bonus_d = dt_in("bonus", [NOWN, 128, 128])
    cmpmask_d = dt_in("cmpmask", [NOWN, 2, 128, 128])
    wm_d = dt_in("wm", [5, 5, 128, 128])
    wp_d = dt_in("wp", [2, 4, 2, 128, 128])
    st_wk = dt_in("st_wk", [NSEQ, 512, 128])
    st_wv = dt_in("st_wv", [NSEQ, 512, 128])
    st_pool = dt_in("st_pool", [NSEQ, 15, 512])
    pt_d = dt_in("pt", [NSEQ, 64], I32)
    caches = [dt_in(nm, [NPHYS * 128, 128]) for nm in ("c_cmp_k", "c_cmp_v", "c_sel_k", "c_sel_v")]
    bonus_s_d = dt_in("bonus_s", [4, 128])
    causal4_d = dt_in("causal4", [4, 4])
    wms0_d = dt_in("wms0", [128, 4])
    wsa_d = dt_in("wsa", [120, 4, 64])
    wsb_d = dt_in("wsb", [120, 4, 64])
    wn_d = dt_in("wn", [64, 4, 64])

    o_kv = dt_out("o_kv", [6, XROWS, 128])
    o_u = dt_out("o_u", [XROWS, 512])
    o_y = dt_out("o_y", [OROWS, D])
    o_swin = dt_out("o_swin", [2, NSEQ, 512, 128])
    o_spool = dt_out("o_spool", [NSEQ, 15, 512])

    modd = dt_int("modd", [M17, 9 * D])
    x1s = dt_int("x1s", [XROWS, D])
    x2s = dt_int("x2s", [OROWS, D])
    ksT_s = dt_int("ksT_s", [128, S], BF16)
    vs_s = dt_int("vs_s", [S, 130], BF16)
    kwT_s = dt_int("kwT_s", [128, (4 + NBLK) * 128], BF16)
    vw_s = dt_int("vw_s", [(4 + NBLK) * 128, 130], BF16)
    u_s = dt_int("u_s", [(1 + NBLK) * 128, 512], BF16)

    with stack:
        sc = Sched(nc, stack)
        ar = Arena(nc, stack, 204 * 1024)
        ps = []
        for i in range(8):
            t = stack.enter_context(nc.psum_tensor("ps%d" % i, [128, 512], F32))
            ps.append((t, Res("ps%d" % i)))
        R_modd = Res("modd"); R_okv = Res("okv"); R_ou = Res("ou"); R_oy = Res("oy")
        R_oswin = Res("oswin"); R_ospool = Res("ospool")
        R_x1s = Res("x1s"); R_x2s = Res("x2s")
        R_ksTs = Res("ksTs"); R_vss = Res("vss"); R_kwTs = Res("kwTs"); R_vws = Res("vws"); R_us = Res("us")

        def op(eng, fn, rd=(), wr=(), dma=None, deferred=False):
            return sc.op(eng, fn, rd=rd, wr=wr, dma=dma, deferred=deferred)

        def new_epoch():
            sc.barrier()

        dyn_ctr = [0]

        def dyn_dma(out_ap, base_ap, axis, idx_ap, mult, const, size, maxv, rd, wr, dma, rearr=None):
            dyn_ctr[0] += 1
            nm = "dr%d" % dyn_ctr[0]

            def fn(e):
                r = e.alloc_register(nm)
                e.reg_load(r, idx_ap)
                v = e.snap(r, donate=True, min_val=0, max_val=maxv)
                e1 = v * mult
                e2 = e1 + const if const else e1
                if axis == 0:
                    src = base_ap[ds(e2, size), :]
                else:
                    src = base_ap[:, ds(e2, size)]
                exprs = [e1, e2, src.offset, src.offset * 2, src.offset * 4]
                if rearr is not None:
                    src = src.rearrange(rearr, p=128)
                ins = e.dma_start(out=out_ap, in_=src)
                vc = e.get_value_cache()
                seen = set()
                for ex in exprs:
                    try:
                        al = vc.lookup(ex)
                    except Exception:
                        al = None
                    if al is not None and al.val.name not in seen and al.val.name != r.name:
                        seen.add(al.val.name)
                        e.free_register(al.val)
                e.free_register(r)
                return ins
            return op("sync", fn, rd=rd, wr=wr, dma=dma, deferred=True)

        ident_f, R_identf = ar.alloc("identf", [128], F32)
        ident_b, R_identb = ar.alloc("identb", [128], BF16)
        identrep, R_identrep = ar.alloc("identrep", [4, 128], BF16)
        op("sync", lambda e: e.dma_start(out=ident_f, in_=ident_d[:, :]), wr=[R_identf], dma=R_identf)
        op("vector", lambda e: e.tensor_copy(out=ident_b, in_=ident_f), rd=[R_identf], wr=[R_identb])
        for h in range(4):
            op("vector", lambda e, h=h: e.tensor_copy(out=identrep[:, h, :], in_=ident_f), rd=[R_identf], wr=[R_identrep])
        ones_f, R_ones = ar.alloc("ones", [128], F32)
        op("vector", lambda e: e.memset(ones_f, 1.0), wr=[R_ones])
        rinfo_t = stack.enter_context(nc.sbuf_tensor("rinfo_sb", [1, 8], I32))
        R_rinfo = Res("rinfo")
        op("sync", lambda e: e.dma_start(out=rinfo_t[:], in_=rinfo_d[:, :]), wr=[R_rinfo], dma=R_rinfo)
        ridx = rinfo_t[0:1, 0:1]
        kcT_sb, R_kcT = ar.alloc("kcT", [256], BF16)
        vcT_sb, R_vcT = ar.alloc("vcT", [256], BF16)
        vc_sb, R_vc = ar.alloc("vc", [2, 2, 65], BF16)
        W4k, R_W4k = ar.alloc("W4k", [4], BF16)
        W4v, R_W4v = ar.alloc("W4v", [4], BF16)
        wcol, R_wcol = ar.alloc("wcol", [2], F32)
        bmask, R_bmask = ar.alloc("bmask", [4], F32)
        pair_b, R_pair = ar.alloc("pair", [64], BF16)
        op("gpsimd", lambda e: e.dma_start(out=pair_b, in_=pair_d[:, :]), wr=[R_pair], dma=R_pair)
        op("sync", lambda e: e.dma_start(out=bmask, in_=bmask_d[:, :]), wr=[R_bmask], dma=R_bmask)
        for n in range(2):
            for qd in range(4):
                op("sync", lambda e, n=n, qd=qd: e.dma_start(out=wcol[32 * qd:32 * qd + 32, n:n + 1],
                                                            in_=cmpw[n:n + 1, :].rearrange("a l -> l a")),
                   wr=[R_wcol], dma=R_wcol)
        op("vector", lambda e: e.tensor_scalar(out=W4k, in0=bmask, scalar1=wcol[:, 0:1], scalar2=None, op0=ALU.mult),
           rd=[R_bmask, R_wcol], wr=[R_W4k])
        op("vector", lambda e: e.tensor_scalar(out=W4v, in0=bmask, scalar1=wcol[:, 1:2], scalar2=None, op0=ALU.mult),
           rd=[R_bmask, R_wcol], wr=[R_W4v])
        op("vector", lambda e: e.memset(vc_sb, 1.0), wr=[R_vc])
        base_mark = ar.mark()

        cs, R_cs = ar.alloc("cs", [D], F32)
        csb, R_csb = ar.alloc("csb", [D], BF16)
        siluT, R_siluT = ar.alloc("siluT", [8, M17], BF16)
        modsb, R_mod = ar.alloc("modsb", [9 * D], F32)
        bada, R_bada = ar.alloc("bada", [9 * D], F32)
        op("sync", lambda e: e.dma_start(out=cs[0:M17, :], in_=c17[:, :]), wr=[R_cs], dma=R_cs)
        op("sync", lambda e: e.dma_start(out=bada[0:1, :], in_=b_ada[:, :]), wr=[R_bada], dma=R_bada)
        op("scalar", lambda e: e.activation(out=csb[0:M17, :], in_=cs[0:M17, :], func=AF.Silu), rd=[R_cs], wr=[R_csb])
        pT, R_pT = ps[0]
        for k in range(8):
            pTk, R_pTk = ps[k % 2]
            op("tensor", lambda e, k=k, pTk=pTk: e.matmul(out=pTk[:, (k // 2) * M17:(k // 2 + 1) * M17], lhsT=csb[0:M17, k * 128:(k + 1) * 128],
                                                           rhs=ident_b[0:M17, 0:M17], start=True, stop=True),
               rd=[R_csb, R_identb], wr=[R_pTk])
        for k in range(8):
            pTk, R_pTk = ps[k % 2]
            op("vector", lambda e, k=k, pTk=pTk: e.tensor_copy(out=siluT[:, k, :], in_=pTk[:, (k // 2) * M17:(k // 2 + 1) * M17]),
               rd=[R_pTk], wr=[R_siluT])
        wad = [ar.alloc("wad%d" % i, [8, 512], BF16) for i in range(2)]
        w_ada_v = w_ada.rearrange("(k p) n -> p k n", p=128)
        for cg in range(18):
            wt, R_wt = wad[cg % 2]
            op("gpsimd", lambda e, cg=cg, wt=wt: e.dma_start(out=wt, in_=w_ada_v[:, :, cg * 512:(cg + 1) * 512]),
               wr=[R_wt], dma=R_wt)
            pm, R_pm = ps[2 + cg % 2]
            for k in range(8):
                op("tensor", lambda e, k=k, wt=wt, pm=pm: e.matmul(out=pm[0:M17, :], lhsT=siluT[:, k, :], rhs=wt[:, k, :],
                                                                   start=(k == 0), stop=False),
                   rd=[R_siluT, R_wt], wr=[R_pm])
            op("tensor", lambda e, cg=cg, pm=pm: e.matmul(out=pm[0:M17, :], lhsT=ones_f[0:1, 0:M17],
                                                          rhs=bada[0:1, cg * 512:(cg + 1) * 512], start=False, stop=True),
               rd=[R_ones, R_bada], wr=[R_pm])
            op("vector", lambda e, cg=cg, pm=pm: e.tensor_copy(out=modsb[0:M17, cg * 512:(cg + 1) * 512], in_=pm[0:M17, :]),
               rd=[R_pm], wr=[R_mod])
        op("sync", lambda e: e.dma_start(out=modd[:, :], in_=modsb[0:M17, :]), rd=[R_mod], wr=[R_modd], dma=R_modd)
        zt, R_zt = ar.alloc("zt", [5, 130], BF16)
        op("vector", lambda e: e.memset(zt, 0.0), wr=[R_zt])
        op("sync", lambda e: e.dma_start(out=kwT_s[:, 0:512], in_=zt.rearrange("p a b -> p (a b)")[:, 0:512]), rd=[R_zt], wr=[R_kwTs], dma=R_zt)
        op("sync", lambda e: e.dma_start(out=vw_s[0:512, :].rearrange("(m p) c -> p m c", p=128), in_=zt[:, 0:4, :]),
           rd=[R_zt], wr=[R_vws], dma=R_zt)
        op("sync", lambda e: e.dma_start(out=u_s[0:128, :], in_=zt.rearrange("p a b -> p (a b)")[:, 0:512]), rd=[R_zt], wr=[R_us], dma=R_zt)
        new_epoch()
        ar.reset(base_mark)

        def load_row(dst, R_dst, kind, mod_i, gain_i, mode, tmpg, R_tmpg):
            msrc = lambda a, b: modd[a:b, mod_i * D:(mod_i + 1) * D]
            if kind == 0:
                op("sync", lambda e: e.dma_start(out=dst, in_=msrc(0, 1).to_broadcast([128, D])), rd=[R_modd], wr=[R_dst], dma=R_dst)
            else:
                r0 = 1 + 16 * (kind - 1)
                for t in range(4):
                    op("sync", lambda e, t=t: e.dma_start(out=dst[16 * t:16 * t + 16, :], in_=msrc(r0, r0 + 16)), rd=[R_modd], wr=[R_dst], dma=R_dst)
            if mode == "B":
                return
            op("sync", lambda e: e.dma_start(out=tmpg, in_=gains[gain_i:gain_i + 1, :].to_broadcast([128, D])), wr=[R_tmpg], dma=R_tmpg)
            if mode == "A":
                op("vector", lambda e: e.scalar_tensor_tensor(out=dst, in0=dst, scalar=1.0, in1=tmpg, op0=ALU.add, op1=ALU.mult),
                   rd=[R_dst, R_tmpg], wr=[R_dst])
            else:
                sclr = 0.5 if mode == "G5" else 1.0
                op("vector", lambda e: e.scalar_tensor_tensor(out=dst, in0=dst, scalar=sclr, in1=tmpg, op0=ALU.mult, op1=ALU.mult),
                   rd=[R_dst, R_tmpg], wr=[R_dst])

        class Work:
            pass

        def alloc_work():
            w = Work()
            w.h32, w.R_h32 = ar.alloc("h32", [D], F32)
            w.hb, w.R_hb = ar.alloc("hb", [D], BF16)
            w.junk, w.R_junk = ar.alloc("junk", [D], BF16)
            w.ssq, w.R_ssq = ar.alloc("ssq", [1], F32)
            w.rstd, w.R_rstd = ar.alloc("rstd", [1], F32)
            w.hT, w.R_hT = ar.alloc("hT", [8, 512], BF16)
            w.tmpg, w.R_tmpg = ar.alloc("tmpg", [D], F32)
            return w

        def rms_rstd(w, nt):
            op("vector", lambda e: e.tensor_scalar(out=w.rstd[0:nt, :], in0=w.ssq[0:nt, :], scalar1=1.0 / D, scalar2=EPS,
                                                    op0=ALU.mult, op1=ALU.add), rd=[w.R_ssq], wr=[w.R_rstd])
            op("scalar", lambda e: e.activation(out=w.rstd[0:nt, :], in_=w.rstd[0:nt, :], func=AF.Sqrt), rd=[w.R_rstd], wr=[w.R_rstd])
            op("vector", lambda e: e.reciprocal(out=w.rstd[0:nt, :], in_=w.rstd[0:nt, :]), rd=[w.R_rstd], wr=[w.R_rstd])

        def norm_mod(w, x_ap, R_x, nt, A, R_A, Bt, R_B, col0):
            op("scalar", lambda e: e.activation(out=w.junk[0:nt, :], in_=x_ap, func=AF.Square, accum_out=w.ssq[0:nt, :]),
               rd=[R_x], wr=[w.R_junk, w.R_ssq])
            rms_rstd(w, nt)
            op("vector", lambda e: e.scalar_tensor_tensor(out=w.h32[0:nt, :], in0=x_ap, scalar=w.rstd[0:nt, :], in1=A[0:nt, :],
                                                           op0=ALU.mult, op1=ALU.mult), rd=[R_x, w.R_rstd, R_A], wr=[w.R_h32])
            op("vector", lambda e: e.tensor_tensor(out=w.hb[0:nt, :], in0=w.h32[0:nt, :], in1=Bt[0:nt, :], op=ALU.add),
               rd=[w.R_h32, R_B], wr=[w.R_hb])
            for half in range(2):
                pt_, R_pt = ps[half]
                for kk in range(4):
                    k = half * 4 + kk
                    op("tensor", lambda e, k=k, kk=kk, pt_=pt_: e.matmul(out=pt_[:, kk * 128:kk * 128 + nt],
                                                                          lhsT=w.hb[0:nt, k * 128:(k + 1) * 128],
                                                                          rhs=ident_b[0:nt, 0:nt], start=True, stop=True),
                       rd=[w.R_hb, R_identb], wr=[R_pt])
                src = pt_[:, :].rearrange("p (a b) -> p a b", a=4)[:, :, 0:nt]
                if half == 0:
                    op("scalar", lambda e, half=half, src=src: e.activation(out=w.hT[:, half * 4:half * 4 + 4, col0:col0 + nt],
                                                                              in_=src, func=AF.Copy), rd=[R_pt], wr=[w.R_hT])
                else:
                    op("vector", lambda e, half=half, src=src: e.tensor_copy(out=w.hT[:, half * 4:half * 4 + 4, col0:col0 + nt],
                                                                               in_=src), rd=[R_pt], wr=[w.R_hT])

        def post_residual(w, py, nt, x_ap, R_x, G, R_G):
            op("scalar", lambda e: e.activation(out=w.junk[0:nt, 0:512], in_=py[0][0][0:nt, :], func=AF.Square,
                                                 accum_out=w.ssq[0:nt, :]), rd=[py[0][1]], wr=[w.R_junk, w.R_ssq])
            op("scalar", lambda e: e.activation(out=w.junk[0:nt, 512:1024], in_=py[1][0][0:nt, :], func=AF.Square,
                                                 accum_out=w.rstd[0:nt, :]), rd=[py[1][1]], wr=[w.R_junk, w.R_rstd])
            op("vector", lambda e: e.tensor_tensor(out=w.ssq[0:nt, :], in0=w.ssq[0:nt, :], in1=w.rstd[0:nt, :], op=ALU.add),
               rd=[w.R_ssq, w.R_rstd], wr=[w.R_ssq])
            rms_rstd(w, nt)
            for half in range(2):
                pyh, R_pyh = py[half]
                hs = slice(half * 512, (half + 1) * 512)
                op("vector", lambda e, pyh=pyh, hs=hs: e.scalar_tensor_tensor(
                    out=w.h32[0:nt, hs], in0=pyh[0:nt, :], scalar=w.rstd[0:nt, :], in1=G[0:nt, hs], op0=ALU.mult, op1=ALU.mult),
                    rd=[R_pyh, w.R_rstd, R_G], wr=[w.R_h32])
            op("vector", lambda e: e.tensor_tensor(out=x_ap[0:nt, :], in0=x_ap[0:nt, :], in1=w.h32[0:nt, :], op=ALU.add),
               rd=[w.R_h32, R_x], wr=[R_x])

        wi_ctr = [0]

        def ffn_pass(w, f, tiles, wi_v, wo, R_wo, rowsA, rowsB, rowsG, post):
            ntok = sum(t[1] for t in tiles)
            A, R_A = rowsA; Bt, R_B = rowsB; G, R_G = rowsG
            col = 0
            cols = []
            for ti, (load_fn, nt, tag) in enumerate(tiles):
                x_ap, R_x = f.xt[ti]
                load_fn(x_ap, R_x)
                norm_mod(w, x_ap[0:nt, :], R_x, nt, A, R_A, Bt, R_B, col)
                cols.append(col)
                col += nt
            for fc in range(NF):
                wb, R_wb = f.wib[wi_ctr[0] % 3]
                wi_ctr[0] += 1
                op("gpsimd", lambda e, fc=fc, wb=wb: e.dma_start(out=wb[:, :, 0:128], in_=wi_v[:, :, fc * 128:(fc + 1) * 128]),
                   wr=[R_wb], dma=R_wb)
                op("gpsimd", lambda e, fc=fc, wb=wb: e.dma_start(out=wb[:, :, 128:256],
                                                                 in_=wi_v[:, :, DFF + fc * 128:DFF + (fc + 1) * 128]),
                   wr=[R_wb], dma=R_wb)
                pa, R_pa = ps[2 + 2 * (fc % 2)]
                pb, R_pb = ps[3 + 2 * (fc % 2)]
                for k in range(8):
                    op("tensor", lambda e, k=k, wb=wb, pa=pa: e.matmul(out=pa[:, 0:ntok], lhsT=wb[:, k, 0:128], rhs=w.hT[:, k, 0:ntok],
                                                                       start=(k == 0), stop=(k == 7)), rd=[R_wb, w.R_hT], wr=[R_pa])
                for k in range(8):
                    op("tensor", lambda e, k=k, wb=wb, pb=pb: e.matmul(out=pb[:, 0:ntok], lhsT=wb[:, k, 128:256], rhs=w.hT[:, k, 0:ntok],
                                                                       start=(k == 0), stop=(k == 7)), rd=[R_wb, w.R_hT], wr=[R_pb])
                op("scalar", lambda e, pa=pa: e.activation(out=f.sa[:, 0:ntok], in_=pa[:, 0:ntok], func=AF.Silu), rd=[R_pa], wr=[f.R_sa])
                op("vector", lambda e, fc=fc, pb=pb: e.tensor_tensor(out=f.gT[:, fc, 0:ntok], in0=f.sa[:, 0:ntok], in1=pb[:, 0:ntok], op=ALU.mult),
                   rd=[f.R_sa, R_pb], wr=[f.R_gT])
            for ti, (load_fn, nt, tag) in enumerate(tiles):
                x_ap, R_x = f.xt[ti]
                c0 = cols[ti]
                py = [ps[0], ps[1]]
                for half in range(2):
                    pyh, R_pyh = py[half]
                    for fc in range(NF):
                        op("tensor", lambda e, fc=fc, half=half, pyh=pyh, c0=c0, nt=nt: e.matmul(
                            out=pyh[0:nt, :], lhsT=f.gT[:, fc, c0:c0 + nt], rhs=wo[:, fc, half * 512:(half + 1) * 512],
                            start=(fc == 0), stop=(fc == NF - 1)), rd=[f.R_gT, R_wo], wr=[R_pyh])
                post_residual(w, py, nt, x_ap, R_x, G, R_G)
                post(ti, x_ap, R_x, nt, c0, tag)

        class FBuf:
            pass

        def alloc_ffn():
            f = FBuf()
            f.xt = [ar.alloc("xt%d" % i, [D], F32) for i in range(4)]
            f.gT, f.R_gT = ar.alloc("gT", [NF, 512], BF16)
            f.sa, f.R_sa = ar.alloc("sa", [512], F32)
            f.wib = [ar.alloc("wib%d" % i, [8, 256], BF16) for i in range(3)]
            return f

        wo1, R_wo1 = ar.alloc("wo1", [NF, D], BF16)
        op("gpsimd", lambda e: e.dma_start(out=wo1, in_=f1wo.rearrange("(f p) n -> p f n", p=128)), wr=[R_wo1], dma=R_wo1)
        winT, R_win = ar.alloc("winT", [8, INW], BF16)
        w_in_v = w_in.rearrange("(k p) n -> p k n", p=128)
        op("gpsimd", lambda e: e.dma_start(out=winT, in_=w_in_v), wr=[R_win], dma=R_win)
        w = alloc_work()
        f = alloc_ffn()
        R_rows1 = Res("rows1")
        rows1 = {nm: ar.alloc(nm, [D], F32, res=R_rows1) for nm in ("A1", "B1", "G1", "A2", "B2")}
        kvst = [ar.alloc("kvst%d" % i, [768], F32) for i in range(2)]
        ust = [ar.alloc("ust%d" % i, [512], F32) for i in range(2)]
        ubs = [ar.alloc("ub%d" % i, [512], BF16) for i in range(2)]
        vst = [ar.alloc("vst%d" % i, [2, 2, 65], BF16) for i in range(2)]
        kTst = [ar.alloc("kTst%d" % i, [256], BF16) for i in range(2)]
        kcrb = [ar.alloc("kcrb%d" % i, [256], BF16) for i in range(2)]
        for i in range(2):
            op("vector", lambda e, i=i: e.memset(vst[i][0], 1.0), wr=[vst[i][1]])

        def load_rows1(kind):
            load_row(*rows1["A1"], kind, 1, 0, "A", w.tmpg, w.R_tmpg)
            load_row(*rows1["B1"], kind, 0, 0, "B", w.tmpg, w.R_tmpg)
            load_row(*rows1["G1"], kind, 2, 1, "G5", w.tmpg, w.R_tmpg)
            load_row(*rows1["A2"], kind, 4, 2, "A", w.tmpg, w.R_tmpg)
            load_row(*rows1["B2"], kind, 3, 2, "B", w.tmpg, w.R_tmpg)

        f1wi_v = f1wi.rearrange("(k p) n -> p k n", p=128)
        f2wi_v = f2wi.rearrange("(k p) n -> p k n", p=128)
        tctr = [0]

        def post1(ti, x_ap, R_x, nt, c0, tag):
            is_s = isinstance(tag, tuple)
            sg = tag[1] if is_s else None
            row0 = (S + 64 * sg) if is_s else tag * 128
            tc_ = tctr[0]
            tctr[0] += 1
            op("sync", lambda e: e.dma_start(out=x1s[row0:row0 + nt, :], in_=x_ap[0:nt, :]), rd=[R_x], wr=[R_x1s], dma=R_x)
            norm_mod(w, x_ap[0:nt, :], R_x, nt, *rows1["A2"], *rows1["B2"], c0)
            pk0, R_pk0 = ps[6]
            pk1, R_pk1 = ps[7]
            pu, R_pu = ps[5]
            for k in range(8):
                op("tensor", lambda e, k=k: e.matmul(out=pk0[0:nt, :], lhsT=w.hT[:, k, c0:c0 + nt], rhs=winT[:, k, 512:1024],
                                                      start=(k == 0), stop=(k == 7)), rd=[w.R_hT, R_win], wr=[R_pk0])
            for k in range(8):
                op("tensor", lambda e, k=k: e.matmul(out=pk1[0:nt, 0:256], lhsT=w.hT[:, k, c0:c0 + nt], rhs=winT[:, k, 1024:1280],
                                                      start=(k == 0), stop=(k == 7)), rd=[w.R_hT, R_win], wr=[R_pk1])
            for k in range(8):
                op("tensor", lambda e, k=k: e.matmul(out=pu[0:nt, :], lhsT=w.hT[:, k, c0:c0 + nt], rhs=winT[:, k, 1304:1816],
                                                      start=(k == 0), stop=(k == 7)), rd=[w.R_hT, R_win], wr=[R_pu])
            ks_, R_ks = kvst[tc_ % 2]
            us_, R_us_ = ust[tc_ % 2]
            op("scalar", lambda e: e.activation(out=ks_[0:nt, 0:512], in_=pk0[0:nt, :], func=AF.Copy), rd=[R_pk0], wr=[R_ks])
            op("scalar", lambda e: e.activation(out=ks_[0:nt, 512:768], in_=pk1[0:nt, 0:256], func=AF.Copy), rd=[R_pk1], wr=[R_ks])
            op("vector", lambda e: e.tensor_copy(out=us_[0:nt, :], in_=pu[0:nt, :]), rd=[R_pu], wr=[R_us_])
            op("sync", lambda e: e.dma_start(out=o_kv[:, row0:row0 + nt, :].rearrange("s t c -> t s c"),
                                             in_=ks_[0:nt, :].rearrange("p (s c) -> p s c", s=6)), rd=[R_ks], wr=[R_okv], dma=R_ks)
            op("sync", lambda e: e.dma_start(out=o_u[row0:row0 + nt, :], in_=us_[0:nt, :]), rd=[R_us_], wr=[R_ou], dma=R_us_)
            if is_s:
                for t in range(4):
                    for n in range(2):
                        dst = bass.AP(tensor=o_swin.tensor, offset=(n * NSEQ + 16 * sg) * 65536 + (508 + t) * 128, ap=[[65536, 16], [1, 128]])
                        op("sync", lambda e, t=t, n=n, dst=dst: e.dma_start(out=dst, in_=ks_[16 * t:16 * t + 16, 512 + 128 * n:640 + 128 * n]),
                           rd=[R_ks], wr=[R_oswin], dma=R_ks)
                    dstp = bass.AP(tensor=o_spool.tensor, offset=(16 * sg * 15 + 11 + t) * 512, ap=[[15 * 512, 16], [1, 512]])
                    op("sync", lambda e, t=t, dstp=dstp: e.dma_start(out=dstp, in_=us_[16 * t:16 * t + 16, :]),
                       rd=[R_us_], wr=[R_ospool], dma=R_us_)
                return
            j = tag
            ub_, R_ub = ubs[tc_ % 2]
            op("vector", lambda e: e.tensor_copy(out=ub_, in_=us_), rd=[R_us_], wr=[R_ub])
            op("sync", lambda e: e.dma_start(out=u_s[(1 + j) * 128:(2 + j) * 128, :], in_=ub_), rd=[R_ub], wr=[R_us], dma=R_ub)
            vs_, R_vs = vst[tc_ % 2]
            op("vector", lambda e: e.tensor_copy(out=vs_[:, 0, :, 0:64], in_=ks_[:, 384:512].rearrange("p (g d) -> p g d", g=2)),
               rd=[R_ks], wr=[R_vs])
            op("vector", lambda e: e.tensor_copy(out=vs_[:, 1, :, 0:64], in_=ks_[:, 640:768].rearrange("p (g d) -> p g d", g=2)),
               rd=[R_ks], wr=[R_vs])
            op("sync", lambda e: e.dma_start(out=vs_s[j * 128:(j + 1) * 128, :], in_=vs_[:, 0, :, :].rearrange("p g c -> p (g c)")),
               rd=[R_vs], wr=[R_vss], dma=R_vs)
            op("sync", lambda e: e.dma_start(out=vw_s[(4 + j) * 128:(5 + j) * 128, :], in_=vs_[:, 1, :, :].rearrange("p g c -> p (g c)")),
               rd=[R_vs], wr=[R_vws], dma=R_vs)
            pf, R_pf = ps[4]
            for n, c_lo in enumerate((768, 1024)):
                for k in range(8):
                    op("tensor", lambda e, k=k, n=n, c_lo=c_lo: e.matmul(out=pf[:, n * 128:(n + 1) * 128], lhsT=winT[:, k, c_lo:c_lo + 128],
                                                                          rhs=w.hT[:, k, c0:c0 + 128], start=(k == 0), stop=(k == 7)),
                       rd=[w.R_hT, R_win], wr=[R_pf])
            kt_, R_kt = kTst[tc_ % 2]
            op("scalar", lambda e: e.activation(out=kt_, in_=pf[:, 0:256], func=AF.Copy), rd=[R_pf], wr=[R_kt])
            op("sync", lambda e: e.dma_start(out=ksT_s[:, j * 128:(j + 1) * 128], in_=kt_[:, 0:128]), rd=[R_kt], wr=[R_ksTs], dma=R_kt)
            op("sync", lambda e: e.dma_start(out=kwT_s[:, (4 + j) * 128:(5 + j) * 128], in_=kt_[:, 128:256]), rd=[R_kt], wr=[R_kwTs], dma=R_kt)
            kb_, R_kb = kcrb[tc_ % 2]
            op("vector", lambda e: e.tensor_copy(out=kb_, in_=ks_[:, 0:256]), rd=[R_ks], wr=[R_kb])
            pc, R_pc = ps[4]
            op("tensor", lambda e: e.matmul(out=pc[:, 256:260], lhsT=kb_[:, 0:128], rhs=W4k, start=True, stop=True),
               rd=[R_kb, R_W4k], wr=[R_pc])
            op("tensor", lambda e: e.matmul(out=pc[:, 260:264], lhsT=kb_[:, 128:256], rhs=W4v, start=True, stop=True),
               rd=[R_kb, R_W4v], wr=[R_pc])
            op("vector", lambda e: e.tensor_copy(out=kcT_sb[:, 4 * j:4 * j + 4], in_=pc[:, 256:260]), rd=[R_pc], wr=[R_kcT])
            op("vector", lambda e: e.tensor_copy(out=vcT_sb[:, 4 * j:4 * j + 4], in_=pc[:, 260:264]), rd=[R_pc], wr=[R_vcT])

        def xload(src):
            return lambda x_ap, R_x: op("sync", lambda e: e.dma_start(out=x_ap[0:src.shape[0], :], in_=src), wr=[R_x], dma=R_x)

        load_rows1(0)
        for p in range(NBLK // 4):
            tiles = [(xload(xp[j * 128:(j + 1) * 128, :]), 128, j) for j in range(4 * p, 4 * p + 4)]
            ffn_pass(w, f, tiles, f1wi_v, wo1, R_wo1, rows1["A1"], rows1["B1"], rows1["G1"], post1)
            if p % 4 == 3:
                new_epoch()
        for sg in range(NSG):
            new_epoch()
            load_rows1(1 + sg)
            ffn_pass(w, f, [(xload(xs[64 * sg:64 * sg + 64, :]), 64, ("S", sg))], f1wi_v, wo1, R_wo1, rows1["A1"], rows1["B1"], rows1["G1"], post1)
        for jt in range(2):
            pv_, R_pv = ps[jt]
            op("tensor", lambda e, jt=jt, pv_=pv_: e.matmul(out=pv_[:, 0:128], lhsT=vcT_sb[:, jt * 128:(jt + 1) * 128], rhs=ident_b,
                                                            start=True, stop=True), rd=[R_vcT, R_identb], wr=[R_pv])
            op("vector", lambda e, jt=jt, pv_=pv_: e.tensor_copy(out=vc_sb[:, jt, :, 0:64], in_=pv_[:, 0:128].rearrange("p (g d) -> p g d", g=2)),
               rd=[R_pv], wr=[R_vc])
        new_epoch()
        ar.reset(base_mark)
        cw, R_cw = ar.alloc("cw", [16, 512], F32)
        cpl, R_cpl = ar.alloc("cpl", [11 * 512], F32)
        for sg in range(NSG):
            for n, st in enumerate((st_wk, st_wv)):
                srcw = bass.AP(tensor=st.tensor, offset=16 * sg * 65536 + 512, ap=[[512, 127], [65536, 16], [1, 512]])
                dstw = bass.AP(tensor=o_swin.tensor, offset=(n * NSEQ + 16 * sg) * 65536, ap=[[512, 127], [65536, 16], [1, 512]])
                op("sync", lambda e, srcw=srcw: e.dma_start(out=cw[0:127, :, :], in_=srcw), wr=[R_cw], dma=R_cw)
                op("sync", lambda e, dstw=dstw: e.dma_start(out=dstw, in_=cw[0:127, :, :]), rd=[R_cw], wr=[R_oswin], dma=R_cw)
            srcp = bass.AP(tensor=st_pool.tensor, offset=(16 * sg * 15 + 4) * 512, ap=[[15 * 512, 16], [1, 11 * 512]])
            dstp2 = bass.AP(tensor=o_spool.tensor, offset=16 * sg * 15 * 512, ap=[[15 * 512, 16], [1, 11 * 512]])
            op("sync", lambda e, srcp=srcp: e.dma_start(out=cpl[0:16, :], in_=srcp), wr=[R_cpl], dma=R_cpl)
            op("sync", lambda e, dstp2=dstp2: e.dma_start(out=dstp2, in_=cpl[0:16, :]), rd=[R_cpl], wr=[R_ospool], dma=R_cpl)
        new_epoch()
        ar.reset(base_mark)

        w = alloc_work()
        R_p2a = Res("p2a"); R_p2b = Res("p2b")
        wq, R_wq = ar.alloc("wq", [8, 4, 128], BF16, res=R_p2a)
        for h in range(8):
            op("gpsimd", lambda e, h=h: e.dma_start(out=wq[:, :, h % 4, (h // 4) * 64:(h // 4) * 64 + 64], in_=w_in_v[:, :, h * 64:(h + 1) * 64]),
               wr=[R_wq], dma=R_wq)
        wg, R_wg = ar.alloc("wg", [8, 24], BF16, res=R_p2a)
        op("gpsimd", lambda e: e.dma_start(out=wg, in_=w_in_v[:, :, 1280:1304]), wr=[R_wg], dma=R_wg)
        wout, R_wout = ar.alloc("wout", [8, D], BF16, res=R_p2a)
        op("gpsimd", lambda e: e.dma_start(out=wout, in_=w_out.rearrange("(k p) n -> p k n", p=128)), wr=[R_wout], dma=R_wout)
        pw, R_pw = ar.alloc("pw", [4, 128], BF16, res=R_p2a)
        op("gpsimd", lambda e: e.dma_start(out=pw, in_=pool_w.rearrange("g c d -> c g d")), wr=[R_pw], dma=R_pw)
        pscl, R_pscl = ar.alloc("pscl", [4], F32)
        for gi in range(4):
            op("sync", lambda e, gi=gi: e.dma_start(out=pscl[:, gi:gi + 1], in_=pool_scale[0:1, gi * 128:(gi + 1) * 128].rearrange("a d -> d a")),
               wr=[R_pscl], dma=R_pscl)
        cm_sb, R_cm = ar.alloc("cm", [NGRP, 128], BF16, res=R_p2b)
        op("gpsimd", lambda e: e.dma_start(out=cm_sb, in_=cm_d.rearrange("k p q -> p k q")), wr=[R_cm], dma=R_cm)
        wm_sb, R_wm = ar.alloc("wm", [25, 128], BF16, res=R_p2b)
        op("gpsimd", lambda e: e.dma_start(out=wm_sb, in_=wm_d.rearrange("a m p q -> p (a m) q")), wr=[R_wm], dma=R_wm)
        wp_sb, R_wp = ar.alloc("wp", [16, 128], BF16, res=R_p2b)
        op("gpsimd", lambda e: e.dma_start(out=wp_sb, in_=wp_d.rearrange("a g b p q -> p (a g b) q")), wr=[R_wp], dma=R_wp)
        R_rows2 = Res("rows2")
        rows2 = {nm: ar.alloc(nm, [D], F32, res=R_rows2) for nm in ("A2", "B2", "G2")}

        def load_rows2(kind):
            load_row(*rows2["A2"], kind, 4, 2, "A", w.tmpg, w.R_tmpg)
            load_row(*rows2["B2"], kind, 3, 2, "B", w.tmpg, w.R_tmpg)
            load_row(*rows2["G2"], kind, 5, 3, "G1", w.tmpg, w.R_tmpg)

        ksT_buf, _ = ar.alloc("ksTb", [S], BF16)
        vs_buf, _ = ar.alloc("vsb", [NBLK, 130], BF16)
        R_ksb = [Res("ksb")] * NOWN
        R_vsb = [Res("vsb")] * NOWN
        kwb = [ar.alloc("kwb%d" % i, [640], BF16) for i in range(2)]
        vwb = [ar.alloc("vwb%d" % i, [5, 130], BF16) for i in range(2)]
        u2b = [ar.alloc("u2b%d" % i, [2, 512], BF16) for i in range(2)]
        x1t = [ar.alloc("x1t%d" % i, [D], F32) for i in range(2)]
        qT_sb, R_qT = ar.alloc("qT", [4, 128], BF16)
        gates, R_gates = ar.alloc("gates", [24], F32)
        cmk = [ar.alloc("cmk%d" % i, [2, 128], BF16) for i in range(2)]
        bon = [ar.alloc("bon%d" % i, [128], F32) for i in range(2)]
        PTb = [ar.alloc("PT%d" % i, [512], BF16) for i in range(4)]
        nexp = [ar.alloc("nexp%d" % g, [S], BF16) for g in range(2)]
        rden, R_rden = ar.alloc("rden", [4], F32)
        coef, R_coef = ar.alloc("coef", [4], F32)
        tmpo, R_tmpo = ar.alloc("tmpo", [4, 64], F32)
        onsa, R_onsa = ar.alloc("onsa", [512], F32)
        onsab, R_onsab = ar.alloc("onsab", [512], BF16)
        onsaT, R_onsaT = ar.alloc("onsaT", [4, 128], BF16)
        tmpi, R_tmpi = ar.alloc("tmpi", [4, 128], F32)
        vals, R_vals = ar.alloc("vals", [128], F32)
        vals2, R_vals2 = ar.alloc("vals2", [128], F32)
        m8, R_m8 = ar.alloc("m8", [16], F32)
        selm, R_selm = ar.alloc("selm", [128], F32)
        negs, R_negs = ar.alloc("negs", [128], BF16)
        dT_sb, R_dT = ar.alloc("dT", [4, 128], BF16)
        ypT, R_ypT = ar.alloc("ypT", [4, 128], BF16)
        pt_ctr = [0]

        def attend(g, nq, q_rhs, R_q, tiles, pO, R_pO, pI=None, R_pI=None, irep=None):
            n = len(tiles)
            oview = pO[:, 0:260].rearrange("p (h c) -> p h c", h=4)
            irep_ap, R_irep = (identrep.rearrange("p a b -> p (a b)"), R_identrep) if irep is None else irep
            for idx, t in enumerate(tiles):
                nk = t.get("nk", 128)
                pS, R_pS = ps[2 + (pt_ctr[0] % 2)]
                PT, R_PT = PTb[pt_ctr[0] % 4]
                pt_ctr[0] += 1
                has_add = t.get("add") is not None
                op("tensor", lambda e: e.matmul(out=pS[0:nk, 0:4 * nq], lhsT=t["kT"], rhs=q_rhs, start=True, stop=not has_add),
                   rd=[t["Rk"], R_q], wr=[R_pS])
                if has_add:
                    a_ap, R_a = t["add"]
                    op("tensor", lambda e: e.matmul(out=pS[0:nk, 0:4 * nq], lhsT=a_ap, rhs=irep_ap, start=False, stop=True),
                       rd=[R_a, R_irep], wr=[R_pS])
                op("scalar", lambda e: e.activation(out=PT[0:nk, 0:4 * nq], in_=pS[0:nk, 0:4 * nq], func=AF.Exp, scale=SCALE),
                   rd=[R_pS], wr=[R_PT])
                if t.get("mul") is not None:
                    m_ap, R_m = t["mul"]
                    pv3 = PT[0:nk, 0:4 * nq].rearrange("p (h q) -> p h q", h=4)
                    op("vector", lambda e: e.tensor_tensor(out=pv3, in0=pv3, in1=bc_mid(m_ap, 4), op=ALU.mult),
                       rd=[R_PT, R_m], wr=[R_PT])
                for hh in range(4):
                    op("tensor", lambda e: e.matmul(out=oview[0:nq, hh, :], lhsT=PT[0:nk, hh * nq:(hh + 1) * nq], rhs=t["v"],
                                                     start=(idx == 0 and hh == 0), stop=(idx == n - 1)),
                       rd=[R_PT, t["Rv"]], wr=[R_pO])
                if pI is not None:
                    jt = t["jt"]
                    iview = pI[:, :].rearrange("p (h b) -> p h b", h=4)
                    for hh in range(4):
                        op("tensor", lambda e: e.matmul(out=iview[0:nq, hh, jt * 64:(jt + 1) * 64], lhsT=PT[0:nk, hh * nq:(hh + 1) * nq],
                                                         rhs=pair_b[0:nk, :], start=True, stop=True),
                           rd=[R_PT, R_pair], wr=[R_pI])

        def finish_branch(g, nq, br, pO, R_pO, gates_ap, first):
            oview = pO[:, 0:260].rearrange("p (h c) -> p h c", h=4)
            op("vector", lambda e: e.tensor_scalar(out=rden[0:nq, :], in0=oview[0:nq, :, 64], scalar1=1e-30, scalar2=None, op0=ALU.max),
               rd=[R_pO], wr=[R_rden])
            op("vector", lambda e: e.reciprocal(out=rden[0:nq, :], in_=rden[0:nq, :]), rd=[R_rden], wr=[R_rden])
            gv = gates_ap.rearrange("p (h b) -> p h b", b=3)[:, 4 * g:4 * g + 4, br]
            op("vector", lambda e: e.tensor_tensor(out=coef[0:nq, :], in0=rden[0:nq, :], in1=gv, op=ALU.mult),
               rd=[R_rden, R_gates], wr=[R_coef])
            ov = onsa[0:nq, g * 256:(g + 1) * 256].rearrange("p (h d) -> p h d", h=4)
            if first:
                op("vector", lambda e: e.tensor_tensor(out=ov, in0=oview[0:nq, :, 0:64], in1=bc_last(coef[0:nq, :], 64), op=ALU.mult),
                   rd=[R_pO, R_coef], wr=[R_onsa])
            else:
                op("vector", lambda e: e.tensor_tensor(out=tmpo[0:nq, :, :], in0=oview[0:nq, :, 0:64], in1=bc_last(coef[0:nq, :], 64), op=ALU.mult),
                   rd=[R_pO, R_coef], wr=[R_tmpo])
                op("vector", lambda e: e.tensor_tensor(out=ov, in0=ov, in1=tmpo[0:nq, :, :], op=ALU.add), rd=[R_tmpo, R_onsa], wr=[R_onsa])

        def select_blocks(g, nq, pI, R_pI, bonus_ap, R_bonus, kth, nblk_exp):
            iview = pI[:, :].rearrange("p (h b) -> p h b", h=4)
            op("vector", lambda e: e.tensor_tensor(out=tmpi[0:nq, :, :], in0=iview[0:nq, :, :], in1=bc_last(rden[0:nq, :], 128), op=ALU.mult),
               rd=[R_pI, R_rden], wr=[R_tmpi])
            op("vector", lambda e: e.tensor_reduce(out=vals[0:nq, :], in_=tmpi[0:nq, :, :].rearrange("p h b -> p b h"),
                                                    axis=mybir.AxisListType.X, op=ALU.add), rd=[R_tmpi], wr=[R_vals])
            op("vector", lambda e: e.tensor_tensor(out=vals[0:nq, :], in0=vals[0:nq, :], in1=bonus_ap, op=ALU.add),
               rd=[R_vals, R_bonus], wr=[R_vals])
            op("vector", lambda e: e.max(out=m8[0:nq, 0:8], in_=vals[0:nq, :]), rd=[R_vals], wr=[R_m8])
            op("vector", lambda e: e.match_replace(out=vals2[0:nq, :], in_to_replace=m8[0:nq, 0:8], in_values=vals[0:nq, :], imm_value=-3.0e38),
               rd=[R_vals, R_m8], wr=[R_vals2])
            op("vector", lambda e: e.max(out=m8[0:nq, 8:16], in_=vals2[0:nq, :]), rd=[R_vals2], wr=[R_m8])
            op("vector", lambda e: e.tensor_scalar(out=selm[0:nq, :], in0=vals[0:nq, :], scalar1=m8[0:nq, 8 + kth - 9:8 + kth - 8], scalar2=None,
                                                    op0=ALU.is_ge), rd=[R_vals, R_m8], wr=[R_selm])
            op("vector", lambda e: e.tensor_scalar(out=negs[0:nq, :], in0=selm[0:nq, :], scalar1=-1.0, scalar2=-NEGM, op0=ALU.add, op1=ALU.mult),
               rd=[R_selm], wr=[R_negs])
            nx, R_nx = nexp[g]
            op("vector", lambda e: e.tensor_copy(out=nx[0:nq, 0:nblk_exp * 64].rearrange("p (b l) -> p b l", l=64),
                                                  in_=bc_last(negs[0:nq, 0:nblk_exp], 64)), rd=[R_negs], wr=[R_nx])

        load_rows2(0)
        for i in range(NOWN):
            x_ap, R_x = x1t[i % 2]
            nkt = NGRP * i + NGRP
            kind = 0 if i == 0 else 1
            kw5 = min(NGRP * i, 4)
            c128 = 128 * NGRP * i
            dyn_dma(x_ap, x1s, 0, ridx, 128, c128, 128, NGRP - 1, rd=[R_x1s, R_rinfo], wr=[R_x], dma=R_x)
            op("sync", lambda e, i=i: e.dma_start(out=ksT_buf[:, c128:c128 + 128 * NGRP], in_=ksT_s[:, c128:c128 + 128 * NGRP]),
               rd=[R_ksTs], wr=[R_ksb[i]], dma=R_ksb[i])
            op("sync", lambda e, i=i: e.dma_start(out=vs_buf[:, NGRP * i:NGRP * i + NGRP, :], in_=vs_s[c128:c128 + 128 * NGRP, :].rearrange("(m p) c -> p m c", p=128)),
               rd=[R_vss], wr=[R_vsb[i]], dma=R_vsb[i])
            kw_, R_kw = kwb[i % 2]
            vw_, R_vw = vwb[i % 2]
            u2_, R_u2 = u2b[i % 2]
            dyn_dma(kw_, kwT_s, 1, ridx, 128, c128, 640, NGRP - 1, rd=[R_kwTs, R_rinfo], wr=[R_kw], dma=R_kw)
            dyn_dma(vw_, vw_s, 0, ridx, 128, c128, 640, NGRP - 1, rd=[R_vws, R_rinfo], wr=[R_vw], dma=R_vw, rearr="(m p) c -> p m c")
            dyn_dma(u2_, u_s, 0, ridx, 128, c128, 256, NGRP - 1, rd=[R_us, R_rinfo], wr=[R_u2], dma=R_u2, rearr="(m p) c -> p m c")
            cmk_, R_cmk = cmk[i % 2]
            bon_, R_bon = bon[i % 2]
            op("gpsimd", lambda e, i=i, cmk_=cmk_: e.dma_start(out=cmk_, in_=cmpmask_d[i].rearrange("t p q -> p t q")), wr=[R_cmk], dma=R_cmk)
            op("sync", lambda e, i=i, bon_=bon_: e.dma_start(out=bon_, in_=bonus_d[i]), wr=[R_bon], dma=R_bon)
            norm_mod(w, x_ap, R_x, 128, *rows2["A2"], *rows2["B2"], 0)
            pq, R_pq = ps[7]
            for c in range(4):
                for k in range(8):
                    op("tensor", lambda e, c=c, k=k: e.matmul(out=pq[:, c * 128:(c + 1) * 128], lhsT=wq[:, k, c, :], rhs=w.hT[:, k, 0:128],
                                                               start=(k == 0), stop=(k == 7)), rd=[R_wq, w.R_hT], wr=[R_pq])
            op("scalar", lambda e: e.activation(out=qT_sb.rearrange("p a b -> p (a b)"), in_=pq[:, :], func=AF.Copy), rd=[R_pq], wr=[R_qT])
            pg_, R_pg = ps[6]
            for k in range(8):
                op("tensor", lambda e, k=k: e.matmul(out=pg_[:, 0:24], lhsT=w.hT[:, k, 0:128], rhs=wg[:, k, :], start=(k == 0), stop=(k == 7)),
                   rd=[R_wg, w.R_hT], wr=[R_pg])
            op("scalar", lambda e: e.activation(out=gates, in_=pg_[:, 0:24], func=AF.Sigmoid), rd=[R_pg], wr=[R_gates])
            for g in range(2):
                gs = slice(64 * g, 64 * g + 64)
                q_rhs = qT_sb[gs, :, :].rearrange("p a b -> p (a b)")
                pO, R_pO = ps[4 + g]
                pI, R_pI = ps[6]
                tl = [dict(kT=kcT_sb[gs, jt * 128:(jt + 1) * 128], Rk=R_kcT, v=vc_sb[:, jt, g, :], Rv=R_vc, mul=(cmk_[:, jt, :], R_cmk), jt=jt)
                      for jt in range(2)]
                attend(g, 128, q_rhs, R_qT, tl, pO, R_pO, pI, R_pI)
                finish_branch(g, 128, 0, pO, R_pO, gates, True)
                select_blocks(g, 128, pI, R_pI, bon_, R_bon, 16, 2 * nkt)
                nx, R_nx = nexp[g]
                tl = []
                for kt in range(nkt):
                    d = dict(kT=ksT_buf[gs, kt * 128:(kt + 1) * 128], Rk=R_ksb[0], v=vs_buf[:, kt, 65 * g:65 * g + 65], Rv=R_vsb[0],
                             add=(nx[:, kt * 128:(kt + 1) * 128], R_nx))
                    if kt >= NGRP * i:
                        d["mul"] = (cm_sb[:, kt - NGRP * i, :], R_cm)
                    tl.append(d)
                attend(g, 128, q_rhs, R_qT, tl, pO, R_pO)
                finish_branch(g, 128, 1, pO, R_pO, gates, False)
                tl = []
                for m in range(5):
                    d = dict(kT=kw_[gs, m * 128:(m + 1) * 128], Rk=R_kw, v=vw_[:, m, 65 * g:65 * g + 65], Rv=R_vw)
                    if kw5 < 4 or m in (0, 4):
                        d["mul"] = (wm_sb[:, kw5 * 5 + m, :], R_wm)
                    tl.append(d)
                attend(g, 128, q_rhs, R_qT, tl, pO, R_pO)
                finish_branch(g, 128, 2, pO, R_pO, gates, False)
            op("scalar", lambda e: e.activation(out=onsab, in_=onsa, func=AF.Copy), rd=[R_onsa], wr=[R_onsab])
            pt_, R_pt = ps[0]
            for c in range(4):
                op("tensor", lambda e, c=c: e.matmul(out=pt_[:, c * 128:(c + 1) * 128], lhsT=onsab[:, c * 128:(c + 1) * 128], rhs=ident_b,
                                                      start=True, stop=True), rd=[R_onsab, R_identb], wr=[R_pt])
            op("vector", lambda e: e.tensor_copy(out=onsaT.rearrange("p a b -> p (a b)"), in_=pt_[:, :]), rd=[R_pt], wr=[R_onsaT])
            pd, R_pd = ps[1]
            for gi in range(4):
                for wh in range(2):
                    op("tensor", lambda e, gi=gi, wh=wh: e.matmul(out=pd[:, gi * 128:(gi + 1) * 128], lhsT=u2_[:, wh, gi * 128:(gi + 1) * 128],
                                                                   rhs=wp_sb[:, (kind * 4 + gi) * 2 + wh, :], start=(wh == 0), stop=(wh == 1)),
                       rd=[R_u2, R_wp], wr=[R_pd])
            op("vector", lambda e: e.tensor_copy(out=dT_sb.rearrange("p a b -> p (a b)"), in_=pd[:, :]), rd=[R_pd], wr=[R_dT])
            pyp, R_pyp = ps[7]
            for gi in range(4):
                op("tensor", lambda e, gi=gi: e.matmul(out=pyp[:, gi * 128:(gi + 1) * 128], lhsT=pw[:, gi, :], rhs=dT_sb[:, gi, :], start=True, stop=True),
                   rd=[R_pw, R_dT], wr=[R_pyp])
            for gi in range(4):
                op("scalar", lambda e, gi=gi: e.activation(out=ypT[:, gi, :], in_=pyp[:, gi * 128:(gi + 1) * 128], func=AF.Identity, scale=pscl[:, gi:gi + 1]),
                   rd=[R_pyp, R_pscl], wr=[R_ypT])
            py = [ps[2], ps[3]]
            for half in range(2):
                pyh, R_pyh = py[half]
                for c in range(8):
                    lh = onsaT[:, c, :] if c < 4 else ypT[:, c - 4, :]
                    Rl = R_onsaT if c < 4 else R_ypT
                    op("tensor", lambda e, c=c, half=half, pyh=pyh, lh=lh: e.matmul(out=pyh[:, :], lhsT=lh, rhs=wout[:, c, half * 512:(half + 1) * 512],
                                                                                     start=(c == 0), stop=(c == 7)), rd=[Rl, R_wout], wr=[R_pyh])
            post_residual(w, py, 128, x_ap, R_x, *rows2["G2"])
            op("sync", lambda e, i=i, x_ap=x_ap: e.dma_start(out=x2s[i * 128:(i + 1) * 128, :], in_=x_ap), rd=[R_x], wr=[R_x2s], dma=R_x)
            if i % 8 == 7:
                new_epoch()
        new_epoch()
        sc.new_sem_epoch()
        ar.reset(base_mark)
        w = alloc_work()
        R_rowsS = Res("rowsS")
        rowsS = {nm: ar.alloc(nm, [D], F32, res=R_rowsS) for nm in ("A2", "B2", "G2")}
        R_p3a = Res("p3a")
        wq, R_wq = ar.alloc("wq", [8, 4, 128], BF16, res=R_p3a)
        for h in range(8):
            op("gpsimd", lambda e: e.dma_start(out=wq[:, :, h % 4, (h // 4) * 64:(h // 4) * 64 + 64], in_=w_in_v[:, :, h * 64:(h + 1) * 64]),
               wr=[R_wq], dma=R_wq)
        wkv, R_wkv = ar.alloc("wkv", [8, 792], BF16, res=R_p3a)
        op("gpsimd", lambda e: e.dma_start(out=wkv, in_=w_in_v[:, :, 512:1304]), wr=[R_wkv], dma=R_wkv)
        R_sc = Res("sconst")
        pt_sb = stack.enter_context(nc.sbuf_tensor("pt_sb", [NSEQ, 64], I32))
        op("sync", lambda e: e.dma_start(out=pt_sb[:], in_=pt_d[:, :]), wr=[R_sc], dma=R_sc)
        bonus_s, _ = ar.alloc("bonus_s", [128], F32)
        op("sync", lambda e: e.dma_start(out=bonus_s[0:4, :], in_=bonus_s_d[:, :]), wr=[R_sc], dma=R_sc)
        causal4, _ = ar.alloc("causal4", [4], BF16)
        op("gpsimd", lambda e: e.dma_start(out=causal4[0:4, :], in_=causal4_d[:, :]), wr=[R_sc], dma=R_sc)
        wms0, _ = ar.alloc("wms0", [4], BF16)
        op("gpsimd", lambda e: e.dma_start(out=wms0, in_=wms0_d[:, :]), wr=[R_sc], dma=R_sc)
        irep4, R_irep4 = ar.alloc("irep4", [4, 4], BF16)
        for h in range(4):
            op("vector", lambda e: e.tensor_copy(out=irep4[0:4, h, :], in_=ident_f[0:4, 0:4]), rd=[R_identf], wr=[R_irep4])
        PTb = [ar.alloc("PT%d" % i, [512], BF16) for i in range(4)]
        nx1, R_nx1 = ar.alloc("nexp", [S], BF16)
        nexp = [(nx1, R_nx1), (nx1, R_nx1)]
        rden, R_rden = ar.alloc("rden", [4], F32)
        coef, R_coef = ar.alloc("coef", [4], F32)
        tmpo, R_tmpo = ar.alloc("tmpo", [4, 64], F32)
        onsa, R_onsa = ar.alloc("onsa", [512], F32)
        onsab, R_onsab = ar.alloc("onsab", [512], BF16)
        tmpi, R_tmpi = ar.alloc("tmpi", [4, 128], F32)
        vals, R_vals = ar.alloc("vals", [128], F32)
        vals2, R_vals2 = ar.alloc("vals2", [128], F32)
        m8, R_m8 = ar.alloc("m8", [16], F32)
        selm, R_selm = ar.alloc("selm", [128], F32)
        negs, R_negs = ar.alloc("negs", [128], BF16)
        gates, R_gates = ar.alloc("gates4", [24], F32)
        x1g, R_x1g = ar.alloc("x1g", [D], F32)
        qs_sb, R_qs = ar.alloc("qs", [16, 4, 4], BF16)
        kTn, R_kTn = ar.alloc("kTn", [2, 64], BF16)
        zn, R_zn = ar.alloc("zn", [792], F32)
        vnew, R_vnew = ar.alloc("vnew", [2, 2, 65], BF16)
        op("vector", lambda e: e.memset(vnew, 1.0), wr=[R_vnew])
        kcT_q, R_kcTq = ar.alloc("kcTq", [256], BF16)
        vcT_q, R_vcTq = ar.alloc("vcTq", [256], BF16)
        vc_q, R_vcq = ar.alloc("vcq", [2, 2, 65], BF16)
        op("vector", lambda e: e.memset(vc_q, 1.0), wr=[R_vcq])
        wst = [ar.alloc("wst%d" % i, [4, 128], F32) for i in range(2)]
        wkb, R_wkb = ar.alloc("wkb", [4, 128], BF16)
        kwT_q, R_kwTq = ar.alloc("kwTq", [512], BF16)
        vwq, R_vwq = ar.alloc("vwq", [4, 2, 65], BF16)
        op("vector", lambda e: e.memset(vwq, 1.0), wr=[R_vwq])
        onsaT_s, R_onsaTs = ar.alloc("onsaTs", [4, 64], BF16)
        mark_loop = ar.mark()
        stg = [ar.alloc("stg%d" % i, [64, 128], F32) for i in range(1)]
        pgb, R_pgb = ar.alloc("pgb", [64, 128], BF16)
        ksT_q, R_ksTq = ar.alloc("ksTq", [S], BF16)
        vpb, R_vpb = ar.alloc("vpb", [64, 2, 65], BF16)
        op("vector", lambda e: e.memset(vpb, 1.0), wr=[R_vpb])
        stg_ctr = [0]

        R_stg8 = [Res("stg%d" % i) for i in range(16)]

        def load_pages(ci, s_glob):
            st_, R_st0 = stg[0]
            R_dm = R_stg8[ci * 4 + (s_glob % 4)]
            stg_ctr[0] += 1
            for pg in range(64):
                dyn_ctr[0] += 1
                nm = "pr%d" % dyn_ctr[0]
                out_ap = st_[:, pg, :]
                idx_ap = pt_sb[s_glob:s_glob + 1, pg:pg + 1]
                base = caches[ci]

                def fn(e, nm=nm, out_ap=out_ap, idx_ap=idx_ap, base=base):
                    r = e.alloc_register(nm)
                    e.reg_load(r, idx_ap)
                    v = e.snap(r, donate=True, min_val=0, max_val=NPHYS - 1)
                    e1 = v * 128
                    src = base[ds(e1, 128), :]
                    ins = e.dma_start(out=out_ap, in_=src)
                    vc = e.get_value_cache()
                    seen = set()
                    for ex in (e1, src.offset, src.offset * 4):
                        try:
                            al = vc.lookup(ex)
                        except Exception:
                            al = None
                        if al is not None and al.val.name not in seen and al.val.name != r.name:
                            seen.add(al.val.name)
                            e.free_register(al.val)
                    e.free_register(r)
                    return ins
                sc.op(("sync", "gpsimd")[pg % 2], fn, rd=[R_sc], wr=[R_st0], dma=R_dm, deferred=True, indep=(pg > 1))
            return st_, R_st0

        def cols4(ap2d_col):
            return bass.AP(tensor=ap2d_col.tensor, offset=ap2d_col.offset, ap=[list(ap2d_col.ap[0]), [16, 4]])

        cast_ctr = [0]

        def cast_op(out_ap, in_ap, rd, wr):
            cast_ctr[0] += 1
            if cast_ctr[0] % 2:
                op("scalar", lambda e: e.activation(out=out_ap, in_=in_ap, func=AF.Copy), rd=rd, wr=wr)
            else:
                op("vector", lambda e: e.tensor_copy(out=out_ap, in_=in_ap), rd=rd, wr=wr)

        R_p3b = Res("p3b")
        R_psclS = Res("psclS")
        for sg in range(NSG):
            load_row(*rowsS["A2"], 1 + sg, 4, 2, "A", w.tmpg, w.R_tmpg)
            load_row(*rowsS["B2"], 1 + sg, 3, 2, "B", w.tmpg, w.R_tmpg)
            load_row(*rowsS["G2"], 1 + sg, 5, 3, "G1", w.tmpg, w.R_tmpg)
            op("sync", lambda e: e.dma_start(out=x1g[0:64, :], in_=x1s[S + 64 * sg:S + 64 * sg + 64, :]), rd=[R_x1s], wr=[R_x1g], dma=R_x1g)
            norm_mod(w, x1g[0:64, :], R_x1g, 64, *rowsS["A2"], *rowsS["B2"], 0)
            pq, R_pq = ps[7]
            for c in range(4):
                for k in range(8):
                    op("tensor", lambda e: e.matmul(out=pq[:, c * 64:(c + 1) * 64], lhsT=wq[:, k, c, :], rhs=w.hT[:, k, 0:64],
                                                     start=(k == 0), stop=(k == 7)), rd=[R_wq, w.R_hT], wr=[R_pq])
            op("scalar", lambda e: e.activation(out=qs_sb, in_=pq[:, 0:256].rearrange("p (c t s) -> p s c t", c=4, t=4), func=AF.Copy),
               rd=[R_pq], wr=[R_qs])
            pkn, R_pkn = ps[6]
            for n, c_lo in enumerate((256, 512)):
                for k in range(8):
                    op("tensor", lambda e: e.matmul(out=pkn[:, n * 64:(n + 1) * 64], lhsT=wkv[:, k, c_lo:c_lo + 128], rhs=w.hT[:, k, 0:64],
                                                     start=(k == 0), stop=(k == 7)), rd=[R_wkv, w.R_hT], wr=[R_pkn])
            op("vector", lambda e: e.tensor_copy(out=kTn.rearrange("p a b -> p (a b)"), in_=pkn[:, 0:128]), rd=[R_pkn], wr=[R_kTn])
            po, R_po = ps[0]
            po_v = po[:, 0:256].rearrange("p (c t) -> p c t", c=4)
            for sl in range(16):
                s_glob = 16 * sg + sl
                pz, R_pz = ps[5]
                pz2, R_pz2 = ps[4]
                for k in range(8):
                    op("tensor", lambda e: e.matmul(out=pz[0:4, :], lhsT=cols4(w.hT[:, k, sl:sl + 1]), rhs=wkv[:, k, 0:512],
                                                     start=(k == 0), stop=(k == 7)), rd=[w.R_hT, R_wkv], wr=[R_pz])
                for k in range(8):
                    op("tensor", lambda e: e.matmul(out=pz2[0:4, 0:280], lhsT=cols4(w.hT[:, k, sl:sl + 1]), rhs=wkv[:, k, 512:792],
                                                     start=(k == 0), stop=(k == 7)), rd=[w.R_hT, R_wkv], wr=[R_pz2])
                op("scalar", lambda e: e.activation(out=zn[0:4, 0:512], in_=pz[0:4, :], func=AF.Copy), rd=[R_pz], wr=[R_zn])
                op("scalar", lambda e: e.activation(out=zn[0:4, 512:768], in_=pz2[0:4, 0:256], func=AF.Copy), rd=[R_pz2], wr=[R_zn])
                op("scalar", lambda e: e.activation(out=gates[0:4, :], in_=pz2[0:4, 256:280], func=AF.Sigmoid), rd=[R_pz2], wr=[R_gates])
                op("vector", lambda e: e.tensor_copy(out=vnew[0:4, 0, :, 0:64], in_=zn[0:4, 384:512].rearrange("p (g d) -> p g d", g=2)),
                   rd=[R_zn], wr=[R_vnew])
                op("vector", lambda e: e.tensor_copy(out=vnew[0:4, 1, :, 0:64], in_=zn[0:4, 640:768].rearrange("p (g d) -> p g d", g=2)),
                   rd=[R_zn], wr=[R_vnew])
                for ci, (dstT, R_dT_, W4) in enumerate(((kcT_q, R_kcTq, W4k), (vcT_q, R_vcTq, W4v))):
                    st_, R_st = load_pages(ci, s_glob)
                    cast_op(pgb, st_, [R_st], [R_pgb])
                    pc_, R_pc_ = ps[6]
                    for pg in range(64):
                        op("tensor", lambda e: e.matmul(out=pc_[:, 4 * pg:4 * pg + 4], lhsT=pgb[:, pg, :], rhs=W4, start=True, stop=True),
                           rd=[R_pgb, R_W4k, R_W4v], wr=[R_pc_])
                    op("vector", lambda e: e.tensor_copy(out=dstT, in_=pc_[:, 0:256]), rd=[R_pc_], wr=[R_dT_])
                for jt in range(2):
                    pv_, R_pv = ps[7]
                    op("tensor", lambda e: e.matmul(out=pv_[:, jt * 128:(jt + 1) * 128], lhsT=vcT_q[:, jt * 128:(jt + 1) * 128], rhs=ident_b,
                                                     start=True, stop=True), rd=[R_vcTq, R_identb], wr=[R_pv])
                op("vector", lambda e: e.tensor_copy(out=vc_q[:, :, :, 0:64], in_=ps[7][0][:, 0:256].rearrange("p (j g d) -> p j g d", j=2, g=2)),
                   rd=[ps[7][1]], wr=[R_vcq])
                st_, R_st = load_pages(2, s_glob)
                cast_op(pgb, st_, [R_st], [R_pgb])
                for q4 in range(16):
                    ptq, R_ptq = ps[1] if q4 % 2 == 0 else ps[7]
                    for pp in range(4):
                        pg = 4 * q4 + pp
                        op("tensor", lambda e: e.matmul(out=ptq[:, pp * 128:(pp + 1) * 128], lhsT=pgb[:, pg, :], rhs=ident_b, start=True, stop=True),
                           rd=[R_pgb, R_identb], wr=[R_ptq])
                    if q4 % 2 == 0:
                        op("scalar", lambda e: e.activation(out=ksT_q[:, q4 * 512:(q4 + 1) * 512], in_=ptq[:, :], func=AF.Copy), rd=[R_ptq], wr=[R_ksTq])
                    else:
                        op("vector", lambda e: e.tensor_copy(out=ksT_q[:, q4 * 512:(q4 + 1) * 512], in_=ptq[:, :]), rd=[R_ptq], wr=[R_ksTq])
                st_, R_st = load_pages(3, s_glob)
                cast_op(vpb[:, :, :, 0:64], st_.rearrange("p a (g d) -> p a g d", g=2), [R_st], [R_vpb])
                for n, st in enumerate((st_wk, st_wv)):
                    ws_, R_ws = wst[n]
                    op("sync", lambda e: e.dma_start(out=ws_, in_=st[s_glob].rearrange("(m p) c -> p m c", p=128)), wr=[R_ws], dma=R_ws)
                op("vector", lambda e: e.tensor_copy(out=wkb, in_=wst[0][0]), rd=[wst[0][1]], wr=[R_wkb])
                pw_, R_pw_ = ps[1]
                for m in range(4):
                    op("tensor", lambda e: e.matmul(out=pw_[:, m * 128:(m + 1) * 128], lhsT=wkb[:, m, :], rhs=ident_b, start=True, stop=True),
                       rd=[R_wkb, R_identb], wr=[R_pw_])
                op("scalar", lambda e: e.activation(out=kwT_q, in_=pw_[:, :], func=AF.Copy), rd=[R_pw_], wr=[R_kwTq])
                op("vector", lambda e: e.tensor_copy(out=vwq[:, :, :, 0:64], in_=wst[1][0].rearrange("p a (g d) -> p a g d", g=2)), rd=[wst[1][1]], wr=[R_vwq])
                for g in range(2):
                    gs = slice(64 * g, 64 * g + 64)
                    q_rhs = qs_sb[gs, sl, :, :].rearrange("p a b -> p (a b)")
                    pO, R_pO = ps[4 + g]
                    pI, R_pI = ps[6]
                    ir4 = (irep4[0:4, :, :].rearrange("p a b -> p (a b)"), R_irep4)
                    tl = [dict(kT=kcT_q[gs, jt * 128:(jt + 1) * 128], Rk=R_kcTq, v=vc_q[:, jt, g, :], Rv=R_vcq, jt=jt) for jt in range(2)]
                    attend(g, 4, q_rhs, R_qs, tl, pO, R_pO, pI, R_pI, irep=ir4)
                    finish_branch(g, 4, 0, pO, R_pO, gates[0:4, :], True)
                    select_blocks(g, 4, pI, R_pI, bonus_s[0:4, :], R_sc, 15, 128)
                    nx, R_nx = nexp[g]
                    tl = [dict(kT=ksT_q[gs, pg * 128:(pg + 1) * 128], Rk=R_ksTq, v=vpb[:, pg, g, :], Rv=R_vpb,
                               add=(nx[0:4, pg * 128:(pg + 1) * 128], R_nx)) for pg in range(64)]
                    tl.append(dict(kT=cols4(kTn[gs, 0, sl:sl + 1]), Rk=R_kTn, v=vnew[0:4, 0, g, :], Rv=R_vnew, mul=(causal4[0:4, :], R_sc), nk=4))
                    attend(g, 4, q_rhs, R_qs, tl, pO, R_pO, irep=ir4)
                    finish_branch(g, 4, 1, pO, R_pO, gates[0:4, :], False)
                    tl = []
                    for m in range(4):
                        d = dict(kT=kwT_q[gs, m * 128:(m + 1) * 128], Rk=R_kwTq, v=vwq[:, m, g, :], Rv=R_vwq)
                        if m == 0:
                            d["mul"] = (wms0, R_sc)
                        tl.append(d)
                    tl.append(dict(kT=cols4(kTn[gs, 1, sl:sl + 1]), Rk=R_kTn, v=vnew[0:4, 1, g, :], Rv=R_vnew, mul=(causal4[0:4, :], R_sc), nk=4))
                    attend(g, 4, q_rhs, R_qs, tl, pO, R_pO, irep=ir4)
                    finish_branch(g, 4, 2, pO, R_pO, gates[0:4, :], False)
                op("scalar", lambda e: e.activation(out=onsab[0:4, :], in_=onsa[0:4, :], func=AF.Copy), rd=[R_onsa], wr=[R_onsab])
                for c in range(4):
                    op("tensor", lambda e: e.matmul(out=cols4(po_v[:, c, sl:sl + 1]), lhsT=onsab[0:4, c * 128:(c + 1) * 128], rhs=ident_b[0:4, 0:4],
                                                     start=True, stop=True), rd=[R_onsab, R_identb], wr=[R_po])
                if sl % 4 == 3:
                    new_epoch()
            op("vector", lambda e: e.tensor_copy(out=onsaT_s.rearrange("p a b -> p (a b)"), in_=po[:, 0:256]), rd=[R_po], wr=[R_onsaTs])
            new_epoch()
            ar.reset(mark_loop)
            wu, R_wu = ar.alloc("wu", [8, 512], BF16, res=R_p3b)
            op("gpsimd", lambda e: e.dma_start(out=wu, in_=w_in_v[:, :, 1304:1816]), wr=[R_wu], dma=R_wu)
            wout, R_wout = ar.alloc("wout", [8, D], BF16, res=R_p3b)
            op("gpsimd", lambda e: e.dma_start(out=wout, in_=w_out.rearrange("(k p) n -> p k n", p=128)), wr=[R_wout], dma=R_wout)
            pw, R_pw = ar.alloc("pw", [4, 128], BF16, res=R_p3b)
            op("gpsimd", lambda e: e.dma_start(out=pw, in_=pool_w.rearrange("g c d -> c g d")), wr=[R_pw], dma=R_pw)
            wsab, R_wsab = ar.alloc("wsab", [2, 4, 64], BF16, res=R_p3b)
            op("gpsimd", lambda e: e.dma_start(out=wsab[0:120, 0, :, :], in_=wsa_d[:, :, :]), wr=[R_wsab], dma=R_wsab)
            op("gpsimd", lambda e: e.dma_start(out=wsab[0:120, 1, :, :], in_=wsb_d[:, :, :]), wr=[R_wsab], dma=R_wsab)
            wnb, R_wnb = ar.alloc("wnb", [4, 64], BF16, res=R_p3b)
            op("gpsimd", lambda e: e.dma_start(out=wnb[0:64, :, :], in_=wn_d[:, :, :]), wr=[R_wnb], dma=R_wnb)
            pscl, R_pscl = ar.alloc("pscl", [4], F32, res=R_psclS)
            for gi in range(4):
                op("sync", lambda e: e.dma_start(out=pscl[:, gi:gi + 1], in_=pool_scale[0:1, gi * 128:(gi + 1) * 128].rearrange("a d -> d a")),
                   wr=[R_pscl], dma=R_pscl)
            stpb, R_stpb = ar.alloc("stpb", [2, 512], BF16, res=R_p3b)
            for hf in range(2):
                op("gpsimd", lambda e: e.dma_start(out=stpb[0:120, hf, :], in_=st_pool[16 * sg + 8 * hf:16 * sg + 8 * hf + 8].rearrange("s r c -> (s r) c")),
                   wr=[R_stpb], dma=R_stpb)
            un, R_un = ar.alloc("un", [512], BF16)
            dTs, R_dTs = ar.alloc("dTs", [4, 64], BF16)
            ypTs, R_ypTs = ar.alloc("ypTs", [4, 64], BF16)
            pu_, R_pu_ = ps[5]
            for k in range(8):
                op("tensor", lambda e: e.matmul(out=pu_[0:64, :], lhsT=w.hT[:, k, 0:64], rhs=wu[:, k, :], start=(k == 0), stop=(k == 7)),
                   rd=[w.R_hT, R_wu], wr=[R_pu_])
            op("vector", lambda e: e.tensor_copy(out=un[0:64, :], in_=pu_[0:64, :]), rd=[R_pu_], wr=[R_un])
            pd, R_pd = ps[1]
            for gi in range(4):
                cs_ = slice(gi * 128, (gi + 1) * 128)
                op("tensor", lambda e: e.matmul(out=pd[:, gi * 64:(gi + 1) * 64], lhsT=stpb[0:120, 0, cs_], rhs=wsab[0:120, 0, gi, :], start=True, stop=False),
                   rd=[R_stpb, R_wsab], wr=[R_pd])
                op("tensor", lambda e: e.matmul(out=pd[:, gi * 64:(gi + 1) * 64], lhsT=stpb[0:120, 1, cs_], rhs=wsab[0:120, 1, gi, :], start=False, stop=False),
                   rd=[R_stpb, R_wsab], wr=[R_pd])
                op("tensor", lambda e: e.matmul(out=pd[:, gi * 64:(gi + 1) * 64], lhsT=un[0:64, cs_], rhs=wnb[0:64, gi, :], start=False, stop=True),
                   rd=[R_un, R_wnb], wr=[R_pd])
            op("vector", lambda e: e.tensor_copy(out=dTs.rearrange("p a b -> p (a b)"), in_=pd[:, 0:256]), rd=[R_pd], wr=[R_dTs])
            pyp, R_pyp = ps[7]
            for gi in range(4):
                op("tensor", lambda e: e.matmul(out=pyp[:, gi * 64:(gi + 1) * 64], lhsT=pw[:, gi, :], rhs=dTs[:, gi, :], start=True, stop=True),
                   rd=[R_pw, R_dTs], wr=[R_pyp])
            for gi in range(4):
                op("scalar", lambda e: e.activation(out=ypTs[:, gi, :], in_=pyp[:, gi * 64:(gi + 1) * 64], func=AF.Identity, scale=pscl[:, gi:gi + 1]),
                   rd=[R_pyp, R_pscl], wr=[R_ypTs])
            py = [ps[2], ps[3]]
            for half in range(2):
                pyh, R_pyh = py[half]
                for c in range(8):
                    lh = onsaT_s[:, c, :] if c < 4 else ypTs[:, c - 4, :]
                    Rl = R_onsaTs if c < 4 else R_ypTs
                    op("tensor", lambda e: e.matmul(out=pyh[0:64, :], lhsT=lh, rhs=wout[:, c, half * 512:(half + 1) * 512],
                                                     start=(c == 0), stop=(c == 7)), rd=[Rl, R_wout], wr=[R_pyh])
            post_residual(w, py, 64, x1g, R_x1g, *rowsS["G2"])
            op("sync", lambda e: e.dma_start(out=x2s[NOWN * 128 + 64 * sg:NOWN * 128 + 64 * sg + 64, :], in_=x1g[0:64, :]), rd=[R_x1g], wr=[R_x2s], dma=R_x1g)
            new_epoch()
            ar.reset(mark_loop)
            if sg + 1 < NSG:
                stg = [ar.alloc("stg%d" % i, [64, 128], F32, res=stg[i][1]) for i in range(1)]
                pgb, _ = ar.alloc("pgb", [64, 128], BF16, res=R_pgb)
                ksT_q, _ = ar.alloc("ksTq", [S], BF16, res=R_ksTq)
                vpb, _ = ar.alloc("vpb", [64, 2, 65], BF16, res=R_vpb)
                op("vector", lambda e: e.memset(vpb, 1.0), wr=[R_vpb])
        new_epoch()
        ar.reset(base_mark)

        wo2, R_wo2 = ar.alloc("wo2", [NF, D], BF16)
        op("gpsimd", lambda e: e.dma_start(out=wo2, in_=f2wo.rearrange("(f p) n -> p f n", p=128)), wr=[R_wo2], dma=R_wo2)
        w = alloc_work()
        f = alloc_ffn()
        R_rows3 = Res("rows3")
        rows3 = {nm: ar.alloc(nm, [D], F32, res=R_rows3) for nm in ("A3", "B3", "G3")}

        def load_rows3(kind):
            load_row(*rows3["A3"], kind, 7, 4, "A", w.tmpg, w.R_tmpg)
            load_row(*rows3["B3"], kind, 6, 4, "B", w.tmpg, w.R_tmpg)
            load_row(*rows3["G3"], kind, 8, 5, "G5", w.tmpg, w.R_tmpg)

        def x2load(row0, nt):
            return lambda x_ap, R_x: op("sync", lambda e: e.dma_start(out=x_ap[0:nt, :], in_=x2s[row0:row0 + nt, :]), rd=[R_x2s], wr=[R_x], dma=R_x)

        def post3(ti, x_ap, R_x, nt, c0, tag):
            op("sync", lambda e: e.dma_start(out=o_y[tag:tag + nt, :], in_=x_ap[0:nt, :]), rd=[R_x], wr=[R_oy], dma=R_x)

        load_rows3(0)
        for p in range(NOWN // 4):
            tiles = [(x2load(i * 128, 128), 128, i * 128) for i in range(4 * p, 4 * p + 4)]
            ffn_pass(w, f, tiles, f2wi_v, wo2, R_wo2, rows3["A3"], rows3["B3"], rows3["G3"], post3)
        for sg in range(NSG):
            new_epoch()
            load_rows3(1 + sg)
            ffn_pass(w, f, [(x2load(NOWN * 128 + 64 * sg, 64), 64, NOWN * 128 + 64 * sg)], f2wi_v, wo2, R_wo2, rows3["A3"], rows3["B3"], rows3["G3"], post3)
        sc.emit()
    return nc


_NC = {}


def _tables(r, ngrp, nown):
    key = np.arange(128)[:, None]
    q = np.arange(128)[None, :]
    cm = np.zeros((ngrp, 128, 128), np.float32)
    for dk in range(ngrp):
        if dk < r:
            cm[dk] = 1.0
        elif dk == r:
            cm[dk] = (key <= q)
    bonus = np.zeros((nown, 128, 128), np.float32)
    cmpmask = np.zeros((nown, 2, 128, 128), np.float32)
    blk = np.arange(128)[None, :]
    for i in range(nown):
        j = ngrp * i + r
        t = j * 128 + np.arange(128)[:, None]
        cur = t // 64
        forced = (blk == 0) | (blk == cur) | (blk == cur - 1)
        start_ok = blk * 64 <= t
        bonus[i] = np.where(forced, 1e4, np.where(start_ok, 0.0, -1e30))
        for jt in range(2):
            jc = jt * 128 + np.arange(128)[:, None]
            tq = j * 128 + np.arange(128)[None, :]
            cmpmask[i, jt] = ((jc + 1) * 32 - 1 <= tq)
    wm = np.zeros((5, 5, 128, 128), np.float32)
    for kw in range(5):
        j = kw + r if kw < 4 else 4 + r
        for m in range(5):
            if j - 4 + m < 0:
                continue
            if m == 0:
                wm[kw, m] = (key > q)
            elif m == 4:
                wm[kw, m] = (key <= q)
            else:
                wm[kw, m] = 1.0
    wp = np.zeros((2, 4, 2, 128, 128), np.float32)
    for kind in range(2):
        j = r if kind == 0 else ngrp + r
        for gi, wdw in enumerate((2, 4, 8, 16)):
            tpos = j * 128 + np.arange(128)[None, :]
            cnt = np.minimum(wdw, tpos + 1).astype(np.float32)
            for wh in range(2):
                spos = (j - 1 + wh) * 128 + np.arange(128)[:, None]
                inwin = (spos > tpos - wdw) & (spos <= tpos)
                wp[kind, gi, wh] = inwin / cnt - (spos == tpos)
    return cm, bonus, cmpmask, wm, wp


def _sample_tables():
    bonus_s = np.zeros((4, 128), np.float32)
    bonus_s[:, 0] = 1e4
    bonus_s[:, 127] = 1e4
    n = np.arange(4)[:, None]
    t = np.arange(4)[None, :]
    causal4 = (n <= t).astype(np.float32)
    wms0 = (np.arange(128)[:, None] > t).astype(np.float32)
    wsa = np.zeros((120, 4, 64), np.float32)
    wsb = np.zeros((120, 4, 64), np.float32)
    wn = np.zeros((64, 4, 64), np.float32)
    for gi, wdw in enumerate((2, 4, 8, 16)):
        for tt in range(4):
            for s in range(16):
                col = tt * 16 + s
                for rr in range(15):
                    if rr > 15 + tt - wdw:
                        if s < 8:
                            wsa[s * 15 + rr, gi, col] = 1.0 / wdw
                        else:
                            wsb[(s - 8) * 15 + rr, gi, col] = 1.0 / wdw
                for t2 in range(4):
                    val = (1.0 / wdw if (t2 <= tt and t2 > tt - wdw) else 0.0) - (1.0 if t2 == tt else 0.0)
                    wn[t2 * 16 + s, gi, col] = val
    return bonus_s, causal4, wms0, wsa, wsb, wn


def kernel(**inputs):
    f = lambda k: np.asarray(inputs[k])
    x_prompt, x_sample = f("x_prompt"), f("x_sample")
    DB = x_sample.shape[0]
    nseq = DB // NCORES
    nsg = nseq // 16
    nphys = f("cache_cmp_k").shape[1]
    key = (nseq, nphys)
    if key not in _NC:
        _NC[key] = build_nc(NSEQ=nseq, NPHYS=nphys)
    nc = _NC[key]
    ident = np.eye(128, dtype=np.float32)
    pair = (np.arange(128)[:, None] // 2 == np.arange(64)[None, :]).astype(np.float32)
    bmask = (np.arange(128)[:, None] // 32 == np.arange(4)[None, :]).astype(np.float32)
    bonus_s, causal4, wms0, wsa, wsb, wn = _sample_tables()
    caches = [np.ascontiguousarray(f(k)[0]).reshape(nphys * 128, 128) for k in ("cache_cmp_k", "cache_cmp_v", "cache_sel_k", "cache_sel_v")]
    in_maps = []
    for c in range(NCORES):
        b, r = c // NGRP, c % NGRP
        cm, bonus, cmpmask, wm, wp = _tables(r, NGRP, NOWN)
        sl = slice(nseq * c, nseq * (c + 1))
        xs_c = np.ascontiguousarray(x_sample[sl].reshape(nsg, 16, 4, D).transpose(0, 2, 1, 3).reshape(4 * nseq, D))
        c17 = np.concatenate([f("c_prompt")[b:b + 1], f("c_sample")[sl]], axis=0)
        rinfo = np.zeros((1, 8), np.int32)
        rinfo[0, 0] = r
        m = {
            "xp": np.ascontiguousarray(x_prompt[b]), "xs": xs_c, "c17": np.ascontiguousarray(c17),
            "w_ada": f("w_ada")[0], "b_ada": f("b_ada")[0][None, :], "gains": f("norm_gains")[0],
            "f1wi": f("ffn1_wi")[0], "f1wo": f("ffn1_wo")[0], "f2wi": f("ffn2_wi")[0], "f2wo": f("ffn2_wo")[0],
            "w_in": f("w_in")[0], "w_out": f("w_out")[0], "cmpw": f("cmp_w")[0],
            "pool_w": f("pool_w")[0], "pool_scale": f("pool_scale")[0][None, :],
            "ident": ident, "pair": pair, "bmask": bmask, "rinfo": rinfo,
            "cm": cm, "bonus": bonus, "cmpmask": cmpmask, "wm": wm, "wp": wp,
            "st_wk": np.ascontiguousarray(f("state_win_k")[0, sl].reshape(nseq, 512, 128)),
            "st_wv": np.ascontiguousarray(f("state_win_v")[0, sl].reshape(nseq, 512, 128)),
            "st_pool": np.ascontiguousarray(f("state_pool")[0, sl]),
            "pt": np.ascontiguousarray(f("page_table")[sl]).astype(np.int32),
            "c_cmp_k": caches[0], "c_cmp_v": caches[1], "c_sel_k": caches[2], "c_sel_v": caches[3],
            "bonus_s": bonus_s, "causal4": causal4, "wms0": wms0, "wsa": wsa, "wsb": wsb, "wn": wn,
        }
        in_maps.append(m)
    res = run_bass_kernel_spmd(nc, in_maps, core_ids=list(range(NCORES)))
    kernel.last = res
    y_p = np.zeros((2, S, D), np.float32)
    y_s = np.zeros((DB, 4, D), np.float32)
    kv_p = [np.zeros((1, 2, S, 2, 64), np.float32) for _ in range(4)]
    kv_s = [np.zeros((1, DB, 4, 2, 64), np.float32) for _ in range(4)]
    p_win = [np.zeros((1, 2, 512, 2, 64), np.float32) for _ in range(2)]
    p_pool = np.zeros((1, 2, 15, 512), np.float32)
    s_win = [np.zeros((1, DB, 512, 2, 64), np.float32) for _ in range(2)]
    s_pool = np.zeros((1, DB, 15, 512), np.float32)
    unperm = lambda a, last: a.reshape((nsg, 4, 16) + last).transpose(0, 2, 1, *range(3, 3 + len(last))).reshape((nseq, 4) + last)
    for c in range(NCORES):
        b, r = c // NGRP, c % NGRP
        sl = slice(nseq * c, nseq * (c + 1))
        rr = res.results[c]
        okv = np.asarray(rr["o_kv"])
        oy = np.asarray(rr["o_y"])
        y_p[b].reshape(NBLK, 128, D)[r::NGRP] = oy[:NOWN * 128].reshape(NOWN, 128, D)
        y_s[sl] = unperm(oy[NOWN * 128:], (D,))
        for n in range(4):
            kv_p[n][0, b].reshape(NBLK, 128, 2, 64)[r::NGRP] = okv[n, :S].reshape(NBLK, 128, 2, 64)[r::NGRP]
            kv_s[n][0, sl] = unperm(okv[n, S:], (2, 64))
        if r == NGRP - 1:
            for n in range(2):
                p_win[n][0, b] = okv[4 + n, S - 512:S].reshape(512, 2, 64)
            p_pool[0, b] = np.asarray(rr["o_u"])[S - 15:S]
        sw = np.asarray(rr["o_swin"])
        for n in range(2):
            s_win[n][0, sl] = sw[n].reshape(nseq, 512, 2, 64)
        s_pool[0, sl] = np.asarray(rr["o_spool"])
    return (y_p, y_s, *kv_p, *p_win, p_pool, *kv_s, *s_win, s_pool)
```

```python
from contextlib import ExitStack
import numpy as np
import ml_dtypes
import concourse.bass as bass
import concourse.mybir as mybir
from concourse.bass import ds
from concourse.bass_utils import run_bass_kernel_spmd

F32 = mybir.dt.float32
BF16 = mybir.dt.bfloat16
I32 = mybir.dt.int32
AF = mybir.ActivationFunctionType
ALU = mybir.AluOpType

NGRP = 1
NCORES = 2 * NGRP
D = 1024
S = 8192
NBLK = 64
NOWN = 64 // NGRP
NSEQ = 16
NTS = 64
DFF = 2816
NF = 22
INW = 1816
EPS = 1e-6
ENG = ("sync", "scalar", "vector", "gpsimd", "tensor")


class Res:
    def __init__(self, name):
        self.name = name
        self.lw = None
        self.rd = []
        self.sem = None
        self.cnt = 0


class _Rec:
    def __init__(self):
        self.call = None

    def __getattr__(self, name):
        def m(*a, **k):
            self.call = (name, a, k)
            return self
        return m


class Sched:
    def __init__(self, nc, stack):
        self.nc = nc
        self.stack = stack
        self.q = {e: [] for e in ENG}
        self.cnt = {e: 0 for e in ENG}
        self.waited = {e: {} for e in ENG}
        self.ep = 0
        self.esem = {(e, 0): stack.enter_context(nc.semaphore("es_" + e)) for e in ENG}
        self.miles = {(e, 0): set() for e in ENG}
        self.dres = []

    def new_sem_epoch(self):
        self.barrier()
        self.ep += 1
        for e in ENG:
            self.esem[(e, self.ep)] = self.stack.enter_context(self.nc.semaphore("es%d_%s" % (self.ep, e)))
            self.miles[(e, self.ep)] = set()
            self.cnt[e] = 0
            self.waited[e] = {k: v for k, v in self.waited[e].items() if k[0] == "D"}

    def _wait(self, eng, tok):
        if tok[0] == "E":
            if tok[3] != self.ep:
                return
            if tok[1] == eng and eng == "tensor":
                return
            key = ("E", tok[1]); val = tok[2]
        else:
            key = ("D", id(tok[1])); val = tok[2]
        if self.waited[eng].get(key, 0) >= val:
            return
        self.waited[eng][key] = val
        if tok[0] == "E":
            self.miles[(tok[1], self.ep)].add(val)
            self.q[eng].append(("we", (tok[1], self.ep), val))
        else:
            self.q[eng].append(("wd", tok[1].sem, val))

    def op(self, eng, fn, rd=(), wr=(), dma=None, deferred=False, indep=False):
        if not deferred:
            rec = _Rec()
            fn(rec)
            name, a, k = rec.call
            fn = (lambda e, name=name, a=a, k=k: getattr(e, name)(*a, **k))
        deps = []
        for b in rd:
            if b.lw is not None:
                deps.append(b.lw)
        for b in wr:
            if b.lw is not None and not indep:
                deps.append(b.lw)
            deps.extend(b.rd)
        for d in deps:
            self._wait(eng, d)
        if dma is None:
            self.cnt[eng] += 1
            tok = ("E", eng, self.cnt[eng], self.ep)
            self.q[eng].append(("oe", fn, self.cnt[eng], self.ep))
        else:
            if dma.sem is None:
                dma.sem = self.stack.enter_context(self.nc.semaphore("ds_" + dma.name))
                self.dres.append(dma)
            dma.cnt += 16
            assert dma.cnt < 30000, ("dma semaphore would overflow", dma.name)
            tok = ("D", dma, dma.cnt)
            self.q[eng].append(("od", fn, dma.sem))
        for b in rd:
            b.rd.append(tok)
        for b in wr:
            b.lw = tok
            b.rd = []
        return tok

    def barrier(self):
        for e in ENG:
            for e2 in ENG:
                if self.cnt[e2] > 0:
                    self._wait(e, ("E", e2, self.cnt[e2], self.ep))
            for d in self.dres:
                if d.cnt > 0:
                    self._wait(e, ("D", d, d.cnt))

    def emit(self):
        nc = self.nc
        self.barrier()
        rank = {}
        for key_ in self.miles:
            ks = sorted(self.miles[key_])
            rank[key_] = {k: i + 1 for i, k in enumerate(ks)}
            assert len(ks) < 30000, ("engine semaphore would overflow", key_, len(ks))
        self.n_miles = {k: len(v) for k, v in rank.items()}
        with nc.Block() as block:
            def mk(ename):
                def run(eng):
                    for it in self.q[ename]:
                        if it[0] == "we":
                            eng.wait_ge(self.esem[it[1]], rank[it[1]][it[2]])
                        elif it[0] == "wd":
                            eng.wait_ge(it[1], it[2])
                        elif it[0] == "oe":
                            ins = it[1](eng)
                            if it[2] in rank[(ename, it[3])]:
                                ins.then_inc(self.esem[(ename, it[3])], 1)
                        else:
                            ins = it[1](eng)
                            ins.then_inc(it[2], 16)
                return run
            block.sync(mk("sync"))
            block.scalar(mk("scalar"))
            block.vector(mk("vector"))
            block.gpsimd(mk("gpsimd"))
            block.tensor(mk("tensor"))


class Arena:
    def __init__(self, nc, stack, nbytes):
        self.t = stack.enter_context(nc.sbuf_tensor("arena", [128, nbytes // 4], F32))
        self.nbytes = nbytes
        self.top = 0
        self.n = 0

    def mark(self):
        return self.top

    def reset(self, m):
        self.top = m

    def alloc(self, name, free_shape, dtype, res=None):
        esz = 2 if dtype == BF16 else 4
        n = int(np.prod(free_shape))
        nb = (n * esz + 31) // 32 * 32
        assert self.top + nb <= self.nbytes, (name, self.top, nb, self.nbytes)
        off = self.top
        self.top += nb
        v = self.t[:, off // 4:(off + nb) // 4]
        if dtype != F32:
            v = v.bitcast(dtype)
        v = v[:, 0:n]
        if len(free_shape) == 2:
            v = v.rearrange("p (a b) -> p a b", a=free_shape[0])
        elif len(free_shape) == 3:
            v = v.rearrange("p (a b c) -> p a b c", a=free_shape[0], b=free_shape[1])
        self.n += 1
        return v, (res if res is not None else Res(name + str(self.n)))


SCALE = 0.125
NEGM = -30000.0


def bc_mid(a, n):
    return bass.AP(tensor=a.tensor, offset=a.offset, ap=[list(a.ap[0]), [0, n], list(a.ap[1])])


def bc_last(a, n):
    return bass.AP(tensor=a.tensor, offset=a.offset, ap=[list(a.ap[0]), list(a.ap[1]), [0, n]])


def build_nc(NSEQ=64, NPHYS=10240):
    NSG = NSEQ // 16
    NTS = 4 * NSEQ
    XROWS = S + NTS
    OROWS = NOWN * 128 + NTS
    M17 = 1 + NSEQ
    nc = bass.Bass("TRN2", target_bir_lowering=False)
    stack = ExitStack()
    dt_in = lambda n, s, d=F32: nc.dram_tensor(n, s, d, kind="ExternalInput").ap()
    dt_out = lambda n, s, d=F32: nc.dram_tensor(n, s, d, kind="ExternalOutput").ap()
    dt_int = lambda n, s, d=F32: nc.dram_tensor(n, s, d, kind="Internal").ap()
    xp = dt_in("xp", [S, D])
    xs = dt_in("xs", [NTS, D])
    c17 = dt_in("c17", [M17, D])
    w_ada = dt_in("w_ada", [D, 9 * D])
    b_ada = dt_in("b_ada", [1, 9 * D])
    gains = dt_in("gains", [6, D])
    f1wi = dt_in("f1wi", [D, 2 * DFF])
    f1wo = dt_in("f1wo", [DFF, D])
    f2wi = dt_in("f2wi", [D, 2 * DFF])
    f2wo = dt_in("f2wo", [DFF, D])
    w_in = dt_in("w_in", [D, INW])
    w_out = dt_in("w_out", [D, D])
    cmpw = dt_in("cmpw", [2, 32])
    pool_w = dt_in("pool_w", [4, 128, 128])
    pool_scale = dt_in("pool_scale", [1, 512])
    ident_d = dt_in("ident", [128, 128])
    pair_d = dt_in("pair", [128, 64])
    bmask_d = dt_in("bmask", [128, 4])
    rinfo_d = dt_in("rinfo", [1, 8], I32)
    cm_d = dt_in("cm", [NGRP, 128, 128])
    bonus_d = dt_in("bonus", [NOWN, 128, 128])
    cmpmask_d = dt_in("cmpmask", [NOWN, 2, 128, 128])
    wm_d = dt_in("wm", [5, 5, 128, 128])
    wp_d = dt_in("wp", [2, 4, 2, 128, 128])
    st_wk = dt_in("st_wk", [NSEQ, 512, 128])
    st_wv = dt_in("st_wv", [NSEQ, 512, 128])
    st_pool = dt_in("st_pool", [NSEQ, 15, 512])
    pt_d = dt_in("pt", [NSEQ, 64], I32)
    caches = [dt_in(nm, [NPHYS * 128, 128]) for nm in ("c_cmp_k", "c_cmp_v", "c_sel_k", "c_sel_v")]
    bonus_s_d = dt_in("bonus_s", [4, 128])
    causal4_d = dt_in("causal4", [4, 4])
    wms0_d = dt_in("wms0", [128, 4])
    wsa_d = dt_in("wsa", [120, 4, 64])
    wsb_d = dt_in("wsb", [120, 4, 64])
    wn_d = dt_in("wn", [64, 4, 64])

    o_kv = dt_out("o_kv", [6, XROWS, 128])
    o_u = dt_out("o_u", [XROWS, 512])
    o_y = dt_out("o_y", [OROWS, D])
    o_swin = dt_out("o_swin", [2, NSEQ, 512, 128])
    o_spool = dt_out("o_spool", [NSEQ, 15, 512])

    modd = dt_int("modd", [M17, 9 * D])
    x1s = dt_int("x1s", [XROWS, D])
    x2s = dt_int("x2s", [OROWS, D])
    ksT_s = dt_int("ksT_s", [128, S], BF16)
    vs_s = dt_int("vs_s", [S, 130], BF16)
    kwT_s = dt_int("kwT_s", [128, (4 + NBLK) * 128], BF16)
    vw_s = dt_int("vw_s", [(4 + NBLK) * 128, 130], BF16)
    u_s = dt_int("u_s", [(1 + NBLK) * 128, 512], BF16)

    with stack:
        sc = Sched(nc, stack)
        ar = Arena(nc, stack, 204 * 1024)
        ps = []
        for i in range(8):
            t = stack.enter_context(nc.psum_tensor("ps%d" % i, [128, 512], F32))
            ps.append((t, Res("ps%d" % i)))
        R_modd = Res("modd"); R_okv = Res("okv"); R_ou = Res("ou"); R_oy = Res("oy")
        R_oswin = Res("oswin"); R_ospool = Res("ospool")
        R_x1s = Res("x1s"); R_x2s = Res("x2s")
        R_ksTs = Res("ksTs"); R_vss = Res("vss"); R_kwTs = Res("kwTs"); R_vws = Res("vws"); R_us = Res("us")

        def op(eng, fn, rd=(), wr=(), dma=None, deferred=False):
            return sc.op(eng, fn, rd=rd, wr=wr, dma=dma, deferred=deferred)

        def new_epoch():
            sc.barrier()

        dyn_ctr = [0]

        def dyn_dma(out_ap, base_ap, axis, idx_ap, mult, const, size, maxv, rd, wr, dma, rearr=None):
            dyn_ctr[0] += 1
            nm = "dr%d" % dyn_ctr[0]

            def fn(e):
                r = e.alloc_register(nm)
                e.reg_load(r, idx_ap)
                v = e.snap(r, donate=True, min_val=0, max_val=maxv)
                e1 = v * mult
                e2 = e1 + const if const else e1
                if axis == 0:
                    src = base_ap[ds(e2, size), :]
                else:
                    src = base_ap[:, ds(e2, size)]
                exprs = [e1, e2, src.offset, src.offset * 2, src.offset * 4]
                if rearr is not None:
                    src = src.rearrange(rearr, p=128)
                ins = e.dma_start(out=out_ap, in_=src)
                vc = e.get_value_cache()
                seen = set()
                for ex in exprs:
                    try:
                        al = vc.lookup(ex)
                    except Exception:
                        al = None
                    if al is not None and al.val.name not in seen and al.val.name != r.name:
                        seen.add(al.val.name)
                        e.free_register(al.val)
                e.free_register(r)
                return ins
            return op("sync", fn, rd=rd, wr=wr, dma=dma, deferred=True)

        ident_f, R_identf = ar.alloc("identf", [128], F32)
        ident_b, R_identb = ar.alloc("identb", [128], BF16)
        identrep, R_identrep = ar.alloc("identrep", [4, 128], BF16)
        op("sync", lambda e: e.dma_start(out=ident_f, in_=ident_d[:, :]), wr=[R_identf], dma=R_identf)
        op("vector", lambda e: e.tensor_copy(out=ident_b, in_=ident_f), rd=[R_identf], wr=[R_identb])
        for h in range(4):
            op("vector", lambda e, h=h: e.tensor_copy(out=identrep[:, h, :], in_=ident_f), rd=[R_identf], wr=[R_identrep])
        ones_f, R_ones = ar.alloc("ones", [128], F32)
        op("vector", lambda e: e.memset(ones_f, 1.0), wr=[R_ones])
        rinfo_t = stack.enter_context(nc.sbuf_tensor("rinfo_sb", [1, 8], I32))
        R_rinfo = Res("rinfo")
        op("sync", lambda e: e.dma_start(out=rinfo_t[:], in_=rinfo_d[:, :]), wr=[R_rinfo], dma=R_rinfo)
        ridx = rinfo_t[0:1, 0:1]
        kcT_sb, R_kcT = ar.alloc("kcT", [256], BF16)
        vcT_sb, R_vcT = ar.alloc("vcT", [256], BF16)
        vc_sb, R_vc = ar.alloc("vc", [2, 2, 65], BF16)
        W4k, R_W4k = ar.alloc("W4k", [4], BF16)
        W4v, R_W4v = ar.alloc("W4v", [4], BF16)
        wcol, R_wcol = ar.alloc("wcol", [2], F32)
        bmask, R_bmask = ar.alloc("bmask", [4], F32)
        pair_b, R_pair = ar.alloc("pair", [64], BF16)
        op("gpsimd", lambda e: e.dma_start(out=pair_b, in_=pair_d[:, :]), wr=[R_pair], dma=R_pair)
        op("sync", lambda e: e.dma_start(out=bmask, in_=bmask_d[:, :]), wr=[R_bmask], dma=R_bmask)
        for n in range(2):
            for qd in range(4):
                op("sync", lambda e, n=n, qd=qd: e.dma_start(out=wcol[32 * qd:32 * qd + 32, n:n + 1],
                                                            in_=cmpw[n:n + 1, :].rearrange("a l -> l a")),
                   wr=[R_wcol], dma=R_wcol)
        op("vector", lambda e: e.tensor_scalar(out=W4k, in0=bmask, scalar1=wcol[:, 0:1], scalar2=None, op0=ALU.mult),
           rd=[R_bmask, R_wcol], wr=[R_W4k])
        op("vector", lambda e: e.tensor_scalar(out=W4v, in0=bmask, scalar1=wcol[:, 1:2], scalar2=None, op0=ALU.mult),
           rd=[R_bmask, R_wcol], wr=[R_W4v])
        op("vector", lambda e: e.memset(vc_sb, 1.0), wr=[R_vc])
        base_mark = ar.mark()

        cs, R_cs = ar.alloc("cs", [D], F32)
        csb, R_csb = ar.alloc("csb", [D], BF16)
        siluT, R_siluT = ar.alloc("siluT", [8, M17], BF16)
        modsb, R_mod = ar.alloc("modsb", [9 * D], F32)
        bada, R_bada = ar.alloc("bada", [9 * D], F32)
        op("sync", lambda e: e.dma_start(out=cs[0:M17, :], in_=c17[:, :]), wr=[R_cs], dma=R_cs)
        op("sync", lambda e: e.dma_start(out=bada[0:1, :], in_=b_ada[:, :]), wr=[R_bada], dma=R_bada)
        op("scalar", lambda e: e.activation(out=csb[0:M17, :], in_=cs[0:M17, :], func=AF.Silu), rd=[R_cs], wr=[R_csb])
        pT, R_pT = ps[0]
        for k in range(8):
            pTk, R_pTk = ps[k % 2]
            op("tensor", lambda e, k=k, pTk=pTk: e.matmul(out=pTk[:, (k // 2) * M17:(k // 2 + 1) * M17], lhsT=csb[0:M17, k * 128:(k + 1) * 128],
                                                           rhs=ident_b[0:M17, 0:M17], start=True, stop=True),
               rd=[R_csb, R_identb], wr=[R_pTk])
        for k in range(8):
            pTk, R_pTk = ps[k % 2]
            op("vector", lambda e, k=k, pTk=pTk: e.tensor_copy(out=siluT[:, k, :], in_=pTk[:, (k // 2) * M17:(k // 2 + 1) * M17]),
               rd=[R_pTk], wr=[R_siluT])
        wad = [ar.alloc("wad%d" % i, [8, 512], BF16) for i in range(2)]
        w_ada_v = w_ada.rearrange("(k p) n -> p k n", p=128)
        for cg in range(18):
            wt, R_wt = wad[cg % 2]
            op("gpsimd", lambda e, cg=cg, wt=wt: e.dma_start(out=wt, in_=w_ada_v[:, :, cg * 512:(cg + 1) * 512]),
               wr=[R_wt], dma=R_wt)
            pm, R_pm = ps[2 + cg % 2]
            for k in range(8):
                op("tensor", lambda e, k=k, wt=wt, pm=pm: e.matmul(out=pm[0:M17, :], lhsT=siluT[:, k, :], rhs=wt[:, k, :],
                                                                   start=(k == 0), stop=False),
                   rd=[R_siluT, R_wt], wr=[R_pm])
            op("tensor", lambda e, cg=cg, pm=pm: e.matmul(out=pm[0:M17, :], lhsT=ones_f[0:1, 0:M17],
                                                          rhs=bada[0:1, cg * 512:(cg + 1) * 512], start=False, stop=True),
               rd=[R_ones, R_bada], wr=[R_pm])
            op("vector", lambda e, cg=cg, pm=pm: e.tensor_copy(out=modsb[0:M17, cg * 512:(cg + 1) * 512], in_=pm[0:M17, :]),
               rd=[R_pm], wr=[R_mod])
        op("sync", lambda e: e.dma_start(out=modd[:, :], in_=modsb[0:M17, :]), rd=[R_mod], wr=[R_modd], dma=R_modd)
        zt, R_zt = ar.alloc("zt", [5, 130], BF16)
        op("vector", lambda e: e.memset(zt, 0.0), wr=[R_zt])
        op("sync", lambda e: e.dma_start(out=kwT_s[:, 0:512], in_=zt.rearrange("p a b -> p (a b)")[:, 0:512]), rd=[R_zt], wr=[R_kwTs], dma=R_zt)
        op("sync", lambda e: e.dma_start(out=vw_s[0:512, :].rearrange("(m p) c -> p m c", p=128), in_=zt[:, 0:4, :]),
           rd=[R_zt], wr=[R_vws], dma=R_zt)
        op("sync", lambda e: e.dma_start(out=u_s[0:128, :], in_=zt.rearrange("p a b -> p (a b)")[:, 0:512]), rd=[R_zt], wr=[R_us], dma=R_zt)
        new_epoch()
        ar.reset(base_mark)

        def load_row(dst, R_dst, kind, mod_i, gain_i, mode, tmpg, R_tmpg):
            msrc = lambda a, b: modd[a:b, mod_i * D:(mod_i + 1) * D]
            if kind == 0:
                op("sync", lambda e: e.dma_start(out=dst, in_=msrc(0, 1).to_broadcast([128, D])), rd=[R_modd], wr=[R_dst], dma=R_dst)
            else:
                r0 = 1 + 16 * (kind - 1)
                for t in range(4):
                    op("sync", lambda e, t=t: e.dma_start(out=dst[16 * t:16 * t + 16, :], in_=msrc(r0, r0 + 16)), rd=[R_modd], wr=[R_dst], dma=R_dst)
            if mode == "B":
                return
            op("sync", lambda e: e.dma_start(out=tmpg, in_=gains[gain_i:gain_i + 1, :].to_broadcast([128, D])), wr=[R_tmpg], dma=R_tmpg)
            if mode == "A":
                op("vector", lambda e: e.scalar_tensor_tensor(out=dst, in0=dst, scalar=1.0, in1=tmpg, op0=ALU.add, op1=ALU.mult),
                   rd=[R_dst, R_tmpg], wr=[R_dst])
            else:
                sclr = 0.5 if mode == "G5" else 1.0
                op("vector", lambda e: e.scalar_tensor_tensor(out=dst, in0=dst, scalar=sclr, in1=tmpg, op0=ALU.mult, op1=ALU.mult),
                   rd=[R_dst, R_tmpg], wr=[R_dst])

        class Work:
            pass

        def alloc_work():
            w = Work()
            w.h32, w.R_h32 = ar.alloc("h32", [D], F32)
            w.hb, w.R_hb = ar.alloc("hb", [D], BF16)
            w.junk, w.R_junk = ar.alloc("junk", [D], BF16)
            w.ssq, w.R_ssq = ar.alloc("ssq", [1], F32)
            w.rstd, w.R_rstd = ar.alloc("rstd", [1], F32)
            w.hT, w.R_hT = ar.alloc("hT", [8, 512], BF16)
            w.tmpg, w.R_tmpg = ar.alloc("tmpg", [D], F32)
            return w

        def rms_rstd(w, nt):
            op("vector", lambda e: e.tensor_scalar(out=w.rstd[0:nt, :], in0=w.ssq[0:nt, :], scalar1=1.0 / D, scalar2=EPS,
                                                    op0=ALU.mult, op1=ALU.add), rd=[w.R_ssq], wr=[w.R_rstd])
            op("scalar", lambda e: e.activation(out=w.rstd[0:nt, :], in_=w.rstd[0:nt, :], func=AF.Sqrt), rd=[w.R_rstd], wr=[w.R_rstd])
            op("vector", lambda e: e.reciprocal(out=w.rstd[0:nt, :], in_=w.rstd[0:nt, :]), rd=[w.R_rstd], wr=[w.R_rstd])

        def norm_mod(w, x_ap, R_x, nt, A, R_A, Bt, R_B, col0):
            op("scalar", lambda e: e.activation(out=w.junk[0:nt, :], in_=x_ap, func=AF.Square, accum_out=w.ssq[0:nt, :]),
               rd=[R_x], wr=[w.R_junk, w.R_ssq])
            rms_rstd(w, nt)
            op("vector", lambda e: e.scalar_tensor_tensor(out=w.h32[0:nt, :], in0=x_ap, scalar=w.rstd[0:nt, :], in1=A[0:nt, :],
                                                           op0=ALU.mult, op1=ALU.mult), rd=[R_x, w.R_rstd, R_A], wr=[w.R_h32])
            op("vector", lambda e: e.tensor_tensor(out=w.hb[0:nt, :], in0=w.h32[0:nt, :], in1=Bt[0:nt, :], op=ALU.add),
               rd=[w.R_h32, R_B], wr=[w.R_hb])
            for half in range(2):
                pt_, R_pt = ps[half]
                for kk in range(4):
                    k = half * 4 + kk
                    op("tensor", lambda e, k=k, kk=kk, pt_=pt_: e.matmul(out=pt_[:, kk * 128:kk * 128 + nt],
                                                                          lhsT=w.hb[0:nt, k * 128:(k + 1) * 128],
                                                                          rhs=ident_b[0:nt, 0:nt], start=True, stop=True),
                       rd=[w.R_hb, R_identb], wr=[R_pt])
                src = pt_[:, :].rearrange("p (a b) -> p a b", a=4)[:, :, 0:nt]
                if half == 0:
                    op("scalar", lambda e, half=half, src=src: e.activation(out=w.hT[:, half * 4:half * 4 + 4, col0:col0 + nt],
                                                                              in_=src, func=AF.Copy), rd=[R_pt], wr=[w.R_hT])
                else:
                    op("vector", lambda e, half=half, src=src: e.tensor_copy(out=w.hT[:, half * 4:half * 4 + 4, col0:col0 + nt],
                                                                               in_=src), rd=[R_pt], wr=[w.R_hT])

        def post_residual(w, py, nt, x_ap, R_x, G, R_G):
            op("scalar", lambda e: e.activation(out=w.junk[0:nt, 0:512], in_=py[0][0][0:nt, :], func=AF.Square,
                                                 accum_out=w.ssq[0:nt, :]), rd=[py[0][1]], wr=[w.R_junk, w.R_ssq])
            op("scalar", lambda e: e.activation(out=w.junk[0:nt, 512:1024], in_=py[1][0][0:nt, :], func=AF.Square,
                                                 accum_out=w.rstd[0:nt, :]), rd=[py[1][1]], wr=[w.R_junk, w.R_rstd])
            op("vector", lambda e: e.tensor_tensor(out=w.ssq[0:nt, :], in0=w.ssq[0:nt, :], in1=w.rstd[0:nt, :], op=ALU.add),
               rd=[w.R_ssq, w.R_rstd], wr=[w.R_ssq])
            rms_rstd(w, nt)
            for half in range(2):
                pyh, R_pyh = py[half]
                hs = slice(half * 512, (half + 1) * 512)
                op("vector", lambda e, pyh=pyh, hs=hs: e.scalar_tensor_tensor(
                    out=w.h32[0:nt, hs], in0=pyh[0:nt, :], scalar=w.rstd[0:nt, :], in1=G[0:nt, hs], op0=ALU.mult, op1=ALU.mult),
                    rd=[R_pyh, w.R_rstd, R_G], wr=[w.R_h32])
            op("vector", lambda e: e.tensor_tensor(out=x_ap[0:nt, :], in0=x_ap[0:nt, :], in1=w.h32[0:nt, :], op=ALU.add),
               rd=[w.R_h32, R_x], wr=[R_x])

        wi_ctr = [0]

        def ffn_pass(w, f, tiles, wi_v, wo, R_wo, rowsA, rowsB, rowsG, post):
            ntok = sum(t[1] for t in tiles)
            A, R_A = rowsA; Bt, R_B = rowsB; G, R_G = rowsG
            col = 0
            cols = []
            for ti, (load_fn, nt, tag) in enumerate(tiles):
                x_ap, R_x = f.xt[ti]
                load_fn(x_ap, R_x)
                norm_mod(w, x_ap[0:nt, :], R_x, nt, A, R_A, Bt, R_B, col)
                cols.append(col)
                col += nt
            for fc in range(NF):
                wb, R_wb = f.wib[wi_ctr[0] % 3]
                wi_ctr[0] += 1
                op("gpsimd", lambda e, fc=fc, wb=wb: e.dma_start(out=wb[:, :, 0:128], in_=wi_v[:, :, fc * 128:(fc + 1) * 128]),
                   wr=[R_wb], dma=R_wb)
                op("gpsimd", lambda e, fc=fc, wb=wb: e.dma_start(out=wb[:, :, 128:256],
                                                                 in_=wi_v[:, :, DFF + fc * 128:DFF + (fc + 1) * 128]),
                   wr=[R_wb], dma=R_wb)
                pa, R_pa = ps[2 + 2 * (fc % 2)]
                pb, R_pb = ps[3 + 2 * (fc % 2)]
                for k in range(8):
                    op("tensor", lambda e, k=k, wb=wb, pa=pa: e.matmul(out=pa[:, 0:ntok], lhsT=wb[:, k, 0:128], rhs=w.hT[:, k, 0:ntok],
                                                                       start=(k == 0), stop=(k == 7)), rd=[R_wb, w.R_hT], wr=[R_pa])
                for k in range(8):
                    op("tensor", lambda e, k=k, wb=wb, pb=pb: e.matmul(out=pb[:, 0:ntok], lhsT=wb[:, k, 128:256], rhs=w.hT[:, k, 0:ntok],
                                                                       start=(k == 0), stop=(k == 7)), rd=[R_wb, w.R_hT], wr=[R_pb])
                op("scalar", lambda e, pa=pa: e.activation(out=f.sa[:, 0:ntok], in_=pa[:, 0:ntok], func=AF.Silu), rd=[R_pa], wr=[f.R_sa])
                op("vector", lambda e, fc=fc, pb=pb: e.tensor_tensor(out=f.gT[:, fc, 0:ntok], in0=f.sa[:, 0:ntok], in1=pb[:, 0:ntok], op=ALU.mult),
                   rd=[f.R_sa, R_pb], wr=[f.R_gT])
            for ti, (load_fn, nt, tag) in enumerate(tiles):
                x_ap, R_x = f.xt[ti]
                c0 = cols[ti]
                py = [ps[0], ps[1]]
                for half in range(2):
                    pyh, R_pyh = py[half]
                    for fc in range(NF):
                        op("tensor", lambda e, fc=fc, half=half, pyh=pyh, c0=c0, nt=nt: e.matmul(
                            out=pyh[0:nt, :], lhsT=f.gT[:, fc, c0:c0 + nt], rhs=wo[:, fc, half * 512:(half + 1) * 512],
                            start=(fc == 0), stop=(fc == NF - 1)), rd=[f.R_gT, R_wo], wr=[R_pyh])
                post_residual(w, py, nt, x_ap, R_x, G, R_G)
                post(ti, x_ap, R_x, nt, c0, tag)

        class FBuf:
            pass

        def alloc_ffn():
            f = FBuf()
            f.xt = [ar.alloc("xt%d" % i, [D], F32) for i in range(4)]
            f.gT, f.R_gT = ar.alloc("gT", [NF, 512], BF16)
            f.sa, f.R_sa = ar.alloc("sa", [512], F32)
            f.wib = [ar.alloc("wib%d" % i, [8, 256], BF16) for i in range(3)]
            return f

        wo1, R_wo1 = ar.alloc("wo1", [NF, D], BF16)
        op("gpsimd", lambda e: e.dma_start(out=wo1, in_=f1wo.rearrange("(f p) n -> p f n", p=128)), wr=[R_wo1], dma=R_wo1)
        winT, R_win = ar.alloc("winT", [8, INW], BF16)
        w_in_v = w_in.rearrange("(k p) n -> p k n", p=128)
        op("gpsimd", lambda e: e.dma_start(out=winT, in_=w_in_v), wr=[R_win], dma=R_win)
        w = alloc_work()
        f = alloc_ffn()
        R_rows1 = Res("rows1")
        rows1 = {nm: ar.alloc(nm, [D], F32, res=R_rows1) for nm in ("A1", "B1", "G1", "A2", "B2")}
        kvst = [ar.alloc("kvst%d" % i, [768], F32) for i in range(2)]
        ust = [ar.alloc("ust%d" % i, [512], F32) for i in range(2)]
        ubs = [ar.alloc("ub%d" % i, [512], BF16) for i in range(2)]
        vst = [ar.alloc("vst%d" % i, [2, 2, 65], BF16) for i in range(2)]
        kTst = [ar.alloc("kTst%d" % i, [256], BF16) for i in range(2)]
        kcrb = [ar.alloc("kcrb%d" % i, [256], BF16) for i in range(2)]
        for i in range(2):
            op("vector", lambda e, i=i: e.memset(vst[i][0], 1.0), wr=[vst[i][1]])

        def load_rows1(kind):
            load_row(*rows1["A1"], kind, 1, 0, "A", w.tmpg, w.R_tmpg)
            load_row(*rows1["B1"], kind, 0, 0, "B", w.tmpg, w.R_tmpg)
            load_row(*rows1["G1"], kind, 2, 1, "G5", w.tmpg, w.R_tmpg)
            load_row(*rows1["A2"], kind, 4, 2, "A", w.tmpg, w.R_tmpg)
            load_row(*rows1["B2"], kind, 3, 2, "B", w.tmpg, w.R_tmpg)

        f1wi_v = f1wi.rearrange("(k p) n -> p k n", p=128)
        f2wi_v = f2wi.rearrange("(k p) n -> p k n", p=128)
        tctr = [0]

        def post1(ti, x_ap, R_x, nt, c0, tag):
            is_s = isinstance(tag, tuple)
            sg = tag[1] if is_s else None
            row0 = (S + 64 * sg) if is_s else tag * 128
            tc_ = tctr[0]
            tctr[0] += 1
            op("sync", lambda e: e.dma_start(out=x1s[row0:row0 + nt, :], in_=x_ap[0:nt, :]), rd=[R_x], wr=[R_x1s], dma=R_x)
            norm_mod(w, x_ap[0:nt, :], R_x, nt, *rows1["A2"], *rows1["B2"], c0)
            pk0, R_pk0 = ps[6]
            pk1, R_pk1 = ps[7]
            pu, R_pu = ps[5]
            for k in range(8):
                op("tensor", lambda e, k=k: e.matmul(out=pk0[0:nt, :], lhsT=w.hT[:, k, c0:c0 + nt], rhs=winT[:, k, 512:1024],
                                                      start=(k == 0), stop=(k == 7)), rd=[w.R_hT, R_win], wr=[R_pk0])
            for k in range(8):
                op("tensor", lambda e, k=k: e.matmul(out=pk1[0:nt, 0:256], lhsT=w.hT[:, k, c0:c0 + nt], rhs=winT[:, k, 1024:1280],
                                                      start=(k == 0), stop=(k == 7)), rd=[w.R_hT, R_win], wr=[R_pk1])
            for k in range(8):
                op("tensor", lambda e, k=k: e.matmul(out=pu[0:nt, :], lhsT=w.hT[:, k, c0:c0 + nt], rhs=winT[:, k, 1304:1816],
                                                      start=(k == 0), stop=(k == 7)), rd=[w.R_hT, R_win], wr=[R_pu])
            ks_, R_ks = kvst[tc_ % 2]
            us_, R_us_ = ust[tc_ % 2]
            op("scalar", lambda e: e.activation(out=ks_[0:nt, 0:512], in_=pk0[0:nt, :], func=AF.Copy), rd=[R_pk0], wr=[R_ks])
            op("scalar", lambda e: e.activation(out=ks_[0:nt, 512:768], in_=pk1[0:nt, 0:256], func=AF.Copy), rd=[R_pk1], wr=[R_ks])
            op("vector", lambda e: e.tensor_copy(out=us_[0:nt, :], in_=pu[0:nt, :]), rd=[R_pu], wr=[R_us_])
            op("sync", lambda e: e.dma_start(out=o_kv[:, row0:row0 + nt, :].rearrange("s t c -> t s c"),
                                             in_=ks_[0:nt, :].rearrange("p (s c) -> p s c", s=6)), rd=[R_ks], wr=[R_okv], dma=R_ks)
            op("sync", lambda e: e.dma_start(out=o_u[row0:row0 + nt, :], in_=us_[0:nt, :]), rd=[R_us_], wr=[R_ou], dma=R_us_)
            if is_s:
                for t in range(4):
                    for n in range(2):
                        dst = bass.AP(tensor=o_swin.tensor, offset=(n * NSEQ + 16 * sg) * 65536 + (508 + t) * 128, ap=[[65536, 16], [1, 128]])
                        op("sync", lambda e, t=t, n=n, dst=dst: e.dma_start(out=dst, in_=ks_[16 * t:16 * t + 16, 512 + 128 * n:640 + 128 * n]),
                           rd=[R_ks], wr=[R_oswin], dma=R_ks)
                    dstp = bass.AP(tensor=o_spool.tensor, offset=(16 * sg * 15 + 11 + t) * 512, ap=[[15 * 512, 16], [1, 512]])
                    op("sync", lambda e, t=t, dstp=dstp: e.dma_start(out=dstp, in_=us_[16 * t:16 * t + 16, :]),
                       rd=[R_us_], wr=[R_ospool], dma=R_us_)
                return
            j = tag
            ub_, R_ub = ubs[tc_ % 2]
            op("vector", lambda e: e.tensor_copy(out=ub_, in_=us_), rd=[R_us_], wr=[R_ub])
            op("sync", lambda e: e.dma_start(out=u_s[(1 + j) * 128:(2 + j) * 128, :], in_=ub_), rd=[R_ub], wr=[R_us], dma=R_ub)
            vs_, R_vs = vst[tc_ % 2]
            op("vector", lambda e: e.tensor_copy(out=vs_[:, 0, :, 0:64], in_=ks_[:, 384:512].rearrange("p (g d) -> p g d", g=2)),
               rd=[R_ks], wr=[R_vs])
            op("vector", lambda e: e.tensor_copy(out=vs_[:, 1, :, 0:64], in_=ks_[:, 640:768].rearrange("p (g d) -> p g d", g=2)),
               rd=[R_ks], wr=[R_vs])
            op("sync", lambda e: e.dma_start(out=vs_s[j * 128:(j + 1) * 128, :], in_=vs_[:, 0, :, :].rearrange("p g c -> p (g c)")),
               rd=[R_vs], wr=[R_vss], dma=R_vs)
            op("sync", lambda e: e.dma_start(out=vw_s[(4 + j) * 128:(5 + j) * 128, :], in_=vs_[:, 1, :, :].rearrange("p g c -> p (g c)")),
               rd=[R_vs], wr=[R_vws], dma=R_vs)
            pf, R_pf = ps[4]
            for n, c_lo in enumerate((768, 1024)):
                for k in range(8):
                    op("tensor", lambda e, k=k, n=n, c_lo=c_lo: e.matmul(out=pf[:, n * 128:(n + 1) * 128], lhsT=winT[:, k, c_lo:c_lo + 128],
                                                                          rhs=w.hT[:, k, c0:c0 + 128], start=(k == 0), stop=(k == 7)),
                       rd=[w.R_hT, R_win], wr=[R_pf])
            kt_, R_kt = kTst[tc_ % 2]
            op("scalar", lambda e: e.activation(out=kt_, in_=pf[:, 0:256], func=AF.Copy), rd=[R_pf], wr=[R_kt])
            op("sync", lambda e: e.dma_start(out=ksT_s[:, j * 128:(j + 1) * 128], in_=kt_[:, 0:128]), rd=[R_kt], wr=[R_ksTs], dma=R_kt)
            op("sync", lambda e: e.dma_start(out=kwT_s[:, (4 + j) * 128:(5 + j) * 128], in_=kt_[:, 128:256]), rd=[R_kt], wr=[R_kwTs], dma=R_kt)
            kb_, R_kb = kcrb[tc_ % 2]
            op("vector", lambda e: e.tensor_copy(out=kb_, in_=ks_[:, 0:256]), rd=[R_ks], wr=[R_kb])
            pc, R_pc = ps[4]
            op("tensor", lambda e: e.matmul(out=pc[:, 256:260], lhsT=kb_[:, 0:128], rhs=W4k, start=True, stop=True),
               rd=[R_kb, R_W4k], wr=[R_pc])
            op("tensor", lambda e: e.matmul(out=pc[:, 260:264], lhsT=kb_[:, 128:256], rhs=W4v, start=True, stop=True),
               rd=[R_kb, R_W4v], wr=[R_pc])
            op("vector", lambda e: e.tensor_copy(out=kcT_sb[:, 4 * j:4 * j + 4], in_=pc[:, 256:260]), rd=[R_pc], wr=[R_kcT])
            op("vector", lambda e: e.tensor_copy(out=vcT_sb[:, 4 * j:4 * j + 4], in_=pc[:, 260:264]), rd=[R_pc], wr=[R_vcT])

        def xload(src):
            return lambda x_ap, R_x: op("sync", lambda e: e.dma_start(out=x_ap[0:src.shape[0], :], in_=src), wr=[R_x], dma=R_x)

        for sg in range(NSG):
            for n, st in enumerate((st_wk, st_wv)):
                srcw = bass.AP(tensor=st.tensor, offset=16 * sg * 65536 + 512, ap=[[65536, 16], [512, 127], [1, 512]])
                dstw = bass.AP(tensor=o_swin.tensor, offset=(n * NSEQ + 16 * sg) * 65536, ap=[[65536, 16], [512, 127], [1, 512]])
                op("sync", lambda e: e.dma_start(out=dstw, in_=srcw), wr=[R_oswin], dma=R_oswin)
            srcp = bass.AP(tensor=st_pool.tensor, offset=(16 * sg * 15 + 4) * 512, ap=[[15 * 512, 16], [1, 11 * 512]])
            dstp2 = bass.AP(tensor=o_spool.tensor, offset=16 * sg * 15 * 512, ap=[[15 * 512, 16], [1, 11 * 512]])
            op("sync", lambda e: e.dma_start(out=dstp2, in_=srcp), wr=[R_ospool], dma=R_ospool)
        load_rows1(0)
        for p in range(NBLK // 4):
            tiles = [(xload(xp[j * 128:(j + 1) * 128, :]), 128, j) for j in range(4 * p, 4 * p + 4)]
            ffn_pass(w, f, tiles, f1wi_v, wo1, R_wo1, rows1["A1"], rows1["B1"], rows1["G1"], post1)
            if p % 4 == 3:
                new_epoch()
        for sg in range(NSG):
            new_epoch()
            load_rows1(1 + sg)
            ffn_pass(w, f, [(xload(xs[64 * sg:64 * sg + 64, :]), 64, ("S", sg))], f1wi_v, wo1, R_wo1, rows1["A1"], rows1["B1"], rows1["G1"], post1)
        for jt in range(2):
            pv_, R_pv = ps[jt]
            op("tensor", lambda e, jt=jt, pv_=pv_: e.matmul(out=pv_[:, 0:128], lhsT=vcT_sb[:, jt * 128:(jt + 1) * 128], rhs=ident_b,
                                                            start=True, stop=True), rd=[R_vcT, R_identb], wr=[R_pv])
            op("vector", lambda e, jt=jt, pv_=pv_: e.tensor_copy(out=vc_sb[:, jt, :, 0:64], in_=pv_[:, 0:128].rearrange("p (g d) -> p g d", g=2)),
               rd=[R_pv], wr=[R_vc])
        new_epoch()
        ar.reset(base_mark)
        new_epoch()
        ar.reset(base_mark)

        w = alloc_work()
        R_p2a = Res("p2a"); R_p2b = Res("p2b")
        wq, R_wq = ar.alloc("wq", [8, 4, 128], BF16, res=R_p2a)
        for h in range(8):
            op("gpsimd", lambda e, h=h: e.dma_start(out=wq[:, :, h % 4, (h // 4) * 64:(h // 4) * 64 + 64], in_=w_in_v[:, :, h * 64:(h + 1) * 64]),
               wr=[R_wq], dma=R_wq)
        wg, R_wg = ar.alloc("wg", [8, 24], BF16, res=R_p2a)
        op("gpsimd", lambda e: e.dma_start(out=wg, in_=w_in_v[:, :, 1280:1304]), wr=[R_wg], dma=R_wg)
        wout, R_wout = ar.alloc("wout", [8, D], BF16, res=R_p2a)
        op("gpsimd", lambda e: e.dma_start(out=wout, in_=w_out.rearrange("(k p) n -> p k n", p=128)), wr=[R_wout], dma=R_wout)
        pw, R_pw = ar.alloc("pw", [4, 128], BF16, res=R_p2a)
        op("gpsimd", lambda e: e.dma_start(out=pw, in_=pool_w.rearrange("g c d -> c g d")), wr=[R_pw], dma=R_pw)
        pscl, R_pscl = ar.alloc("pscl", [4], F32)
        for gi in range(4):
            op("sync", lambda e, gi=gi: e.dma_start(out=pscl[:, gi:gi + 1], in_=pool_scale[0:1, gi * 128:(gi + 1) * 128].rearrange("a d -> d a")),
               wr=[R_pscl], dma=R_pscl)
        cm_sb, R_cm = ar.alloc("cm", [NGRP, 128], BF16, res=R_p2b)
        op("gpsimd", lambda e: e.dma_start(out=cm_sb, in_=cm_d.rearrange("k p q -> p k q")), wr=[R_cm], dma=R_cm)
        wm_sb, R_wm = ar.alloc("wm", [25, 128], BF16, res=R_p2b)
        op("gpsimd", lambda e: e.dma_start(out=wm_sb, in_=wm_d.rearrange("a m p q -> p (a m) q")), wr=[R_wm], dma=R_wm)
        wp_sb, R_wp = ar.alloc("wp", [16, 128], BF16, res=R_p2b)
        op("gpsimd", lambda e: e.dma_start(out=wp_sb, in_=wp_d.rearrange("a g b p q -> p (a g b) q")), wr=[R_wp], dma=R_wp)
        R_rows2 = Res("rows2")
        rows2 = {nm: ar.alloc(nm, [D], F32, res=R_rows2) for nm in ("A2", "B2", "G2")}

        def load_rows2(kind):
            load_row(*rows2["A2"], kind, 4, 2, "A", w.tmpg, w.R_tmpg)
            load_row(*rows2["B2"], kind, 3, 2, "B", w.tmpg, w.R_tmpg)
            load_row(*rows2["G2"], kind, 5, 3, "G1", w.tmpg, w.R_tmpg)

        ksT_buf, _ = ar.alloc("ksTb", [S], BF16)
        vs_buf, _ = ar.alloc("vsb", [NBLK, 130], BF16)
        R_ksb = [Res("ksb")] * NOWN
        R_vsb = [Res("vsb")] * NOWN
        kwb = [ar.alloc("kwb%d" % i, [640], BF16) for i in range(2)]
        vwb = [ar.alloc("vwb%d" % i, [5, 130], BF16) for i in range(2)]
        u2b = [ar.alloc("u2b%d" % i, [2, 512], BF16) for i in range(2)]
        x1t = [ar.alloc("x1t%d" % i, [D], F32) for i in range(2)]
        qT_sb, R_qT = ar.alloc("qT", [4, 128], BF16)
        gates, R_gates = ar.alloc("gates", [24], F32)
        cmk = [ar.alloc("cmk%d" % i, [2, 128], BF16) for i in range(2)]
        bon = [ar.alloc("bon%d" % i, [128], F32) for i in range(2)]
        PTb = [ar.alloc("PT%d" % i, [512], BF16) for i in range(4)]
        nexp = [ar.alloc("nexp%d" % g, [S], BF16) for g in range(2)]
        rden, R_rden = ar.alloc("rden", [4], F32)
        coef, R_coef = ar.alloc("coef", [4], F32)
        tmpo, R_tmpo = ar.alloc("tmpo", [4, 64], F32)
        onsa, R_onsa = ar.alloc("onsa", [512], F32)
        onsab, R_onsab = ar.alloc("onsab", [512], BF16)
        onsaT, R_onsaT = ar.alloc("onsaT", [4, 128], BF16)
        tmpi, R_tmpi = ar.alloc("tmpi", [4, 128], F32)
        vals, R_vals = ar.alloc("vals", [128], F32)
        vals2, R_vals2 = ar.alloc("vals2", [128], F32)
        m8, R_m8 = ar.alloc("m8", [16], F32)
        selm, R_selm = ar.alloc("selm", [128], F32)
        negs, R_negs = ar.alloc("negs", [128], BF16)
        dT_sb, R_dT = ar.alloc("dT", [4, 128], BF16)
        ypT, R_ypT = ar.alloc("ypT", [4, 128], BF16)
        pt_ctr = [0]

        def attend(g, nq, q_rhs, R_q, tiles, pO, R_pO, pI=None, R_pI=None, irep=None):
            n = len(tiles)
            oview = pO[:, 0:260].rearrange("p (h c) -> p h c", h=4)
            irep_ap, R_irep = (identrep.rearrange("p a b -> p (a b)"), R_identrep) if irep is None else irep
            for idx, t in enumerate(tiles):
                nk = t.get("nk", 128)
                pS, R_pS = ps[2 + (pt_ctr[0] % 2)]
                PT, R_PT = PTb[pt_ctr[0] % 4]
                pt_ctr[0] += 1
                has_add = t.get("add") is not None
                op("tensor", lambda e: e.matmul(out=pS[0:nk, 0:4 * nq], lhsT=t["kT"], rhs=q_rhs, start=True, stop=not has_add),
                   rd=[t["Rk"], R_q], wr=[R_pS])
                if has_add:
                    a_ap, R_a = t["add"]
                    op("tensor", lambda e: e.matmul(out=pS[0:nk, 0:4 * nq], lhsT=a_ap, rhs=irep_ap, start=False, stop=True),
                       rd=[R_a, R_irep], wr=[R_pS])
                op("scalar", lambda e: e.activation(out=PT[0:nk, 0:4 * nq], in_=pS[0:nk, 0:4 * nq], func=AF.Exp, scale=SCALE),
                   rd=[R_pS], wr=[R_PT])
                if t.get("mul") is not None:
                    m_ap, R_m = t["mul"]
                    pv3 = PT[0:nk, 0:4 * nq].rearrange("p (h q) -> p h q", h=4)
                    op("vector", lambda e: e.tensor_tensor(out=pv3, in0=pv3, in1=bc_mid(m_ap, 4), op=ALU.mult),
                       rd=[R_PT, R_m], wr=[R_PT])
                for hh in range(4):
                    op("tensor", lambda e: e.matmul(out=oview[0:nq, hh, :], lhsT=PT[0:nk, hh * nq:(hh + 1) * nq], rhs=t["v"],
                                                     start=(idx == 0 and hh == 0), stop=(idx == n - 1)),
                       rd=[R_PT, t["Rv"]], wr=[R_pO])
                if pI is not None:
                    jt = t["jt"]
                    iview = pI[:, :].rearrange("p (h b) -> p h b", h=4)
                    for hh in range(4):
                        op("tensor", lambda e: e.matmul(out=iview[0:nq, hh, jt * 64:(jt + 1) * 64], lhsT=PT[0:nk, hh * nq:(hh + 1) * nq],
                                                         rhs=pair_b[0:nk, :], start=True, stop=True),
                           rd=[R_PT, R_pair], wr=[R_pI])

        def finish_branch(g, nq, br, pO, R_pO, gates_ap, first):
            oview = pO[:, 0:260].rearrange("p (h c) -> p h c", h=4)
            op("vector", lambda e: e.tensor_scalar(out=rden[0:nq, :], in0=oview[0:nq, :, 64], scalar1=1e-30, scalar2=None, op0=ALU.max),
               rd=[R_pO], wr=[R_rden])
            op("vector", lambda e: e.reciprocal(out=rden[0:nq, :], in_=rden[0:nq, :]), rd=[R_rden], wr=[R_rden])
            gv = gates_ap.rearrange("p (h b) -> p h b", b=3)[:, 4 * g:4 * g + 4, br]
            op("vector", lambda e: e.tensor_tensor(out=coef[0:nq, :], in0=rden[0:nq, :], in1=gv, op=ALU.mult),
               rd=[R_rden, R_gates], wr=[R_coef])
            ov = onsa[0:nq, g * 256:(g + 1) * 256].rearrange("p (h d) -> p h d", h=4)
            if first:
                op("vector", lambda e: e.tensor_tensor(out=ov, in0=oview[0:nq, :, 0:64], in1=bc_last(coef[0:nq, :], 64), op=ALU.mult),
                   rd=[R_pO, R_coef], wr=[R_onsa])
            else:
                op("vector", lambda e: e.tensor_tensor(out=tmpo[0:nq, :, :], in0=oview[0:nq, :, 0:64], in1=bc_last(coef[0:nq, :], 64), op=ALU.mult),
                   rd=[R_pO, R_coef], wr=[R_tmpo])
                op("vector", lambda e: e.tensor_tensor(out=ov, in0=ov, in1=tmpo[0:nq, :, :], op=ALU.add), rd=[R_tmpo, R_onsa], wr=[R_onsa])

        def select_blocks(g, nq, pI, R_pI, bonus_ap, R_bonus, kth, nblk_exp):
            iview = pI[:, :].rearrange("p (h b) -> p h b", h=4)
            op("vector", lambda e: e.tensor_tensor(out=tmpi[0:nq, :, :], in0=iview[0:nq, :, :], in1=bc_last(rden[0:nq, :], 128), op=ALU.mult),
               rd=[R_pI, R_rden], wr=[R_tmpi])
            op("vector", lambda e: e.tensor_reduce(out=vals[0:nq, :], in_=tmpi[0:nq, :, :].rearrange("p h b -> p b h"),
                                                    axis=mybir.AxisListType.X, op=ALU.add), rd=[R_tmpi], wr=[R_vals])
            op("vector", lambda e: e.tensor_tensor(out=vals[0:nq, :], in0=vals[0:nq, :], in1=bonus_ap, op=ALU.add),
               rd=[R_vals, R_bonus], wr=[R_vals])
            op("vector", lambda e: e.max(out=m8[0:nq, 0:8], in_=vals[0:nq, :]), rd=[R_vals], wr=[R_m8])
            op("vector", lambda e: e.match_replace(out=vals2[0:nq, :], in_to_replace=m8[0:nq, 0:8], in_values=vals[0:nq, :], imm_value=-3.0e38),
               rd=[R_vals, R_m8], wr=[R_vals2])
            op("vector", lambda e: e.max(out=m8[0:nq, 8:16], in_=vals2[0:nq, :]), rd=[R_vals2], wr=[R_m8])
            op("vector", lambda e: e.tensor_scalar(out=selm[0:nq, :], in0=vals[0:nq, :], scalar1=m8[0:nq, 8 + kth - 9:8 + kth - 8], scalar2=None,
                                                    op0=ALU.is_ge), rd=[R_vals, R_m8], wr=[R_selm])
            op("vector", lambda e: e.tensor_scalar(out=negs[0:nq, :], in0=selm[0:nq, :], scalar1=-1.0, scalar2=-NEGM, op0=ALU.add, op1=ALU.mult),
               rd=[R_selm], wr=[R_negs])
            nx, R_nx = nexp[g]
            op("vector", lambda e: e.tensor_copy(out=nx[0:nq, 0:nblk_exp * 64].rearrange("p (b l) -> p b l", l=64),
                                                  in_=bc_last(negs[0:nq, 0:nblk_exp], 64)), rd=[R_negs], wr=[R_nx])

        load_rows2(0)
        for i in range(NOWN):
            x_ap, R_x = x1t[i % 2]
            nkt = NGRP * i + NGRP
            kind = 0 if i == 0 else 1
            kw5 = min(NGRP * i, 4)
            c128 = 128 * NGRP * i
            dyn_dma(x_ap, x1s, 0, ridx, 128, c128, 128, NGRP - 1, rd=[R_x1s, R_rinfo], wr=[R_x], dma=R_x)
            op("sync", lambda e, i=i: e.dma_start(out=ksT_buf[:, c128:c128 + 128 * NGRP], in_=ksT_s[:, c128:c128 + 128 * NGRP]),
               rd=[R_ksTs], wr=[R_ksb[i]], dma=R_ksb[i])
            op("sync", lambda e, i=i: e.dma_start(out=vs_buf[:, NGRP * i:NGRP * i + NGRP, :], in_=vs_s[c128:c128 + 128 * NGRP, :].rearrange("(m p) c -> p m c", p=128)),
               rd=[R_vss], wr=[R_vsb[i]], dma=R_vsb[i])
            kw_, R_kw = kwb[i % 2]
            vw_, R_vw = vwb[i % 2]
            u2_, R_u2 = u2b[i % 2]
            dyn_dma(kw_, kwT_s, 1, ridx, 128, c128, 640, NGRP - 1, rd=[R_kwTs, R_rinfo], wr=[R_kw], dma=R_kw)
            dyn_dma(vw_, vw_s, 0, ridx, 128, c128, 640, NGRP - 1, rd=[R_vws, R_rinfo], wr=[R_vw], dma=R_vw, rearr="(m p) c -> p m c")
            dyn_dma(u2_, u_s, 0, ridx, 128, c128, 256, NGRP - 1, rd=[R_us, R_rinfo], wr=[R_u2], dma=R_u2, rearr="(m p) c -> p m c")
            cmk_, R_cmk = cmk[i % 2]
            bon_, R_bon = bon[i % 2]
            op("gpsimd", lambda e, i=i, cmk_=cmk_: e.dma_start(out=cmk_, in_=cmpmask_d[i].rearrange("t p q -> p t q")), wr=[R_cmk], dma=R_cmk)
            op("sync", lambda e, i=i, bon_=bon_: e.dma_start(out=bon_, in_=bonus_d[i]), wr=[R_bon], dma=R_bon)
            norm_mod(w, x_ap, R_x, 128, *rows2["A2"], *rows2["B2"], 0)
            pq, R_pq = ps[7]
            for c in range(4):
                for k in range(8):
                    op("tensor", lambda e, c=c, k=k: e.matmul(out=pq[:, c * 128:(c + 1) * 128], lhsT=wq[:, k, c, :], rhs=w.hT[:, k, 0:128],
                                                               start=(k == 0), stop=(k == 7)), rd=[R_wq, w.R_hT], wr=[R_pq])
            op("scalar", lambda e: e.activation(out=qT_sb.rearrange("p a b -> p (a b)"), in_=pq[:, :], func=AF.Copy), rd=[R_pq], wr=[R_qT])
            pg_, R_pg = ps[6]
            for k in range(8):
                op("tensor", lambda e, k=k: e.matmul(out=pg_[:, 0:24], lhsT=w.hT[:, k, 0:128], rhs=wg[:, k, :], start=(k == 0), stop=(k == 7)),
                   rd=[R_wg, w.R_hT], wr=[R_pg])
            op("scalar", lambda e: e.activation(out=gates, in_=pg_[:, 0:24], func=AF.Sigmoid), rd=[R_pg], wr=[R_gates])
            for g in range(2):
                gs = slice(64 * g, 64 * g + 64)
                q_rhs = qT_sb[gs, :, :].rearrange("p a b -> p (a b)")
                pO, R_pO = ps[4 + g]
                pI, R_pI = ps[6]
                tl = [dict(kT=kcT_sb[gs, jt * 128:(jt + 1) * 128], Rk=R_kcT, v=vc_sb[:, jt, g, :], Rv=R_vc, mul=(cmk_[:, jt, :], R_cmk), jt=jt)
                      for jt in range(2)]
                attend(g, 128, q_rhs, R_qT, tl, pO, R_pO, pI, R_pI)
                finish_branch(g, 128, 0, pO, R_pO, gates, True)
                select_blocks(g, 128, pI, R_pI, bon_, R_bon, 16, 2 * nkt)
                nx, R_nx = nexp[g]
                tl = []
                for kt in range(nkt):
                    d = dict(kT=ksT_buf[gs, kt * 128:(kt + 1) * 128], Rk=R_ksb[0], v=vs_buf[:, kt, 65 * g:65 * g + 65], Rv=R_vsb[0],
                             add=(nx[:, kt * 128:(kt + 1) * 128], R_nx))
                    if kt >= NGRP * i:
                        d["mul"] = (cm_sb[:, kt - NGRP * i, :], R_cm)
                    tl.append(d)
                attend(g, 128, q_rhs, R_qT, tl, pO, R_pO)
                finish_branch(g, 128, 1, pO, R_pO, gates, False)
                tl = []
                for m in range(5):
                    d = dict(kT=kw_[gs, m * 128:(m + 1) * 128], Rk=R_kw, v=vw_[:, m, 65 * g:65 * g + 65], Rv=R_vw)
                    if kw5 < 4 or m in (0, 4):
                        d["mul"] = (wm_sb[:, kw5 * 5 + m, :], R_wm)
                    tl.append(d)
                attend(g, 128, q_rhs, R_qT, tl, pO, R_pO)
                finish_branch(g, 128, 2, pO, R_pO, gates, False)
            op("scalar", lambda e: e.activation(out=onsab, in_=onsa, func=AF.Copy), rd=[R_onsa], wr=[R_onsab])
            pt_, R_pt = ps[0]
            for c in range(4):
                op("tensor", lambda e, c=c: e.matmul(out=pt_[:, c * 128:(c + 1) * 128], lhsT=onsab[:, c * 128:(c + 1) * 128], rhs=ident_b,
                                                      start=True, stop=True), rd=[R_onsab, R_identb], wr=[R_pt])
            op("vector", lambda e: e.tensor_copy(out=onsaT.rearrange("p a b -> p (a b)"), in_=pt_[:, :]), rd=[R_pt], wr=[R_onsaT])
            pd, R_pd = ps[1]
            for gi in range(4):
                for wh in range(2):
                    op("tensor", lambda e, gi=gi, wh=wh: e.matmul(out=pd[:, gi * 128:(gi + 1) * 128], lhsT=u2_[:, wh, gi * 128:(gi + 1) * 128],
                                                                   rhs=wp_sb[:, (kind * 4 + gi) * 2 + wh, :], start=(wh == 0), stop=(wh == 1)),
                       rd=[R_u2, R_wp], wr=[R_pd])
            op("vector", lambda e: e.tensor_copy(out=dT_sb.rearrange("p a b -> p (a b)"), in_=pd[:, :]), rd=[R_pd], wr=[R_dT])
            pyp, R_pyp = ps[7]
            for gi in range(4):
                op("tensor", lambda e, gi=gi: e.matmul(out=pyp[:, gi * 128:(gi + 1) * 128], lhsT=pw[:, gi, :], rhs=dT_sb[:, gi, :], start=True, stop=True),
                   rd=[R_pw, R_dT], wr=[R_pyp])
            for gi in range(4):
                op("scalar", lambda e, gi=gi: e.activation(out=ypT[:, gi, :], in_=pyp[:, gi * 128:(gi + 1) * 128], func=AF.Identity, scale=pscl[:, gi:gi + 1]),
                   rd=[R_pyp, R_pscl], wr=[R_ypT])
            py = [ps[2], ps[3]]
            for half in range(2):
                pyh, R_pyh = py[half]
                for c in range(8):
                    lh = onsaT[:, c, :] if c < 4 else ypT[:, c - 4, :]
                    Rl = R_onsaT if c < 4 else R_ypT
                    op("tensor", lambda e, c=c, half=half, pyh=pyh, lh=lh: e.matmul(out=pyh[:, :], lhsT=lh, rhs=wout[:, c, half * 512:(half + 1) * 512],
                                                                                     start=(c == 0), stop=(c == 7)), rd=[Rl, R_wout], wr=[R_pyh])
            post_residual(w, py, 128, x_ap, R_x, *rows2["G2"])
            op("sync", lambda e, i=i, x_ap=x_ap: e.dma_start(out=x2s[i * 128:(i + 1) * 128, :], in_=x_ap), rd=[R_x], wr=[R_x2s], dma=R_x)
            if i % 8 == 7:
                new_epoch()
        new_epoch()
        sc.new_sem_epoch()
        ar.reset(base_mark)
        w = alloc_work()
        R_rowsS = Res("rowsS")
        rowsS = {nm: ar.alloc(nm, [D], F32, res=R_rowsS) for nm in ("A2", "B2", "G2")}
        R_p3a = Res("p3a")
        wq, R_wq = ar.alloc("wq", [8, 4, 128], BF16, res=R_p3a)
        for h in range(8):
            op("gpsimd", lambda e: e.dma_start(out=wq[:, :, h % 4, (h // 4) * 64:(h // 4) * 64 + 64], in_=w_in_v[:, :, h * 64:(h + 1) * 64]),
               wr=[R_wq], dma=R_wq)
        wkv, R_wkv = ar.alloc("wkv", [8, 792], BF16, res=R_p3a)
        op("gpsimd", lambda e: e.dma_start(out=wkv, in_=w_in_v[:, :, 512:1304]), wr=[R_wkv], dma=R_wkv)
        R_sc = Res("sconst")
        pt_sb = stack.enter_context(nc.sbuf_tensor("pt_sb", [NSEQ, 64], I32))
        op("sync", lambda e: e.dma_start(out=pt_sb[:], in_=pt_d[:, :]), wr=[R_sc], dma=R_sc)
        bonus_s, _ = ar.alloc("bonus_s", [128], F32)
        op("sync", lambda e: e.dma_start(out=bonus_s[0:4, :], in_=bonus_s_d[:, :]), wr=[R_sc], dma=R_sc)
        causal4, _ = ar.alloc("causal4", [4], BF16)
        op("gpsimd", lambda e: e.dma_start(out=causal4[0:4, :], in_=causal4_d[:, :]), wr=[R_sc], dma=R_sc)
        wms0, _ = ar.alloc("wms0", [4], BF16)
        op("gpsimd", lambda e: e.dma_start(out=wms0, in_=wms0_d[:, :]), wr=[R_sc], dma=R_sc)
        irep4, R_irep4 = ar.alloc("irep4", [4, 4], BF16)
        for h in range(4):
            op("vector", lambda e: e.tensor_copy(out=irep4[0:4, h, :], in_=ident_f[0:4, 0:4]), rd=[R_identf], wr=[R_irep4])
        PTb = [ar.alloc("PT%d" % i, [512], BF16) for i in range(4)]
        nx1, R_nx1 = ar.alloc("nexp", [S], BF16)
        nexp = [(nx1, R_nx1), (nx1, R_nx1)]
        rden, R_rden = ar.alloc("rden", [4], F32)
        coef, R_coef = ar.alloc("coef", [4], F32)
        tmpo, R_tmpo = ar.alloc("tmpo", [4, 64], F32)
        onsa, R_onsa = ar.alloc("onsa", [512], F32)
        onsab, R_onsab = ar.alloc("onsab", [512], BF16)
        tmpi, R_tmpi = ar.alloc("tmpi", [4, 128], F32)
        vals, R_vals = ar.alloc("vals", [128], F32)
        vals2, R_vals2 = ar.alloc("vals2", [128], F32)
        m8, R_m8 = ar.alloc("m8", [16], F32)
        selm, R_selm = ar.alloc("selm", [128], F32)
        negs, R_negs = ar.alloc("negs", [128], BF16)
        gates, R_gates = ar.alloc("gates4", [24], F32)
        x1g, R_x1g = ar.alloc("x1g", [D], F32)
        qs_sb, R_qs = ar.alloc("qs", [16, 4, 4], BF16)
        kTn, R_kTn = ar.alloc("kTn", [2, 64], BF16)
        zn, R_zn = ar.alloc("zn", [792], F32)
        vnew, R_vnew = ar.alloc("vnew", [2, 2, 65], BF16)
        op("vector", lambda e: e.memset(vnew, 1.0), wr=[R_vnew])
        kcT_q, R_kcTq = ar.alloc("kcTq", [256], BF16)
        vcT_q, R_vcTq = ar.alloc("vcTq", [256], BF16)
        vc_q, R_vcq = ar.alloc("vcq", [2, 2, 65], BF16)
        op("vector", lambda e: e.memset(vc_q, 1.0), wr=[R_vcq])
        wst = [ar.alloc("wst%d" % i, [4, 128], F32) for i in range(2)]
        wkb, R_wkb = ar.alloc("wkb", [4, 128], BF16)
        kwT_q, R_kwTq = ar.alloc("kwTq", [512], BF16)
        vwq, R_vwq = ar.alloc("vwq", [4, 2, 65], BF16)
        op("vector", lambda e: e.memset(vwq, 1.0), wr=[R_vwq])
        onsaT_s, R_onsaTs = ar.alloc("onsaTs", [4, 64], BF16)
        mark_loop = ar.mark()
        stg = [ar.alloc("stg%d" % i, [64, 128], F32) for i in range(1)]
        pgb, R_pgb = ar.alloc("pgb", [64, 128], BF16)
        ksT_q, R_ksTq = ar.alloc("ksTq", [S], BF16)
        vpb, R_vpb = ar.alloc("vpb", [64, 2, 65], BF16)
        op("vector", lambda e: e.memset(vpb, 1.0), wr=[R_vpb])
        stg_ctr = [0]

        R_stg8 = [Res("stg%d" % i) for i in range(16)]

        def load_pages(ci, s_glob):
            st_, R_st0 = stg[0]
            R_dm = R_stg8[ci * 4 + (s_glob % 4)]
            stg_ctr[0] += 1
            for pg in range(64):
                dyn_ctr[0] += 1
                nm = "pr%d" % dyn_ctr[0]
                out_ap = st_[:, pg, :]
                idx_ap = pt_sb[s_glob:s_glob + 1, pg:pg + 1]
                base = caches[ci]

                def fn(e, nm=nm, out_ap=out_ap, idx_ap=idx_ap, base=base):
                    r = e.alloc_register(nm)
                    e.reg_load(r, idx_ap)
                    v = e.snap(r, donate=True, min_val=0, max_val=NPHYS - 1)
                    e1 = v * 128
                    src = base[ds(e1, 128), :]
                    ins = e.dma_start(out=out_ap, in_=src)
                    vc = e.get_value_cache()
                    seen = set()
                    for ex in (e1, src.offset, src.offset * 4):
                        try:
                            al = vc.lookup(ex)
                        except Exception:
                            al = None
                        if al is not None and al.val.name not in seen and al.val.name != r.name:
                            seen.add(al.val.name)
                            e.free_register(al.val)
                    e.free_register(r)
                    return ins
                sc.op(("sync", "gpsimd")[pg % 2], fn, rd=[R_sc], wr=[R_st0], dma=R_dm, deferred=True, indep=(pg > 1))
            return st_, R_st0

        def cols4(ap2d_col):
            return bass.AP(tensor=ap2d_col.tensor, offset=ap2d_col.offset, ap=[list(ap2d_col.ap[0]), [16, 4]])

        cast_ctr = [0]

        def cast_op(out_ap, in_ap, rd, wr):
            cast_ctr[0] += 1
            if cast_ctr[0] % 2:
                op("scalar", lambda e: e.activation(out=out_ap, in_=in_ap, func=AF.Copy), rd=rd, wr=wr)
            else:
                op("vector", lambda e: e.tensor_copy(out=out_ap, in_=in_ap), rd=rd, wr=wr)

        R_p3b = Res("p3b")
        R_psclS = Res("psclS")
        for sg in range(NSG):
            load_row(*rowsS["A2"], 1 + sg, 4, 2, "A", w.tmpg, w.R_tmpg)
            load_row(*rowsS["B2"], 1 + sg, 3, 2, "B", w.tmpg, w.R_tmpg)
            load_row(*rowsS["G2"], 1 + sg, 5, 3, "G1", w.tmpg, w.R_tmpg)
            op("sync", lambda e: e.dma_start(out=x1g[0:64, :], in_=x1s[S + 64 * sg:S + 64 * sg + 64, :]), rd=[R_x1s], wr=[R_x1g], dma=R_x1g)
            norm_mod(w, x1g[0:64, :], R_x1g, 64, *rowsS["A2"], *rowsS["B2"], 0)
            pq, R_pq = ps[7]
            for c in range(4):
                for k in range(8):
                    op("tensor", lambda e: e.matmul(out=pq[:, c * 64:(c + 1) * 64], lhsT=wq[:, k, c, :], rhs=w.hT[:, k, 0:64],
                                                     start=(k == 0), stop=(k == 7)), rd=[R_wq, w.R_hT], wr=[R_pq])
            op("scalar", lambda e: e.activation(out=qs_sb, in_=pq[:, 0:256].rearrange("p (c t s) -> p s c t", c=4, t=4), func=AF.Copy),
               rd=[R_pq], wr=[R_qs])
            pkn, R_pkn = ps[6]
            for n, c_lo in enumerate((256, 512)):
                for k in range(8):
                    op("tensor", lambda e: e.matmul(out=pkn[:, n * 64:(n + 1) * 64], lhsT=wkv[:, k, c_lo:c_lo + 128], rhs=w.hT[:, k, 0:64],
                                                     start=(k == 0), stop=(k == 7)), rd=[R_wkv, w.R_hT], wr=[R_pkn])
            op("vector", lambda e: e.tensor_copy(out=kTn.rearrange("p a b -> p (a b)"), in_=pkn[:, 0:128]), rd=[R_pkn], wr=[R_kTn])
            po, R_po = ps[0]
            po_v = po[:, 0:256].rearrange("p (c t) -> p c t", c=4)
            for sl in range(16):
                s_glob = 16 * sg + sl
                pz, R_pz = ps[5]
                pz2, R_pz2 = ps[4]
                for k in range(8):
                    op("tensor", lambda e: e.matmul(out=pz[0:4, :], lhsT=cols4(w.hT[:, k, sl:sl + 1]), rhs=wkv[:, k, 0:512],
                                                     start=(k == 0), stop=(k == 7)), rd=[w.R_hT, R_wkv], wr=[R_pz])
                for k in range(8):
                    op("tensor", lambda e: e.matmul(out=pz2[0:4, 0:280], lhsT=cols4(w.hT[:, k, sl:sl + 1]), rhs=wkv[:, k, 512:792],
                                                     start=(k == 0), stop=(k == 7)), rd=[w.R_hT, R_wkv], wr=[R_pz2])
                op("scalar", lambda e: e.activation(out=zn[0:4, 0:512], in_=pz[0:4, :], func=AF.Copy), rd=[R_pz], wr=[R_zn])
                op("scalar", lambda e: e.activation(out=zn[0:4, 512:768], in_=pz2[0:4, 0:256], func=AF.Copy), rd=[R_pz2], wr=[R_zn])
                op("scalar", lambda e: e.activation(out=gates[0:4, :], in_=pz2[0:4, 256:280], func=AF.Sigmoid), rd=[R_pz2], wr=[R_gates])
                op("vector", lambda e: e.tensor_copy(out=vnew[0:4, 0, :, 0:64], in_=zn[0:4, 384:512].rearrange("p (g d) -> p g d", g=2)),
                   rd=[R_zn], wr=[R_vnew])
                op("vector", lambda e: e.tensor_copy(out=vnew[0:4, 1, :, 0:64], in_=zn[0:4, 640:768].rearrange("p (g d) -> p g d", g=2)),
                   rd=[R_zn], wr=[R_vnew])
                for ci, (dstT, R_dT_, W4) in enumerate(((kcT_q, R_kcTq, W4k), (vcT_q, R_vcTq, W4v))):
                    st_, R_st = load_pages(ci, s_glob)
                    cast_op(pgb, st_, [R_st], [R_pgb])
                    pc_, R_pc_ = ps[6]
                    for pg in range(64):
                        op("tensor", lambda e: e.matmul(out=pc_[:, 4 * pg:4 * pg + 4], lhsT=pgb[:, pg, :], rhs=W4, start=True, stop=True),
                           rd=[R_pgb, R_W4k, R_W4v], wr=[R_pc_])
                    op("vector", lambda e: e.tensor_copy(out=dstT, in_=pc_[:, 0:256]), rd=[R_pc_], wr=[R_dT_])
                for jt in range(2):
                    pv_, R_pv = ps[7]
                    op("tensor", lambda e: e.matmul(out=pv_[:, jt * 128:(jt + 1) * 128], lhsT=vcT_q[:, jt * 128:(jt + 1) * 128], rhs=ident_b,
                                                     start=True, stop=True), rd=[R_vcTq, R_identb], wr=[R_pv])
                op("vector", lambda e: e.tensor_copy(out=vc_q[:, :, :, 0:64], in_=ps[7][0][:, 0:256].rearrange("p (j g d) -> p j g d", j=2, g=2)),
                   rd=[ps[7][1]], wr=[R_vcq])
                st_, R_st = load_pages(2, s_glob)
                cast_op(pgb, st_, [R_st], [R_pgb])
                for q4 in range(16):
                    ptq, R_ptq = ps[1] if q4 % 2 == 0 else ps[7]
                    for pp in range(4):
                        pg = 4 * q4 + pp
                        op("tensor", lambda e: e.matmul(out=ptq[:, pp * 128:(pp + 1) * 128], lhsT=pgb[:, pg, :], rhs=ident_b, start=True, stop=True),
                           rd=[R_pgb, R_identb], wr=[R_ptq])
                    if q4 % 2 == 0:
                        op("scalar", lambda e: e.activation(out=ksT_q[:, q4 * 512:(q4 + 1) * 512], in_=ptq[:, :], func=AF.Copy), rd=[R_ptq], wr=[R_ksTq])
                    else:
                        op("vector", lambda e: e.tensor_copy(out=ksT_q[:, q4 * 512:(q4 + 1) * 512], in_=ptq[:, :]), rd=[R_ptq], wr=[R_ksTq])
                st_, R_st = load_pages(3, s_glob)
                cast_op(vpb[:, :, :, 0:64], st_.rearrange("p a (g d) -> p a g d", g=2), [R_st], [R_vpb])
                for n, st in enumerate((st_wk, st_wv)):
                    ws_, R_ws = wst[n]
                    op("sync", lambda e: e.dma_start(out=ws_, in_=st[s_glob].rearrange("(m p) c -> p m c", p=128)), wr=[R_ws], dma=R_ws)
                op("vector", lambda e: e.tensor_copy(out=wkb, in_=wst[0][0]), rd=[wst[0][1]], wr=[R_wkb])
                pw_, R_pw_ = ps[1]
                for m in range(4):
                    op("tensor", lambda e: e.matmul(out=pw_[:, m * 128:(m + 1) * 128], lhsT=wkb[:, m, :], rhs=ident_b, start=True, stop=True),
                       rd=[R_wkb, R_identb], wr=[R_pw_])
                op("scalar", lambda e: e.activation(out=kwT_q, in_=pw_[:, :], func=AF.Copy), rd=[R_pw_], wr=[R_kwTq])
                op("vector", lambda e: e.tensor_copy(out=vwq[:, :, :, 0:64], in_=wst[1][0].rearrange("p a (g d) -> p a g d", g=2)), rd=[wst[1][1]], wr=[R_vwq])
                for g in range(2):
                    gs = slice(64 * g, 64 * g + 64)
                    q_rhs = qs_sb[gs, sl, :, :].rearrange("p a b -> p (a b)")
                    pO, R_pO = ps[4 + g]
                    pI, R_pI = ps[6]
                    ir4 = (irep4[0:4, :, :].rearrange("p a b -> p (a b)"), R_irep4)
                    tl = [dict(kT=kcT_q[gs, jt * 128:(jt + 1) * 128], Rk=R_kcTq, v=vc_q[:, jt, g, :], Rv=R_vcq, jt=jt) for jt in range(2)]
                    attend(g, 4, q_rhs, R_qs, tl, pO, R_pO, pI, R_pI, irep=ir4)
                    finish_branch(g, 4, 0, pO, R_pO, gates[0:4, :], True)
                    select_blocks(g, 4, pI, R_pI, bonus_s[0:4, :], R_sc, 15, 128)
                    nx, R_nx = nexp[g]
                    tl = [dict(kT=ksT_q[gs, pg * 128:(pg + 1) * 128], Rk=R_ksTq, v=vpb[:, pg, g, :], Rv=R_vpb,
                               add=(nx[0:4, pg * 128:(pg + 1) * 128], R_nx)) for pg in range(64)]
                    tl.append(dict(kT=cols4(kTn[gs, 0, sl:sl + 1]), Rk=R_kTn, v=vnew[0:4, 0, g, :], Rv=R_vnew, mul=(causal4[0:4, :], R_sc), nk=4))
                    attend(g, 4, q_rhs, R_qs, tl, pO, R_pO, irep=ir4)
                    finish_branch(g, 4, 1, pO, R_pO, gates[0:4, :], False)
                    tl = []
                    for m in range(4):
                        d = dict(kT=kwT_q[gs, m * 128:(m + 1) * 128], Rk=R_kwTq, v=vwq[:, m, g, :], Rv=R_vwq)
                        if m == 0:
                            d["mul"] = (wms0, R_sc)
                        tl.append(d)
                    tl.append(dict(kT=cols4(kTn[gs, 1, sl:sl + 1]), Rk=R_kTn, v=vnew[0:4, 1, g, :], Rv=R_vnew, mul=(causal4[0:4, :], R_sc), nk=4))
                    attend(g, 4, q_rhs, R_qs, tl, pO, R_pO, irep=ir4)
                    finish_branch(g, 4, 2, pO, R_pO, gates[0:4, :], False)
                op("scalar", lambda e: e.activation(out=onsab[0:4, :], in_=onsa[0:4, :], func=AF.Copy), rd=[R_onsa], wr=[R_onsab])
                for c in range(4):
                    op("tensor", lambda e: e.matmul(out=cols4(po_v[:, c, sl:sl + 1]), lhsT=onsab[0:4, c * 128:(c + 1) * 128], rhs=ident_b[0:4, 0:4],
                                                     start=True, stop=True), rd=[R_onsab, R_identb], wr=[R_po])
                if sl % 4 == 3:
                    new_epoch()
            op("vector", lambda e: e.tensor_copy(out=onsaT_s.rearrange("p a b -> p (a b)"), in_=po[:, 0:256]), rd=[R_po], wr=[R_onsaTs])
            new_epoch()
            ar.reset(mark_loop)
            wu, R_wu = ar.alloc("wu", [8, 512], BF16, res=R_p3b)
            op("gpsimd", lambda e: e.dma_start(out=wu, in_=w_in_v[:, :, 1304:1816]), wr=[R_wu], dma=R_wu)
            wout, R_wout = ar.alloc("wout", [8, D], BF16, res=R_p3b)
            op("gpsimd", lambda e: e.dma_start(out=wout, in_=w_out.rearrange("(k p) n -> p k n", p=128)), wr=[R_wout], dma=R_wout)
            pw, R_pw = ar.alloc("pw", [4, 128], BF16, res=R_p3b)
            op("gpsimd", lambda e: e.dma_start(out=pw, in_=pool_w.rearrange("g c d -> c g d")), wr=[R_pw], dma=R_pw)
            wsab, R_wsab = ar.alloc("wsab", [2, 4, 64], BF16, res=R_p3b)
            op("gpsimd", lambda e: e.dma_start(out=wsab[0:120, 0, :, :], in_=wsa_d[:, :, :]), wr=[R_wsab], dma=R_wsab)
            op("gpsimd", lambda e: e.dma_start(out=wsab[0:120, 1, :, :], in_=wsb_d[:, :, :]), wr=[R_wsab], dma=R_wsab)
            wnb, R_wnb = ar.alloc("wnb", [4, 64], BF16, res=R_p3b)
            op("gpsimd", lambda e: e.dma_start(out=wnb[0:64, :, :], in_=wn_d[:, :, :]), wr=[R_wnb], dma=R_wnb)
            pscl, R_pscl = ar.alloc("pscl", [4], F32, res=R_psclS)
            for gi in range(4):
                op("sync", lambda e: e.dma_start(out=pscl[:, gi:gi + 1], in_=pool_scale[0:1, gi * 128:(gi + 1) * 128].rearrange("a d -> d a")),
                   wr=[R_pscl], dma=R_pscl)
            stpb, R_stpb = ar.alloc("stpb", [2, 512], BF16, res=R_p3b)
            for hf in range(2):
                op("gpsimd", lambda e: e.dma_start(out=stpb[0:120, hf, :], in_=st_pool[16 * sg + 8 * hf:16 * sg + 8 * hf + 8].rearrange("s r c -> (s r) c")),
                   wr=[R_stpb], dma=R_stpb)
            un, R_un = ar.alloc("un", [512], BF16)
            dTs, R_dTs = ar.alloc("dTs", [4, 64], BF16)
            ypTs, R_ypTs = ar.alloc("ypTs", [4, 64], BF16)
            pu_, R_pu_ = ps[5]
            for k in range(8):
                op("tensor", lambda e: e.matmul(out=pu_[0:64, :], lhsT=w.hT[:, k, 0:64], rhs=wu[:, k, :], start=(k == 0), stop=(k == 7)),
                   rd=[w.R_hT, R_wu], wr=[R_pu_])
            op("vector", lambda e: e.tensor_copy(out=un[0:64, :], in_=pu_[0:64, :]), rd=[R_pu_], wr=[R_un])
            pd, R_pd = ps[1]
            for gi in range(4):
                cs_ = slice(gi * 128, (gi + 1) * 128)
                op("tensor", lambda e: e.matmul(out=pd[:, gi * 64:(gi + 1) * 64], lhsT=stpb[0:120, 0, cs_], rhs=wsab[0:120, 0, gi, :], start=True, stop=False),
                   rd=[R_stpb, R_wsab], wr=[R_pd])
                op("tensor", lambda e: e.matmul(out=pd[:, gi * 64:(gi + 1) * 64], lhsT=stpb[0:120, 1, cs_], rhs=wsab[0:120, 1, gi, :], start=False, stop=False),
                   rd=[R_stpb, R_wsab], wr=[R_pd])
                op("tensor", lambda e: e.matmul(out=pd[:, gi * 64:(gi + 1) * 64], lhsT=un[0:64, cs_], rhs=wnb[0:64, gi, :], start=False, stop=True),
                   rd=[R_un, R_wnb], wr=[R_pd])
            op("vector", lambda e: e.tensor_copy(out=dTs.rearrange("p a b -> p (a b)"), in_=pd[:, 0:256]), rd=[R_pd], wr=[R_dTs])
            pyp, R_pyp = ps[7]
            for gi in range(4):
                op("tensor", lambda e: e.matmul(out=pyp[:, gi * 64:(gi + 1) * 64], lhsT=pw[:, gi, :], rhs=dTs[:, gi, :], start=True, stop=True),
                   rd=[R_pw, R_dTs], wr=[R_pyp])
            for gi in range(4):
                op("scalar", lambda e: e.activation(out=ypTs[:, gi, :], in_=pyp[:, gi * 64:(gi + 1) * 64], func=AF.Identity, scale=pscl[:, gi:gi + 1]),
                   rd=[R_pyp, R_pscl], wr=[R_ypTs])
            py = [ps[2], ps[3]]
            for half in range(2):
                pyh, R_pyh = py[half]
                for c in range(8):
                    lh = onsaT_s[:, c, :] if c < 4 else ypTs[:, c - 4, :]
                    Rl = R_onsaTs if c < 4 else R_ypTs
                    op("tensor", lambda e: e.matmul(out=pyh[0:64, :], lhsT=lh, rhs=wout[:, c, half * 512:(half + 1) * 512],
                                                     start=(c == 0), stop=(c == 7)), rd=[Rl, R_wout], wr=[R_pyh])
            post_residual(w, py, 64, x1g, R_x1g, *rowsS["G2"])
            op("sync", lambda e: e.dma_start(out=x2s[NOWN * 128 + 64 * sg:NOWN * 128 + 64 * sg + 64, :], in_=x1g[0:64, :]), rd=[R_x1g], wr=[R_x2s], dma=R_x1g)
            new_epoch()
            ar.reset(mark_loop)
            if sg + 1 < NSG:
                stg = [ar.alloc("stg%d" % i, [64, 128], F32, res=stg[i][1]) for i in range(1)]
                pgb, _ = ar.alloc("pgb", [64, 128], BF16, res=R_pgb)
                ksT_q, _ = ar.alloc("ksTq", [S], BF16, res=R_ksTq)
                vpb, _ = ar.alloc("vpb", [64, 2, 65], BF16, res=R_vpb)
                op("vector", lambda e: e.memset(vpb, 1.0), wr=[R_vpb])
        new_epoch()
        ar.reset(base_mark)

        wo2, R_wo2 = ar.alloc("wo2", [NF, D], BF16)
        op("gpsimd", lambda e: e.dma_start(out=wo2, in_=f2wo.rearrange("(f p) n -> p f n", p=128)), wr=[R_wo2], dma=R_wo2)
        w = alloc_work()
        f = alloc_ffn()
        R_rows3 = Res("rows3")
        rows3 = {nm: ar.alloc(nm, [D], F32, res=R_rows3) for nm in ("A3", "B3", "G3")}

        def load_rows3(kind):
            load_row(*rows3["A3"], kind, 7, 4, "A", w.tmpg, w.R_tmpg)
            load_row(*rows3["B3"], kind, 6, 4, "B", w.tmpg, w.R_tmpg)
            load_row(*rows3["G3"], kind, 8, 5, "G5", w.tmpg, w.R_tmpg)

        def x2load(row0, nt):
            return lambda x_ap, R_x: op("sync", lambda e: e.dma_start(out=x_ap[0:nt, :], in_=x2s[row0:row0 + nt, :]), rd=[R_x2s], wr=[R_x], dma=R_x)

        def post3(ti, x_ap, R_x, nt, c0, tag):
            op("sync", lambda e: e.dma_start(out=o_y[tag:tag + nt, :], in_=x_ap[0:nt, :]), rd=[R_x], wr=[R_oy], dma=R_x)

        load_rows3(0)
        for p in range(NOWN // 4):
            tiles = [(x2load(i * 128, 128), 128, i * 128) for i in range(4 * p, 4 * p + 4)]
            ffn_pass(w, f, tiles, f2wi_v, wo2, R_wo2, rows3["A3"], rows3["B3"], rows3["G3"], post3)
        for sg in range(NSG):
            new_epoch()
            load_rows3(1 + sg)
            ffn_pass(w, f, [(x2load(NOWN * 128 + 64 * sg, 64), 64, NOWN * 128 + 64 * sg)], f2wi_v, wo2, R_wo2, rows3["A3"], rows3["B3"], rows3["G3"], post3)
        sc.emit()
    return nc


_NC = {}


def _tables(r, ngrp, nown):
    key = np.arange(128)[:, None]
    q = np.arange(128)[None, :]
    cm = np.zeros((ngrp, 128, 128), np.float32)
    for dk in range(ngrp):
        if dk < r:
            cm[dk] = 1.0
        elif dk == r:
            cm[dk] = (key <= q)
    bonus = np.zeros((nown, 128, 128), np.float32)
    cmpmask = np.zeros((nown, 2, 128, 128), np.float32)
    blk = np.arange(128)[None, :]
    for i in range(nown):
        j = ngrp * i + r
        t = j * 128 + np.arange(128)[:, None]
        cur = t // 64
        forced = (blk == 0) | (blk == cur) | (blk == cur - 1)
        start_ok = blk * 64 <= t
        bonus[i] = np.where(forced, 1e4, np.where(start_ok, 0.0, -1e30))
        for jt in range(2):
            jc = jt * 128 + np.arange(128)[:, None]
            tq = j * 128 + np.arange(128)[None, :]
            cmpmask[i, jt] = ((jc + 1) * 32 - 1 <= tq)
    wm = np.zeros((5, 5, 128, 128), np.float32)
    for kw in range(5):
        j = kw + r if kw < 4 else 4 + r
        for m in range(5):
            if j - 4 + m < 0:
                continue
            if m == 0:
                wm[kw, m] = (key > q)
            elif m == 4:
                wm[kw, m] = (key <= q)
            else:
                wm[kw, m] = 1.0
    wp = np.zeros((2, 4, 2, 128, 128), np.float32)
    for kind in range(2):
        j = r if kind == 0 else ngrp + r
        for gi, wdw in enumerate((2, 4, 8, 16)):
            tpos = j * 128 + np.arange(128)[None, :]
            cnt = np.minimum(wdw, tpos + 1).astype(np.float32)
            for wh in range(2):
                spos = (j - 1 + wh) * 128 + np.arange(128)[:, None]
                inwin = (spos > tpos - wdw) & (spos <= tpos)
                wp[kind, gi, wh] = inwin / cnt - (spos == tpos)
    return cm, bonus, cmpmask, wm, wp


def _sample_tables():
    bonus_s = np.zeros((4, 128), np.float32)
    bonus_s[:, 0] = 1e4
    bonus_s[:, 127] = 1e4
    n = np.arange(4)[:, None]
    t = np.arange(4)[None, :]
    causal4 = (n <= t).astype(np.float32)
    wms0 = (np.arange(128)[:, None] > t).astype(np.float32)
    wsa = np.zeros((120, 4, 64), np.float32)
    wsb = np.zeros((120, 4, 64), np.float32)
    wn = np.zeros((64, 4, 64), np.float32)
    for gi, wdw in enumerate((2, 4, 8, 16)):
        for tt in range(4):
            for s in range(16):
                col = tt * 16 + s
                for rr in range(15):
                    if rr > 15 + tt - wdw:
                        if s < 8:
                            wsa[s * 15 + rr, gi, col] = 1.0 / wdw
                        else:
                            wsb[(s - 8) * 15 + rr, gi, col] = 1.0 / wdw
                for t2 in range(4):
                    val = (1.0 / wdw if (t2 <= tt and t2 > tt - wdw) else 0.0) - (1.0 if t2 == tt else 0.0)
                    wn[t2 * 16 + s, gi, col] = val
    return bonus_s, causal4, wms0, wsa, wsb, wn


def kernel(**inputs):
    f = lambda k: np.asarray(inputs[k])
    x_prompt, x_sample = f("x_prompt"), f("x_sample")
    DB = x_sample.shape[0]
    nseq = DB // NCORES
    nsg = nseq // 16
    nphys = f("cache_cmp_k").shape[1]
    key = (nseq, nphys)
    if key not in _NC:
        _NC[key] = build_nc(NSEQ=nseq, NPHYS=nphys)
    nc = _NC[key]
    ident = np.eye(128, dtype=np.float32)
    pair = (np.arange(128)[:, None] // 2 == np.arange(64)[None, :]).astype(np.float32)
    bmask = (np.arange(128)[:, None] // 32 == np.arange(4)[None, :]).astype(np.float32)
    bonus_s, causal4, wms0, wsa, wsb, wn = _sample_tables()
    caches = [np.ascontiguousarray(f(k)[0]).reshape(nphys * 128, 128) for k in ("cache_cmp_k", "cache_cmp_v", "cache_sel_k", "cache_sel_v")]
    in_maps = []
    for c in range(NCORES):
        b, r = c // NGRP, c % NGRP
        cm, bonus, cmpmask, wm, wp = _tables(r, NGRP, NOWN)
        sl = slice(nseq * c, nseq * (c + 1))
        xs_c = np.ascontiguousarray(x_sample[sl].reshape(nsg, 16, 4, D).transpose(0, 2, 1, 3).reshape(4 * nseq, D))
        c17 = np.concatenate([f("c_prompt")[b:b + 1], f("c_sample")[sl]], axis=0)
        rinfo = np.zeros((1, 8), np.int32)
        rinfo[0, 0] = r
        m = {
            "xp": np.ascontiguousarray(x_prompt[b]), "xs": xs_c, "c17": np.ascontiguousarray(c17),
            "w_ada": f("w_ada")[0], "b_ada": f("b_ada")[0][None, :], "gains": f("norm_gains")[0],
            "f1wi": f("ffn1_wi")[0], "f1wo": f("ffn1_wo")[0], "f2wi": f("ffn2_wi")[0], "f2wo": f("ffn2_wo")[0],
            "w_in": f("w_in")[0], "w_out": f("w_out")[0], "cmpw": f("cmp_w")[0],
            "pool_w": f("pool_w")[0], "pool_scale": f("pool_scale")[0][None, :],
            "ident": ident, "pair": pair, "bmask": bmask, "rinfo": rinfo,
            "cm": cm, "bonus": bonus, "cmpmask": cmpmask, "wm": wm, "wp": wp,
            "st_wk": np.ascontiguousarray(f("state_win_k")[0, sl].reshape(nseq, 512, 128)),
            "st_wv": np.ascontiguousarray(f("state_win_v")[0, sl].reshape(nseq, 512, 128)),
            "st_pool": np.ascontiguousarray(f("state_pool")[0, sl]),
            "pt": np.ascontiguousarray(f("page_table")[sl]).astype(np.int32),
            "c_cmp_k": caches[0], "c_cmp_v": caches[1], "c_sel_k": caches[2], "c_sel_v": caches[3],
            "bonus_s": bonus_s, "causal4": causal4, "wms0": wms0, "wsa": wsa, "wsb": wsb, "wn": wn,
        }
        in_maps.append(m)
    res = run_bass_kernel_spmd(nc, in_maps, core_ids=list(range(NCORES)))
    kernel.last = res
    y_p = np.zeros((2, S, D), np.float32)
    y_s = np.zeros((DB, 4, D), np.float32)
    kv_p = [np.zeros((1, 2, S, 2, 64), np.float32) for _ in range(4)]
    kv_s = [np.zeros((1, DB, 4, 2, 64), np.float32) for _ in range(4)]
    p_win = [np.zeros((1, 2, 512, 2, 64), np.float32) for _ in range(2)]
    p_pool = np.zeros((1, 2, 15, 512), np.float32)
    s_win = [np.zeros((1, DB, 512, 2, 64), np.float32) for _ in range(2)]
    s_pool = np.zeros((1, DB, 15, 512), np.float32)
    unperm = lambda a, last: a.reshape((nsg, 4, 16) + last).transpose(0, 2, 1, *range(3, 3 + len(last))).reshape((nseq, 4) + last)
    for c in range(NCORES):
        b, r = c // NGRP, c % NGRP
        sl = slice(nseq * c, nseq * (c + 1))
        rr = res.results[c]
        okv = np.asarray(rr["o_kv"])
        oy = np.asarray(rr["o_y"])
        y_p[b].reshape(NBLK, 128, D)[r::NGRP] = oy[:NOWN * 128].reshape(NOWN, 128, D)
        y_s[sl] = unperm(oy[NOWN * 128:], (D,))
        for n in range(4):
            kv_p[n][0, b].reshape(NBLK, 128, 2, 64)[r::NGRP] = okv[n, :S].reshape(NBLK, 128, 2, 64)[r::NGRP]
            kv_s[n][0, sl] = unperm(okv[n, S:], (2, 64))
        if r == NGRP - 1:
            for n in range(2):
                p_win[n][0, b] = okv[4 + n, S - 512:S].reshape(512, 2, 64)
            p_pool[0, b] = np.asarray(rr["o_u"])[S - 15:S]
        sw = np.asarray(rr["o_swin"])
        for n in range(2):
            s_win[n][0, sl] = sw[n].reshape(nseq, 512, 2, 64)
        s_pool[0, sl] = np.asarray(rr["o_spool"])
    return (y_p, y_s, *kv_p, *p_win, p_pool, *kv_s, *s_win, s_pool)
```

```python
from contextlib import ExitStack
import numpy as np
import ml_dtypes
import concourse.bass as bass
import concourse.mybir as mybir
from concourse.bass import ds
from concourse.bass_utils import run_bass_kernel_spmd

F32 = mybir.dt.float32
BF16 = mybir.dt.bfloat16
I32 = mybir.dt.int32
AF = mybir.ActivationFunctionType
ALU = mybir.AluOpType

NGRP = 1
NCORES = 2 * NGRP
D = 1024
S = 8192
NBLK = 64
NOWN = 64 // NGRP
NSEQ = 16
NTS = 64
DFF = 2816
NF = 22
INW = 1816
EPS = 1e-6
ENG = ("sync", "scalar", "vector", "gpsimd", "tensor")


class Res:
    def __init__(self, name):
        self.name = name
        self.lw = None
        self.rd = []
        self.sem = None
        self.cnt = 0


class _Rec:
    def __init__(self):
        self.call = None

    def __getattr__(self, name):
        def m(*a, **k):
            self.call = (name, a, k)
            return self
        return m


class Sched:
    def __init__(self, nc, stack):
        self.nc = nc
        self.stack = stack
        self.q = {e: [] for e in ENG}
        self.cnt = {e: 0 for e in ENG}
        self.waited = {e: {} for e in ENG}
        self.ep = 0
        self.esem = {(e, 0): stack.enter_context(nc.semaphore("es_" + e)) for e in ENG}
        self.miles = {(e, 0): set() for e in ENG}
        self.dres = []

    def new_sem_epoch(self):
        self.barrier()
        self.ep += 1
        for e in ENG:
            self.esem[(e, self.ep)] = self.stack.enter_context(self.nc.semaphore("es%d_%s" % (self.ep, e)))
            self.miles[(e, self.ep)] = set()
            self.cnt[e] = 0
            self.waited[e] = {k: v for k, v in self.waited[e].items() if k[0] == "D"}

    def _wait(self, eng, tok):
        if tok[0] == "E":
            if tok[3] != self.ep:
                return
            if tok[1] == eng and eng == "tensor":
                return
            key = ("E", tok[1]); val = tok[2]
        else:
            key = ("D", id(tok[1])); val = tok[2]
        if self.waited[eng].get(key, 0) >= val:
            return
        self.waited[eng][key] = val
        if tok[0] == "E":
            self.miles[(tok[1], self.ep)].add(val)
            self.q[eng].append(("we", (tok[1], self.ep), val))
        else:
            self.q[eng].append(("wd", tok[1].sem, val))

    def op(self, eng, fn, rd=(), wr=(), dma=None, deferred=False, indep=False):
        if not deferred:
            rec = _Rec()
            fn(rec)
            name, a, k = rec.call
            fn = (lambda e, name=name, a=a, k=k: getattr(e, name)(*a, **k))
        deps = []
        for b in rd:
            if b.lw is not None:
                deps.append(b.lw)
        for b in wr:
            if b.lw is not None and not indep:
                deps.append(b.lw)
            deps.extend(b.rd)
        for d in deps:
            self._wait(eng, d)
        if dma is None:
            self.cnt[eng] += 1
            tok = ("E", eng, self.cnt[eng], self.ep)
            self.q[eng].append(("oe", fn, self.cnt[eng], self.ep))
        else:
            if dma.sem is None:
                dma.sem = self.stack.enter_context(self.nc.semaphore("ds_" + dma.name))
                self.dres.append(dma)
            dma.cnt += 16
            assert dma.cnt < 30000, ("dma semaphore would overflow", dma.name)
            tok = ("D", dma, dma.cnt)
            self.q[eng].append(("od", fn, dma.sem))
        for b in rd:
            b.rd.append(tok)
        for b in wr:
            b.lw = tok
            b.rd = []
        return tok

    def barrier(self):
        for e in ENG:
            for e2 in ENG:
                if self.cnt[e2] > 0:
                    self._wait(e, ("E", e2, self.cnt[e2], self.ep))
            for d in self.dres:
                if d.cnt > 0:
                    self._wait(e, ("D", d, d.cnt))

    def emit(self):
        nc = self.nc
        self.barrier()
        rank = {}
        for key_ in self.miles:
            ks = sorted(self.miles[key_])
            rank[key_] = {k: i + 1 for i, k in enumerate(ks)}
            assert len(ks) < 30000, ("engine semaphore would overflow", key_, len(ks))
        self.n_miles = {k: len(v) for k, v in rank.items()}
        with nc.Block() as block:
            def mk(ename):
                def run(eng):
                    for it in self.q[ename]:
                        if it[0] == "we":
                            eng.wait_ge(self.esem[it[1]], rank[it[1]][it[2]])
                        elif it[0] == "wd":
                            eng.wait_ge(it[1], it[2])
                        elif it[0] == "oe":
                            ins = it[1](eng)
                            if it[2] in rank[(ename, it[3])]:
                                ins.then_inc(self.esem[(ename, it[3])], 1)
                        else:
                            ins = it[1](eng)
                            ins.then_inc(it[2], 16)
                return run
            block.sync(mk("sync"))
            block.scalar(mk("scalar"))
            block.vector(mk("vector"))
            block.gpsimd(mk("gpsimd"))
            block.tensor(mk("tensor"))


class Arena:
    def __init__(self, nc, stack, nbytes):
        self.t = stack.enter_context(nc.sbuf_tensor("arena", [128, nbytes // 4], F32))
        self.nbytes = nbytes
        self.top = 0
        self.n = 0

    def mark(self):
        return self.top

    def reset(self, m):
        self.top = m

    def alloc(self, name, free_shape, dtype, res=None):
        esz = 2 if dtype == BF16 else 4
        n = int(np.prod(free_shape))
        nb = (n * esz + 31) // 32 * 32
        assert self.top + nb <= self.nbytes, (name, self.top, nb, self.nbytes)
        off = self.top
        self.top += nb
        v = self.t[:, off // 4:(off + nb) // 4]
        if dtype != F32:
            v = v.bitcast(dtype)
        v = v[:, 0:n]
        if len(free_shape) == 2:
            v = v.rearrange("p (a b) -> p a b", a=free_shape[0])
        elif len(free_shape) == 3:
            v = v.rearrange("p (a b c) -> p a b c", a=free_shape[0], b=free_shape[1])
        self.n += 1
        return v, (res if res is not None else Res(name + str(self.n)))


SCALE = 0.125
NEGM = -30000.0


def bc_mid(a, n):
    return bass.AP(tensor=a.tensor, offset=a.offset, ap=[list(a.ap[0]), [0, n], list(a.ap[1])])


def bc_last(a, n):
    return bass.AP(tensor=a.tensor, offset=a.offset, ap=[list(a.ap[0]), list(a.ap[1]), [0, n]])


def build_nc(NSEQ=64, NPHYS=10240):
    NSG = NSEQ // 16
    NTS = 4 * NSEQ
    XROWS = S + NTS
    OROWS = NOWN * 128 + NTS
    M17 = 1 + NSEQ
    nc = bass.Bass("TRN2", target_bir_lowering=False)
    stack = ExitStack()
    dt_in = lambda n, s, d=F32: nc.dram_tensor(n, s, d, kind="ExternalInput").ap()
    dt_out = lambda n, s, d=F32: nc.dram_tensor(n, s, d, kind="ExternalOutput").ap()
    dt_int = lambda n, s, d=F32: nc.dram_tensor(n, s, d, kind="Internal").ap()
    xp = dt_in("xp", [S, D])
    xs = dt_in("xs", [NTS, D])
    c17 = dt_in("c17", [M17, D])
    w_ada = dt_in("w_ada", [D, 9 * D])
    b_ada = dt_in("b_ada", [1, 9 * D])
    gains = dt_in("gains", [6, D])
    f1wi = dt_in("f1wi", [D, 2 * DFF])
    f1wo = dt_in("f1wo", [DFF, D])
    f2wi = dt_in("f2wi", [D, 2 * DFF])
    f2wo = dt_in("f2wo", [DFF, D])
    w_in = dt_in("w_in", [D, INW])
    w_out = dt_in("w_out", [D, D])
    cmpw = dt_in("cmpw", [2, 32])
    pool_w = dt_in("pool_w", [4, 128, 128])
    pool_scale = dt_in("pool_scale", [1, 512])
    ident_d = dt_in("ident", [128, 128])
    pair_d = dt_in("pair", [128, 64])
    bmask_d = dt_in("bmask", [128, 4])
    rinfo_d = dt_in("rinfo", [1, 8], I32)
    cm_d = dt_in("cm", [NGRP, 128, 128])
    bonus_d = dt_in("bonus", [NOWN, 128, 128])
    cmpmask_d = dt_in("cmpmask", [NOWN, 2, 128, 128])
    wm_d = dt_in("wm", [5, 5, 128, 128])
    wp_d = dt_in("wp", [2, 4, 2, 128, 128])
    st_wk = dt_in("st_wk", [NSEQ, 512, 128])
    st_wv = dt_in("st_wv", [NSEQ, 512, 128])
    st_pool = dt_in("st_pool", [NSEQ, 15, 512])
    pt_d = dt_in("pt", [NSEQ, 64], I32)
    caches = [dt_in(nm, [NPHYS * 128, 128]) for nm in ("c_cmp_k", "c_cmp_v", "c_sel_k", "c_sel_v")]
    bonus_s_d = dt_in("bonus_s", [4, 128])
    causal4_d = dt_in("causal4", [4, 4])
    wms0_d = dt_in("wms0", [128, 4])
    wsa_d = dt_in("wsa", [120, 4, 64])
    wsb_d = dt_in("wsb", [120, 4, 64])
    wn_d = dt_in("wn", [64, 4, 64])

    o_kv = dt_out("o_kv", [6, XROWS, 128])
    o_u = dt_out("o_u", [XROWS, 512])
    o_y = dt_out("o_y", [OROWS, D])
    o_swin = dt_out("o_swin", [2, NSEQ, 512, 128])
    o_spool = dt_out("o_spool", [NSEQ, 15, 512])

    modd = dt_int("modd", [M17, 9 * D])
    x1s = dt_int("x1s", [XROWS, D])
    x2s = dt_int("x2s", [OROWS, D])
    ksT_s = dt_int("ksT_s", [128, S], BF16)
    vs_s = dt_int("vs_s", [S, 130], BF16)
    kwT_s = dt_int("kwT_s", [128, (4 + NBLK) * 128], BF16)
    vw_s = dt_int("vw_s", [(4 + NBLK) * 128, 130], BF16)
    u_s = dt_int("u_s", [(1 + NBLK) * 128, 512], BF16)

    with stack:
        sc = Sched(nc, stack)
        ar = Arena(nc, stack, 204 * 1024)
        ps = []
        for i in range(8):
            t = stack.enter_context(nc.psum_tensor("ps%d" % i, [128, 512], F32))
            ps.append((t, Res("ps%d" % i)))
        R_modd = Res("modd"); R_okv = Res("okv"); R_ou = Res("ou"); R_oy = Res("oy")
        R_oswin = Res("oswin"); R_ospool = Res("ospool")
        R_x1s = Res("x1s"); R_x2s = Res("x2s")
        R_ksTs = Res("ksTs"); R_vss = Res("vss"); R_kwTs = Res("kwTs"); R_vws = Res("vws"); R_us = Res("us")

        def op(eng, fn, rd=(), wr=(), dma=None, deferred=False):
            return sc.op(eng, fn, rd=rd, wr=wr, dma=dma, deferred=deferred)

        def new_epoch():
            sc.barrier()

        dyn_ctr = [0]

        def dyn_dma(out_ap, base_ap, axis, idx_ap, mult, const, size, maxv, rd, wr, dma, rearr=None):
            dyn_ctr[0] += 1
            nm = "dr%d" % dyn_ctr[0]

            def fn(e):
                r = e.alloc_register(nm)
                e.reg_load(r, idx_ap)
                v = e.snap(r, donate=True, min_val=0, max_val=maxv)
                e1 = v * mult
                e2 = e1 + const if const else e1
                if axis == 0:
                    src = base_ap[ds(e2, size), :]
                else:
                    src = base_ap[:, ds(e2, size)]
                exprs = [e1, e2, src.offset, src.offset * 2, src.offset * 4]
                if rearr is not None:
                    src = src.rearrange(rearr, p=128)
                ins = e.dma_start(out=out_ap, in_=src)
                vc = e.get_value_cache()
                seen = set()
                for ex in exprs:
                    try:
                        al = vc.lookup(ex)
                    except Exception:
                        al = None
                    if al is not None and al.val.name not in seen and al.val.name != r.name:
                        seen.add(al.val.name)
                        e.free_register(al.val)
                e.free_register(r)
                return ins
            return op("sync", fn, rd=rd, wr=wr, dma=dma, deferred=True)

        ident_f, R_identf = ar.alloc("identf", [128], F32)
        ident_b, R_identb = ar.alloc("identb", [128], BF16)
        identrep, R_identrep = ar.alloc("identrep", [4, 128], BF16)
        op("sync", lambda e: e.dma_start(out=ident_f, in_=ident_d[:, :]), wr=[R_identf], dma=R_identf)
        op("vector", lambda e: e.tensor_copy(out=ident_b, in_=ident_f), rd=[R_identf], wr=[R_identb])
        for h in range(4):
            op("vector", lambda e, h=h: e.tensor_copy(out=identrep[:, h, :], in_=ident_f), rd=[R_identf], wr=[R_identrep])
        ones_f, R_ones = ar.alloc("ones", [128], F32)
        op("vector", lambda e: e.memset(ones_f, 1.0), wr=[R_ones])
        rinfo_t = stack.enter_context(nc.sbuf_tensor("rinfo_sb", [1, 8], I32))
        R_rinfo = Res("rinfo")
        op("sync", lambda e: e.dma_start(out=rinfo_t[:], in_=rinfo_d[:, :]), wr=[R_rinfo], dma=R_rinfo)
        ridx = rinfo_t[0:1, 0:1]
        kcT_sb, R_kcT = ar.alloc("kcT", [256], BF16)
        vcT_sb, R_vcT = ar.alloc("vcT", [256], BF16)
        vc_sb, R_vc = ar.alloc("vc", [2, 2, 65], BF16)
        W4k, R_W4k = ar.alloc("W4k", [4], BF16)
        W4v, R_W4v = ar.alloc("W4v", [4], BF16)
        wcol, R_wcol = ar.alloc("wcol", [2], F32)
        bmask, R_bmask = ar.alloc("bmask", [4], F32)
        pair_b, R_pair = ar.alloc("pair", [64], BF16)
        op("gpsimd", lambda e: e.dma_start(out=pair_b, in_=pair_d[:, :]), wr=[R_pair], dma=R_pair)
        op("sync", lambda e: e.dma_start(out=bmask, in_=bmask_d[:, :]), wr=[R_bmask], dma=R_bmask)
        for n in range(2):
            for qd in range(4):
                op("sync", lambda e, n=n, qd=qd: e.dma_start(out=wcol[32 * qd:32 * qd + 32, n:n + 1],
                                                            in_=cmpw[n:n + 1, :].rearrange("a l -> l a")),
                   wr=[R_wcol], dma=R_wcol)
        op("vector", lambda e: e.tensor_scalar(out=W4k, in0=bmask, scalar1=wcol[:, 0:1], scalar2=None, op0=ALU.mult),
           rd=[R_bmask, R_wcol], wr=[R_W4k])
        op("vector", lambda e: e.tensor_scalar(out=W4v, in0=bmask, scalar1=wcol[:, 1:2], scalar2=None, op0=ALU.mult),
           rd=[R_bmask, R_wcol], wr=[R_W4v])
        op("vector", lambda e: e.memset(vc_sb, 1.0), wr=[R_vc])
        base_mark = ar.mark()

        cs, R_cs = ar.alloc("cs", [D], F32)
        csb, R_csb = ar.alloc("csb", [D], BF16)
        siluT, R_siluT = ar.alloc("siluT", [8, M17], BF16)
        modsb, R_mod = ar.alloc("modsb", [9 * D], F32)
        bada, R_bada = ar.alloc("bada", [9 * D], F32)
        op("sync", lambda e: e.dma_start(out=cs[0:M17, :], in_=c17[:, :]), wr=[R_cs], dma=R_cs)
        op("sync", lambda e: e.dma_start(out=bada[0:1, :], in_=b_ada[:, :]), wr=[R_bada], dma=R_bada)
        op("scalar", lambda e: e.activation(out=csb[0:M17, :], in_=cs[0:M17, :], func=AF.Silu), rd=[R_cs], wr=[R_csb])
        pT, R_pT = ps[0]
        for k in range(8):
            pTk, R_pTk = ps[k % 2]
            op("tensor", lambda e, k=k, pTk=pTk: e.matmul(out=pTk[:, (k // 2) * M17:(k // 2 + 1) * M17], lhsT=csb[0:M17, k * 128:(k + 1) * 128],
                                                           rhs=ident_b[0:M17, 0:M17], start=True, stop=True),
               rd=[R_csb, R_identb], wr=[R_pTk])
        for k in range(8):
            pTk, R_pTk = ps[k % 2]
            op("vector", lambda e, k=k, pTk=pTk: e.tensor_copy(out=siluT[:, k, :], in_=pTk[:, (k // 2) * M17:(k // 2 + 1) * M17]),
               rd=[R_pTk], wr=[R_siluT])
        wad = [ar.alloc("wad%d" % i, [8, 512], BF16) for i in range(2)]
        w_ada_v = w_ada.rearrange("(k p) n -> p k n", p=128)
        for cg in range(18):
            wt, R_wt = wad[cg % 2]
            op("gpsimd", lambda e, cg=cg, wt=wt: e.dma_start(out=wt, in_=w_ada_v[:, :, cg * 512:(cg + 1) * 512]),
               wr=[R_wt], dma=R_wt)
            pm, R_pm = ps[2 + cg % 2]
            for k in range(8):
                op("tensor", lambda e, k=k, wt=wt, pm=pm: e.matmul(out=pm[0:M17, :], lhsT=siluT[:, k, :], rhs=wt[:, k, :],
                                                                   start=(k == 0), stop=False),
                   rd=[R_siluT, R_wt], wr=[R_pm])
            op("tensor", lambda e, cg=cg, pm=pm: e.matmul(out=pm[0:M17, :], lhsT=ones_f[0:1, 0:M17],
                                                          rhs=bada[0:1, cg * 512:(cg + 1) * 512], start=False, stop=True),
               rd=[R_ones, R_bada], wr=[R_pm])
            op("vector", lambda e, cg=cg, pm=pm: e.tensor_copy(out=modsb[0:M17, cg * 512:(cg + 1) * 512], in_=pm[0:M17, :]),
               rd=[R_pm], wr=[R_mod])
        op("sync", lambda e: e.dma_start(out=modd[:, :], in_=modsb[0:M17, :]), rd=[R_mod], wr=[R_modd], dma=R_modd)
        zt, R_zt = ar.alloc("zt", [5, 130], BF16)
        op("vector", lambda e: e.memset(zt, 0.0), wr=[R_zt])
        op("sync", lambda e: e.dma_start(out=kwT_s[:, 0:512], in_=zt.rearrange("p a b -> p (a b)")[:, 0:512]), rd=[R_zt], wr=[R_kwTs], dma=R_zt)
        op("sync", lambda e: e.dma_start(out=vw_s[0:512, :].rearrange("(m p) c -> p m c", p=128), in_=zt[:, 0:4, :]),
           rd=[R_zt], wr=[R_vws], dma=R_zt)
        op("sync", lambda e: e.dma_start(out=u_s[0:128, :], in_=zt.rearrange("p a b -> p (a b)")[:, 0:512]), rd=[R_zt], wr=[R_us], dma=R_zt)
        new_epoch()
        ar.reset(base_mark)

        def load_row(dst, R_dst, kind, mod_i, gain_i, mode, tmpg, R_tmpg):
            msrc = lambda a, b: modd[a:b, mod_i * D:(mod_i + 1) * D]
            if kind == 0:
                op("sync", lambda e: e.dma_start(out=dst, in_=msrc(0, 1).to_broadcast([128, D])), rd=[R_modd], wr=[R_dst], dma=R_dst)
            else:
                r0 = 1 + 16 * (kind - 1)
                for t in range(4):
                    op("sync", lambda e, t=t: e.dma_start(out=dst[16 * t:16 * t + 16, :], in_=msrc(r0, r0 + 16)), rd=[R_modd], wr=[R_dst], dma=R_dst)
            if mode == "B":
                return
            op("sync", lambda e: e.dma_start(out=tmpg, in_=gains[gain_i:gain_i + 1, :].to_broadcast([128, D])), wr=[R_tmpg], dma=R_tmpg)
            if mode == "A":
                op("vector", lambda e: e.scalar_tensor_tensor(out=dst, in0=dst, scalar=1.0, in1=tmpg, op0=ALU.add, op1=ALU.mult),
                   rd=[R_dst, R_tmpg], wr=[R_dst])
            else:
                sclr = 0.5 if mode == "G5" else 1.0
                op("vector", lambda e: e.scalar_tensor_tensor(out=dst, in0=dst, scalar=sclr, in1=tmpg, op0=ALU.mult, op1=ALU.mult),
                   rd=[R_dst, R_tmpg], wr=[R_dst])

        class Work:
            pass

        def alloc_work():
            w = Work()
            w.h32, w.R_h32 = ar.alloc("h32", [D], F32)
            w.hb, w.R_hb = ar.alloc("hb", [D], BF16)
            w.junk, w.R_junk = ar.alloc("junk", [D], BF16)
            w.ssq, w.R_ssq = ar.alloc("ssq", [1], F32)
            w.rstd, w.R_rstd = ar.alloc("rstd", [1], F32)
            w.hT, w.R_hT = ar.alloc("hT", [8, 512], BF16)
            w.tmpg, w.R_tmpg = ar.alloc("tmpg", [D], F32)
            return w

        def rms_rstd(w, nt):
            op("vector", lambda e: e.tensor_scalar(out=w.rstd[0:nt, :], in0=w.ssq[0:nt, :], scalar1=1.0 / D, scalar2=EPS,
                                                    op0=ALU.mult, op1=ALU.add), rd=[w.R_ssq], wr=[w.R_rstd])
            op("scalar", lambda e: e.activation(out=w.rstd[0:nt, :], in_=w.rstd[0:nt, :], func=AF.Sqrt), rd=[w.R_rstd], wr=[w.R_rstd])
            op("vector", lambda e: e.reciprocal(out=w.rstd[0:nt, :], in_=w.rstd[0:nt, :]), rd=[w.R_rstd], wr=[w.R_rstd])

        def norm_mod(w, x_ap, R_x, nt, A, R_A, Bt, R_B, col0):
            op("scalar", lambda e: e.activation(out=w.junk[0:nt, :], in_=x_ap, func=AF.Square, accum_out=w.ssq[0:nt, :]),
               rd=[R_x], wr=[w.R_junk, w.R_ssq])
            rms_rstd(w, nt)
            op("vector", lambda e: e.scalar_tensor_tensor(out=w.h32[0:nt, :], in0=x_ap, scalar=w.rstd[0:nt, :], in1=A[0:nt, :],
                                                           op0=ALU.mult, op1=ALU.mult), rd=[R_x, w.R_rstd, R_A], wr=[w.R_h32])
            op("vector", lambda e: e.tensor_tensor(out=w.hb[0:nt, :], in0=w.h32[0:nt, :], in1=Bt[0:nt, :], op=ALU.add),
               rd=[w.R_h32, R_B], wr=[w.R_hb])
            for half in range(2):
                pt_, R_pt = ps[half]
                for kk in range(4):
                    k = half * 4 + kk
                    op("tensor", lambda e, k=k, kk=kk, pt_=pt_: e.matmul(out=pt_[:, kk * 128:kk * 128 + nt],
                                                                          lhsT=w.hb[0:nt, k * 128:(k + 1) * 128],
                                                                          rhs=ident_b[0:nt, 0:nt], start=True, stop=True),
                       rd=[w.R_hb, R_identb], wr=[R_pt])
                src = pt_[:, :].rearrange("p (a b) -> p a b", a=4)[:, :, 0:nt]
                if half == 0:
                    op("scalar", lambda e, half=half, src=src: e.activation(out=w.hT[:, half * 4:half * 4 + 4, col0:col0 + nt],
                                                                              in_=src, func=AF.Copy), rd=[R_pt], wr=[w.R_hT])
                else:
                    op("vector", lambda e, half=half, src=src: e.tensor_copy(out=w.hT[:, half * 4:half * 4 + 4, col0:col0 + nt],
                                                                               in_=src), rd=[R_pt], wr=[w.R_hT])

        def post_residual(w, py, nt, x_ap, R_x, G, R_G):
            op("scalar", lambda e: e.activation(out=w.junk[0:nt, 0:512], in_=py[0][0][0:nt, :], func=AF.Square,
                                                 accum_out=w.ssq[0:nt, :]), rd=[py[0][1]], wr=[w.R_junk, w.R_ssq])
            op("scalar", lambda e: e.activation(out=w.junk[0:nt, 512:1024], in_=py[1][0][0:nt, :], func=AF.Square,
                                                 accum_out=w.rstd[0:nt, :]), rd=[py[1][1]], wr=[w.R_junk, w.R_rstd])
            op("vector", lambda e: e.tensor_tensor(out=w.ssq[0:nt, :], in0=w.ssq[0:nt, :], in1=w.rstd[0:nt, :], op=ALU.add),
               rd=[w.R_ssq, w.R_rstd], wr=[w.R_ssq])
            rms_rstd(w, nt)
            for half in range(2):
                pyh, R_pyh = py[half]
                hs = slice(half * 512, (half + 1) * 512)
                op("vector", lambda e, pyh=pyh, hs=hs: e.scalar_tensor_tensor(
                    out=w.h32[0:nt, hs], in0=pyh[0:nt, :], scalar=w.rstd[0:nt, :], in1=G[0:nt, hs], op0=ALU.mult, op1=ALU.mult),
                    rd=[R_pyh, w.R_rstd, R_G], wr=[w.R_h32])
            op("vector", lambda e: e.tensor_tensor(out=x_ap[0:nt, :], in0=x_ap[0:nt, :], in1=w.h32[0:nt, :], op=ALU.add),
               rd=[w.R_h32, R_x], wr=[R_x])

        wi_ctr = [0]

        def ffn_pass(w, f, tiles, wi_v, wo, R_wo, rowsA, rowsB, rowsG, post):
            ntok = sum(t[1] for t in tiles)
            A, R_A = rowsA; Bt, R_B = rowsB; G, R_G = rowsG
            col = 0
            cols = []
            for ti, (load_fn, nt, tag) in enumerate(tiles):
                x_ap, R_x = f.xt[ti]
                load_fn(x_ap, R_x)
                norm_mod(w, x_ap[0:nt, :], R_x, nt, A, R_A, Bt, R_B, col)
                cols.append(col)
                col += nt
            for fc in range(NF):
                wb, R_wb = f.wib[wi_ctr[0] % 3]
                wi_ctr[0] += 1
                op("gpsimd", lambda e, fc=fc, wb=wb: e.dma_start(out=wb[:, :, 0:128], in_=wi_v[:, :, fc * 128:(fc + 1) * 128]),
                   wr=[R_wb], dma=R_wb)
                op("gpsimd", lambda e, fc=fc, wb=wb: e.dma_start(out=wb[:, :, 128:256],
                                                                 in_=wi_v[:, :, DFF + fc * 128:DFF + (fc + 1) * 128]),
                   wr=[R_wb], dma=R_wb)
                pa, R_pa = ps[2 + 2 * (fc % 2)]
                pb, R_pb = ps[3 + 2 * (fc % 2)]
                for k in range(8):
                    op("tensor", lambda e, k=k, wb=wb, pa=pa: e.matmul(out=pa[:, 0:ntok], lhsT=wb[:, k, 0:128], rhs=w.hT[:, k, 0:ntok],
                                                                       start=(k == 0), stop=(k == 7)), rd=[R_wb, w.R_hT], wr=[R_pa])
                for k in range(8):
                    op("tensor", lambda e, k=k, wb=wb, pb=pb: e.matmul(out=pb[:, 0:ntok], lhsT=wb[:, k, 128:256], rhs=w.hT[:, k, 0:ntok],
                                                                       start=(k == 0), stop=(k == 7)), rd=[R_wb, w.R_hT], wr=[R_pb])
                op("scalar", lambda e, pa=pa: e.activation(out=f.sa[:, 0:ntok], in_=pa[:, 0:ntok], func=AF.Silu), rd=[R_pa], wr=[f.R_sa])
                op("vector", lambda e, fc=fc, pb=pb: e.tensor_tensor(out=f.gT[:, fc, 0:ntok], in0=f.sa[:, 0:ntok], in1=pb[:, 0:ntok], op=ALU.mult),
                   rd=[f.R_sa, R_pb], wr=[f.R_gT])
            for ti, (load_fn, nt, tag) in enumerate(tiles):
                x_ap, R_x = f.xt[ti]
                c0 = cols[ti]
                py = [ps[0], ps[1]]
                for half in range(2):
                    pyh, R_pyh = py[half]
                    for fc in range(NF):
                        op("tensor", lambda e, fc=fc, half=half, pyh=pyh, c0=c0, nt=nt: e.matmul(
                            out=pyh[0:nt, :], lhsT=f.gT[:, fc, c0:c0 + nt], rhs=wo[:, fc, half * 512:(half + 1) * 512],
                            start=(fc == 0), stop=(fc == NF - 1)), rd=[f.R_gT, R_wo], wr=[R_pyh])
                post_residual(w, py, nt, x_ap, R_x, G, R_G)
                post(ti, x_ap, R_x, nt, c0, tag)

        class FBuf:
            pass

        def alloc_ffn():
            f = FBuf()
            f.xt = [ar.alloc("xt%d" % i, [D], F32) for i in range(4)]
            f.gT, f.R_gT = ar.alloc("gT", [NF, 512], BF16)
            f.sa, f.R_sa = ar.alloc("sa", [512], F32)
            f.wib = [ar.alloc("wib%d" % i, [8, 256], BF16) for i in range(3)]
            return f

        wo1, R_wo1 = ar.alloc("wo1", [NF, D], BF16)
        op("gpsimd", lambda e: e.dma_start(out=wo1, in_=f1wo.rearrange("(f p) n -> p f n", p=128)), wr=[R_wo1], dma=R_wo1)
        winT, R_win = ar.alloc("winT", [8, INW], BF16)
        w_in_v = w_in.rearrange("(k p) n -> p k n", p=128)
        op("gpsimd", lambda e: e.dma_start(out=winT, in_=w_in_v), wr=[R_win], dma=R_win)
        w = alloc_work()
        f = alloc_ffn()
        R_rows1 = Res("rows1")
        rows1 = {nm: ar.alloc(nm, [D], F32, res=R_rows1) for nm in ("A1", "B1", "G1", "A2", "B2")}
        kvst = [ar.alloc("kvst%d" % i, [768], F32) for i in range(2)]
        ust = [ar.alloc("ust%d" % i, [512], F32) for i in range(2)]
        ubs = [ar.alloc("ub%d" % i, [512], BF16) for i in range(2)]
        vst = [ar.alloc("vst%d" % i, [2, 2, 65], BF16) for i in range(2)]
        kTst = [ar.alloc("kTst%d" % i, [256], BF16) for i in range(2)]
        kcrb = [ar.alloc("kcrb%d" % i, [256], BF16) for i in range(2)]
        for i in range(2):
            op("vector", lambda e, i=i: e.memset(vst[i][0], 1.0), wr=[vst[i][1]])

        def load_rows1(kind):
            load_row(*rows1["A1"], kind, 1, 0, "A", w.tmpg, w.R_tmpg)
            load_row(*rows1["B1"], kind, 0, 0, "B", w.tmpg, w.R_tmpg)
            load_row(*rows1["G1"], kind, 2, 1, "G5", w.tmpg, w.R_tmpg)
            load_row(*rows1["A2"], kind, 4, 2, "A", w.tmpg, w.R_tmpg)
            load_row(*rows1["B2"], kind, 3, 2, "B", w.tmpg, w.R_tmpg)

        f1wi_v = f1wi.rearrange("(k p) n -> p k n", p=128)
        f2wi_v = f2wi.rearrange("(k p) n -> p k n", p=128)
        tctr = [0]

        def post1(ti, x_ap, R_x, nt, c0, tag):
            is_s = isinstance(tag, tuple)
            sg = tag[1] if is_s else None
            row0 = (S + 64 * sg) if is_s else tag * 128
            tc_ = tctr[0]
            tctr[0] += 1
            op("sync", lambda e: e.dma_start(out=x1s[row0:row0 + nt, :], in_=x_ap[0:nt, :]), rd=[R_x], wr=[R_x1s], dma=R_x)
            norm_mod(w, x_ap[0:nt, :], R_x, nt, *rows1["A2"], *rows1["B2"], c0)
            pk0, R_pk0 = ps[6]
            pk1, R_pk1 = ps[7]
            pu, R_pu = ps[5]
            for k in range(8):
                op("tensor", lambda e, k=k: e.matmul(out=pk0[0:nt, :], lhsT=w.hT[:, k, c0:c0 + nt], rhs=winT[:, k, 512:1024],
                                                      start=(k == 0), stop=(k == 7)), rd=[w.R_hT, R_win], wr=[R_pk0])
            for k in range(8):
                op("tensor", lambda e, k=k: e.matmul(out=pk1[0:nt, 0:256], lhsT=w.hT[:, k, c0:c0 + nt], rhs=winT[:, k, 1024:1280],
                                                      start=(k == 0), stop=(k == 7)), rd=[w.R_hT, R_win], wr=[R_pk1])
            for k in range(8):
                op("tensor", lambda e, k=k: e.matmul(out=pu[0:nt, :], lhsT=w.hT[:, k, c0:c0 + nt], rhs=winT[:, k, 1304:1816],
                                                      start=(k == 0), stop=(k == 7)), rd=[w.R_hT, R_win], wr=[R_pu])
            ks_, R_ks = kvst[tc_ % 2]
            us_, R_us_ = ust[tc_ % 2]
            op("scalar", lambda e: e.activation(out=ks_[0:nt, 0:512], in_=pk0[0:nt, :], func=AF.Copy), rd=[R_pk0], wr=[R_ks])
            op("scalar", lambda e: e.activation(out=ks_[0:nt, 512:768], in_=pk1[0:nt, 0:256], func=AF.Copy), rd=[R_pk1], wr=[R_ks])
            op("vector", lambda e: e.tensor_copy(out=us_[0:nt, :], in_=pu[0:nt, :]), rd=[R_pu], wr=[R_us_])
            op("sync", lambda e: e.dma_start(out=o_kv[:, row0:row0 + nt, :].rearrange("s t c -> t s c"),
                                             in_=ks_[0:nt, :].rearrange("p (s c) -> p s c", s=6)), rd=[R_ks], wr=[R_okv], dma=R_ks)
            op("sync", lambda e: e.dma_start(out=o_u[row0:row0 + nt, :], in_=us_[0:nt, :]), rd=[R_us_], wr=[R_ou], dma=R_us_)
            if is_s:
                for t in range(4):
                    for n in range(2):
                        dst = bass.AP(tensor=o_swin.tensor, offset=(n * NSEQ + 16 * sg) * 65536 + (508 + t) * 128, ap=[[65536, 16], [1, 128]])
                        op("sync", lambda e, t=t, n=n, dst=dst: e.dma_start(out=dst, in_=ks_[16 * t:16 * t + 16, 512 + 128 * n:640 + 128 * n]),
                           rd=[R_ks], wr=[R_oswin], dma=R_ks)
                    dstp = bass.AP(tensor=o_spool.tensor, offset=(16 * sg * 15 + 11 + t) * 512, ap=[[15 * 512, 16], [1, 512]])
                    op("sync", lambda e, t=t, dstp=dstp: e.dma_start(out=dstp, in_=us_[16 * t:16 * t + 16, :]),
                       rd=[R_us_], wr=[R_ospool], dma=R_us_)
                return
            j = tag
            ub_, R_ub = ubs[tc_ % 2]
            op("vector", lambda e: e.tensor_copy(out=ub_, in_=us_), rd=[R_us_], wr=[R_ub])
            op("sync", lambda e: e.dma_start(out=u_s[(1 + j) * 128:(2 + j) * 128, :], in_=ub_), rd=[R_ub], wr=[R_us], dma=R_ub)
            vs_, R_vs = vst[tc_ % 2]
            op("vector", lambda e: e.tensor_copy(out=vs_[:, 0, :, 0:64], in_=ks_[:, 384:512].rearrange("p (g d) -> p g d", g=2)),
               rd=[R_ks], wr=[R_vs])
            op("vector", lambda e: e.tensor_copy(out=vs_[:, 1, :, 0:64], in_=ks_[:, 640:768].rearrange("p (g d) -> p g d", g=2)),
               rd=[R_ks], wr=[R_vs])
            op("sync", lambda e: e.dma_start(out=vs_s[j * 128:(j + 1) * 128, :], in_=vs_[:, 0, :, :].rearrange("p g c -> p (g c)")),
               rd=[R_vs], wr=[R_vss], dma=R_vs)
            op("sync", lambda e: e.dma_start(out=vw_s[(4 + j) * 128:(5 + j) * 128, :], in_=vs_[:, 1, :, :].rearrange("p g c -> p (g c)")),
               rd=[R_vs], wr=[R_vws], dma=R_vs)
            pf, R_pf = ps[4]
            for n, c_lo in enumerate((768, 1024)):
                for k in range(8):
                    op("tensor", lambda e, k=k, n=n, c_lo=c_lo: e.matmul(out=pf[:, n * 128:(n + 1) * 128], lhsT=winT[:, k, c_lo:c_lo + 128],
                                                                          rhs=w.hT[:, k, c0:c0 + 128], start=(k == 0), stop=(k == 7)),
                       rd=[w.R_hT, R_win], wr=[R_pf])
            kt_, R_kt = kTst[tc_ % 2]
            op("scalar", lambda e: e.activation(out=kt_, in_=pf[:, 0:256], func=AF.Copy), rd=[R_pf], wr=[R_kt])
            op("sync", lambda e: e.dma_start(out=ksT_s[:, j * 128:(j + 1) * 128], in_=kt_[:, 0:128]), rd=[R_kt], wr=[R_ksTs], dma=R_kt)
            op("sync", lambda e: e.dma_start(out=kwT_s[:, (4 + j) * 128:(5 + j) * 128], in_=kt_[:, 128:256]), rd=[R_kt], wr=[R_kwTs], dma=R_kt)
            kb_, R_kb = kcrb[tc_ % 2]
            op("vector", lambda e: e.tensor_copy(out=kb_, in_=ks_[:, 0:256]), rd=[R_ks], wr=[R_kb])
            pc, R_pc = ps[4]
            op("tensor", lambda e: e.matmul(out=pc[:, 256:260], lhsT=kb_[:, 0:128], rhs=W4k, start=True, stop=True),
               rd=[R_kb, R_W4k], wr=[R_pc])
            op("tensor", lambda e: e.matmul(out=pc[:, 260:264], lhsT=kb_[:, 128:256], rhs=W4v, start=True, stop=True),
               rd=[R_kb, R_W4v], wr=[R_pc])
            op("vector", lambda e: e.tensor_copy(out=kcT_sb[:, 4 * j:4 * j + 4], in_=pc[:, 256:260]), rd=[R_pc], wr=[R_kcT])
            op("vector", lambda e: e.tensor_copy(out=vcT_sb[:, 4 * j:4 * j + 4], in_=pc[:, 260:264]), rd=[R_pc], wr=[R_vcT])

        def xload(src):
            return lambda x_ap, R_x: op("sync", lambda e: e.dma_start(out=x_ap[0:src.shape[0], :], in_=src), wr=[R_x], dma=R_x)

        for sg in range(NSG):
            for n, st in enumerate((st_wk, st_wv)):
                srcw = bass.AP(tensor=st.tensor, offset=16 * sg * 65536 + 512, ap=[[65536, 16], [512, 127], [1, 512]])
                dstw = bass.AP(tensor=o_swin.tensor, offset=(n * NSEQ + 16 * sg) * 65536, ap=[[65536, 16], [512, 127], [1, 512]])
                op("sync", lambda e: e.dma_start(out=dstw, in_=srcw), wr=[R_oswin], dma=R_oswin)
            srcp = bass.AP(tensor=st_pool.tensor, offset=(16 * sg * 15 + 4) * 512, ap=[[15 * 512, 16], [1, 11 * 512]])
            dstp2 = bass.AP(tensor=o_spool.tensor, offset=16 * sg * 15 * 512, ap=[[15 * 512, 16], [1, 11 * 512]])
            op("sync", lambda e: e.dma_start(out=dstp2, in_=srcp), wr=[R_ospool], dma=R_ospool)
        load_rows1(0)
        for p in range(NBLK // 4):
            tiles = [(xload(xp[j * 128:(j + 1) * 128, :]), 128, j) for j in range(4 * p, 4 * p + 4)]
            ffn_pass(w, f, tiles, f1wi_v, wo1, R_wo1, rows1["A1"], rows1["B1"], rows1["G1"], post1)
        for sg in range(NSG):
            new_epoch()
            load_rows1(1 + sg)
            ffn_pass(w, f, [(xload(xs[64 * sg:64 * sg + 64, :]), 64, ("S", sg))], f1wi_v, wo1, R_wo1, rows1["A1"], rows1["B1"], rows1["G1"], post1)
        for jt in range(2):
            pv_, R_pv = ps[jt]
            op("tensor", lambda e, jt=jt, pv_=pv_: e.matmul(out=pv_[:, 0:128], lhsT=vcT_sb[:, jt * 128:(jt + 1) * 128], rhs=ident_b,
                                                            start=True, stop=True), rd=[R_vcT, R_identb], wr=[R_pv])
            op("vector", lambda e, jt=jt, pv_=pv_: e.tensor_copy(out=vc_sb[:, jt, :, 0:64], in_=pv_[:, 0:128].rearrange("p (g d) -> p g d", g=2)),
               rd=[R_pv], wr=[R_vc])
        new_epoch()
        ar.reset(base_mark)
        new_epoch()
        ar.reset(base_mark)

        w = alloc_work()
        R_p2a = Res("p2a"); R_p2b = Res("p2b")
        wq, R_wq = ar.alloc("wq", [8, 4, 128], BF16, res=R_p2a)
        for h in range(8):
            op("gpsimd", lambda e, h=h: e.dma_start(out=wq[:, :, h % 4, (h // 4) * 64:(h // 4) * 64 + 64], in_=w_in_v[:, :, h * 64:(h + 1) * 64]),
               wr=[R_wq], dma=R_wq)
        wg, R_wg = ar.alloc("wg", [8, 24], BF16, res=R_p2a)
        op("gpsimd", lambda e: e.dma_start(out=wg, in_=w_in_v[:, :, 1280:1304]), wr=[R_wg], dma=R_wg)
        wout, R_wout = ar.alloc("wout", [8, D], BF16, res=R_p2a)
        op("gpsimd", lambda e: e.dma_start(out=wout, in_=w_out.rearrange("(k p) n -> p k n", p=128)), wr=[R_wout], dma=R_wout)
        pw, R_pw = ar.alloc("pw", [4, 128], BF16, res=R_p2a)
        op("gpsimd", lambda e: e.dma_start(out=pw, in_=pool_w.rearrange("g c d -> c g d")), wr=[R_pw], dma=R_pw)
        pscl, R_pscl = ar.alloc("pscl", [4], F32)
        for gi in range(4):
            op("sync", lambda e, gi=gi: e.dma_start(out=pscl[:, gi:gi + 1], in_=pool_scale[0:1, gi * 128:(gi + 1) * 128].rearrange("a d -> d a")),
               wr=[R_pscl], dma=R_pscl)
        cm_sb, R_cm = ar.alloc("cm", [NGRP, 128], BF16, res=R_p2b)
        op("gpsimd", lambda e: e.dma_start(out=cm_sb, in_=cm_d.rearrange("k p q -> p k q")), wr=[R_cm], dma=R_cm)
        wm_sb, R_wm = ar.alloc("wm", [25, 128], BF16, res=R_p2b)
        op("gpsimd", lambda e: e.dma_start(out=wm_sb, in_=wm_d.rearrange("a m p q -> p (a m) q")), wr=[R_wm], dma=R_wm)
        wp_sb, R_wp = ar.alloc("wp", [16, 128], BF16, res=R_p2b)
        op("gpsimd", lambda e: e.dma_start(out=wp_sb, in_=wp_d.rearrange("a g b p q -> p (a g b) q")), wr=[R_wp], dma=R_wp)
        R_rows2 = Res("rows2")
        rows2 = {nm: ar.alloc(nm, [D], F32, res=R_rows2) for nm in ("A2", "B2", "G2")}

        def load_rows2(kind):
            load_row(*rows2["A2"], kind, 4, 2, "A", w.tmpg, w.R_tmpg)
            load_row(*rows2["B2"], kind, 3, 2, "B", w.tmpg, w.R_tmpg)
            load_row(*rows2["G2"], kind, 5, 3, "G1", w.tmpg, w.R_tmpg)

        ksT_buf, _ = ar.alloc("ksTb", [S], BF16)
        vs_buf, _ = ar.alloc("vsb", [NBLK, 130], BF16)
        R_ksb = [Res("ksb")] * NOWN
        R_vsb = [Res("vsb")] * NOWN
        kwb = [ar.alloc("kwb%d" % i, [640], BF16) for i in range(2)]
        vwb = [ar.alloc("vwb%d" % i, [5, 130], BF16) for i in range(2)]
        u2b = [ar.alloc("u2b%d" % i, [2, 512], BF16) for i in range(2)]
        x1t = [ar.alloc("x1t%d" % i, [D], F32) for i in range(2)]
        qT_sb, R_qT = ar.alloc("qT", [4, 128], BF16)
        gates, R_gates = ar.alloc("gates", [24], F32)
        cmk = [ar.alloc("cmk%d" % i, [2, 128], BF16) for i in range(2)]
        bon = [ar.alloc("bon%d" % i, [128], F32) for i in range(2)]
        PTb = [ar.alloc("PT%d" % i, [512], BF16) for i in range(4)]
        nexp = [ar.alloc("nexp%d" % g, [S], BF16) for g in range(2)]
        rden, R_rden = ar.alloc("rden", [4], F32)
        coef, R_coef = ar.alloc("coef", [4], F32)
        tmpo, R_tmpo = ar.alloc("tmpo", [4, 64], F32)
        onsa, R_onsa = ar.alloc("onsa", [512], F32)
        onsab, R_onsab = ar.alloc("onsab", [512], BF16)
        onsaT, R_onsaT = ar.alloc("onsaT", [4, 128], BF16)
        tmpi, R_tmpi = ar.alloc("tmpi", [4, 128], F32)
        vals, R_vals = ar.alloc("vals", [128], F32)
        vals2, R_vals2 = ar.alloc("vals2", [128], F32)
        m8, R_m8 = ar.alloc("m8", [16], F32)
        selm, R_selm = ar.alloc("selm", [128], F32)
        negs, R_negs = ar.alloc("negs", [128], BF16)
        dT_sb, R_dT = ar.alloc("dT", [4, 128], BF16)
        ypT, R_ypT = ar.alloc("ypT", [4, 128], BF16)
        pt_ctr = [0]

        def attend(g, nq, q_rhs, R_q, tiles, pO, R_pO, pI=None, R_pI=None, irep=None):
            n = len(tiles)
            oview = pO[:, 0:260].rearrange("p (h c) -> p h c", h=4)
            irep_ap, R_irep = (identrep.rearrange("p a b -> p (a b)"), R_identrep) if irep is None else irep
            for idx, t in enumerate(tiles):
                nk = t.get("nk", 128)
                pS, R_pS = ps[2 + (pt_ctr[0] % 2)]
                PT, R_PT = PTb[pt_ctr[0] % 4]
                pt_ctr[0] += 1
                has_add = t.get("add") is not None
                op("tensor", lambda e: e.matmul(out=pS[0:nk, 0:4 * nq], lhsT=t["kT"], rhs=q_rhs, start=True, stop=not has_add),
                   rd=[t["Rk"], R_q], wr=[R_pS])
                if has_add:
                    a_ap, R_a = t["add"]
                    op("tensor", lambda e: e.matmul(out=pS[0:nk, 0:4 * nq], lhsT=a_ap, rhs=irep_ap, start=False, stop=True),
                       rd=[R_a, R_irep], wr=[R_pS])
                op("scalar", lambda e: e.activation(out=PT[0:nk, 0:4 * nq], in_=pS[0:nk, 0:4 * nq], func=AF.Exp, scale=SCALE),
                   rd=[R_pS], wr=[R_PT])
                if t.get("mul") is not None:
                    m_ap, R_m = t["mul"]
                    pv3 = PT[0:nk, 0:4 * nq].rearrange("p (h q) -> p h q", h=4)
                    op("vector", lambda e: e.tensor_tensor(out=pv3, in0=pv3, in1=bc_mid(m_ap, 4), op=ALU.mult),
                       rd=[R_PT, R_m], wr=[R_PT])
                for hh in range(4):
                    op("tensor", lambda e: e.matmul(out=oview[0:nq, hh, :], lhsT=PT[0:nk, hh * nq:(hh + 1) * nq], rhs=t["v"],
                                                     start=(idx == 0 and hh == 0), stop=(idx == n - 1)),
                       rd=[R_PT, t["Rv"]], wr=[R_pO])
                if pI is not None:
                    jt = t["jt"]
                    iview = pI[:, :].rearrange("p (h b) -> p h b", h=4)
                    for hh in range(4):
                        op("tensor", lambda e: e.matmul(out=iview[0:nq, hh, jt * 64:(jt + 1) * 64], lhsT=PT[0:nk, hh * nq:(hh + 1) * nq],
                                                         rhs=pair_b[0:nk, :], start=True, stop=True),
                           rd=[R_PT, R_pair], wr=[R_pI])

        def finish_branch(g, nq, br, pO, R_pO, gates_ap, first):
            oview = pO[:, 0:260].rearrange("p (h c) -> p h c", h=4)
            op("vector", lambda e: e.tensor_scalar(out=rden[0:nq, :], in0=oview[0:nq, :, 64], scalar1=1e-30, scalar2=None, op0=ALU.max),
               rd=[R_pO], wr=[R_rden])
            op("vector", lambda e: e.reciprocal(out=rden[0:nq, :], in_=rden[0:nq, :]), rd=[R_rden], wr=[R_rden])
            gv = gates_ap.rearrange("p (h b) -> p h b", b=3)[:, 4 * g:4 * g + 4, br]
            op("vector", lambda e: e.tensor_tensor(out=coef[0:nq, :], in0=rden[0:nq, :], in1=gv, op=ALU.mult),
               rd=[R_rden, R_gates], wr=[R_coef])
            ov = onsa[0:nq, g * 256:(g + 1) * 256].rearrange("p (h d) -> p h d", h=4)
            if first:
                op("vector", lambda e: e.tensor_tensor(out=ov, in0=oview[0:nq, :, 0:64], in1=bc_last(coef[0:nq, :], 64), op=ALU.mult),
                   rd=[R_pO, R_coef], wr=[R_onsa])
            else:
                op("vector", lambda e: e.tensor_tensor(out=tmpo[0:nq, :, :], in0=oview[0:nq, :, 0:64], in1=bc_last(coef[0:nq, :], 64), op=ALU.mult),
                   rd=[R_pO, R_coef], wr=[R_tmpo])
                op("vector", lambda e: e.tensor_tensor(out=ov, in0=ov, in1=tmpo[0:nq, :, :], op=ALU.add), rd=[R_tmpo, R_onsa], wr=[R_onsa])

        def select_blocks(g, nq, pI, R_pI, bonus_ap, R_bonus, kth, nblk_exp):
            iview = pI[:, :].rearrange("p (h b) -> p h b", h=4)
            op("vector", lambda e: e.tensor_tensor(out=tmpi[0:nq, :, :], in0=iview[0:nq, :, :], in1=bc_last(rden[0:nq, :], 128), op=ALU.mult),
               rd=[R_pI, R_rden], wr=[R_tmpi])
            op("vector", lambda e: e.tensor_reduce(out=vals[0:nq, :], in_=tmpi[0:nq, :, :].rearrange("p h b -> p b h"),
                                                    axis=mybir.AxisListType.X, op=ALU.add), rd=[R_tmpi], wr=[R_vals])
            op("vector", lambda e: e.tensor_tensor(out=vals[0:nq, :], in0=vals[0:nq, :], in1=bonus_ap, op=ALU.add),
               rd=[R_vals, R_bonus], wr=[R_vals])
            op("vector", lambda e: e.max(out=m8[0:nq, 0:8], in_=vals[0:nq, :]), rd=[R_vals], wr=[R_m8])
            op("vector", lambda e: e.match_replace(out=vals2[0:nq, :], in_to_replace=m8[0:nq, 0:8], in_values=vals[0:nq, :], imm_value=-3.0e38),
               rd=[R_vals, R_m8], wr=[R_vals2])
            op("vector", lambda e: e.max(out=m8[0:nq, 8:16], in_=vals2[0:nq, :]), rd=[R_vals2], wr=[R_m8])
            op("vector", lambda e: e.tensor_scalar(out=selm[0:nq, :], in0=vals[0:nq, :], scalar1=m8[0:nq, 8 + kth - 9:8 + kth - 8], scalar2=None,
                                                    op0=ALU.is_ge), rd=[R_vals, R_m8], wr=[R_selm])
            op("vector", lambda e: e.tensor_scalar(out=negs[0:nq, :], in0=selm[0:nq, :], scalar1=-1.0, scalar2=-NEGM, op0=ALU.add, op1=ALU.mult),
               rd=[R_selm], wr=[R_negs])
            nx, R_nx = nexp[g]
            op("vector", lambda e: e.tensor_copy(out=nx[0:nq, 0:nblk_exp * 64].rearrange("p (b l) -> p b l", l=64),
                                                  in_=bc_last(negs[0:nq, 0:nblk_exp], 64)), rd=[R_negs], wr=[R_nx])

        load_rows2(0)
        for i in range(NOWN):
            x_ap, R_x = x1t[i % 2]
            nkt = NGRP * i + NGRP
            kind = 0 if i == 0 else 1
            kw5 = min(NGRP * i, 4)
            c128 = 128 * NGRP * i
            dyn_dma(x_ap, x1s, 0, ridx, 128, c128, 128, NGRP - 1, rd=[R_x1s, R_rinfo], wr=[R_x], dma=R_x)
            op("sync", lambda e, i=i: e.dma_start(out=ksT_buf[:, c128:c128 + 128 * NGRP], in_=ksT_s[:, c128:c128 + 128 * NGRP]),
               rd=[R_ksTs], wr=[R_ksb[i]], dma=R_ksb[i])
            op("sync", lambda e, i=i: e.dma_start(out=vs_buf[:, NGRP * i:NGRP * i + NGRP, :], in_=vs_s[c128:c128 + 128 * NGRP, :].rearrange("(m p) c -> p m c", p=128)),
               rd=[R_vss], wr=[R_vsb[i]], dma=R_vsb[i])
            kw_, R_kw = kwb[i % 2]
            vw_, R_vw = vwb[i % 2]
            u2_, R_u2 = u2b[i % 2]
            dyn_dma(kw_, kwT_s, 1, ridx, 128, c128, 640, NGRP - 1, rd=[R_kwTs, R_rinfo], wr=[R_kw], dma=R_kw)
            dyn_dma(vw_, vw_s, 0, ridx, 128, c128, 640, NGRP - 1, rd=[R_vws, R_rinfo], wr=[R_vw], dma=R_vw, rearr="(m p) c -> p m c")
            dyn_dma(u2_, u_s, 0, ridx, 128, c128, 256, NGRP - 1, rd=[R_us, R_rinfo], wr=[R_u2], dma=R_u2, rearr="(m p) c -> p m c")
            cmk_, R_cmk = cmk[i % 2]
            bon_, R_bon = bon[i % 2]
            op("gpsimd", lambda e, i=i, cmk_=cmk_: e.dma_start(out=cmk_, in_=cmpmask_d[i].rearrange("t p q -> p t q")), wr=[R_cmk], dma=R_cmk)
            op("sync", lambda e, i=i, bon_=bon_: e.dma_start(out=bon_, in_=bonus_d[i]), wr=[R_bon], dma=R_bon)
            norm_mod(w, x_ap, R_x, 128, *rows2["A2"], *rows2["B2"], 0)
            pq, R_pq = ps[7]
            for c in range(4):
                for k in range(8):
                    op("tensor", lambda e, c=c, k=k: e.matmul(out=pq[:, c * 128:(c + 1) * 128], lhsT=wq[:, k, c, :], rhs=w.hT[:, k, 0:128],
                                                               start=(k == 0), stop=(k == 7)), rd=[R_wq, w.R_hT], wr=[R_pq])
            op("scalar", lambda e: e.activation(out=qT_sb.rearrange("p a b -> p (a b)"), in_=pq[:, :], func=AF.Copy), rd=[R_pq], wr=[R_qT])
            pg_, R_pg = ps[6]
            for k in range(8):
                op("tensor", lambda e, k=k: e.matmul(out=pg_[:, 0:24], lhsT=w.hT[:, k, 0:128], rhs=wg[:, k, :], start=(k == 0), stop=(k == 7)),
                   rd=[R_wg, w.R_hT], wr=[R_pg])
            op("scalar", lambda e: e.activation(out=gates, in_=pg_[:, 0:24], func=AF.Sigmoid), rd=[R_pg], wr=[R_gates])
            for g in range(2):
                gs = slice(64 * g, 64 * g + 64)
                q_rhs = qT_sb[gs, :, :].rearrange("p a b -> p (a b)")
                pO, R_pO = ps[4 + g]
                pI, R_pI = ps[6]
                tl = [dict(kT=kcT_sb[gs, jt * 128:(jt + 1) * 128], Rk=R_kcT, v=vc_sb[:, jt, g, :], Rv=R_vc, mul=(cmk_[:, jt, :], R_cmk), jt=jt)
                      for jt in range(2)]
                attend(g, 128, q_rhs, R_qT, tl, pO, R_pO, pI, R_pI)
                finish_branch(g, 128, 0, pO, R_pO, gates, True)
                select_blocks(g, 128, pI, R_pI, bon_, R_bon, 16, 2 * nkt)
                nx, R_nx = nexp[g]
                tl = []
                for kt in range(nkt):
                    d = dict(kT=ksT_buf[gs, kt * 128:(kt + 1) * 128], Rk=R_ksb[0], v=vs_buf[:, kt, 65 * g:65 * g + 65], Rv=R_vsb[0],
                             add=(nx[:, kt * 128:(kt + 1) * 128], R_nx))
                    if kt >= NGRP * i:
                        d["mul"] = (cm_sb[:, kt - NGRP * i, :], R_cm)
                    tl.append(d)
                attend(g, 128, q_rhs, R_qT, tl, pO, R_pO)
                finish_branch(g, 128, 1, pO, R_pO, gates, False)
                tl = []
                for m in range(5):
                    d = dict(kT=kw_[gs, m * 128:(m + 1) * 128], Rk=R_kw, v=vw_[:, m, 65 * g:65 * g + 65], Rv=R_vw)
                    if kw5 < 4 or m in (0, 4):
                        d["mul"] = (wm_sb[:, kw5 * 5 + m, :], R_wm)
                    tl.append(d)
                attend(g, 128, q_rhs, R_qT, tl, pO, R_pO)
                finish_branch(g, 128, 2, pO, R_pO, gates, False)
            op("scalar", lambda e: e.activation(out=onsab, in_=onsa, func=AF.Copy), rd=[R_onsa], wr=[R_onsab])
            pt_, R_pt = ps[0]
            for c in range(4):
                op("tensor", lambda e, c=c: e.matmul(out=pt_[:, c * 128:(c + 1) * 128], lhsT=onsab[:, c * 128:(c + 1) * 128], rhs=ident_b,
                                                      start=True, stop=True), rd=[R_onsab, R_identb], wr=[R_pt])
            op("vector", lambda e: e.tensor_copy(out=onsaT.rearrange("p a b -> p (a b)"), in_=pt_[:, :]), rd=[R_pt], wr=[R_onsaT])
            pd, R_pd = ps[1]
            for gi in range(4):
                for wh in range(2):
                    op("tensor", lambda e, gi=gi, wh=wh: e.matmul(out=pd[:, gi * 128:(gi + 1) * 128], lhsT=u2_[:, wh, gi * 128:(gi + 1) * 128],
                                                                   rhs=wp_sb[:, (kind * 4 + gi) * 2 + wh, :], start=(wh == 0), stop=(wh == 1)),
                       rd=[R_u2, R_wp], wr=[R_pd])
            op("vector", lambda e: e.tensor_copy(out=dT_sb.rearrange("p a b -> p (a b)"), in_=pd[:, :]), rd=[R_pd], wr=[R_dT])
            pyp, R_pyp = ps[7]
            for gi in range(4):
                op("tensor", lambda e, gi=gi: e.matmul(out=pyp[:, gi * 128:(gi + 1) * 128], lhsT=pw[:, gi, :], rhs=dT_sb[:, gi, :], start=True, stop=True),
                   rd=[R_pw, R_dT], wr=[R_pyp])
            for gi in range(4):
                op("scalar", lambda e, gi=gi: e.activation(out=ypT[:, gi, :], in_=pyp[:, gi * 128:(gi + 1) * 128], func=AF.Identity, scale=pscl[:, gi:gi + 1]),
                   rd=[R_pyp, R_pscl], wr=[R_ypT])
            py = [ps[2], ps[3]]
            for half in range(2):
                pyh, R_pyh = py[half]
                for c in range(8):
                    lh = onsaT[:, c, :] if c < 4 else ypT[:, c - 4, :]
                    Rl = R_onsaT if c < 4 else R_ypT
                    op("tensor", lambda e, c=c, half=half, pyh=pyh, lh=lh: e.matmul(out=pyh[:, :], lhsT=lh, rhs=wout[:, c, half * 512:(half + 1) * 512],
                                                                                     start=(c == 0), stop=(c == 7)), rd=[Rl, R_wout], wr=[R_pyh])
            post_residual(w, py, 128, x_ap, R_x, *rows2["G2"])
            op("sync", lambda e, i=i, x_ap=x_ap: e.dma_start(out=x2s[i * 128:(i + 1) * 128, :], in_=x_ap), rd=[R_x], wr=[R_x2s], dma=R_x)
        new_epoch()
        sc.new_sem_epoch()
        ar.reset(base_mark)
        w = alloc_work()
        R_rowsS = Res("rowsS")
        rowsS = {nm: ar.alloc(nm, [D], F32, res=R_rowsS) for nm in ("A2", "B2", "G2")}
        R_p3a = Res("p3a")
        wq, R_wq = ar.alloc("wq", [8, 4, 128], BF16, res=R_p3a)
        for h in range(8):
            op("gpsimd", lambda e: e.dma_start(out=wq[:, :, h % 4, (h // 4) * 64:(h // 4) * 64 + 64], in_=w_in_v[:, :, h * 64:(h + 1) * 64]),
               wr=[R_wq], dma=R_wq)
        wkv, R_wkv = ar.alloc("wkv", [8, 792], BF16, res=R_p3a)
        op("gpsimd", lambda e: e.dma_start(out=wkv, in_=w_in_v[:, :, 512:1304]), wr=[R_wkv], dma=R_wkv)
        R_sc = Res("sconst")
        pt_sb = stack.enter_context(nc.sbuf_tensor("pt_sb", [NSEQ, 64], I32))
        op("sync", lambda e: e.dma_start(out=pt_sb[:], in_=pt_d[:, :]), wr=[R_sc], dma=R_sc)
        bonus_s, _ = ar.alloc("bonus_s", [128], F32)
        op("sync", lambda e: e.dma_start(out=bonus_s[0:4, :], in_=bonus_s_d[:, :]), wr=[R_sc], dma=R_sc)
        causal4, _ = ar.alloc("causal4", [4], BF16)
        op("gpsimd", lambda e: e.dma_start(out=causal4[0:4, :], in_=causal4_d[:, :]), wr=[R_sc], dma=R_sc)
        wms0, _ = ar.alloc("wms0", [4], BF16)
        op("gpsimd", lambda e: e.dma_start(out=wms0, in_=wms0_d[:, :]), wr=[R_sc], dma=R_sc)
        irep4, R_irep4 = ar.alloc("irep4", [4, 4], BF16)
        for h in range(4):
            op("vector", lambda e: e.tensor_copy(out=irep4[0:4, h, :], in_=ident_f[0:4, 0:4]), rd=[R_identf], wr=[R_irep4])
        PTb = [ar.alloc("PT%d" % i, [512], BF16) for i in range(4)]
        nx1, R_nx1 = ar.alloc("nexp", [S], BF16)
        nexp = [(nx1, R_nx1), (nx1, R_nx1)]
        rden, R_rden = ar.alloc("rden", [4], F32)
        coef, R_coef = ar.alloc("coef", [4], F32)
        tmpo, R_tmpo = ar.alloc("tmpo", [4, 64], F32)
        onsa, R_onsa = ar.alloc("onsa", [512], F32)
        onsab, R_onsab = ar.alloc("onsab", [512], BF16)
        tmpi, R_tmpi = ar.alloc("tmpi", [4, 128], F32)
        vals, R_vals = ar.alloc("vals", [128], F32)
        vals2, R_vals2 = ar.alloc("vals2", [128], F32)
        m8, R_m8 = ar.alloc("m8", [16], F32)
        selm, R_selm = ar.alloc("selm", [128], F32)
        negs, R_negs = ar.alloc("negs", [128], BF16)
        gates, R_gates = ar.alloc("gates4", [24], F32)
        x1g, R_x1g = ar.alloc("x1g", [D], F32)
        qs_sb, R_qs = ar.alloc("qs", [16, 4, 4], BF16)
        kTn, R_kTn = ar.alloc("kTn", [2, 64], BF16)
        zn, R_zn = ar.alloc("zn", [792], F32)
        vnew, R_vnew = ar.alloc("vnew", [2, 2, 65], BF16)
        op("vector", lambda e: e.memset(vnew, 1.0), wr=[R_vnew])
        kcT_q, R_kcTq = ar.alloc("kcTq", [256], BF16)
        vcT_q, R_vcTq = ar.alloc("vcTq", [256], BF16)
        vc_q, R_vcq = ar.alloc("vcq", [2, 2, 65], BF16)
        op("vector", lambda e: e.memset(vc_q, 1.0), wr=[R_vcq])
        wst = [ar.alloc("wst%d" % i, [4, 128], F32) for i in range(2)]
        wkb, R_wkb = ar.alloc("wkb", [4, 128], BF16)
        kwT_q, R_kwTq = ar.alloc("kwTq", [512], BF16)
        vwq, R_vwq = ar.alloc("vwq", [4, 2, 65], BF16)
        op("vector", lambda e: e.memset(vwq, 1.0), wr=[R_vwq])
        onsaT_s, R_onsaTs = ar.alloc("onsaTs", [4, 64], BF16)
        mark_loop = ar.mark()
        stg = [ar.alloc("stg%d" % i, [64, 128], F32) for i in range(1)]
        pgb, R_pgb = ar.alloc("pgb", [64, 128], BF16)
        ksT_q, R_ksTq = ar.alloc("ksTq", [S], BF16)
        vpb, R_vpb = ar.alloc("vpb", [64, 2, 65], BF16)
        op("vector", lambda e: e.memset(vpb, 1.0), wr=[R_vpb])
        stg_ctr = [0]

        R_stg8 = [Res("stg%d" % i) for i in range(16)]

        def load_pages(ci, s_glob):
            st_, R_st0 = stg[0]
            R_dm = R_stg8[ci * 4 + (s_glob % 4)]
            stg_ctr[0] += 1
            for pg in range(64):
                dyn_ctr[0] += 1
                nm = "pr%d" % dyn_ctr[0]
                out_ap = st_[:, pg, :]
                idx_ap = pt_sb[s_glob:s_glob + 1, pg:pg + 1]
                base = caches[ci]

                def fn(e, nm=nm, out_ap=out_ap, idx_ap=idx_ap, base=base):
                    r = e.alloc_register(nm)
                    e.reg_load(r, idx_ap)
                    v = e.snap(r, donate=True, min_val=0, max_val=NPHYS - 1)
                    e1 = v * 128
                    src = base[ds(e1, 128), :]
                    ins = e.dma_start(out=out_ap, in_=src)
                    vc = e.get_value_cache()
                    seen = set()
                    for ex in (e1, src.offset, src.offset * 4):
                        try:
                            al = vc.lookup(ex)
                        except Exception:
                            al = None
                        if al is not None and al.val.name not in seen and al.val.name != r.name:
                            seen.add(al.val.name)
                            e.free_register(al.val)
                    e.free_register(r)
                    return ins
                sc.op(("sync", "gpsimd")[pg % 2], fn, rd=[R_sc], wr=[R_st0], dma=R_dm, deferred=True, indep=(pg > 1))
            return st_, R_st0

        def cols4(ap2d_col):
            return bass.AP(tensor=ap2d_col.tensor, offset=ap2d_col.offset, ap=[list(ap2d_col.ap[0]), [16, 4]])

        cast_ctr = [0]

        def cast_op(out_ap, in_ap, rd, wr):
            cast_ctr[0] += 1
            if cast_ctr[0] % 2:
                op("scalar", lambda e: e.activation(out=out_ap, in_=in_ap, func=AF.Copy), rd=rd, wr=wr)
            else:
                op("vector", lambda e: e.tensor_copy(out=out_ap, in_=in_ap), rd=rd, wr=wr)

        R_p3b = Res("p3b")
        R_psclS = Res("psclS")
        for sg in range(NSG):
            load_row(*rowsS["A2"], 1 + sg, 4, 2, "A", w.tmpg, w.R_tmpg)
            load_row(*rowsS["B2"], 1 + sg, 3, 2, "B", w.tmpg, w.R_tmpg)
            load_row(*rowsS["G2"], 1 + sg, 5, 3, "G1", w.tmpg, w.R_tmpg)
            op("sync", lambda e: e.dma_start(out=x1g[0:64, :], in_=x1s[S + 64 * sg:S + 64 * sg + 64, :]), rd=[R_x1s], wr=[R_x1g], dma=R_x1g)
            norm_mod(w, x1g[0:64, :], R_x1g, 64, *rowsS["A2"], *rowsS["B2"], 0)
            pq, R_pq = ps[7]
            for c in range(4):
                for k in range(8):
                    op("tensor", lambda e: e.matmul(out=pq[:, c * 64:(c + 1) * 64], lhsT=wq[:, k, c, :], rhs=w.hT[:, k, 0:64],
                                                     start=(k == 0), stop=(k == 7)), rd=[R_wq, w.R_hT], wr=[R_pq])
            op("scalar", lambda e: e.activation(out=qs_sb, in_=pq[:, 0:256].rearrange("p (c t s) -> p s c t", c=4, t=4), func=AF.Copy),
               rd=[R_pq], wr=[R_qs])
            pkn, R_pkn = ps[6]
            for n, c_lo in enumerate((256, 512)):
                for k in range(8):
                    op("tensor", lambda e: e.matmul(out=pkn[:, n * 64:(n + 1) * 64], lhsT=wkv[:, k, c_lo:c_lo + 128], rhs=w.hT[:, k, 0:64],
                                                     start=(k == 0), stop=(k == 7)), rd=[R_wkv, w.R_hT], wr=[R_pkn])
            op("vector", lambda e: e.tensor_copy(out=kTn.rearrange("p a b -> p (a b)"), in_=pkn[:, 0:128]), rd=[R_pkn], wr=[R_kTn])
            po, R_po = ps[0]
            po_v = po[:, 0:256].rearrange("p (c t) -> p c t", c=4)
            for sl in range(16):
                s_glob = 16 * sg + sl
                pz, R_pz = ps[5]
                pz2, R_pz2 = ps[4]
                for k in range(8):
                    op("tensor", lambda e: e.matmul(out=pz[0:4, :], lhsT=cols4(w.hT[:, k, sl:sl + 1]), rhs=wkv[:, k, 0:512],
                                                     start=(k == 0), stop=(k == 7)), rd=[w.R_hT, R_wkv], wr=[R_pz])
                for k in range(8):
                    op("tensor", lambda e: e.matmul(out=pz2[0:4, 0:280], lhsT=cols4(w.hT[:, k, sl:sl + 1]), rhs=wkv[:, k, 512:792],
                                                     start=(k == 0), stop=(k == 7)), rd=[w.R_hT, R_wkv], wr=[R_pz2])
                op("scalar", lambda e: e.activation(out=zn[0:4, 0:512], in_=pz[0:4, :], func=AF.Copy), rd=[R_pz], wr=[R_zn])
                op("scalar", lambda e: e.activation(out=zn[0:4, 512:768], in_=pz2[0:4, 0:256], func=AF.Copy), rd=[R_pz2], wr=[R_zn])
                op("scalar", lambda e: e.activation(out=gates[0:4, :], in_=pz2[0:4, 256:280], func=AF.Sigmoid), rd=[R_pz2], wr=[R_gates])
                op("vector", lambda e: e.tensor_copy(out=vnew[0:4, 0, :, 0:64], in_=zn[0:4, 384:512].rearrange("p (g d) -> p g d", g=2)),
                   rd=[R_zn], wr=[R_vnew])
                op("vector", lambda e: e.tensor_copy(out=vnew[0:4, 1, :, 0:64], in_=zn[0:4, 640:768].rearrange("p (g d) -> p g d", g=2)),
                   rd=[R_zn], wr=[R_vnew])
                for ci, (dstT, R_dT_, W4) in enumerate(((kcT_q, R_kcTq, W4k), (vcT_q, R_vcTq, W4v))):
                    st_, R_st = load_pages(ci, s_glob)
                    cast_op(pgb, st_, [R_st], [R_pgb])
                    pc_, R_pc_ = ps[6]
                    for pg in range(64):
                        op("tensor", lambda e: e.matmul(out=pc_[:, 4 * pg:4 * pg + 4], lhsT=pgb[:, pg, :], rhs=W4, start=True, stop=True),
                           rd=[R_pgb, R_W4k, R_W4v], wr=[R_pc_])
                    op("vector", lambda e: e.tensor_copy(out=dstT, in_=pc_[:, 0:256]), rd=[R_pc_], wr=[R_dT_])
                for jt in range(2):
                    pv_, R_pv = ps[7]
                    op("tensor", lambda e: e.matmul(out=pv_[:, jt * 128:(jt + 1) * 128], lhsT=vcT_q[:, jt * 128:(jt + 1) * 128], rhs=ident_b,
                                                     start=True, stop=True), rd=[R_vcTq, R_identb], wr=[R_pv])
                op("vector", lambda e: e.tensor_copy(out=vc_q[:, :, :, 0:64], in_=ps[7][0][:, 0:256].rearrange("p (j g d) -> p j g d", j=2, g=2)),
                   rd=[ps[7][1]], wr=[R_vcq])
                st_, R_st = load_pages(2, s_glob)
                cast_op(pgb, st_, [R_st], [R_pgb])
                for q4 in range(16):
                    ptq, R_ptq = ps[1] if q4 % 2 == 0 else ps[7]
                    for pp in range(4):
                        pg = 4 * q4 + pp
                        op("tensor", lambda e: e.matmul(out=ptq[:, pp * 128:(pp + 1) * 128], lhsT=pgb[:, pg, :], rhs=ident_b, start=True, stop=True),
                           rd=[R_pgb, R_identb], wr=[R_ptq])
                    if q4 % 2 == 0:
                        op("scalar", lambda e: e.activation(out=ksT_q[:, q4 * 512:(q4 + 1) * 512], in_=ptq[:, :], func=AF.Copy), rd=[R_ptq], wr=[R_ksTq])
                    else:
                        op("vector", lambda e: e.tensor_copy(out=ksT_q[:, q4 * 512:(q4 + 1) * 512], in_=ptq[:, :]), rd=[R_ptq], wr=[R_ksTq])
                st_, R_st = load_pages(3, s_glob)
                cast_op(vpb[:, :, :, 0:64], st_.rearrange("p a (g d) -> p a g d", g=2), [R_st], [R_vpb])
                for n, st in enumerate((st_wk, st_wv)):
                    ws_, R_ws = wst[n]
                    op("sync", lambda e: e.dma_start(out=ws_, in_=st[s_glob].rearrange("(m p) c -> p m c", p=128)), wr=[R_ws], dma=R_ws)
                op("vector", lambda e: e.tensor_copy(out=wkb, in_=wst[0][0]), rd=[wst[0][1]], wr=[R_wkb])
                pw_, R_pw_ = ps[1]
                for m in range(4):
                    op("tensor", lambda e: e.matmul(out=pw_[:, m * 128:(m + 1) * 128], lhsT=wkb[:, m, :], rhs=ident_b, start=True, stop=True),
                       rd=[R_wkb, R_identb], wr=[R_pw_])
                op("scalar", lambda e: e.activation(out=kwT_q, in_=pw_[:, :], func=AF.Copy), rd=[R_pw_], wr=[R_kwTq])
                op("vector", lambda e: e.tensor_copy(out=vwq[:, :, :, 0:64], in_=wst[1][0].rearrange("p a (g d) -> p a g d", g=2)), rd=[wst[1][1]], wr=[R_vwq])
                for g in range(2):
                    gs = slice(64 * g, 64 * g + 64)
                    q_rhs = qs_sb[gs, sl, :, :].rearrange("p a b -> p (a b)")
                    pO, R_pO = ps[4 + g]
                    pI, R_pI = ps[6]
                    ir4 = (irep4[0:4, :, :].rearrange("p a b -> p (a b)"), R_irep4)
                    tl = [dict(kT=kcT_q[gs, jt * 128:(jt + 1) * 128], Rk=R_kcTq, v=vc_q[:, jt, g, :], Rv=R_vcq, jt=jt) for jt in range(2)]
                    attend(g, 4, q_rhs, R_qs, tl, pO, R_pO, pI, R_pI, irep=ir4)
                    finish_branch(g, 4, 0, pO, R_pO, gates[0:4, :], True)
                    select_blocks(g, 4, pI, R_pI, bonus_s[0:4, :], R_sc, 15, 128)
                    nx, R_nx = nexp[g]
                    tl = [dict(kT=ksT_q[gs, pg * 128:(pg + 1) * 128], Rk=R_ksTq, v=vpb[:, pg, g, :], Rv=R_vpb,
                               add=(nx[0:4, pg * 128:(pg + 1) * 128], R_nx)) for pg in range(64)]
                    tl.append(dict(kT=cols4(kTn[gs, 0, sl:sl + 1]), Rk=R_kTn, v=vnew[0:4, 0, g, :], Rv=R_vnew, mul=(causal4[0:4, :], R_sc), nk=4))
                    attend(g, 4, q_rhs, R_qs, tl, pO, R_pO, irep=ir4)
                    finish_branch(g, 4, 1, pO, R_pO, gates[0:4, :], False)
                    tl = []
                    for m in range(4):
                        d = dict(kT=kwT_q[gs, m * 128:(m + 1) * 128], Rk=R_kwTq, v=vwq[:, m, g, :], Rv=R_vwq)
                        if m == 0:
                            d["mul"] = (wms0, R_sc)
                        tl.append(d)
                    tl.append(dict(kT=cols4(kTn[gs, 1, sl:sl + 1]), Rk=R_kTn, v=vnew[0:4, 1, g, :], Rv=R_vnew, mul=(causal4[0:4, :], R_sc), nk=4))
                    attend(g, 4, q_rhs, R_qs, tl, pO, R_pO, irep=ir4)
                    finish_branch(g, 4, 2, pO, R_pO, gates[0:4, :], False)
                op("scalar", lambda e: e.activation(out=onsab[0:4, :], in_=onsa[0:4, :], func=AF.Copy), rd=[R_onsa], wr=[R_onsab])
                for c in range(4):
                    op("tensor", lambda e: e.matmul(out=cols4(po_v[:, c, sl:sl + 1]), lhsT=onsab[0:4, c * 128:(c + 1) * 128], rhs=ident_b[0:4, 0:4],
                                                     start=True, stop=True), rd=[R_onsab, R_identb], wr=[R_po])
            op("vector", lambda e: e.tensor_copy(out=onsaT_s.rearrange("p a b -> p (a b)"), in_=po[:, 0:256]), rd=[R_po], wr=[R_onsaTs])
            new_epoch()
            ar.reset(mark_loop)
            wu, R_wu = ar.alloc("wu", [8, 512], BF16, res=R_p3b)
            op("gpsimd", lambda e: e.dma_start(out=wu, in_=w_in_v[:, :, 1304:1816]), wr=[R_wu], dma=R_wu)
            wout, R_wout = ar.alloc("wout", [8, D], BF16, res=R_p3b)
            op("gpsimd", lambda e: e.dma_start(out=wout, in_=w_out.rearrange("(k p) n -> p k n", p=128)), wr=[R_wout], dma=R_wout)
            pw, R_pw = ar.alloc("pw", [4, 128], BF16, res=R_p3b)
            op("gpsimd", lambda e: e.dma_start(out=pw, in_=pool_w.rearrange("g c d -> c g d")), wr=[R_pw], dma=R_pw)
            wsab, R_wsab = ar.alloc("wsab", [2, 4, 64], BF16, res=R_p3b)
            op("gpsimd", lambda e: e.dma_start(out=wsab[0:120, 0, :, :], in_=wsa_d[:, :, :]), wr=[R_wsab], dma=R_wsab)
            op("gpsimd", lambda e: e.dma_start(out=wsab[0:120, 1, :, :], in_=wsb_d[:, :, :]), wr=[R_wsab], dma=R_wsab)
            wnb, R_wnb = ar.alloc("wnb", [4, 64], BF16, res=R_p3b)
            op("gpsimd", lambda e: e.dma_start(out=wnb[0:64, :, :], in_=wn_d[:, :, :]), wr=[R_wnb], dma=R_wnb)
            pscl, R_pscl = ar.alloc("pscl", [4], F32, res=R_psclS)
            for gi in range(4):
                op("sync", lambda e: e.dma_start(out=pscl[:, gi:gi + 1], in_=pool_scale[0:1, gi * 128:(gi + 1) * 128].rearrange("a d -> d a")),
                   wr=[R_pscl], dma=R_pscl)
            stpb, R_stpb = ar.alloc("stpb", [2, 512], BF16, res=R_p3b)
            for hf in range(2):
                op("gpsimd", lambda e: e.dma_start(out=stpb[0:120, hf, :], in_=st_pool[16 * sg + 8 * hf:16 * sg + 8 * hf + 8].rearrange("s r c -> (s r) c")),
                   wr=[R_stpb], dma=R_stpb)
            un, R_un = ar.alloc("un", [512], BF16)
            dTs, R_dTs = ar.alloc("dTs", [4, 64], BF16)
            ypTs, R_ypTs = ar.alloc("ypTs", [4, 64], BF16)
            pu_, R_pu_ = ps[5]
            for k in range(8):
                op("tensor", lambda e: e.matmul(out=pu_[0:64, :], lhsT=w.hT[:, k, 0:64], rhs=wu[:, k, :], start=(k == 0), stop=(k == 7)),
                   rd=[w.R_hT, R_wu], wr=[R_pu_])
            op("vector", lambda e: e.tensor_copy(out=un[0:64, :], in_=pu_[0:64, :]), rd=[R_pu_], wr=[R_un])
            pd, R_pd = ps[1]
            for gi in range(4):
                cs_ = slice(gi * 128, (gi + 1) * 128)
                op("tensor", lambda e: e.matmul(out=pd[:, gi * 64:(gi + 1) * 64], lhsT=stpb[0:120, 0, cs_], rhs=wsab[0:120, 0, gi, :], start=True, stop=False),
                   rd=[R_stpb, R_wsab], wr=[R_pd])
                op("tensor", lambda e: e.matmul(out=pd[:, gi * 64:(gi + 1) * 64], lhsT=stpb[0:120, 1, cs_], rhs=wsab[0:120, 1, gi, :], start=False, stop=False),
                   rd=[R_stpb, R_wsab], wr=[R_pd])
                op("tensor", lambda e: e.matmul(out=pd[:, gi * 64:(gi + 1) * 64], lhsT=un[0:64, cs_], rhs=wnb[0:64, gi, :], start=False, stop=True),
                   rd=[R_un, R_wnb], wr=[R_pd])
            op("vector", lambda e: e.tensor_copy(out=dTs.rearrange("p a b -> p (a b)"), in_=pd[:, 0:256]), rd=[R_pd], wr=[R_dTs])
            pyp, R_pyp = ps[7]
            for gi in range(4):
                op("tensor", lambda e: e.matmul(out=pyp[:, gi * 64:(gi + 1) * 64], lhsT=pw[:, gi, :], rhs=dTs[:, gi, :], start=True, stop=True),
                   rd=[R_pw, R_dTs], wr=[R_pyp])
            for gi in range(4):
                op("scalar", lambda e: e.activation(out=ypTs[:, gi, :], in_=pyp[:, gi * 64:(gi + 1) * 64], func=AF.Identity, scale=pscl[:, gi:gi + 1]),
                   rd=[R_pyp, R_pscl], wr=[R_ypTs])
            py = [ps[2], ps[3]]
            for half in range(2):
                pyh, R_pyh = py[half]
                for c in range(8):
                    lh = onsaT_s[:, c, :] if c < 4 else ypTs[:, c - 4, :]
                    Rl = R_onsaTs if c < 4 else R_ypTs
                    op("tensor", lambda e: e.matmul(out=pyh[0:64, :], lhsT=lh, rhs=wout[:, c, half * 512:(half + 1) * 512],
                                                     start=(c == 0), stop=(c == 7)), rd=[Rl, R_wout], wr=[R_pyh])
            post_residual(w, py, 64, x1g, R_x1g, *rowsS["G2"])
            op("sync", lambda e: e.dma_start(out=x2s[NOWN * 128 + 64 * sg:NOWN * 128 + 64 * sg + 64, :], in_=x1g[0:64, :]), rd=[R_x1g], wr=[R_x2s], dma=R_x1g)
            new_epoch()
            ar.reset(mark_loop)
            if sg + 1 < NSG:
                stg = [ar.alloc("stg%d" % i, [64, 128], F32, res=stg[i][1]) for i in range(1)]
                pgb, _ = ar.alloc("pgb", [64, 128], BF16, res=R_pgb)
                ksT_q, _ = ar.alloc("ksTq", [S], BF16, res=R_ksTq)
                vpb, _ = ar.alloc("vpb", [64, 2, 65], BF16, res=R_vpb)
                op("vector", lambda e: e.memset(vpb, 1.0), wr=[R_vpb])
        new_epoch()
        ar.reset(base_mark)

        wo2, R_wo2 = ar.alloc("wo2", [NF, D], BF16)
        op("gpsimd", lambda e: e.dma_start(out=wo2, in_=f2wo.rearrange("(f p) n -> p f n", p=128)), wr=[R_wo2], dma=R_wo2)
        w = alloc_work()
        f = alloc_ffn()
        R_rows3 = Res("rows3")
        rows3 = {nm: ar.alloc(nm, [D], F32, res=R_rows3) for nm in ("A3", "B3", "G3")}

        def load_rows3(kind):
            load_row(*rows3["A3"], kind, 7, 4, "A", w.tmpg, w.R_tmpg)
            load_row(*rows3["B3"], kind, 6, 4, "B", w.tmpg, w.R_tmpg)
            load_row(*rows3["G3"], kind, 8, 5, "G5", w.tmpg, w.R_tmpg)

        def x2load(row0, nt):
            return lambda x_ap, R_x: op("sync", lambda e: e.dma_start(out=x_ap[0:nt, :], in_=x2s[row0:row0 + nt, :]), rd=[R_x2s], wr=[R_x], dma=R_x)

        def post3(ti, x_ap, R_x, nt, c0, tag):
            op("sync", lambda e: e.dma_start(out=o_y[tag:tag + nt, :], in_=x_ap[0:nt, :]), rd=[R_x], wr=[R_oy], dma=R_x)

        load_rows3(0)
        for p in range(NOWN // 4):
            tiles = [(x2load(i * 128, 128), 128, i * 128) for i in range(4 * p, 4 * p + 4)]
            ffn_pass(w, f, tiles, f2wi_v, wo2, R_wo2, rows3["A3"], rows3["B3"], rows3["G3"], post3)
        for sg in range(NSG):
            new_epoch()
            load_rows3(1 + sg)
            ffn_pass(w, f, [(x2load(NOWN * 128 + 64 * sg, 64), 64, NOWN * 128 + 64 * sg)], f2wi_v, wo2, R_wo2, rows3["A3"], rows3["B3"], rows3["G3"], post3)
        sc.emit()
    return nc


_NC = {}


def _tables(r, ngrp, nown):
    key = np.arange(128)[:, None]
    q = np.arange(128)[None, :]
    cm = np.zeros((ngrp, 128, 128), np.float32)
    for dk in range(ngrp):
        if dk < r:
            cm[dk] = 1.0
        elif dk == r:
            cm[dk] = (key <= q)
    bonus = np.zeros((nown, 128, 128), np.float32)
    cmpmask = np.zeros((nown, 2, 128, 128), np.float32)
    blk = np.arange(128)[None, :]
    for i in range(nown):
        j = ngrp * i + r
        t = j * 128 + np.arange(128)[:, None]
        cur = t // 64
        forced = (blk == 0) | (blk == cur) | (blk == cur - 1)
        start_ok = blk * 64 <= t
        bonus[i] = np.where(forced, 1e4, np.where(start_ok, 0.0, -1e30))
        for jt in range(2):
            jc = jt * 128 + np.arange(128)[:, None]
            tq = j * 128 + np.arange(128)[None, :]
            cmpmask[i, jt] = ((jc + 1) * 32 - 1 <= tq)
    wm = np.zeros((5, 5, 128, 128), np.float32)
    for kw in range(5):
        j = kw + r if kw < 4 else 4 + r
        for m in range(5):
            if j - 4 + m < 0:
                continue
            if m == 0:
                wm[kw, m] = (key > q)
            elif m == 4:
                wm[kw, m] = (key <= q)
            else:
                wm[kw, m] = 1.0
    wp = np.zeros((2, 4, 2, 128, 128), np.float32)
    for kind in range(2):
        j = r if kind == 0 else ngrp + r
        for gi, wdw in enumerate((2, 4, 8, 16)):
            tpos = j * 128 + np.arange(128)[None, :]
            cnt = np.minimum(wdw, tpos + 1).astype(np.float32)
            for wh in range(2):
                spos = (j - 1 + wh) * 128 + np.arange(128)[:, None]
                inwin = (spos > tpos - wdw) & (spos <= tpos)
                wp[kind, gi, wh] = inwin / cnt - (spos == tpos)
    return cm, bonus, cmpmask, wm, wp


def _sample_tables():
    bonus_s = np.zeros((4, 128), np.float32)
    bonus_s[:, 0] = 1e4
    bonus_s[:, 127] = 1e4
    n = np.arange(4)[:, None]
    t = np.arange(4)[None, :]
    causal4 = (n <= t).astype(np.float32)
    wms0 = (np.arange(128)[:, None] > t).astype(np.float32)
    wsa = np.zeros((120, 4, 64), np.float32)
    wsb = np.zeros((120, 4, 64), np.float32)
    wn = np.zeros((64, 4, 64), np.float32)
    for gi, wdw in enumerate((2, 4, 8, 16)):
        for tt in range(4):
            for s in range(16):
                col = tt * 16 + s
                for rr in range(15):
                    if rr > 15 + tt - wdw:
                        if s < 8:
                            wsa[s * 15 + rr, gi, col] = 1.0 / wdw
                        else:
                            wsb[(s - 8) * 15 + rr, gi, col] = 1.0 / wdw
                for t2 in range(4):
                    val = (1.0 / wdw if (t2 <= tt and t2 > tt - wdw) else 0.0) - (1.0 if t2 == tt else 0.0)
                    wn[t2 * 16 + s, gi, col] = val
    return bonus_s, causal4, wms0, wsa, wsb, wn


def kernel(**inputs):
    f = lambda k: np.asarray(inputs[k])
    x_prompt, x_sample = f("x_prompt"), f("x_sample")
    DB = x_sample.shape[0]
    nseq = DB // NCORES
    nsg = nseq // 16
    nphys = f("cache_cmp_k").shape[1]
    key = (nseq, nphys)
    if key not in _NC:
        _NC[key] = build_nc(NSEQ=nseq, NPHYS=nphys)
    nc = _NC[key]
    ident = np.eye(128, dtype=np.float32)
    pair = (np.arange(128)[:, None] // 2 == np.arange(64)[None, :]).astype(np.float32)
    bmask = (np.arange(128)[:, None] // 32 == np.arange(4)[None, :]).astype(np.float32)
    bonus_s, causal4, wms0, wsa, wsb, wn = _sample_tables()
    caches = [np.ascontiguousarray(f(k)[0]).reshape(nphys * 128, 128) for k in ("cache_cmp_k", "cache_cmp_v", "cache_sel_k", "cache_sel_v")]
    in_maps = []
    for c in range(NCORES):
        b, r = c // NGRP, c % NGRP
        cm, bonus, cmpmask, wm, wp = _tables(r, NGRP, NOWN)
        sl = slice(nseq * c, nseq * (c + 1))
        xs_c = np.ascontiguousarray(x_sample[sl].reshape(nsg, 16, 4, D).transpose(0, 2, 1, 3).reshape(4 * nseq, D))
        c17 = np.concatenate([f("c_prompt")[b:b + 1], f("c_sample")[sl]], axis=0)
        rinfo = np.zeros((1, 8), np.int32)
        rinfo[0, 0] = r
        m = {
            "xp": np.ascontiguousarray(x_prompt[b]), "xs": xs_c, "c17": np.ascontiguousarray(c17),
            "w_ada": f("w_ada")[0], "b_ada": f("b_ada")[0][None, :], "gains": f("norm_gains")[0],
            "f1wi": f("ffn1_wi")[0], "f1wo": f("ffn1_wo")[0], "f2wi": f("ffn2_wi")[0], "f2wo": f("ffn2_wo")[0],
            "w_in": f("w_in")[0], "w_out": f("w_out")[0], "cmpw": f("cmp_w")[0],
            "pool_w": f("pool_w")[0], "pool_scale": f("pool_scale")[0][None, :],
            "ident": ident, "pair": pair, "bmask": bmask, "rinfo": rinfo,
            "cm": cm, "bonus": bonus, "cmpmask": cmpmask, "wm": wm, "wp": wp,
            "st_wk": np.ascontiguousarray(f("state_win_k")[0, sl].reshape(nseq, 512, 128)),
            "st_wv": np.ascontiguousarray(f("state_win_v")[0, sl].reshape(nseq, 512, 128)),
            "st_pool": np.ascontiguousarray(f("state_pool")[0, sl]),
            "pt": np.ascontiguousarray(f("page_table")[sl]).astype(np.int32),
            "c_cmp_k": caches[0], "c_cmp_v": caches[1], "c_sel_k": caches[2], "c_sel_v": caches[3],
            "bonus_s": bonus_s, "causal4": causal4, "wms0": wms0, "wsa": wsa, "wsb": wsb, "wn": wn,
        }
        in_maps.append(m)
    res = run_bass_kernel_spmd(nc, in_maps, core_ids=list(range(NCORES)))
    kernel.last = res
    y_p = np.zeros((2, S, D), np.float32)
    y_s = np.zeros((DB, 4, D), np.float32)
    kv_p = [np.zeros((1, 2, S, 2, 64), np.float32) for _ in range(4)]
    kv_s = [np.zeros((1, DB, 4, 2, 64), np.float32) for _ in range(4)]
    p_win = [np.zeros((1, 2, 512, 2, 64), np.float32) for _ in range(2)]
    p_pool = np.zeros((1, 2, 15, 512), np.float32)
    s_win = [np.zeros((1, DB, 512, 2, 64), np.float32) for _ in range(2)]
    s_pool = np.zeros((1, DB, 15, 512), np.float32)
    unperm = lambda a, last: a.reshape((nsg, 4, 16) + last).transpose(0, 2, 1, *range(3, 3 + len(last))).reshape((nseq, 4) + last)
    for c in range(NCORES):
        b, r = c // NGRP, c % NGRP
        sl = slice(nseq * c, nseq * (c + 1))
        rr = res.results[c]
        okv = np.asarray(rr["o_kv"])
        oy = np.asarray(rr["o_y"])
        y_p[b].reshape(NBLK, 128, D)[r::NGRP] = oy[:NOWN * 128].reshape(NOWN, 128, D)
        y_s[sl] = unperm(oy[NOWN * 128:], (D,))
        for n in range(4):
            kv_p[n][0, b].reshape(NBLK, 128, 2, 64)[r::NGRP] = okv[n, :S].reshape(NBLK, 128, 2, 64)[r::NGRP]
            kv_s[n][0, sl] = unperm(okv[n, S:], (2, 64))
        if r == NGRP - 1:
            for n in range(2):
                p_win[n][0, b] = okv[4 + n, S - 512:S].reshape(512, 2, 64)
            p_pool[0, b] = np.asarray(rr["o_u"])[S - 15:S]
        sw = np.asarray(rr["o_swin"])
        for n in range(2):
            s_win[n][0, sl] = sw[n].reshape(nseq, 512, 2, 64)
        s_pool[0, sl] = np.asarray(rr["o_spool"])
    return (y_p, y_s, *kv_p, *p_win, p_pool, *kv_s, *s_win, s_pool)
```
